# Optimizing a Trainium2 kernel written in Bass

```python
import math
import jax, jax.numpy as jnp
from jax import lax
import numpy as np

D_MODEL = 1024
BATCH = 32
SEQ = 256
DEPTH = 2
DEC_BATCH = 8
DEC_SEQ = 4096
PAST_LEN = 256

GRID_W = 64
D_MIX = D_MODEL
W_SSM = D_MIX // 4
W_SWA = D_MIX // 4
W_AX = D_MIX // 4
W_FNET = D_MIX - W_SSM - W_SWA - W_AX
HEAD_DIM = 64
N_Q_HEADS = W_SWA // HEAD_DIM
N_KV_HEADS = 2
Q_PER_KV = N_Q_HEADS // N_KV_HEADS
KV_W = N_KV_HEADS * HEAD_DIM
SSM_GROUP = 16
N_SSM_GROUPS = W_SSM // SSM_GROUP
SSM_STATE = 64
WINDOW = 128
Q_BLOCK = 128
ROPE_BASE = 10000.0
D_FF = 2816
N_MOD = 9
EPS = 1e-6
P_IN = W_SSM + W_SWA + 2 * KV_W + W_AX + 2 * KV_W + W_FNET

kernel_name = "hybrid_diffusion_parallel_heads_step"


def rmsnorm(x, g):
    xf = x.astype(jnp.float32)
    y = xf * lax.rsqrt(jnp.mean(xf * xf, axis=-1, keepdims=True) + EPS)
    return (y * g.astype(jnp.float32)).astype(x.dtype)


def swiglu(x, wg, wu, wd):
    return (jax.nn.silu(x @ wg) * (x @ wu)) @ wd


def modulation(cond, w_mod, b_mod):
    m = jax.nn.silu(cond) @ w_mod + b_mod
    return m.reshape(cond.shape[0], 1, N_MOD, D_MODEL)


def axial_rope_tables(n_tokens):
    rows = n_tokens // GRID_W
    row_id = jnp.repeat(jnp.arange(rows), GRID_W).astype(jnp.float32)
    col_id = jnp.tile(jnp.arange(GRID_W), rows).astype(jnp.float32)
    n_freq = HEAD_DIM // 4
    inv = ROPE_BASE ** (-jnp.arange(n_freq, dtype=jnp.float32) / n_freq)
    ang = jnp.concatenate([row_id[:, None] * inv, col_id[:, None] * inv], axis=-1)
    return jnp.cos(ang), jnp.sin(ang)


def apply_rope(x, cos, sin):
    half = HEAD_DIM // 2
    bshape = (cos.shape[0],) + (1,) * (x.ndim - 3) + (half,)
    c = cos.reshape(bshape).astype(x.dtype)
    s = sin.reshape(bshape).astype(x.dtype)
    x1, x2 = x[..., :half], x[..., half:]
    return jnp.concatenate([x1 * c - x2 * s, x2 * c + x1 * s], axis=-1)


def sweep_attention(q, k, v, sink=None, band=False, q_ctx=None, k_ctx=None, v_ctx=None):
    B, L = q.shape[0], q.shape[1]
    scale = HEAD_DIM ** -0.5
    if band:
        pad = ((0, 0), (Q_BLOCK, Q_BLOCK), (0, 0), (0, 0))
        k_src, v_src = jnp.pad(k, pad), jnp.pad(v, pad)

    def one_block(i):
        start = i * Q_BLOCK
        qb = lax.dynamic_slice_in_dim(q, start, Q_BLOCK, axis=1)
        if band:
            kb = lax.dynamic_slice_in_dim(k_src, start, 3 * Q_BLOCK, axis=1)
            vb = lax.dynamic_slice_in_dim(v_src, start, 3 * Q_BLOCK, axis=1)
        else:
            kb, vb = k, v
        logits = jnp.einsum('bqkgd,bskd->bkgqs', qb, kb).astype(jnp.float32) * scale
        if band:
            qpos = start + jnp.arange(Q_BLOCK)
            kpos = start - Q_BLOCK + jnp.arange(3 * Q_BLOCK)
            ok = (kpos[None, :] >= 0) & (kpos[None, :] < L) & (jnp.abs(qpos[:, None] - kpos[None, :]) <= WINDOW)
            logits = jnp.where(ok, logits, -jnp.inf)
        n_main = logits.shape[-1]
        parts = [logits]
        if q_ctx is not None:
            qc = lax.dynamic_slice_in_dim(q_ctx, start, Q_BLOCK, axis=1)
            parts.append(jnp.einsum('bqkgd,bskd->bkgqs', qc, k_ctx).astype(jnp.float32) * scale)
        if sink is not None:
            parts.append(jnp.broadcast_to(sink.astype(jnp.float32)[None, :, :, None, None], logits.shape[:-1] + (1,)))
        probs = jax.nn.softmax(jnp.concatenate(parts, axis=-1), axis=-1)
        out = jnp.einsum('bkgqs,bskd->bqkgd', probs[..., :n_main].astype(vb.dtype), vb)
        if q_ctx is not None:
            n_ctx = k_ctx.shape[1]
            out = out + jnp.einsum('bkgqs,bskd->bqkgd', probs[..., n_main:n_main + n_ctx].astype(v_ctx.dtype), v_ctx)
        return out

    blocks = lax.map(one_block, jnp.arange(L // Q_BLOCK))
    return jnp.moveaxis(blocks, 0, 1).reshape(B, L, q.shape[2], q.shape[3], HEAD_DIM)


def _linear_combine(e1, e2):
    a1, b1 = e1
    a2, b2 = e2
    return a1 * a2, a2 * b1 + b2


def ssm_mixer(u, lam_re, lam_im, b_re, b_im, c_re, c_im, log_dt, d_skip, w_glu, b_glu, s0=None):
    B, L, _ = u.shape
    uf = u.astype(jnp.float32).reshape(B, L, N_SSM_GROUPS, SSM_GROUP)
    lam = lax.complex(lam_re.astype(jnp.float32), lam_im.astype(jnp.float32))
    dt = jnp.exp(log_dt.astype(jnp.float32))[..., None]
    a = jnp.exp(lam * dt)
    b_bar = ((a - 1.0) / lam)[..., None] * lax.complex(b_re.astype(jnp.float32), b_im.astype(jnp.float32))
    c_re32, c_im32 = c_re.astype(jnp.float32), c_im.astype(jnp.float32)
    y = d_skip.astype(jnp.float32) * uf.reshape(B, L, W_SSM)
    finals = []
    for direction in range(2):
        bu = lax.complex(jnp.einsum('blgc,gpc->blgp', uf, jnp.real(b_bar[direction])),
                         jnp.einsum('blgc,gpc->blgp', uf, jnp.imag(b_bar[direction])))
        ad = a[direction]
        edge = 0 if direction == 0 else L - 1
        if s0 is not None:
            bu = bu.at[:, edge].add(ad * s0[:, direction])
        _, states = lax.associative_scan(_linear_combine, (jnp.broadcast_to(ad, bu.shape), bu),
                                         axis=1, reverse=(direction == 1))
        if s0 is None:
            finals.append(states[:, L - 1 - edge])
        y = y + (jnp.einsum('blgp,gcp->blgc', jnp.real(states), c_re32[direction])
                 - jnp.einsum('blgp,gcp->blgc', jnp.imag(states), c_im32[direction])).reshape(B, L, W_SSM)
    y = jax.nn.gelu(y)
    out = (y * jax.nn.sigmoid(y @ w_glu.astype(jnp.float32) + b_glu.astype(jnp.float32))).astype(u.dtype)
    if s0 is None:
        return out, jnp.stack(finals, axis=1)
    return out, None


def fourier_mixer(u, w, b):
    f = jnp.real(jnp.fft.fft2(u.astype(jnp.float32), axes=(1, 2), norm="ortho"))
    return f.astype(u.dtype) @ w + b


def token_mixing(x, p, ctx):
    B, L, _ = x.shape
    proj = x @ p["w_in"]
    sizes = [W_SSM, W_SWA, KV_W, KV_W, W_AX, KV_W, KV_W, W_FNET]
    points, acc = [], 0
    for s in sizes[:-1]:
        acc += s
        points.append(acc)
    u_ssm, q_s, k_s, v_s, q_a, k_a, v_a, u_f = jnp.split(proj, points, axis=-1)
    q_s = q_s.reshape(B, L, N_KV_HEADS, Q_PER_KV, HEAD_DIM)
    k_s = k_s.reshape(B, L, N_KV_HEADS, HEAD_DIM)
    v_s = v_s.reshape(B, L, N_KV_HEADS, HEAD_DIM)
    q_a = rmsnorm(q_a.reshape(B, L, N_KV_HEADS, Q_PER_KV, HEAD_DIM), p["ax_q_norm"])
    k_a = rmsnorm(k_a.reshape(B, L, N_KV_HEADS, HEAD_DIM), p["ax_k_norm"])
    v_a = v_a.reshape(B, L, N_KV_HEADS, HEAD_DIM)
    sink = p["swa_sink"].reshape(N_KV_HEADS, Q_PER_KV)
    ssm_params = (p["ssm_lambda_re"], p["ssm_lambda_im"], p["ssm_b_re"], p["ssm_b_im"],
                  p["ssm_c_re"], p["ssm_c_im"], p["ssm_log_dt"], p["ssm_d"], p["ssm_w_glu"], p["ssm_b_glu"])
    if ctx is None:
        out_a, s_fin = ssm_mixer(u_ssm, *ssm_params, s0=None)
        out_b = sweep_attention(q_s, k_s, v_s, sink=sink)
        out_c = sweep_attention(q_a, k_a, v_a)
        new_ctx = (jnp.stack([k_s, v_s], axis=1),
                   jnp.stack([k_a, v_a], axis=1),
                   jnp.stack([jnp.real(s_fin), jnp.imag(s_fin)], axis=-1))
    else:
        swa_kv, ax_kv, ssm_state = ctx
        s0 = lax.complex(ssm_state[..., 0].astype(jnp.float32), ssm_state[..., 1].astype(jnp.float32))
        out_a, _ = ssm_mixer(u_ssm, *ssm_params, s0=s0)
        cos, sin = axial_rope_tables(L)
        out_b = sweep_attention(apply_rope(q_s, cos, sin), apply_rope(k_s, cos, sin), v_s, sink=sink, band=True,
                                q_ctx=q_s, k_ctx=swa_kv[:, 0], v_ctx=swa_kv[:, 1])
        out_c = sweep_attention(apply_rope(q_a, cos, sin), apply_rope(k_a, cos, sin), v_a,
                                q_ctx=q_a, k_ctx=ax_kv[:, 0], v_ctx=ax_kv[:, 1])
        new_ctx = None
    out_d = fourier_mixer(u_f, p["fnet_w"], p["fnet_b"])
    merged = jnp.concatenate([out_a, out_b.reshape(B, L, W_SWA), out_c.reshape(B, L, W_AX), out_d], axis=-1)
    return merged @ p["w_out"], new_ctx


def trunk_layer(h, cond, p, ctx):
    m = modulation(cond, p["w_mod"], p["b_mod"])
    sh1, sc1, g1, sh2, sc2, g2, sh3, sc3, g3 = [m[:, :, j] for j in range(N_MOD)]
    x = rmsnorm(h, p["norm_ffn1"]) * (1 + sc1) + sh1
    h = h + 0.5 * g1 * swiglu(x, p["ffn1_w_gate"], p["ffn1_w_up"], p["ffn1_w_down"])
    x = rmsnorm(h, p["norm_mix"]) * (1 + sc2) + sh2
    mixed, new_ctx = token_mixing(x, p, ctx)
    h = h + g2 * mixed
    x = rmsnorm(h, p["norm_ffn2"]) * (1 + sc3) + sh3
    h = h + 0.5 * g3 * swiglu(x, p["ffn2_w_gate"], p["ffn2_w_up"], p["ffn2_w_down"])
    return h, new_ctx


def setup_inputs(seed: int = 0) -> dict:
    key = jax.random.key(seed)
    ks = iter(jax.random.split(key, 48))

    def nrm(shape, scale=1.0):
        return jax.random.normal(next(ks), shape, jnp.float32) * scale

    def gain(shape):
        return 1.0 + nrm(shape, 0.05)

    lam_im = jnp.broadcast_to(math.pi * jnp.arange(SSM_STATE, dtype=jnp.float32),
                              (DEPTH, 2, N_SSM_GROUPS, SSM_STATE)) + nrm((DEPTH, 2, N_SSM_GROUPS, SSM_STATE), 0.01)
    log_dt = jax.random.uniform(next(ks), (DEPTH, 2, N_SSM_GROUPS), jnp.float32, math.log(1e-3), math.log(1e-1))
    return {
        "x_prompt": nrm((BATCH, SEQ, D_MODEL)),
        "x_sample": nrm((DEC_BATCH, DEC_SEQ, D_MODEL)),
        "cache_swa_kv": nrm((DEC_BATCH, DEPTH, 2, PAST_LEN, N_KV_HEADS, HEAD_DIM)),
        "cache_axial_kv": nrm((DEC_BATCH, DEPTH, 2, PAST_LEN, N_KV_HEADS, HEAD_DIM)),
        "state_ssm": nrm((DEC_BATCH, DEPTH, 2, N_SSM_GROUPS, SSM_STATE, 2), 0.1),
        "c": nrm((DEC_BATCH, D_MODEL)),
        "c_ctx": nrm((D_MODEL,)),
        "w_mod": nrm((DEPTH, D_MODEL, N_MOD * D_MODEL), 0.5 * D_MODEL ** -0.5),
        "b_mod": nrm((DEPTH, N_MOD * D_MODEL), 0.01),
        "norm_ffn1": gain((DEPTH, D_MODEL)),
        "norm_mix": gain((DEPTH, D_MODEL)),
        "norm_ffn2": gain((DEPTH, D_MODEL)),
        "ffn1_w_gate": nrm((DEPTH, D_MODEL, D_FF), D_MODEL ** -0.5),
        "ffn1_w_up": nrm((DEPTH, D_MODEL, D_FF), D_MODEL ** -0.5),
        "ffn1_w_down": nrm((DEPTH, D_FF, D_MODEL), D_FF ** -0.5),
        "ffn2_w_gate": nrm((DEPTH, D_MODEL, D_FF), D_MODEL ** -0.5),
        "ffn2_w_up": nrm((DEPTH, D_MODEL, D_FF), D_MODEL ** -0.5),
        "ffn2_w_down": nrm((DEPTH, D_FF, D_MODEL), D_FF ** -0.5),
        "w_in": nrm((DEPTH, D_MODEL, P_IN), D_MODEL ** -0.5),
        "w_out": nrm((DEPTH, D_MIX, D_MODEL), D_MIX ** -0.5),
        "ssm_lambda_re": -0.5 + nrm((DEPTH, 2, N_SSM_GROUPS, SSM_STATE), 0.02),
        "ssm_lambda_im": lam_im,
        "ssm_b_re": nrm((DEPTH, 2, N_SSM_GROUPS, SSM_STATE, SSM_GROUP), (2 * SSM_GROUP) ** -0.5),
        "ssm_b_im": nrm((DEPTH, 2, N_SSM_GROUPS, SSM_STATE, SSM_GROUP), (2 * SSM_GROUP) ** -0.5),
        "ssm_c_re": nrm((DEPTH, 2, N_SSM_GROUPS, SSM_GROUP, SSM_STATE), SSM_STATE ** -0.5),
        "ssm_c_im": nrm((DEPTH, 2, N_SSM_GROUPS, SSM_GROUP, SSM_STATE), SSM_STATE ** -0.5),
        "ssm_log_dt": log_dt,
        "ssm_d": nrm((DEPTH, W_SSM)),
        "ssm_w_glu": nrm((DEPTH, W_SSM, W_SSM), W_SSM ** -0.5),
        "ssm_b_glu": nrm((DEPTH, W_SSM), 0.01),
        "swa_sink": nrm((DEPTH, N_Q_HEADS)),
        "ax_q_norm": gain((DEPTH, HEAD_DIM)),
        "ax_k_norm": gain((DEPTH, HEAD_DIM)),
        "fnet_w": nrm((DEPTH, W_FNET, W_FNET), W_FNET ** -0.5),
        "fnet_b": nrm((DEPTH, W_FNET), 0.01),
        "final_norm": gain((D_MODEL,)),
    }


def reference(x_prompt, x_sample, cache_swa_kv, cache_axial_kv, state_ssm, c, c_ctx,
              w_mod, b_mod, norm_ffn1, norm_mix, norm_ffn2,
              ffn1_w_gate, ffn1_w_up, ffn1_w_down, ffn2_w_gate, ffn2_w_up, ffn2_w_down,
              w_in, w_out, ssm_lambda_re, ssm_lambda_im, ssm_b_re, ssm_b_im, ssm_c_re, ssm_c_im,
              ssm_log_dt, ssm_d, ssm_w_glu, ssm_b_glu, swa_sink, ax_q_norm, ax_k_norm,
              fnet_w, fnet_b, final_norm):
    h_ctx, h_lat = x_prompt, x_sample
    swa_list, ax_list, ssm_list = [], [], []
    for l in range(DEPTH):
        p = dict(w_mod=w_mod[l], b_mod=b_mod[l], norm_ffn1=norm_ffn1[l], norm_mix=norm_mix[l],
                 norm_ffn2=norm_ffn2[l], ffn1_w_gate=ffn1_w_gate[l], ffn1_w_up=ffn1_w_up[l],
                 ffn1_w_down=ffn1_w_down[l], ffn2_w_gate=ffn2_w_gate[l], ffn2_w_up=ffn2_w_up[l],
                 ffn2_w_down=ffn2_w_down[l], w_in=w_in[l], w_out=w_out[l],
                 ssm_lambda_re=ssm_lambda_re[l], ssm_lambda_im=ssm_lambda_im[l],
                 ssm_b_re=ssm_b_re[l], ssm_b_im=ssm_b_im[l], ssm_c_re=ssm_c_re[l], ssm_c_im=ssm_c_im[l],
                 ssm_log_dt=ssm_log_dt[l], ssm_d=ssm_d[l], ssm_w_glu=ssm_w_glu[l], ssm_b_glu=ssm_b_glu[l],
                 swa_sink=swa_sink[l], ax_q_norm=ax_q_norm[l], ax_k_norm=ax_k_norm[l],
                 fnet_w=fnet_w[l], fnet_b=fnet_b[l])
        h_ctx, (swa_kv, ax_kv, s_state) = trunk_layer(h_ctx, c_ctx[None, :], p, None)
        swa_list.append(swa_kv)
        ax_list.append(ax_kv)
        ssm_list.append(s_state)
        h_lat, _ = trunk_layer(h_lat, c, p, (cache_swa_kv[:, l], cache_axial_kv[:, l], state_ssm[:, l]))
    y_prompt = rmsnorm(h_ctx, final_norm)
    y_sample = rmsnorm(h_lat, final_norm)
    new_swa_kv = jnp.stack(swa_list, axis=1)
    new_axial_kv = jnp.stack(ax_list, axis=1)
    new_state_ssm = jnp.stack(ssm_list, axis=1)
    return (y_prompt, y_sample, new_swa_kv, new_axial_kv, new_state_ssm)
```

```python
import numpy as np
import ml_dtypes
import concourse.bass as bass
import concourse.mybir as mybir
from concourse.bass_utils import run_bass_kernel_spmd
from contextlib import ExitStack

F32 = mybir.dt.float32
BF16 = mybir.dt.bfloat16
AF = mybir.ActivationFunctionType
ALU = mybir.AluOpType

D = 1024
DFF = 2816
NF = 22
PIN = 1536
DEPTH = 2
NCORE = 8
L_CTX = 256
NSEQ = 4
TS = 512
EPS = 1e-6


class Sem:
    def __init__(self, h):
        self.h = h
        self.val = 0


class Buf:
    __slots__ = ("name", "last_w", "readers")

    def __init__(self, name=""):
        self.name = name
        self.last_w = None
        self.readers = []


class TB:
    def __init__(self, t, name=""):
        self.t = t
        self.b = Buf(name)


class Prog:
    SAME_ENGINE_SYNC = True
    NDMA = 6
    SEM_LIMIT = 24000

    def __init__(self, nc, stack):
        self.nc = nc
        self.stack = stack
        self.engs = {"pe": nc.tensor, "act": nc.scalar, "dve": nc.vector,
                     "pool": nc.gpsimd, "sp": nc.sync}
        self.nsem = 0
        self.esem = {k: self._newsem() for k in self.engs}
        self.waited = {k: {} for k in self.engs}
        self.lists = {k: [] for k in self.engs}
        self.dsems = {}
        self.drr = {}
        for q in ("sp", "pool", "act"):
            self.dsems[q] = [self._newsem() for i in range(self.NDMA)]
            self.drr[q] = 0
        self.ntile = 0

    def _newsem(self):
        self.nsem += 1
        return Sem(self.stack.enter_context(self.nc.semaphore("sem%d" % self.nsem)))

    def sb(self, shape, dtype, stack=None):
        self.ntile += 1
        return (stack or self.stack).enter_context(
            self.nc.sbuf_tensor("t%d" % self.ntile, list(shape), dtype))

    def ps(self, shape, dtype=F32, stack=None):
        self.ntile += 1
        return (stack or self.stack).enter_context(
            self.nc.psum_tensor("p%d" % self.ntile, list(shape), dtype))

    def _deps(self, eng, reads, writes, is_dma=False):
        deps = {}

        def add(sv):
            s, v = sv
            if deps.get(s, 0) < v:
                deps[s] = v
        for b in reads:
            if b.last_w is not None:
                add(b.last_w)
        for b in writes:
            if b.last_w is not None:
                add(b.last_w)
            for r in b.readers:
                add(r)
        own = self.esem.get(eng)
        waits = []
        w = self.waited[eng]
        for s, v in deps.items():
            if s is own and not is_dma and (eng == "pe" or not self.SAME_ENGINE_SYNC):
                continue
            if w.get(s, 0) >= v:
                continue
            w[s] = v
            waits.append((s.h, v))
        return waits

    def op(self, eng, fn, reads=(), writes=()):
        waits = self._deps(eng, reads, writes)
        own = self.esem[eng]
        if own.val >= self.SEM_LIMIT:
            own = self._newsem()
            self.esem[eng] = own
        own.val += 1
        val = own.val
        for b in reads:
            b.readers.append((own, val))
            if len(b.readers) > 24:
                last = {}
                for s, v in b.readers:
                    if last.get(s, 0) < v:
                        last[s] = v
                b.readers = list(last.items())
        for b in writes:
            b.last_w = (own, val)
            b.readers = []
        oh = own.h

        def run(e):
            for s, v in waits:
                e.wait_ge(s, v)
            fn(e).then_inc(oh, 1)
        self.lists[eng].append(run)

    def dma(self, q, out, in_, reads=(), writes=(), **kw):
        sems = self.dsems[q]
        i = self.drr[q] % len(sems)
        s = sems[i]
        if s.val >= self.SEM_LIMIT:
            old = s
            s = self._newsem()
            sems[i] = s
            w = self.waited[q]
            extra = [(old.h, old.val)] if w.get(old, 0) < old.val else []
            w[old] = old.val
        else:
            extra = []
        self.drr[q] += 1
        waits = extra + self._deps(q, reads, writes, is_dma=True)
        w = self.waited[q]
        if s.val > 0 and w.get(s, 0) < s.val:
            w[s] = s.val
            waits.append((s.h, s.val))
        s.val += 16
        val = s.val
        for b in reads:
            b.readers.append((s, val))
        for b in writes:
            b.last_w = (s, val)
            b.readers = []
        sh = s.h

        def run(e):
            for ss, v in waits:
                e.wait_ge(ss, v)
            e.dma_start(out=out, in_=in_, **kw).then_inc(sh, 16)
        self.lists[q].append(run)

    def barrier(self):
        allv = []
        for k, s in self.esem.items():
            if s.val > 0:
                allv.append(s)
        for q in self.dsems:
            for s in self.dsems[q]:
                if s.val > 0:
                    allv.append(s)
        for eng in self.engs:
            waits = []
            w = self.waited[eng]
            own = self.esem[eng]
            for s in allv:
                if s is own:
                    continue
                if w.get(s, 0) >= s.val:
                    continue
                w[s] = s.val
                waits.append((s.h, s.val))
            if waits:
                def run(e, waits=waits):
                    for s, v in waits:
                        e.wait_ge(s, v)
                self.lists[eng].append(run)

    def emit(self):
        nc = self.nc
        lists = self.lists
        with nc.Block() as block:
            @block.tensor
            def _(e):
                for f in lists["pe"]:
                    f(e)

            @block.scalar
            def _(e):
                for f in lists["act"]:
                    f(e)

            @block.vector
            def _(e):
                for f in lists["dve"]:
                    f(e)

            @block.gpsimd
            def _(e):
                for f in lists["pool"]:
                    f(e)

            @block.sync
            def _(e):
                for f in lists["sp"]:
                    f(e)


class Cfg:
    def __init__(self, l_lat=4096, mixers=("fnet", "attn", "ssm"), depth=DEPTH, debug=False):
        self.debug = debug
        self.l_lat = l_lat
        self.nt_lat = l_lat // TS
        self.nt = self.nt_lat + (NSEQ * L_CTX) // TS
        self.ntok = self.nt * TS
        self.mixers = mixers
        self.depth = depth


def build_program(cfg):
    nc = bass.Bass("TRN2", target_bir_lowering=False)
    L = cfg.l_lat
    NT = cfg.nt
    NTOK = cfg.ntok
    DP = cfg.depth

    def din(name, shape, dt=F32):
        return nc.dram_tensor(name, list(shape), dt, kind="ExternalInput").ap()

    def dout(name, shape, dt=F32):
        return nc.dram_tensor(name, list(shape), dt, kind="ExternalOutput").ap()

    def dscr(name, shape, dt=F32):
        return nc.dram_tensor(name, list(shape), dt, kind="Internal").ap()

    I = {}
    I["x_lat"] = din("x_lat", [L, D])
    I["x_ctx"] = din("x_ctx", [NSEQ * L_CTX, D])
    I["c_b"] = din("c_b", [8, 128])
    I["c_ctx"] = din("c_ctx", [8, 128])
    I["w_mod"] = din("w_mod", [DEPTH, D, 9 * D])
    I["b_mod"] = din("b_mod", [DEPTH, 72, 128])
    for n in ("norm_ffn1", "norm_mix", "norm_ffn2"):
        I[n] = din(n, [DEPTH, 8, 128])
    for n in ("ffn1_w_gate", "ffn1_w_up", "ffn2_w_gate", "ffn2_w_up"):
        I[n] = din(n, [DEPTH, D, DFF])
    for n in ("ffn1_w_down", "ffn2_w_down"):
        I[n] = din(n, [DEPTH, DFF, D])
    I["w_in"] = din("w_in", [DEPTH, D, PIN])
    I["w_out"] = din("w_out", [DEPTH, D, D])
    I["ssm_d"] = din("ssm_d", [DEPTH, 2, 128])
    I["ssm_b_glu"] = din("ssm_b_glu", [DEPTH, 2, 128])
    I["fnet_b"] = din("fnet_b", [DEPTH, 2, 128])
    I["ax_q_norm"] = din("ax_q_norm", [DEPTH, 1, 64])
    I["ax_k_norm"] = din("ax_k_norm", [DEPTH, 1, 64])
    I["final_norm"] = din("final_norm", [8, 128])
    I["ident"] = din("ident", [128, 128])
    I["cache_swa"] = din("cache_swa", [DEPTH, 2, L_CTX, 128])
    I["cache_ax"] = din("cache_ax", [DEPTH, 2, L_CTX, 128])
    I["ropeC"] = din("ropeC", [128, L])
    I["ropeS"] = din("ropeS", [128, L])
    I["rotm"] = din("rotm", [128, 128])
    I["masks"] = din("masks", [128, 256])
    I["swa_sink"] = din("swa_sink", [DEPTH, 4])
    I["cs256"] = din("cs256", [2, 128, 512])
    I["fn_ec"] = din("fn_ec", [L, 512])
    I["fn_nes"] = din("fn_nes", [L, 512])
    I["fn_c256s"] = din("fn_c256s", [256, 256])
    I["fn_ns256s"] = din("fn_ns256s", [256, 256])
    I["fn_phi"] = din("fn_phi", [128, 4, 8])
    I["fnet_w"] = din("fnet_w", [DEPTH, 256, 256])
    I["ssm_colp"] = din("ssm_colp", [DEPTH, 128, 768])
    I["ssm_rowp"] = din("ssm_rowp", [DEPTH, 128, 1536])
    I["ssm_bt"] = din("ssm_bt", [DEPTH, 128, 1024])
    I["ssm_ct"] = din("ssm_ct", [DEPTH, 128, 4096])
    I["ssm_dd"] = din("ssm_dd", [DEPTH, 128, 2, 128])
    I["ssm_st0"] = din("ssm_st0", [DEPTH, 128, 32])
    I["ssm_w_glu"] = din("ssm_w_glu", [DEPTH, 256, 256])

    O = {}
    O["y_lat"] = dout("y_lat", [L, D])
    O["y_ctx"] = dout("y_ctx", [NSEQ * L_CTX, D])
    O["o_swa"] = dout("o_swa", [NSEQ, DEPTH, 2, L_CTX, 128])
    O["o_ax"] = dout("o_ax", [NSEQ, DEPTH, 2, L_CTX, 128])
    O["o_st"] = dout("o_st", [128, NSEQ * DEPTH * 32])

    if cfg.debug:
        O["dbg_E"] = dout("dbg_E", [128, 2, 16, 256])
        O["dbg_rho"] = dout("dbg_rho", [128, 16])
        O["dbg_rhofull"] = dout("dbg_rhofull", [128, 256])
        O["dbg_sc"] = dout("dbg_sc", [128, 512])
        O["dbg_colp"] = dout("dbg_colp", [128, 768])
        O["dbg_bt"] = dout("dbg_bt", [128, 2, 512])
    hbuf = dscr("hbuf", [128, 8, NTOK])
    projT = dscr("projT", [128, 12, NTOK])
    if cfg.debug:
        mergT = dout("mergT", [128, 8, NTOK], BF16)
    else:
        mergT = dscr("mergT", [128, 8, NTOK], BF16)
    b_hbuf = [Buf() for _ in range(NT)]
    b_proj = [Buf() for _ in range(NT)]
    b_merg = [Buf() for _ in range(NT)]

    def x_rows(t, s):
        tok = t * TS + s * 128
        if tok < L:
            return I["x_lat"][tok:tok + 128, :]
        tok -= L
        return I["x_ctx"][tok:tok + 128, :]

    def y_rows(t, s):
        tok = t * TS + s * 128
        if tok < L:
            return O["y_lat"][tok:tok + 128, :]
        tok -= L
        return O["y_ctx"][tok:tok + 128, :]

    with ExitStack() as st:
        P = Prog(nc, st)

        ident = TB(P.sb([128, 128], F32))
        ones_bf = TB(P.sb([128, 128], BF16))
        bd64 = TB(P.sb([128, 128], BF16))
        epst = TB(P.sb([128, 1], F32))
        vec = [TB(P.sb([128, 128], F32)), TB(P.sb([128, 128], F32))]
        mod = [TB(P.sb([128, 2, 72], F32)) for _ in range(DEPTH)]
        coef = [[TB(P.sb([128, 6, 8], F32)) for _ in range(2)] for _ in range(DEPTH)]
        gps = [TB(P.ps([128, TS])) for _ in range(2)]
        ups = [TB(P.ps([128, TS])) for _ in range(2)]
        pd = P.ps([128, 2 * TS])
        b_pd = [Buf(), Buf()]
        ops_ = [TB(pd[:, 0:TS]), TB(pd[:, TS:2 * TS])]
        ops_[0].b = b_pd[0]
        ops_[1].b = b_pd[1]
        stat = TB(P.ps([128, TS]))
        pmisc = TB(P.ps([128, TS]))
        stout = TB(P.sb([128, NSEQ, DEPTH, 32], F32))
        P.op("dve", lambda e: e.memset(stout.t[:], 0.0), writes=[stout.b])

        P.op("dve", lambda e: e.memset(ones_bf.t[:], 1.0 / 1024.0), writes=[ones_bf.b])
        P.op("dve", lambda e: e.memset(bd64.t[:], 0.0), writes=[bd64.b])
        P.op("dve", lambda e: e.memset(bd64.t[0:64, 0:64], 1.0 / 64.0), writes=[bd64.b])
        P.op("dve", lambda e: e.memset(bd64.t[64:128, 64:128], 1.0 / 64.0), writes=[bd64.b])
        P.op("dve", lambda e: e.memset(epst.t[:], EPS), writes=[epst.b])
        P.dma("sp", ident.t[:], I["ident"], writes=[ident.b])

        class NS:
            pass

        def alloc_W(stk):
            W = NS()
            W.g = P.sb([128, 8, DFF], BF16, stk)
            W.u = P.sb([128, 8, DFF], BF16, stk)
            W.d = P.sb([128, NF, D], BF16, stk)
            W.bg = [Buf(), Buf()]
            W.bu = [Buf(), Buf()]
            W.bd = [Buf(), Buf()]
            return W

        def load_ffn(W, l, which):
            g = I["ffn%d_w_gate" % which][l].rearrange("(k p) n -> p k n", p=128)
            u = I["ffn%d_w_up" % which][l].rearrange("(k p) n -> p k n", p=128)
            d = I["ffn%d_w_down" % which][l].rearrange("(f p) n -> p f n", p=128)
            HC = 11 * 128
            for half in range(2):
                P.dma("pool", W.g[:, :, half * HC:(half + 1) * HC], g[:, :, half * HC:(half + 1) * HC],
                      writes=[W.bg[half]])
                P.dma("pool", W.u[:, :, half * HC:(half + 1) * HC], u[:, :, half * HC:(half + 1) * HC],
                      writes=[W.bu[half]])
                P.dma("pool", W.d[:, half * 11:(half + 1) * 11, :], d[:, half * 11:(half + 1) * 11, :],
                      writes=[W.bd[half]])

        def phase0():
            with ExitStack() as ph:
                stage = [P.sb([128, 128], F32, ph) for _ in range(2)]
                for l in range(DEPTH):
                    sg_ = stage[l]
                    bz = Buf()
                    P.op("dve", lambda e, sg_=sg_: e.memset(sg_[:], 0.0), writes=[bz])
                    bl = []
                    r = 0

                    def ld(dst, src):
                        b = Buf()
                        b.last_w = bz.last_w
                        P.dma("sp", dst, src, writes=[b])
                        bl.append(b)
                    for nm, nr in (("b_mod", 72), ("norm_ffn1", 8), ("norm_mix", 8), ("norm_ffn2", 8),
                                   ("ssm_d", 2), ("ssm_b_glu", 2), ("fnet_b", 2)):
                        ld(sg_[r:r + nr, :], I[nm][l])
                        r += nr
                    for j, nm in enumerate(("ax_q_norm", "ax_k_norm")):
                        for hh in range(2):
                            ld(sg_[102 + j:103 + j, hh * 64:(hh + 1) * 64], I[nm][l])
                    if l == 0:
                        ld(sg_[104:112, :], I["c_b"])
                        ld(sg_[112:120, :], I["c_ctx"])
                        ld(sg_[120:128, :], I["final_norm"])
                    P.op("pe", lambda e, sg_=sg_: e.transpose(out=pmisc.t[:, 0:128], in_=sg_[:], identity=ident.t[:]),
                         reads=bl + [ident.b], writes=[pmisc.b])
                    P.op("dve", lambda e, l=l: e.tensor_copy(out=vec[l].t[:], in_=pmisc.t[:, 0:128]),
                         reads=[pmisc.b], writes=[vec[l].b])
                scond = TB(P.sb([128, 8, 2], BF16, ph))
                P.op("act", lambda e: e.activation(out=scond.t[:, :, 0], in_=vec[0].t[:, 104:112], func=AF.Silu),
                     reads=[vec[0].b], writes=[scond.b])
                P.op("act", lambda e: e.activation(out=scond.t[:, :, 1], in_=vec[0].t[:, 112:120], func=AF.Silu),
                     reads=[vec[0].b], writes=[scond.b])
                wm = [TB(P.sb([128, 8, 512], F32, ph)) for _ in range(2)]
                wmb = [TB(P.sb([128, 8, 512], BF16, ph)) for _ in range(2)]
                nblk = 0
                for l in range(DP):
                    wsrc = I["w_mod"][l].rearrange("(k p) n -> p k n", p=128)
                    for cb in range(18):
                        w_ = wm[nblk % 2]
                        wb_ = wmb[nblk % 2]
                        ceng = ("act", "pool", "dve")[nblk % 3]
                        nblk += 1
                        P.dma("sp", w_.t[:], wsrc[:, :, cb * 512:(cb + 1) * 512], writes=[w_.b])
                        if ceng == "act":
                            P.op("act", lambda e, w_=w_, wb_=wb_: e.copy(out=wb_.t[:], in_=w_.t[:]), reads=[w_.b], writes=[wb_.b])
                        else:
                            P.op(ceng, lambda e, w_=w_, wb_=wb_: e.tensor_copy(out=wb_.t[:], in_=w_.t[:]), reads=[w_.b], writes=[wb_.b])
                        for j in range(4):
                            ch = cb * 4 + j
                            for k in range(8):
                                P.op("pe", lambda e, wb_=wb_, j=j, k=k, ch=ch: e.matmul(
                                    pmisc.t[:, ch * 2:ch * 2 + 2], lhsT=wb_.t[:, k, j * 128:(j + 1) * 128],
                                    rhs=scond.t[:, k, :], start=(k == 0), stop=(k == 7)),
                                    reads=[wb_.b, scond.b], writes=[pmisc.b])
                    pm = pmisc.t[:, 0:144].rearrange("p (c t) -> p c t", t=2)
                    for cond in range(2):
                        P.op("dve", lambda e, l=l, cond=cond, pm=pm: e.tensor_tensor(
                            out=mod[l].t[:, cond, :], in0=pm[:, :, cond], in1=vec[l].t[:, 0:72], op=ALU.add),
                            reads=[pmisc.b, vec[l].b], writes=[mod[l].b])
                        cf = coef[l][cond]
                        for j in range(3):
                            P.op("dve", lambda e, l=l, cond=cond, j=j, cf=cf: e.scalar_tensor_tensor(
                                out=cf.t[:, j, :], in0=mod[l].t[:, cond, (3 * j + 1) * 8:(3 * j + 2) * 8], scalar=1.0,
                                in1=vec[l].t[:, 72 + 8 * j:80 + 8 * j], op0=ALU.add, op1=ALU.mult),
                                reads=[mod[l].b, vec[l].b], writes=[cf.b])
                            gs = 0.5 if j != 1 else 1.0
                            P.op("dve", lambda e, l=l, cond=cond, j=j, cf=cf, gs=gs: e.tensor_scalar(
                                out=cf.t[:, 3 + j, :], in0=mod[l].t[:, cond, (3 * j + 2) * 8:(3 * j + 3) * 8],
                                scalar1=gs, scalar2=None, op0=ALU.mult),
                                reads=[mod[l].b], writes=[cf.b])
                P.barrier()

        def cond_of(t):
            return 0 if t < cfg.nt_lat else 1

        def alloc_common(ph, with_hff=False, with_xin=False):
            C = NS()
            C.h = P.sb([128, 8, TS], F32, ph)
            C.b_h = [Buf() for _ in range(8)]
            C.xT = P.sb([128, 8, TS], BF16, ph)
            C.b_x = [Buf() for _ in range(8)]
            C.sqb = [TB(P.sb([128, TS], BF16, ph)) for _ in range(2)]
            C.tmpb = [TB(P.sb([128, TS], F32, ph)) for _ in range(2)]
            C.rt = TB(P.sb([128, TS], F32, ph))
            C.rstd = TB(P.sb([128, TS], F32, ph))
            if with_hff:
                C.hff = P.sb([128, 11, TS], BF16, ph)
                C.b_hff = [Buf() for _ in range(11)]
                C.sgb = [TB(P.sb([128, TS], F32, ph)) for _ in range(2)]
            if with_xin:
                C.xin = [TB(P.sb([128, D], F32, ph)) for _ in range(2)]
                C.xin_ctr = 0
            return C

        def norm(C, A, S, coefb, out_fn):
            h, b_h = C.h, C.b_h
            for m in range(8):
                sq = C.sqb[m % 2]
                P.op("act", lambda e, sq=sq, m=m: e.activation(out=sq.t[:], in_=h[:, m, :], func=AF.Square),
                     reads=[b_h[m]], writes=[sq.b])
                P.op("pe", lambda e, sq=sq, m=m: e.matmul(stat.t[:], lhsT=ones_bf.t[:], rhs=sq.t[:],
                                                          start=(m == 0), stop=(m == 7)),
                     reads=[sq.b, ones_bf.b], writes=[stat.b])
            rt, rstd = C.rt, C.rstd
            P.op("act", lambda e: e.activation(out=rt.t[:], in_=stat.t[:], func=AF.Ln, bias=epst.t[:, 0:1], scale=1.0),
                 reads=[stat.b, epst.b], writes=[rt.b])
            P.op("act", lambda e: e.activation(out=rstd.t[:], in_=rt.t[:], func=AF.Exp, scale=-0.5),
                 reads=[rt.b], writes=[rstd.b])
            for m in range(8):
                tm = C.tmpb[m % 2]
                P.op("dve", lambda e, tm=tm, m=m: e.tensor_tensor(out=tm.t[:], in0=h[:, m, :], in1=rstd.t[:], op=ALU.mult),
                     reads=[b_h[m], rstd.b], writes=[tm.b])
                oap, ob = out_fn(m)
                if S is not None:
                    P.op("act", lambda e, tm=tm, m=m, oap=oap: e.activation(
                        out=oap, in_=tm.t[:], func=AF.Identity, scale=A(m), bias=S(m)),
                        reads=[tm.b] + coefb, writes=[ob])
                else:
                    P.op("act", lambda e, tm=tm, m=m, oap=oap: e.activation(
                        out=oap, in_=tm.t[:], func=AF.Identity, scale=A(m)),
                        reads=[tm.b] + coefb, writes=[ob])

        def norm_to_x(C, l, cond, j):
            cf = coef[l][cond]
            norm(C, lambda m: cf.t[:, j, m:m + 1],
                 lambda m: mod[l].t[:, cond, 3 * j * 8 + m:3 * j * 8 + m + 1],
                 [cf.b, mod[l].b],
                 lambda m: (C.xT[:, m, :], C.b_x[m]))

        def ffn(C, W, l, cond, j, mid_hook=None):
            cf = coef[l][cond]
            h, b_h, xT, b_x, hff, b_hff = C.h, C.b_h, C.xT, C.b_x, C.hff, C.b_hff
            for half in range(2):
                for f in range(11):
                    fc = half * 11 + f
                    gp, up = gps[f % 2], ups[f % 2]
                    for k in range(8):
                        P.op("pe", lambda e, gp=gp, k=k, fc=fc: e.matmul(
                            gp.t[:], lhsT=W.g[:, k, fc * 128:(fc + 1) * 128], rhs=xT[:, k, :],
                            start=(k == 0), stop=(k == 7)),
                            reads=[W.bg[half], b_x[k]], writes=[gp.b])
                    for k in range(8):
                        P.op("pe", lambda e, up=up, k=k, fc=fc: e.matmul(
                            up.t[:], lhsT=W.u[:, k, fc * 128:(fc + 1) * 128], rhs=xT[:, k, :],
                            start=(k == 0), stop=(k == 7)),
                            reads=[W.bu[half], b_x[k]], writes=[up.b])
                    sg = C.sgb[f % 2]
                    P.op("act", lambda e, sg=sg, gp=gp: e.activation(out=sg.t[:], in_=gp.t[:], func=AF.Silu),
                         reads=[gp.b], writes=[sg.b])
                    P.op("dve", lambda e, sg=sg, up=up, f=f: e.tensor_tensor(
                        out=hff[:, f, :], in0=sg.t[:], in1=up.t[:], op=ALU.mult),
                        reads=[sg.b, up.b], writes=[b_hff[f]])
                for m in range(8):
                    if half == 1 and m == 4 and mid_hook is not None:
                        mid_hook()
                    o_ = ops_[m % 2]
                    for f in range(11):
                        fc = half * 11 + f
                        P.op("pe", lambda e, o_=o_, f=f, fc=fc, m=m: e.matmul(
                            o_.t, lhsT=W.d[:, fc, m * 128:(m + 1) * 128], rhs=hff[:, f, :],
                            start=(f == 0), stop=(f == 10)),
                            reads=[W.bd[half], b_hff[f]], writes=[o_.b])
                    P.op("dve", lambda e, o_=o_, m=m, cf=cf, j=j: e.scalar_tensor_tensor(
                        out=h[:, m, :], in0=o_.t, scalar=cf.t[:, 3 + j, m:m + 1], in1=h[:, m, :],
                        op0=ALU.mult, op1=ALU.add),
                        reads=[o_.b, b_h[m], cf.b], writes=[b_h[m]])

        def load_h_x(C, t):
            for s in range(4):
                xi = C.xin[C.xin_ctr % 2]
                C.xin_ctr += 1
                P.dma("sp", xi.t[:], x_rows(t, s), writes=[xi.b])
                for m in range(8):
                    P.op("pe", lambda e, xi=xi, m=m: e.transpose(
                        out=pd[:, m * 128:(m + 1) * 128], in_=xi.t[:, m * 128:(m + 1) * 128], identity=ident.t[:]),
                        reads=[xi.b, ident.b], writes=[b_pd[m // 4]])
                P.op("act", lambda e, s=s: e.copy(out=C.h[:, :, s * 128:(s + 1) * 128],
                                                  in_=pd[:, :].rearrange("p (m t) -> p m t", m=8)),
                     reads=b_pd, writes=C.b_h)

        def load_h(C, t):
            P.dma("sp", C.h[:], hbuf[:, :, t * TS:(t + 1) * TS], reads=[b_hbuf[t]], writes=C.b_h)

        def store_h(C, t):
            P.dma("sp", hbuf[:, :, t * TS:(t + 1) * TS], C.h[:], reads=C.b_h, writes=[b_hbuf[t]])

        def final_out(C, t):
            fw = vec[0]
            h, b_h = C.h, C.b_h
            norm(C, lambda m: fw.t[:, 120 + m:121 + m], None, [fw.b], lambda m: (h[:, m, :], b_h[m]))
            for s in range(4):
                for hf in range(2):
                    xi = C.xin[C.xin_ctr[0] % 2]
                    C.xin_ctr[0] += 1
                    for m4 in range(4):
                        m = hf * 4 + m4
                        P.op("pe", lambda e, m=m, m4=m4, s=s, hf=hf: e.transpose(
                            out=pd[:, hf * 512 + m4 * 128:hf * 512 + (m4 + 1) * 128], in_=h[:, m, s * 128:(s + 1) * 128],
                            identity=ident.t[:]),
                            reads=[b_h[m], ident.b], writes=[b_pd[hf]])
                    if hf == 0:
                        P.op("act", lambda e, xi=xi, hf=hf: e.copy(out=xi.t[:], in_=pd[:, hf * 512:(hf + 1) * 512]),
                             reads=[b_pd[hf]], writes=[xi.b])
                    else:
                        P.op("dve", lambda e, xi=xi, hf=hf: e.tensor_copy(out=xi.t[:], in_=pd[:, hf * 512:(hf + 1) * 512]),
                             reads=[b_pd[hf]], writes=[xi.b])
                    P.dma("sp", y_rows(t, s)[:, hf * 512:(hf + 1) * 512], xi.t[:], reads=[xi.b])

        def phase_X0():
            with ExitStack() as ph:
                xin = [TB(P.sb([128, D], F32, ph)) for _ in range(3)]
                hh = [P.sb([128, 8, TS], F32, ph) for _ in range(2)]
                b_hh = [[Buf() for _ in range(8)] for _ in range(2)]
                ctr = 0
                for t in range(NT):
                    h = hh[t % 2]
                    bh = b_hh[t % 2]
                    for s in range(4):
                        xi = xin[ctr % 3]
                        ctr += 1
                        P.dma("sp", xi.t[:], x_rows(t, s), writes=[xi.b])
                        for m in range(8):
                            P.op("pe", lambda e, xi=xi, m=m: e.transpose(
                                out=pd[:, m * 128:(m + 1) * 128], in_=xi.t[:, m * 128:(m + 1) * 128], identity=ident.t[:]),
                                reads=[xi.b, ident.b], writes=[b_pd[m // 4]])
                        if s % 2 == 0:
                            P.op("act", lambda e, s=s, h=h: e.copy(out=h[:, :, s * 128:(s + 1) * 128],
                                                              in_=pd[:, :].rearrange("p (m t) -> p m t", m=8)),
                                 reads=b_pd, writes=bh)
                        else:
                            P.op("dve", lambda e, s=s, h=h: e.tensor_copy(out=h[:, :, s * 128:(s + 1) * 128],
                                                                     in_=pd[:, :].rearrange("p (m t) -> p m t", m=8)),
                                 reads=b_pd, writes=bh)
                    P.dma("act", hbuf[:, :, t * TS:(t + 1) * TS], h[:], reads=bh, writes=[b_hbuf[t]])
                P.barrier()

        def phase_F(W, l, which, last=False):
            with ExitStack() as ph:
                C0 = alloc_common(ph, with_hff=True)
                C0.xin = [TB(P.sb([128, TS], F32, ph)) for _ in range(2)]
                C0.xin_ctr = [0]
                C1 = NS()
                C1.__dict__.update(C0.__dict__)
                C1.h = P.sb([128, 8, TS], F32, ph)
                C1.b_h = [Buf() for _ in range(8)]
                Cs = [C0, C1]
                j = 0 if which == 1 else 2
                load_h(Cs[0], 0)
                norm_to_x(Cs[0], l, cond_of(0), j)
                for t in range(NT):
                    C = Cs[t % 2]
                    cond = cond_of(t)
                    hook = None
                    if t + 1 < NT:
                        Cn = Cs[(t + 1) % 2]
                        load_h(Cn, t + 1)
                        hook = (lambda Cn=Cn, t=t: norm_to_x(Cn, l, cond_of(t + 1), j))
                    ffn(C, W, l, cond, j, mid_hook=hook)
                    if last:
                        final_out(C, t)
                    else:
                        P.dma("act", hbuf[:, :, t * TS:(t + 1) * TS], C.h[:], reads=C.b_h, writes=[b_hbuf[t]])
                P.barrier()

        def phase_PJ(l):
            with ExitStack() as ph:
                Cs = [alloc_common(ph), alloc_common(ph)]
                win = P.sb([128, 8, PIN], BF16, ph)
                b_win = Buf()
                pjs = [P.sb([128, 12, TS], F32, ph) for _ in range(2)]
                b_pjs = [[Buf() for _ in range(12)] for _ in range(2)]
                kvo = [TB(P.sb([128, 512], F32, ph)) for _ in range(2)]
                kctr = 0
                sq3 = [TB(P.sb([128, TS], BF16, ph)) for _ in range(3)]
                rt3 = [TB(P.sb([128, TS], F32, ph)) for _ in range(3)]
                rs3 = [TB(P.sb([128, TS], F32, ph)) for _ in range(3)]
                tm3 = [TB(P.sb([128, TS], F32, ph)) for _ in range(3)]
                P.dma("pool", win[:], I["w_in"][l].rearrange("(k p) n -> p k n", p=128), writes=[b_win])
                load_h(Cs[0], 0)
                for t in range(NT):
                    cond = cond_of(t)
                    C = Cs[t % 2]
                    pj = pjs[t % 2]
                    b_pj = b_pjs[t % 2]
                    if t + 1 < NT:
                        load_h(Cs[(t + 1) % 2], t + 1)
                    norm_to_x(C, l, cond, 1)
                    for c in range(12):
                        pp = gps[c % 2] if (c // 2) % 2 == 0 else ups[c % 2]
                        for k in range(8):
                            P.op("pe", lambda e, pp=pp, k=k, c=c, C=C: e.matmul(
                                pp.t[:], lhsT=win[:, k, c * 128:(c + 1) * 128], rhs=C.xT[:, k, :],
                                start=(k == 0), stop=(k == 7)),
                                reads=[b_win, C.b_x[k]], writes=[pp.b])
                        if c % 2 == 0:
                            P.op("act", lambda e, pp=pp, c=c, pj=pj: e.copy(out=pj[:, c, :], in_=pp.t[:]),
                                 reads=[pp.b], writes=[b_pj[c]])
                        else:
                            P.op("dve", lambda e, pp=pp, c=c, pj=pj: e.tensor_copy(out=pj[:, c, :], in_=pp.t[:]),
                                 reads=[pp.b], writes=[b_pj[c]])
                    fx = [(6, 102, stat), (7, 102, pmisc), (8, 103, ops_[1])]
                    for i3, (c, gcol, bank) in enumerate(fx):
                        sq = sq3[i3]
                        P.op("act", lambda e, sq=sq, c=c, pj=pj: e.activation(out=sq.t[:], in_=pj[:, c, :], func=AF.Square),
                             reads=[b_pj[c]], writes=[sq.b])
                    for i3, (c, gcol, bank) in enumerate(fx):
                        sq = sq3[i3]
                        P.op("pe", lambda e, sq=sq, bank=bank: e.matmul(bank.t[:, 0:TS] if bank is not ops_[1] else bank.t, lhsT=bd64.t[:], rhs=sq.t[:], start=True, stop=True),
                             reads=[sq.b, bd64.b], writes=[bank.b])
                    for i3, (c, gcol, bank) in enumerate(fx):
                        rt_ = rt3[i3]
                        P.op("act", lambda e, rt_=rt_, bank=bank: e.activation(out=rt_.t[:], in_=bank.t[:, 0:TS] if bank is not ops_[1] else bank.t, func=AF.Ln, bias=epst.t[:, 0:1], scale=1.0),
                             reads=[bank.b, epst.b], writes=[rt_.b])
                    for i3, (c, gcol, bank) in enumerate(fx):
                        rt_, rs_ = rt3[i3], rs3[i3]
                        P.op("act", lambda e, rt_=rt_, rs_=rs_: e.activation(out=rs_.t[:], in_=rt_.t[:], func=AF.Exp, scale=-0.5),
                             reads=[rt_.b], writes=[rs_.b])
                    for i3, (c, gcol, bank) in enumerate(fx):
                        rs_, tm = rs3[i3], tm3[i3]
                        P.op("dve", lambda e, tm=tm, c=c, pj=pj, rs_=rs_: e.tensor_tensor(out=tm.t[:], in0=pj[:, c, :], in1=rs_.t[:], op=ALU.mult),
                             reads=[b_pj[c], rs_.b], writes=[tm.b])
                    for i3, (c, gcol, bank) in enumerate(fx):
                        tm = tm3[i3]
                        P.op("act", lambda e, tm=tm, c=c, gcol=gcol, pj=pj: e.activation(
                            out=pj[:, c, :], in_=tm.t[:], func=AF.Identity, scale=vec[l].t[:, gcol:gcol + 1]),
                            reads=[tm.b, vec[l].b], writes=[b_pj[c]])
                    P.dma("act", projT[:, :, t * TS:(t + 1) * TS], pj[:], reads=b_pj, writes=[b_proj[t]])
                    if cond == 1:
                        for s in range(4):
                            seq = (t - cfg.nt_lat) * 2 + s // 2
                            pos0 = (s % 2) * 128
                            xi = kvo[kctr % 2]
                            kctr += 1
                            for jj, c in enumerate((4, 5, 8, 9)):
                                P.op("pe", lambda e, jj=jj, c=c, s=s, pj=pj: e.transpose(
                                    out=pd[:, jj * 128:(jj + 1) * 128], in_=pj[:, c, s * 128:(s + 1) * 128],
                                    identity=ident.t[:]),
                                    reads=[b_pj[c], ident.b], writes=[b_pd[0]])
                            P.op("act", lambda e, xi=xi: e.copy(out=xi.t[:], in_=pd[:, 0:512]),
                                 reads=[b_pd[0]], writes=[xi.b])
                            P.dma("act", O["o_swa"][seq, l, 0, pos0:pos0 + 128, :], xi.t[:, 0:128], reads=[xi.b])
                            P.dma("act", O["o_swa"][seq, l, 1, pos0:pos0 + 128, :], xi.t[:, 128:256], reads=[xi.b])
                            P.dma("act", O["o_ax"][seq, l, 0, pos0:pos0 + 128, :], xi.t[:, 256:384], reads=[xi.b])
                            P.dma("act", O["o_ax"][seq, l, 1, pos0:pos0 + 128, :], xi.t[:, 384:512], reads=[xi.b])
                P.barrier()

        def phase_WO(l):
            with ExitStack() as ph:
                hs = [P.sb([128, 8, TS], F32, ph) for _ in range(2)]
                b_hs = [[Buf() for _ in range(8)] for _ in range(2)]
                xs = [P.sb([128, 8, TS], BF16, ph) for _ in range(2)]
                b_xs = [[Buf() for _ in range(8)] for _ in range(2)]
                wout = P.sb([128, 8, D], BF16, ph)
                b_wout = Buf()
                P.dma("pool", wout[:], I["w_out"][l].rearrange("(k p) n -> p k n", p=128), writes=[b_wout])

                def ld(t):
                    i = t % 2
                    P.dma("sp", hs[i][:], hbuf[:, :, t * TS:(t + 1) * TS], reads=[b_hbuf[t]], writes=b_hs[i])
                    P.dma("sp", xs[i][:], mergT[:, :, t * TS:(t + 1) * TS], reads=[b_merg[t]], writes=b_xs[i])
                ld(0)
                for t in range(NT):
                    cf = coef[l][cond_of(t)]
                    i = t % 2
                    h_, bh_, x_, bx_ = hs[i], b_hs[i], xs[i], b_xs[i]
                    if t + 1 < NT:
                        ld(t + 1)
                    for m in range(8):
                        o_ = ops_[m % 2]
                        for k in range(8):
                            P.op("pe", lambda e, o_=o_, k=k, m=m, x_=x_: e.matmul(
                                o_.t, lhsT=wout[:, k, m * 128:(m + 1) * 128], rhs=x_[:, k, :],
                                start=(k == 0), stop=(k == 7)),
                                reads=[b_wout, bx_[k]], writes=[o_.b])
                        P.op("dve", lambda e, o_=o_, m=m, cf=cf, h_=h_: e.scalar_tensor_tensor(
                            out=h_[:, m, :], in0=o_.t, scalar=cf.t[:, 4, m:m + 1], in1=h_[:, m, :],
                            op0=ALU.mult, op1=ALU.add),
                            reads=[o_.b, bh_[m], cf.b], writes=[bh_[m]])
                    P.dma("act", hbuf[:, :, t * TS:(t + 1) * TS], h_[:], reads=bh_, writes=[b_hbuf[t]])
                P.barrier()

        have_mix = len(cfg.mixers) > 0
        MIX = NS()

        def zero_merg(chunks):
            with ExitStack() as ph:
                z = TB(P.sb([128, TS], BF16, ph))
                P.op("dve", lambda e: e.memset(z.t[:], 0.0), writes=[z.b])
                for t in range(NT):
                    for c in chunks:
                        P.dma("sp", mergT[:, c, t * TS:(t + 1) * TS], z.t[:], reads=[z.b], writes=[b_merg[t]])
                P.barrier()

        def mx_attn(l, grp, ph, SH):
            qc0, kc, vc, mch = (2, 4, 5, 2) if grp == "swa" else (6, 8, 9, 4)
            cache = I["cache_swa"] if grp == "swa" else I["cache_ax"]
            if True:
                NKB = L // 128
                K2 = [[P.sb([128, L], BF16, ph) for _ in range(2)] for _ in range(2)]
                b_K2 = [Buf(), Buf()]
                Kc2 = [[TB(P.sb([128, 256], BF16, ph)) for _ in range(2)] for _ in range(2)]
                for kv in range(2):
                    for hh in range(2):
                        P.op("pool", lambda e, kv=kv, hh=hh: e.memset(K2[kv][hh][:], 0.0), writes=[b_K2[kv]])
                        P.op("pool", lambda e, kv=kv, hh=hh: e.memset(Kc2[kv][hh].t[:], 0.0), writes=[Kc2[kv][hh].b])
                Vx = P.sb([128, NKB + 2, 2, 192], BF16, ph)
                b_Vx = Buf()
                P.op("pool", lambda e: e.memset(Vx[:], 1.0), writes=[b_Vx])
                if not hasattr(SH, "ropeC"):
                    SH.ropeC = TB(P.sb([128, L], F32, ph))
                    SH.ropeS = TB(P.sb([128, L], F32, ph))
                    SH.rotm = TB(P.sb([128, 128], F32, ph))
                    SH.identb = TB(P.sb([128, 128], BF16, ph))
                    SH.masks = TB(P.sb([128, 256], BF16, ph))
                    SH.raw = [TB(P.sb([128, TS], F32, ph)) for _ in range(4)]
                    SH.rctr = [0]
                    SH.t1b = [TB(P.sb([128, TS], F32, ph)) for _ in range(2)]
                    SH.t2b = [TB(P.sb([128, TS], F32, ph)) for _ in range(2)]
                    SH.qr = [TB(P.sb([128, TS], BF16, ph)) for _ in range(2)]
                    SH.qu = [TB(P.sb([128, TS], BF16, ph)) for _ in range(2)]
                    SH.pT = [TB(P.sb([128, TS], BF16, ph)) for _ in range(3)]
                    SH.mg = [TB(P.sb([128, TS], BF16, ph)) for _ in range(2)]
                    SH.rect = [TB(P.sb([128, TS], F32, ph)) for _ in range(2)]
                    SH.ckv = TB(P.sb([128, 2, 128], F32, ph))
                    SH.sk = TB(P.sb([1, 4], F32, ph))
                    SH.esrow = TB(P.sb([1, 4, TS], F32, ph))
                    SH.onesrow = TB(P.sb([1, TS], F32, ph))
                    SH.sel = TB(P.sb([1, 2, 128], F32, ph))
                    P.dma("sp", SH.ropeC.t[:], I["ropeC"], writes=[SH.ropeC.b])
                    P.dma("sp", SH.ropeS.t[:], I["ropeS"], writes=[SH.ropeS.b])
                    P.dma("sp", SH.rotm.t[:], I["rotm"], writes=[SH.rotm.b])
                    P.dma("pool", SH.masks.t[:], I["masks"], writes=[SH.masks.b])
                    P.op("dve", lambda e: e.tensor_copy(out=SH.identb.t[:], in_=ident.t[:]), reads=[ident.b], writes=[SH.identb.b])
                    P.op("dve", lambda e: e.memset(SH.onesrow.t[:], 1.0), writes=[SH.onesrow.b])
                    P.op("dve", lambda e: e.memset(SH.sel.t[:], 0.0), writes=[SH.sel.b])
                    P.op("dve", lambda e: e.memset(SH.sel.t[0:1, 0, 64:128], 1.0), writes=[SH.sel.b])
                    P.op("dve", lambda e: e.memset(SH.sel.t[0:1, 1, 0:64], 1.0), writes=[SH.sel.b])
                ropeC, ropeS, rotm, identb, masks = SH.ropeC, SH.ropeS, SH.rotm, SH.identb, SH.masks
                raw, rctr, t1b, t2b, qr, qu, pT, mg, rect = SH.raw, SH.rctr, SH.t1b, SH.t2b, SH.qr, SH.qu, SH.pT, SH.mg, SH.rect
                ckv, sk, esrow, onesrow, sel = SH.ckv, SH.sk, SH.esrow, SH.onesrow, SH.sel
                sbank = [gps[0], ups[0], gps[1], ups[1]]
                if grp == "swa":
                    P.dma("sp", sk.t[:], I["swa_sink"][l:l + 1, :], writes=[sk.b])
                    P.op("act", lambda e: e.activation(out=sk.t[:], in_=sk.t[:], func=AF.Exp), reads=[sk.b], writes=[sk.b])
                    for hd in range(4):
                        P.op("dve", lambda e, hd=hd: e.tensor_scalar(
                            out=esrow.t[0:1, hd, :], in0=onesrow.t[:], scalar1=sk.t[0:1, hd:hd + 1], scalar2=None,
                            op0=ALU.mult), reads=[sk.b, onesrow.b], writes=[esrow.b])

                def rope(src, dsts, dst_b, p0, w):
                    P.op("pe", lambda e: e.matmul(stat.t[:, 0:w], lhsT=rotm.t[:], rhs=src.t[:, 0:w], start=True, stop=True),
                         reads=[src.b, rotm.b], writes=[stat.b])
                    ta, tb_ = t1b[rctr[0] % 2], t2b[rctr[0] % 2]
                    P.op("dve", lambda e: e.tensor_tensor(out=ta.t[:, 0:w], in0=src.t[:, 0:w], in1=ropeC.t[:, p0:p0 + w], op=ALU.mult),
                         reads=[src.b, ropeC.b], writes=[ta.b])
                    P.op("dve", lambda e: e.tensor_tensor(out=tb_.t[:, 0:w], in0=stat.t[:, 0:w], in1=ropeS.t[:, p0:p0 + w], op=ALU.mult),
                         reads=[stat.b, ropeS.b], writes=[tb_.b])
                    for (dap, ps_) in dsts:
                        P.op("dve", lambda e, dap=dap, ps_=ps_: e.tensor_tensor(out=dap, in0=ta.t[ps_, 0:w], in1=tb_.t[ps_, 0:w], op=ALU.add),
                             reads=[ta.b, tb_.b], writes=[dst_b])

                def attn_seq(tok0, Ls, latent, kcol0, voff, bK, b_Vx):
                    TW = TS if (latent or Ls == TS) else 256
                    nqt = Ls // TW
                    nkb = Ls // 128
                    for kv in range(2):
                        for it in range(nqt):
                            c0 = tok0 + it * TW
                            r_ = raw[rctr[0] % 4]
                            rctr[0] += 1
                            for hh in range(2):
                                P.dma("sp", r_.t[hh * 64:(hh + 1) * 64, 0:TW], projT[kv * 64:(kv + 1) * 64, kc, c0:c0 + TW],
                                      reads=[b_proj[c0 // TS]], writes=[r_.b])
                            if latent:
                                rope(r_, [(K2[kv][0][0:64, kcol0 + it * TW:kcol0 + (it + 1) * TW], slice(0, 64)),
                                          (K2[kv][1][64:128, kcol0 + it * TW:kcol0 + (it + 1) * TW], slice(64, 128))], bK[kv], it * TW, TW)
                            else:
                                P.op("act", lambda e, r_=r_, kv=kv, it=it: e.copy(out=K2[kv][0][0:64, kcol0 + it * TW:kcol0 + (it + 1) * TW], in_=r_.t[0:64, 0:TW]),
                                     reads=[r_.b], writes=[bK[kv]])
                                P.op("act", lambda e, r_=r_, kv=kv, it=it: e.copy(out=K2[kv][1][64:128, kcol0 + it * TW:kcol0 + (it + 1) * TW], in_=r_.t[64:128, 0:TW]),
                                     reads=[r_.b], writes=[bK[kv]])
                    for it in range(nqt):
                        c0 = tok0 + it * TW
                        r_ = raw[rctr[0] % 4]
                        rctr[0] += 1
                        P.dma("sp", r_.t[:, 0:TW], projT[:, vc, c0:c0 + TW], reads=[b_proj[c0 // TS]], writes=[r_.b])
                        for s in range(TW // 128):
                            blk = voff + it * (TW // 128) + s
                            P.op("pe", lambda e, r_=r_, s=s: e.transpose(out=pmisc.t[:, 0:128], in_=r_.t[:, s * 128:(s + 1) * 128],
                                                                     identity=ident.t[:]),
                                 reads=[r_.b, ident.b], writes=[pmisc.b])
                            P.op("dve", lambda e, blk=blk: e.tensor_copy(
                                out=Vx[:, blk, :, 64:128], in_=pmisc.t[:, 0:128].rearrange("p (k d) -> p k d", k=2)),
                                reads=[pmisc.b], writes=[b_Vx])
                    if latent:
                        for cb in range(2):
                            for kv in range(2):
                                for hh in range(2):
                                    P.dma("sp", ckv.t[:, kv, hh * 64:(hh + 1) * 64],
                                          cache[l, 0, cb * 128:(cb + 1) * 128, kv * 64:(kv + 1) * 64], writes=[ckv.b])
                            for kv in range(2):
                                P.op("pe", lambda e, kv=kv: e.transpose(out=pmisc.t[:, 0:128], in_=ckv.t[:, kv, 0:128], identity=ident.t[:]),
                                     reads=[ckv.b, ident.b], writes=[pmisc.b])
                                P.op("dve", lambda e, kv=kv, cb=cb: e.tensor_copy(out=Kc2[kv][0].t[0:64, cb * 128:(cb + 1) * 128], in_=pmisc.t[0:64, 0:128]),
                                     reads=[pmisc.b], writes=[Kc2[kv][0].b])
                                P.op("dve", lambda e, kv=kv, cb=cb: e.tensor_copy(out=Kc2[kv][1].t[64:128, cb * 128:(cb + 1) * 128], in_=pmisc.t[64:128, 0:128]),
                                     reads=[pmisc.b], writes=[Kc2[kv][1].b])
                            P.dma("pool", Vx[:, cb, :, 64:128],
                                  cache[l, 1, cb * 128:(cb + 1) * 128, :].rearrange("p (k d) -> p k d", k=2), writes=[b_Vx])
                    def head_entries(it, qi, hh):
                        kv = qi
                        qu_, qr_ = qu[qi], (qr[qi] if latent else qu[qi])
                        ent = []
                        if latent:
                            for cb in range(2):
                                ent.append((Kc2[kv][hh].t[:, cb * 128:(cb + 1) * 128], Kc2[kv][hh].b, qu_, 0, TW, [], cb))
                        if latent and grp == "swa":
                            qb0 = it * 4
                            for j in range(max(0, qb0 - 1), min(nkb, qb0 + 5)):
                                lo = max(j - 1, qb0)
                                hi = min(j + 1, qb0 + 3)
                                ml = []
                                if j - 1 >= qb0 and j - 1 <= qb0 + 3:
                                    ml.append(((j - 1 - qb0) * 128, 1))
                                if j + 1 >= qb0 and j + 1 <= qb0 + 3:
                                    ml.append(((j + 1 - qb0) * 128, 0))
                                ent.append((K2[kv][hh][:, kcol0 + j * 128:kcol0 + (j + 1) * 128], bK[kv], qr_, (lo - qb0) * 128,
                                            (hi - qb0 + 1) * 128, ml, voff + j))
                        elif latent:
                            for j in range(nkb):
                                ent.append((K2[kv][hh][:, kcol0 + j * 128:kcol0 + (j + 1) * 128], bK[kv], qr_, 0, TW, [], voff + j))
                        else:
                            for j in range(nkb):
                                a_ = (j // 2) * L_CTX
                                ent.append((K2[kv][hh][:, kcol0 + j * 128:kcol0 + (j + 1) * 128], bK[kv], qr_, a_, a_ + L_CTX, [], voff + j))
                        return ent

                    def prep_q(it, qi):
                        c0 = tok0 + it * TW
                        r_ = raw[rctr[0] % 4]
                        rctr[0] += 1
                        P.dma("sp", r_.t[:, 0:TW], projT[:, qc0 + qi, c0:c0 + TW], reads=[b_proj[c0 // TS]], writes=[r_.b])
                        qu_ = qu[qi]
                        P.op("act", lambda e: e.copy(out=qu_.t[:, 0:TW], in_=r_.t[:, 0:TW]), reads=[r_.b], writes=[qu_.b])
                        if latent:
                            qr_ = qr[qi]
                            rope(r_, [(qr_.t[:, 0:TW], slice(0, 128))], qr_.b, it * TW, TW)

                    def run_queries():
                        items = [(it, qi) for it in range(nqt) for qi in range(2)]
                        flat = []
                        for k, (it, qi) in enumerate(items):
                            for hh in range(2):
                                ent = head_entries(it, qi, hh)
                                H = NS()
                                H.kv, H.hd, H.hh, H.qi, H.it = qi, 2 * qi + hh, hh, qi, it
                                H.pr = slice(hh * 64, hh * 64 + 64)
                                H.sr = slice(64 - hh * 64, 128 - hh * 64)
                                H.vs = slice(64, 192) if hh == 0 else slice(0, 128)
                                H.po = ops_[hh]
                                H.mg = mg[qi]
                                H.c0 = tok0 + it * TW
                                n = len(ent)
                                for i, en in enumerate(ent):
                                    flat.append((en, H, i, n, k, (hh == 0 and i == 0), (hh == 1 and i == n - 1)))
                        N = len(flat)

                        def emit_S(g, G):
                            (kap, kb_, q_, a, b, ml, vb), H, i, n, k, fi, li = flat[g]
                            sbk = sbank[G % 4]
                            nm = len(ml)
                            qap = q_.t[:, a:b]
                            P.op("pe", lambda e: e.matmul(sbk.t[:, a:b], lhsT=kap, rhs=qap, start=True, stop=(nm == 0)),
                                 reads=[kb_, q_.b], writes=[sbk.b])
                            for mi, (mc, mk) in enumerate(ml):
                                P.op("pe", lambda e, mc=mc, mk=mk, mi=mi: e.matmul(
                                    sbk.t[:, mc:mc + 128], lhsT=identb.t[:], rhs=masks.t[:, mk * 128:(mk + 1) * 128],
                                    start=False, stop=(mi == nm - 1)),
                                    reads=[identb.b, masks.b], writes=[sbk.b])

                        def emit_PV(g, G):
                            (kap, kb_, q_, a, b, ml, vb), H, i, n, k, fi, li = flat[g]
                            sbk = sbank[G % 4]
                            p_ = pT[G % 3]
                            po = H.po
                            P.op("act", lambda e: e.activation(out=p_.t[:, a:b], in_=sbk.t[:, a:b], func=AF.Exp, scale=0.125),
                                 reads=[sbk.b], writes=[p_.b])
                            last = (i == n - 1) and grp != "swa"
                            vap = Vx[:, vb, H.kv, H.vs]
                            P.op("pe", lambda e: e.matmul(po.t[:, a:b], lhsT=vap, rhs=p_.t[:, a:b], start=(i == 0), stop=last),
                                 reads=[b_Vx, p_.b], writes=[po.b])
                            if i == n - 1:
                                pr, sr, mg_ = H.pr, H.sr, H.mg
                                if grp == "swa":
                                    P.op("pe", lambda e: e.matmul(po.t[:, 0:TW], lhsT=sel.t[0:1, H.hh, :], rhs=esrow.t[0:1, H.hd, 0:TW],
                                                                  start=False, stop=True),
                                         reads=[sel.b, esrow.b], writes=[po.b])
                                rc = rect[H.hh]
                                P.op("dve", lambda e: e.reciprocal(out=rc.t[pr, 0:TW], in_=po.t[sr, 0:TW]),
                                     reads=[po.b], writes=[rc.b])
                                P.op("dve", lambda e: e.tensor_tensor(
                                    out=mg_.t[pr, 0:TW], in0=po.t[pr, 0:TW], in1=rc.t[pr, 0:TW], op=ALU.mult),
                                    reads=[po.b, rc.b], writes=[mg_.b])
                                if li:
                                    P.dma("sp", mergT[:, mch + H.qi, H.c0:H.c0 + TW], mg_.t[:, 0:TW], reads=[mg_.b],
                                          writes=[b_merg[H.c0 // TS]])

                        steps = []
                        for g in range(N):
                            fi, k = flat[g][5], flat[g][4]
                            before = (lambda: prep_q(*items[0])) if g == 0 else None
                            after = (lambda k=k: prep_q(*items[k + 1])) if (fi and k + 1 < len(items)) else None
                            steps.append((before, (lambda G, g=g: emit_S(g, G)), after, (lambda G, g=g: emit_PV(g, G))))
                        return steps
                    return run_queries

                G = NS()

                def lat():
                    return attn_seq(0, L, True, 0, 2, b_K2, b_Vx)

                def ctx():
                    fs = []
                    for s in range(NSEQ // 2):
                        bK = [Buf(), Buf()]
                        for kv in range(2):
                            bK[kv].last_w = b_K2[kv].last_w
                            bK[kv].readers = list(b_K2[kv].readers)
                        bV = Buf()
                        bV.last_w = b_Vx.last_w
                        bV.readers = list(b_Vx.readers)
                        fs.append(attn_seq(L + s * 2 * L_CTX, 2 * L_CTX, False, s * 2 * L_CTX, 2 + 4 * s, bK, bV))
                    return fs
                G.lat, G.ctx = lat, ctx
                return G

        def mx_attn_all(l):
            with ExitStack() as ph:
                SH = NS()
                A = mx_attn(l, "swa", ph, SH)
                B = mx_attn(l, "ax", ph, SH)
                def drive(step_lists):
                    allsteps = [s for sl in step_lists for s in sl]
                    n_ = len(allsteps)
                    for G in range(n_ + 2):
                        if G < n_:
                            before, S_, after, _ = allsteps[G]
                            if before is not None:
                                before()
                            S_(G)
                            if after is not None:
                                after()
                        if G >= 2:
                            allsteps[G - 2][3](G - 2)
                qa = A.lat()
                qb = B.lat()
                drive([qa(), qb()])
                fa = A.ctx()
                fb = B.ctx()
                drive([f() for f in fa + fb])
                P.barrier()

        def mx_fnet(l):
            with ExitStack() as ph:
                NTB = L // 128
                NKT = L // TS
                cs256 = TB(P.sb([128, 2, 512], BF16, ph))
                Ec = TB(P.sb([128, NTB, 512], BF16, ph))
                nEs = TB(P.sb([128, NTB, 512], BF16, ph))
                c256s = TB(P.sb([128, 2, 256], BF16, ph))
                ns256s = TB(P.sb([128, 2, 256], BF16, ph))
                phi = TB(P.sb([128, 4, 8], F32, ph))
                fw = TB(P.sb([128, 2, 256], BF16, ph))
                ufT = TB(P.sb([128, 2, L], BF16, ph))
                UCS = P.sb([128, NTB, 512], BF16, ph)
                b_UCS = [Buf() for _ in range(NTB)]
                Yf = TB(P.sb([128, 2, L], BF16, ph))
                tA = [TB(P.sb([128, 256], BF16, ph)) for _ in range(3)]
                tBm = [TB(P.sb([128, 256], BF16, ph)) for _ in range(3)]
                UA = [TB(P.sb([128, 256], BF16, ph)) for _ in range(3)]
                UB = [TB(P.sb([128, 256], BF16, ph)) for _ in range(3)]
                mgf = [TB(P.sb([128, TS], BF16, ph)) for _ in range(2)]
                sbank = [gps[0], ups[0], gps[1], ups[1]]
                P.dma("pool", cs256.t[:], I["cs256"].rearrange("c p k -> p c k"), writes=[cs256.b])
                P.dma("pool", Ec.t[:], I["fn_ec"].rearrange("(tb p) k -> p tb k", p=128), writes=[Ec.b])
                P.dma("pool", nEs.t[:], I["fn_nes"].rearrange("(tb p) k -> p tb k", p=128), writes=[nEs.b])
                P.dma("pool", c256s.t[:], I["fn_c256s"].rearrange("(tb p) k -> p tb k", p=128), writes=[c256s.b])
                P.dma("pool", ns256s.t[:], I["fn_ns256s"].rearrange("(tb p) k -> p tb k", p=128), writes=[ns256s.b])
                P.dma("sp", phi.t[:], I["fn_phi"], writes=[phi.b])
                P.dma("pool", fw.t[:], I["fnet_w"][l].rearrange("(j p) n -> p j n", p=128), writes=[fw.b])

                def fnet_seq(tok0, Ls, latent):
                    ntb = Ls // 128
                    for c in range(2):
                        for t0 in range(0, Ls, TS):
                            w = min(TS, Ls - t0)
                            P.dma("pool", ufT.t[:, c, t0:t0 + w], projT[:, 10 + c, tok0 + t0:tok0 + t0 + w],
                                  reads=[b_proj[(tok0 + t0) // TS]], writes=[ufT.b])
                    for tb in range(ntb):
                        sbk = sbank[tb % 4]
                        for c in range(2):
                            P.op("pe", lambda e, sbk=sbk, c=c, tb=tb: e.matmul(
                                sbk.t[:], lhsT=ufT.t[:, c, tb * 128:(tb + 1) * 128], rhs=cs256.t[:, c, :],
                                start=(c == 0), stop=(c == 1)), reads=[ufT.b, cs256.b], writes=[sbk.b])
                        if tb % 2 == 0:
                            P.op("act", lambda e, sbk=sbk, tb=tb: e.copy(out=UCS[:, tb, :], in_=sbk.t[:]),
                                 reads=[sbk.b], writes=[b_UCS[tb]])
                        else:
                            P.op("dve", lambda e, sbk=sbk, tb=tb: e.tensor_copy(out=UCS[:, tb, :], in_=sbk.t[:]),
                                 reads=[sbk.b], writes=[b_UCS[tb]])
                    if latent:
                        for kt in range(NKT):
                            yb = [ops_[0], ops_[1]]
                            for tb in range(ntb):
                                if kt == 0:
                                    ua_ap, ub_ap = UCS[:, tb, 0:256], UCS[:, tb, 256:512]
                                    ua_b, ub_b = b_UCS[tb], b_UCS[tb]
                                else:
                                    i3 = tb % 3
                                    ta_, tb2_, ua_, ub_ = tA[i3], tBm[i3], UA[i3], UB[i3]
                                    P.op("act", lambda e, ta_=ta_, tb=tb, kt=kt: e.activation(
                                        out=ta_.t[:], in_=UCS[:, tb, 0:256], func=AF.Identity, scale=phi.t[:, 0, kt:kt + 1]),
                                        reads=[b_UCS[tb], phi.b], writes=[ta_.b])
                                    P.op("dve", lambda e, ua_=ua_, ta_=ta_, tb=tb, kt=kt: e.scalar_tensor_tensor(
                                        out=ua_.t[:], in0=UCS[:, tb, 256:512], scalar=phi.t[:, 2, kt:kt + 1], in1=ta_.t[:],
                                        op0=ALU.mult, op1=ALU.add), reads=[b_UCS[tb], phi.b, ta_.b], writes=[ua_.b])
                                    P.op("act", lambda e, tb2_=tb2_, tb=tb, kt=kt: e.activation(
                                        out=tb2_.t[:], in_=UCS[:, tb, 0:256], func=AF.Identity, scale=phi.t[:, 1, kt:kt + 1]),
                                        reads=[b_UCS[tb], phi.b], writes=[tb2_.b])
                                    P.op("dve", lambda e, ub_=ub_, tb2_=tb2_, tb=tb, kt=kt: e.scalar_tensor_tensor(
                                        out=ub_.t[:], in0=UCS[:, tb, 256:512], scalar=phi.t[:, 0, kt:kt + 1], in1=tb2_.t[:],
                                        op0=ALU.mult, op1=ALU.add), reads=[b_UCS[tb], phi.b, tb2_.b], writes=[ub_.b])
                                    ua_ap, ub_ap = ua_.t[:], ub_.t[:]
                                    ua_b, ub_b = ua_.b, ub_.b
                                for j in range(2):
                                    P.op("pe", lambda e, j=j, tb=tb, ua_ap=ua_ap: e.matmul(
                                        yb[j].t, lhsT=ua_ap[:, j * 128:(j + 1) * 128], rhs=Ec.t[:, tb, :],
                                        start=(tb == 0), stop=False), reads=[ua_b, Ec.b], writes=[yb[j].b])
                                    P.op("pe", lambda e, j=j, tb=tb, ub_ap=ub_ap: e.matmul(
                                        yb[j].t, lhsT=ub_ap[:, j * 128:(j + 1) * 128], rhs=nEs.t[:, tb, :],
                                        start=False, stop=(tb == ntb - 1)), reads=[ub_b, nEs.b], writes=[yb[j].b])
                            P.op("act", lambda e, kt=kt: e.copy(out=Yf.t[:, 0, kt * TS:(kt + 1) * TS], in_=yb[0].t),
                                 reads=[yb[0].b], writes=[Yf.b])
                            P.op("dve", lambda e, kt=kt: e.tensor_copy(out=Yf.t[:, 1, kt * TS:(kt + 1) * TS], in_=yb[1].t),
                                 reads=[yb[1].b], writes=[Yf.b])
                    else:
                        for j in range(2):
                            yb = ops_[j]
                            for tb in range(2):
                                P.op("pe", lambda e, j=j, tb=tb, yb=yb: e.matmul(
                                    yb.t[:, 0:256], lhsT=UCS[:, tb, j * 128:(j + 1) * 128], rhs=c256s.t[:, tb, :],
                                    start=(tb == 0), stop=False), reads=[b_UCS[tb], c256s.b], writes=[yb.b])
                                P.op("pe", lambda e, j=j, tb=tb, yb=yb: e.matmul(
                                    yb.t[:, 0:256], lhsT=UCS[:, tb, 256 + j * 128:256 + (j + 1) * 128], rhs=ns256s.t[:, tb, :],
                                    start=False, stop=(tb == 1)), reads=[b_UCS[tb], ns256s.b], writes=[yb.b])
                            P.op("act", lambda e, j=j, yb=yb: e.copy(out=Yf.t[:, j, 0:256], in_=yb.t[:, 0:256]),
                                 reads=[yb.b], writes=[Yf.b])
                    TW = TS if latent else 256
                    for t0 in range(0, Ls, TW):
                        for jo in range(2):
                            sbk = sbank[jo]
                            for ji in range(2):
                                P.op("pe", lambda e, sbk=sbk, ji=ji, jo=jo, t0=t0: e.matmul(
                                    sbk.t[:, 0:TW], lhsT=fw.t[:, ji, jo * 128:(jo + 1) * 128], rhs=Yf.t[:, ji, t0:t0 + TW],
                                    start=(ji == 0), stop=(ji == 1)), reads=[fw.b, Yf.b], writes=[sbk.b])
                            m_ = mgf[jo]
                            P.op("act", lambda e, sbk=sbk, m_=m_, jo=jo: e.activation(
                                out=m_.t[:, 0:TW], in_=sbk.t[:, 0:TW], func=AF.Identity, bias=vec[l].t[:, 100 + jo:101 + jo], scale=1.0),
                                reads=[sbk.b, vec[l].b], writes=[m_.b])
                            P.dma("sp", mergT[:, 6 + jo, tok0 + t0:tok0 + t0 + TW], m_.t[:, 0:TW], reads=[m_.b],
                                  writes=[b_merg[(tok0 + t0) // TS]])

                fnet_seq(0, L, True)
                for s in range(NSEQ):
                    fnet_seq(L + s * L_CTX, L_CTX, False)
                P.barrier()

        def mx_ssm(l):
            T = 256
            TWO_PI = 6.283185307179586
            with ExitStack() as ph:
                Ere = TB(P.sb([128, 16, T], F32, ph))
                Eim = TB(P.sb([128, 16, T], F32, ph))
                rho_c = TB(P.sb([128, 16], F32, ph))
                BtR = TB(P.sb([128, 2, 2, 128], BF16, ph))
                BtI = TB(P.sb([128, 2, 2, 128], BF16, ph))
                Ct = TB(P.sb([128, 2, 3, 8, 128], BF16, ph))
                Dd = TB(P.sb([128, 2, 128], BF16, ph))
                wglu = TB(P.sb([128, 2, 256], BF16, ph))
                uT = TB(P.sb([128, 2, L], BF16, ph))
                yacc = TB(P.sb([128, 2, L], F32, ph))
                carry = TB(P.sb([128, 2, 8, 2], F32, ph))
                one1 = TB(P.sb([128, 1], F32, ph))
                P.op("dve", lambda e: e.memset(one1.t[:], 1.0), writes=[one1.b])
                sbank = [gps[0], ups[0], gps[1], ups[1]]
                P.dma("pool", Ct.t[:, :, 0:2, :, :], I["ssm_ct"][l].rearrange("p (d r s c) -> p d r s c", d=2, r=2, s=8), writes=[Ct.b])
                P.op("act", lambda e: e.activation(out=Ct.t[:, :, 1, :, :], in_=Ct.t[:, :, 1, :, :], func=AF.Identity, scale=-1.0),
                     reads=[Ct.b], writes=[Ct.b])
                P.op("act", lambda e: e.activation(out=Ct.t[:, :, 2, :, :], in_=Ct.t[:, :, 0, :, :], func=AF.Identity, scale=-1.0),
                     reads=[Ct.b], writes=[Ct.b])
                P.dma("pool", Dd.t[:], I["ssm_dd"][l], writes=[Dd.b])
                P.dma("pool", wglu.t[:], I["ssm_w_glu"][l].rearrange("(j p) n -> p j n", p=128), writes=[wglu.b])

                def sincos(th, n, stk):
                    a2 = TB(P.sb([128, 2 * n], F32, stk))
                    ki = TB(P.sb([128, 2 * n], mybir.dt.int32, stk))
                    kf = TB(P.sb([128, 2 * n], F32, stk))
                    mk = TB(P.sb([128, 2 * n], F32, stk))
                    P.op("dve", lambda e: e.tensor_copy(out=a2.t[:, 0:n], in_=th.t[:]), reads=[th.b], writes=[a2.b])
                    P.op("dve", lambda e: e.tensor_scalar(out=a2.t[:, n:2 * n], in0=th.t[:], scalar1=TWO_PI / 4, scalar2=None, op0=ALU.add),
                         reads=[th.b], writes=[a2.b])
                    P.op("dve", lambda e: e.tensor_scalar(out=kf.t[:], in0=a2.t[:], scalar1=1.0 / TWO_PI, scalar2=0.5, op0=ALU.mult, op1=ALU.add),
                         reads=[a2.b], writes=[kf.b])
                    P.op("dve", lambda e: e.tensor_copy(out=ki.t[:], in_=kf.t[:]), reads=[kf.b], writes=[ki.b])
                    P.op("dve", lambda e: e.tensor_copy(out=kf.t[:], in_=ki.t[:]), reads=[ki.b], writes=[kf.b])
                    C1 = 6.28125
                    C2 = TWO_PI - C1
                    P.op("dve", lambda e: e.scalar_tensor_tensor(out=a2.t[:], in0=kf.t[:], scalar=-C1, in1=a2.t[:], op0=ALU.mult, op1=ALU.add),
                         reads=[kf.b, a2.b], writes=[a2.b])
                    P.op("dve", lambda e: e.scalar_tensor_tensor(out=a2.t[:], in0=kf.t[:], scalar=-C2, in1=a2.t[:], op0=ALU.mult, op1=ALU.add),
                         reads=[kf.b, a2.b], writes=[a2.b])
                    P.op("dve", lambda e: e.tensor_scalar(out=mk.t[:], in0=a2.t[:], scalar1=-TWO_PI / 2, scalar2=TWO_PI, op0=ALU.is_lt, op1=ALU.mult),
                         reads=[a2.b], writes=[mk.b])
                    P.op("dve", lambda e: e.tensor_tensor(out=a2.t[:], in0=a2.t[:], in1=mk.t[:], op=ALU.add), reads=[a2.b, mk.b], writes=[a2.b])
                    P.op("dve", lambda e: e.tensor_scalar(out=mk.t[:], in0=a2.t[:], scalar1=TWO_PI / 2, scalar2=-TWO_PI, op0=ALU.is_gt, op1=ALU.mult),
                         reads=[a2.b], writes=[mk.b])
                    P.op("dve", lambda e: e.tensor_tensor(out=a2.t[:], in0=a2.t[:], in1=mk.t[:], op=ALU.add), reads=[a2.b, mk.b], writes=[a2.b])
                    P.op("dve", lambda e: e.tensor_scalar(out=a2.t[:], in0=a2.t[:], scalar1=-3.1415925, scalar2=3.1415925, op0=ALU.max, op1=ALU.min),
                         reads=[a2.b], writes=[a2.b])
                    sc_ = TB(P.sb([128, 2 * n], F32, stk))
                    P.op("act", lambda e: e.activation(out=sc_.t[:], in_=a2.t[:], func=AF.Sin), reads=[a2.b], writes=[sc_.b])
                    return sc_

                def zoh(lre_ap, lim_ap, ldt_ap, n, srcb, stk):
                    dt_ = TB(P.sb([128, n], F32, stk))
                    er = TB(P.sb([128, n], F32, stk))
                    th = TB(P.sb([128, n], F32, stk))
                    rho = TB(P.sb([128, n], F32, stk))
                    P.op("act", lambda e: e.activation(out=dt_.t[:], in_=ldt_ap, func=AF.Exp), reads=[srcb], writes=[dt_.b])
                    P.op("dve", lambda e: e.tensor_tensor(out=er.t[:], in0=lre_ap, in1=dt_.t[:], op=ALU.mult), reads=[srcb, dt_.b], writes=[er.b])
                    P.op("dve", lambda e: e.tensor_tensor(out=th.t[:], in0=lim_ap, in1=dt_.t[:], op=ALU.mult), reads=[srcb, dt_.b], writes=[th.b])
                    P.op("act", lambda e: e.activation(out=rho.t[:], in_=er.t[:], func=AF.Exp), reads=[er.b], writes=[rho.b])
                    sc_ = sincos(th, n, stk)
                    return rho, sc_

                with ExitStack() as pp:
                    colP = TB(P.sb([128, 3, 256], F32, pp))
                    P.dma("sp", colP.t[:], I["ssm_colp"][l].rearrange("p (k s) -> p k s", k=3), writes=[colP.b])
                    rho, sc_ = zoh(colP.t[:, 0, :], colP.t[:, 1, :], colP.t[:, 2, :], 256, colP.b, pp)
                    P.op("act", lambda e, rho=rho: e.copy(out=rho_c.t[:], in_=rho.t[:].rearrange("p (j r) -> p j r", r=16)[:, :, 0]), reads=[rho.b], writes=[rho_c.b])
                    P.op("act", lambda e, sc_=sc_: e.copy(out=Eim.t[:, :, 0], in_=sc_.t[:, 0:256].rearrange("p (j r) -> p j r", r=16)[:, :, 0]), reads=[sc_.b], writes=[Eim.b])
                    P.op("act", lambda e, sc_=sc_: e.copy(out=Ere.t[:, :, 0], in_=sc_.t[:, 256:512].rearrange("p (j r) -> p j r", r=16)[:, :, 0]), reads=[sc_.b], writes=[Ere.b])
                    if cfg.debug:
                        P.dma("sp", O["dbg_rhofull"], rho.t[:], reads=[rho.b])
                        P.dma("sp", O["dbg_sc"], sc_.t[:], reads=[sc_.b])
                        P.dma("sp", O["dbg_colp"], colP.t[:].rearrange("p k s -> p (k s)"), reads=[colP.b])
                    tq = [TB(P.sb([128, 16, T // 2], F32, pp)) for _ in range(2)]
                    n = 1
                    while n < T:
                        cb_ = Ere.t[:, :, n - 1:n].to_broadcast([128, 16, n])
                        sb_ = Eim.t[:, :, n - 1:n].to_broadcast([128, 16, n])
                        a_, b_ = tq[0], tq[1]
                        P.op("dve", lambda e, n=n, cb_=cb_: e.tensor_tensor(out=a_.t[:, :, 0:n], in0=Ere.t[:, :, 0:n], in1=cb_, op=ALU.mult),
                             reads=[Ere.b], writes=[a_.b])
                        P.op("dve", lambda e, n=n, sb_=sb_: e.tensor_tensor(out=b_.t[:, :, 0:n], in0=Eim.t[:, :, 0:n], in1=sb_, op=ALU.mult),
                             reads=[Eim.b], writes=[b_.b])
                        P.op("dve", lambda e, n=n: e.tensor_tensor(out=Ere.t[:, :, n:2 * n], in0=a_.t[:, :, 0:n], in1=b_.t[:, :, 0:n], op=ALU.subtract),
                             reads=[a_.b, b_.b], writes=[Ere.b])
                        P.op("dve", lambda e, n=n, sb_=sb_: e.tensor_tensor(out=a_.t[:, :, 0:n], in0=Ere.t[:, :, 0:n], in1=sb_, op=ALU.mult),
                             reads=[Ere.b], writes=[a_.b])
                        P.op("dve", lambda e, n=n, cb_=cb_: e.tensor_tensor(out=b_.t[:, :, 0:n], in0=Eim.t[:, :, 0:n], in1=cb_, op=ALU.mult),
                             reads=[Eim.b], writes=[b_.b])
                        P.op("dve", lambda e, n=n: e.tensor_tensor(out=Eim.t[:, :, n:2 * n], in0=a_.t[:, :, 0:n], in1=b_.t[:, :, 0:n], op=ALU.add),
                             reads=[a_.b, b_.b], writes=[Eim.b])
                        n *= 2
                    rowP = TB(P.sb([128, 2, 3, 256], F32, pp))
                    P.dma("sp", rowP.t[:], I["ssm_rowp"][l].rearrange("p (d k q) -> p d k q", d=2, k=3), writes=[rowP.b])
                    braw = TB(P.sb([128, 2, 2, 256], F32, pp))
                    P.dma("sp", braw.t[:], I["ssm_bt"][l].rearrange("p (d r q) -> p d r q", d=2, r=2), writes=[braw.b])
                    for d in range(2):
                        lre, lim = rowP.t[:, d, 0, :], rowP.t[:, d, 1, :]
                        rho, sc_ = zoh(lre, lim, rowP.t[:, d, 2, :], 256, rowP.b, pp)
                        nr = TB(P.sb([128, 256], F32, pp))
                        ni = TB(P.sb([128, 256], F32, pp))
                        den = TB(P.sb([128, 256], F32, pp))
                        t_ = TB(P.sb([128, 256], F32, pp))
                        cr = TB(P.sb([128, 256], F32, pp))
                        ci = TB(P.sb([128, 256], F32, pp))
                        P.op("dve", lambda e, rho=rho, sc_=sc_, nr=nr: e.tensor_tensor(out=nr.t[:], in0=rho.t[:], in1=sc_.t[:, 256:512], op=ALU.mult),
                             reads=[rho.b, sc_.b], writes=[nr.b])
                        P.op("dve", lambda e, nr=nr: e.tensor_scalar(out=nr.t[:], in0=nr.t[:], scalar1=-1.0, scalar2=None, op0=ALU.add),
                             reads=[nr.b], writes=[nr.b])
                        P.op("dve", lambda e, rho=rho, sc_=sc_, ni=ni: e.tensor_tensor(out=ni.t[:], in0=rho.t[:], in1=sc_.t[:, 0:256], op=ALU.mult),
                             reads=[rho.b, sc_.b], writes=[ni.b])
                        P.op("dve", lambda e, den=den, lre=lre: e.tensor_tensor(out=den.t[:], in0=lre, in1=lre, op=ALU.mult), reads=[rowP.b], writes=[den.b])
                        P.op("dve", lambda e, t_=t_, lim=lim: e.tensor_tensor(out=t_.t[:], in0=lim, in1=lim, op=ALU.mult), reads=[rowP.b], writes=[t_.b])
                        P.op("dve", lambda e, den=den, t_=t_: e.tensor_tensor(out=den.t[:], in0=den.t[:], in1=t_.t[:], op=ALU.add), reads=[den.b, t_.b], writes=[den.b])
                        P.op("dve", lambda e, den=den: e.reciprocal(out=den.t[:], in_=den.t[:]), reads=[den.b], writes=[den.b])
                        P.op("dve", lambda e, cr=cr, nr=nr, lre=lre: e.tensor_tensor(out=cr.t[:], in0=nr.t[:], in1=lre, op=ALU.mult), reads=[nr.b, rowP.b], writes=[cr.b])
                        P.op("dve", lambda e, t_=t_, ni=ni, lim=lim: e.tensor_tensor(out=t_.t[:], in0=ni.t[:], in1=lim, op=ALU.mult), reads=[ni.b, rowP.b], writes=[t_.b])
                        P.op("dve", lambda e, cr=cr, t_=t_: e.tensor_tensor(out=cr.t[:], in0=cr.t[:], in1=t_.t[:], op=ALU.add), reads=[cr.b, t_.b], writes=[cr.b])
                        P.op("dve", lambda e, cr=cr, den=den: e.tensor_tensor(out=cr.t[:], in0=cr.t[:], in1=den.t[:], op=ALU.mult), reads=[cr.b, den.b], writes=[cr.b])
                        P.op("dve", lambda e, ci=ci, ni=ni, lre=lre: e.tensor_tensor(out=ci.t[:], in0=ni.t[:], in1=lre, op=ALU.mult), reads=[ni.b, rowP.b], writes=[ci.b])
                        P.op("dve", lambda e, t_=t_, nr=nr, lim=lim: e.tensor_tensor(out=t_.t[:], in0=nr.t[:], in1=lim, op=ALU.mult), reads=[nr.b, rowP.b], writes=[t_.b])
                        P.op("dve", lambda e, ci=ci, t_=t_: e.tensor_tensor(out=ci.t[:], in0=ci.t[:], in1=t_.t[:], op=ALU.subtract), reads=[ci.b, t_.b], writes=[ci.b])
                        P.op("dve", lambda e, ci=ci, den=den: e.tensor_tensor(out=ci.t[:], in0=ci.t[:], in1=den.t[:], op=ALU.mult), reads=[ci.b, den.b], writes=[ci.b])
                        bre, bim = braw.t[:, d, 0, :], braw.t[:, d, 1, :]
                        x1 = TB(P.sb([128, 256], F32, pp))
                        x2 = TB(P.sb([128, 256], F32, pp))
                        btr = BtR.t[:, d, :, :].rearrange("p c q -> p (c q)")
                        bti = BtI.t[:, d, :, :].rearrange("p c q -> p (c q)")
                        P.op("dve", lambda e, x1=x1, bre=bre, cr=cr: e.tensor_tensor(out=x1.t[:], in0=bre, in1=cr.t[:], op=ALU.mult), reads=[braw.b, cr.b], writes=[x1.b])
                        P.op("dve", lambda e, x2=x2, bim=bim, ci=ci: e.tensor_tensor(out=x2.t[:], in0=bim, in1=ci.t[:], op=ALU.mult), reads=[braw.b, ci.b], writes=[x2.b])
                        P.op("dve", lambda e, x1=x1, x2=x2, btr=btr: e.tensor_tensor(out=btr, in0=x1.t[:], in1=x2.t[:], op=ALU.subtract), reads=[x1.b, x2.b], writes=[BtR.b])
                        P.op("dve", lambda e, x1=x1, bre=bre, ci=ci: e.tensor_tensor(out=x1.t[:], in0=bre, in1=ci.t[:], op=ALU.mult), reads=[braw.b, ci.b], writes=[x1.b])
                        P.op("dve", lambda e, x2=x2, bim=bim, cr=cr: e.tensor_tensor(out=x2.t[:], in0=bim, in1=cr.t[:], op=ALU.mult), reads=[braw.b, cr.b], writes=[x2.b])
                        P.op("dve", lambda e, x1=x1, x2=x2, bti=bti: e.tensor_tensor(out=bti, in0=x1.t[:], in1=x2.t[:], op=ALU.add), reads=[x1.b, x2.b], writes=[BtI.b])
                    P.barrier()

                if cfg.debug:
                    P.dma("sp", O["dbg_E"][:, 0], Ere.t[:], reads=[Ere.b])
                    P.dma("sp", O["dbg_E"][:, 1], Eim.t[:], reads=[Eim.b])
                    P.dma("sp", O["dbg_rho"], rho_c.t[:], reads=[rho_c.b])
                    P.dma("pool", O["dbg_bt"][:, 0], BtR.t[:].rearrange("p d c q -> p (d c q)"), reads=[BtR.b])
                    P.dma("pool", O["dbg_bt"][:, 1], BtI.t[:].rearrange("p d c q -> p (d c q)"), reads=[BtI.b])
                NB = 4
                NBA = 8
                def mk(n, dt=F32):
                    return [TB(P.sb([128, T], dt, ph)) for _ in range(n)]
                bre_t, bim_t = mk(NBA), mk(NBA)
                t1, t2, t3, t4 = mk(NB), mk(NB), mk(NB), mk(NB)
                brp, bip = mk(NB), mk(NB)
                rr_t, ri_t = mk(NB), mk(NB)
                o1, o2, o3, o4 = mk(NB, BF16), mk(NB, BF16), mk(NB, BF16), mk(NB, BF16)
                ctmp = [TB(P.sb([128, 4], F32, ph)) for _ in range(NB)]
                y32 = [TB(P.sb([128, T], F32, ph)) for _ in range(2)]
                g1 = [TB(P.sb([128, T], F32, ph)) for _ in range(2)]
                g2 = [TB(P.sb([128, T], F32, ph)) for _ in range(2)]
                z32 = [TB(P.sb([128, T], F32, ph)) for _ in range(2)]
                zb = [TB(P.sb([128, T], BF16, ph)) for _ in range(2)]
                sg = [TB(P.sb([128, T], F32, ph)) for _ in range(2)]
                ob = [TB(P.sb([128, T], BF16, ph)) for _ in range(2)]
                st0 = TB(P.sb([128, 32], F32, ph))
                uctr = [0]

                def tt(eng, o, a, b, op, rb, wb):
                    P.op(eng, lambda e: e.tensor_tensor(out=o, in0=a, in1=b, op=op), reads=rb, writes=wb)

                def unit_pre(d, tc, sc):
                    cc, pg = sc // 4, sc % 4
                    rows = slice(pg * 32, pg * 32 + 32)
                    t0 = tc * T
                    i = uctr[0] % NB
                    ia = uctr[0] % NBA
                    sbk = sbank[uctr[0] % 4]
                    uctr[0] += 1
                    j = d * 8 + sc
                    u_ap = uT.t[rows, cc, t0:t0 + T]
                    if d == 1:
                        u_ap = u_ap[:, ::-1]
                    P.op("pe", lambda e: e.matmul(sbk.t[:, 0:T], lhsT=BtR.t[rows, d, cc, :], rhs=u_ap, start=True, stop=True,
                                                  tile_position=(pg * 32, 0)),
                         reads=[BtR.b, uT.b], writes=[sbk.b])
                    P.op("pe", lambda e: e.matmul(sbk.t[:, T:2 * T], lhsT=BtI.t[rows, d, cc, :], rhs=u_ap, start=True, stop=True,
                                                  tile_position=(pg * 32, 0)),
                         reads=[BtI.b, uT.b], writes=[sbk.b])
                    bre, bim = bre_t[ia], bim_t[ia]
                    P.op("act", lambda e: e.copy(out=bre.t[:], in_=sbk.t[:, 0:T]), reads=[sbk.b], writes=[bre.b])
                    P.op("act", lambda e: e.copy(out=bim.t[:], in_=sbk.t[:, T:2 * T]), reads=[sbk.b], writes=[bim.b])
                    return (d, tc, sc, i, j, ia)

                def unit_pre_b(stt):
                    d, tc, sc, i, j, ia = stt
                    bre, bim = bre_t[ia], bim_t[ia]
                    ec, es = Ere.t[:, j, :], Eim.t[:, j, :]
                    return [
                        lambda: tt("dve", t1[i].t[:], bre.t[:], ec, ALU.mult, [bre.b, Ere.b], [t1[i].b]),
                        lambda: tt("dve", t2[i].t[:], bim.t[:], es, ALU.mult, [bim.b, Eim.b], [t2[i].b]),
                        lambda: tt("dve", t3[i].t[:], bim.t[:], ec, ALU.mult, [bim.b, Ere.b], [t3[i].b]),
                        lambda: tt("dve", t4[i].t[:], bre.t[:], es, ALU.mult, [bre.b, Eim.b], [t4[i].b]),
                        lambda: tt("dve", brp[i].t[:], t1[i].t[:], t2[i].t[:], ALU.add, [t1[i].b, t2[i].b], [brp[i].b]),
                        lambda: tt("dve", bip[i].t[:], t3[i].t[:], t4[i].t[:], ALU.subtract, [t3[i].b, t4[i].b], [bip[i].b]),
                    ]

                def unit_post(stt, first, last_extra):
                    d, tc, sc, i, j, ia = stt
                    cc = sc // 4
                    ec, es = Ere.t[:, j, :], Eim.t[:, j, :]
                    rb_ = rho_c.t[:, j:j + 1].to_broadcast([128, T])
                    rr, ri = rr_t[i], ri_t[i]
                    dv = [
                        lambda: P.op("dve", lambda e: e.tensor_tensor_scan(out=rr.t[:], data0=rb_, data1=brp[i].t[:],
                                                                           initial=carry.t[:, d, sc, 0:1], op0=ALU.mult, op1=ALU.add),
                                     reads=[rho_c.b, brp[i].b, carry.b], writes=[rr.b]),
                        lambda: P.op("dve", lambda e: e.tensor_tensor_scan(out=ri.t[:], data0=rb_, data1=bip[i].t[:],
                                                                           initial=carry.t[:, d, sc, 1:2], op0=ALU.mult, op1=ALU.add),
                                     reads=[rho_c.b, bip[i].b, carry.b], writes=[ri.b]),
                        lambda: tt("dve", o1[i].t[:], rr.t[:], ec, ALU.mult, [rr.b, Ere.b], [o1[i].b]),
                        lambda: tt("dve", o3[i].t[:], rr.t[:], es, ALU.mult, [rr.b, Eim.b], [o3[i].b]),
                        lambda: tt("dve", o2[i].t[:], ri.t[:], es, ALU.mult, [ri.b, Eim.b], [o2[i].b]),
                        lambda: tt("dve", o4[i].t[:], ri.t[:], ec, ALU.mult, [ri.b, Ere.b], [o4[i].b]),
                    ]

                    def rest():
                        ct_ = ctmp[i]
                        ecl, esl = Ere.t[:, j, T - 1:T], Eim.t[:, j, T - 1:T]
                        rrl, ril = rr.t[:, T - 1:T], ri.t[:, T - 1:T]
                        P.op("act", lambda e: e.activation(out=ct_.t[:, 0:1], in_=rrl, func=AF.Identity, scale=ecl), reads=[rr.b, Ere.b], writes=[ct_.b])
                        P.op("act", lambda e: e.activation(out=ct_.t[:, 1:2], in_=rrl, func=AF.Identity, scale=esl), reads=[rr.b, Eim.b], writes=[ct_.b])
                        P.op("act", lambda e: e.activation(out=ct_.t[:, 2:3], in_=ril, func=AF.Identity, scale=esl), reads=[ri.b, Eim.b], writes=[ct_.b])
                        P.op("act", lambda e: e.activation(out=ct_.t[:, 3:4], in_=ril, func=AF.Identity, scale=ecl), reads=[ri.b, Ere.b], writes=[ct_.b])
                        P.op("act", lambda e: e.activation(out=carry.t[:, d, sc, 0:1], in_=ct_.t[:, 2:3], func=AF.Identity, scale=-1.0, bias=ct_.t[:, 0:1]),
                             reads=[ct_.b], writes=[carry.b])
                        P.op("act", lambda e: e.activation(out=carry.t[:, d, sc, 1:2], in_=ct_.t[:, 3:4], func=AF.Identity, scale=1.0, bias=ct_.t[:, 1:2]),
                             reads=[ct_.b], writes=[carry.b])
                        yp = ops_[cc]
                        aps = [o1[i].t[:], o2[i].t[:], o3[i].t[:], o4[i].t[:]]
                        if d == 1:
                            aps = [a_[:, ::-1] for a_ in aps]
                        var = [0, 2, 1, 1]
                        bufs = [o1[i].b, o2[i].b, o3[i].b, o4[i].b]
                        for q_ in range(4):
                            P.op("pe", lambda e, q_=q_: e.matmul(yp.t[:, 0:T], lhsT=Ct.t[:, d, var[q_], sc, :], rhs=aps[q_],
                                                                start=(first and q_ == 0), stop=(last_extra and q_ == 3)),
                                 reads=[Ct.b, bufs[q_]], writes=[yp.b])
                    return dv, rest

                def run_units(ulist, tail_fn):
                    LA, LB = 5, 2
                    n = len(ulist)
                    stts = [None] * n
                    for step in range(n + LA):
                        ia = step
                        ib = step - (LA - LB)
                        ip = step - LA
                        if ia < n:
                            d, tc, sc, first, last_extra = ulist[ia]
                            stts[ia] = unit_pre(d, tc, sc)
                        pre = unit_pre_b(stts[ib]) if 0 <= ib < n else []
                        if 0 <= ip < n:
                            d, tc, sc, first, last_extra = ulist[ip]
                            dv, rest = unit_post(stts[ip], first, last_extra)
                        else:
                            dv, rest = [], None
                        if pre and dv:
                            order = [dv[0], pre[0], dv[1], pre[1], dv[2], pre[2], dv[3], pre[3], dv[4], pre[4], dv[5], pre[5]]
                        else:
                            order = list(pre) + list(dv)
                        for th in order:
                            th()
                        if rest is not None:
                            rest()
                            d, tc, sc, first, last_extra = ulist[ip]
                            if sc == 7:
                                tail_fn(d, tc)

                def ssm_seq(tok0, Ls, latent, seq):
                    nT = Ls // T
                    for c in range(2):
                        for t0 in range(0, Ls, TS):
                            w = min(TS, Ls - t0)
                            P.dma("pool", uT.t[:, c, t0:t0 + w], projT[:, c, tok0 + t0:tok0 + t0 + w],
                                  reads=[b_proj[(tok0 + t0) // TS]], writes=[uT.b])
                    if latent:
                        P.dma("sp", st0.t[:], I["ssm_st0"][l], writes=[st0.b])
                        P.op("dve", lambda e: e.tensor_copy(out=carry.t[:].rearrange("p d s r -> p (d s r)"), in_=st0.t[:]),
                             reads=[st0.b], writes=[carry.b])
                    else:
                        P.op("dve", lambda e: e.memset(carry.t[:], 0.0), writes=[carry.b])
                    def tail(d, tc):
                        t0 = tc * T
                        if d == 0:
                            for cc in range(2):
                                P.op("act", lambda e, cc=cc, tc=tc: e.copy(out=yacc.t[:, cc, tc * T:(tc + 1) * T], in_=ops_[cc].t[:, 0:T]),
                                     reads=[ops_[cc].b], writes=[yacc.b])
                            return
                        for cc in range(2):
                            yp = ops_[cc]
                            P.op("pe", lambda e, cc=cc, yp=yp, t0=t0: e.matmul(yp.t[:, 0:T], lhsT=Dd.t[:, cc, :], rhs=uT.t[:, cc, t0:t0 + T],
                                                                        start=False, stop=True),
                                 reads=[Dd.b, uT.b], writes=[yp.b])
                            y_, a_, b_, z_, zb_ = y32[cc], g1[cc], g2[cc], z32[cc], zb[cc]
                            P.op("dve", lambda e, y_=y_, yp=yp, cc=cc, t0=t0: e.tensor_tensor(
                                out=y_.t[:], in0=yp.t[:, 0:T], in1=yacc.t[:, cc, t0:t0 + T], op=ALU.add),
                                reads=[yp.b, yacc.b], writes=[y_.b])
                            P.op("act", lambda e, y_=y_, a_=a_: e.activation(out=a_.t[:], in_=y_.t[:], func=AF.Square),
                                 reads=[y_.b], writes=[a_.b])
                            P.op("act", lambda e, a_=a_: e.activation(out=a_.t[:], in_=a_.t[:], func=AF.Identity, scale=0.044715,
                                                                      bias=one1.t[:, 0:1]), reads=[a_.b, one1.b], writes=[a_.b])
                            P.op("pool", lambda e, a_=a_, b_=b_, y_=y_: e.tensor_tensor(out=b_.t[:], in0=a_.t[:], in1=y_.t[:], op=ALU.mult),
                                 reads=[a_.b, y_.b], writes=[b_.b])
                            P.op("act", lambda e, a_=a_, b_=b_: e.activation(out=a_.t[:], in_=b_.t[:], func=AF.Sigmoid, scale=1.5957691216057308),
                                 reads=[b_.b], writes=[a_.b])
                            P.op("pool", lambda e, a_=a_, z_=z_, y_=y_: e.tensor_tensor(out=z_.t[:], in0=a_.t[:], in1=y_.t[:], op=ALU.mult),
                                 reads=[a_.b, y_.b], writes=[z_.b])
                            P.op("act", lambda e, z_=z_, zb_=zb_: e.copy(out=zb_.t[:], in_=z_.t[:]), reads=[z_.b], writes=[zb_.b])
                        for jo in range(2):
                            gp_ = stat if jo == 0 else pmisc
                            for ji in range(2):
                                P.op("pe", lambda e, gp_=gp_, ji=ji, jo=jo: e.matmul(
                                    gp_.t[:, 0:T], lhsT=wglu.t[:, ji, jo * 128:(jo + 1) * 128], rhs=zb[ji].t[:],
                                    start=(ji == 0), stop=(ji == 1)), reads=[wglu.b, zb[ji].b], writes=[gp_.b])
                            P.op("act", lambda e, gp_=gp_, jo=jo: e.activation(
                                out=sg[jo].t[:], in_=gp_.t[:, 0:T], func=AF.Sigmoid, bias=vec[l].t[:, 98 + jo:99 + jo], scale=1.0),
                                reads=[gp_.b, vec[l].b], writes=[sg[jo].b])
                            P.op("pool", lambda e, jo=jo: e.tensor_tensor(out=ob[jo].t[:], in0=z32[jo].t[:], in1=sg[jo].t[:], op=ALU.mult),
                                 reads=[z32[jo].b, sg[jo].b], writes=[ob[jo].b])
                            P.dma("sp", mergT[:, jo, tok0 + t0:tok0 + t0 + T], ob[jo].t[:], reads=[ob[jo].b],
                                  writes=[b_merg[(tok0 + t0) // TS]])

                    ul = []
                    for tc in range(nT):
                        for sc in range(8):
                            ul.append((0, tc, sc, sc % 4 == 0, sc % 4 == 3))
                    for tc in reversed(range(nT)):
                        for sc in range(8):
                            ul.append((1, tc, sc, sc % 4 == 0, False))
                    run_units(ul, tail)
                    if not latent:
                        P.op("dve", lambda e, seq=seq: e.tensor_copy(
                            out=stout.t[:, seq, l, :], in_=carry.t[:].rearrange("p d s r -> p (d s r)")),
                            reads=[carry.b], writes=[stout.b])

                ssm_seq(0, L, True, -1)
                for s in range(NSEQ):
                    ssm_seq(L + s * L_CTX, L_CTX, False, s)
                P.barrier()

        def phase_MX(l):
            zc = []
            if "attn" in cfg.mixers:
                mx_attn_all(l)
            else:
                zc += [2, 3, 4, 5]
            if "fnet" in cfg.mixers:
                mx_fnet(l)
            else:
                zc += [6, 7]
            if "ssm" in cfg.mixers:
                mx_ssm(l)
            else:
                zc += [0, 1]
            if zc:
                zero_merg(zc)

        with ExitStack() as wst:
            W = alloc_W(wst)
            load_ffn(W, 0, 1)
            phase_X0()
            phase0()
            phase_F(W, 0, 1)
        for l in range(DP):
            phase_PJ(l)
            if have_mix:
                phase_MX(l)
            with ExitStack() as wst:
                W = alloc_W(wst)
                load_ffn(W, l, 2)
                if have_mix:
                    phase_WO(l)
                phase_F(W, l, 2, last=(l == DP - 1))
                if l + 1 < DP:
                    load_ffn(W, l + 1, 1)
                    phase_F(W, l + 1, 1)

        P.dma("sp", O["o_st"], stout.t[:].rearrange("p s l x -> p (s l x)"), reads=[stout.b])
        P.barrier()
        P.emit()
    return nc


_CACHE = {}


def _get_program(cfg_key):
    if cfg_key not in _CACHE:
        _CACHE[cfg_key] = build_program(Cfg(*cfg_key))
    return _CACHE[cfg_key]


def host_constants(cfg):
    f = np.float32
    L = cfg.l_lat
    C = {}
    t = np.arange(L)
    row = (t // 64).astype(f)
    col = (t % 64).astype(f)
    inv = (f(10000.0) ** (-np.arange(16, dtype=f) / f(16))).astype(f)
    ang = np.concatenate([row[:, None] * inv[None, :], col[:, None] * inv[None, :]], axis=1).astype(f)
    cosT = np.cos(ang).astype(f).T
    sinT = np.sin(ang).astype(f).T
    C["ropeC"] = np.ascontiguousarray(np.tile(cosT, (4, 1)))
    C["ropeS"] = np.ascontiguousarray(np.tile(sinT, (4, 1)))
    R = np.zeros((128, 128), f)
    for base in (0, 64):
        for i in range(32):
            R[base + i + 32, base + i] = -1.0
            R[base + i, base + i + 32] = 1.0
    C["rotm"] = R
    kp = np.arange(128)[:, None]
    qf = np.arange(128)[None, :]
    NEG = -30000.0
    m1 = np.where(qf <= kp, 0.0, NEG)
    m2 = np.where(kp <= qf, 0.0, NEG)
    C["masks"] = np.concatenate([m1, m2], axis=1).astype(f)
    c = np.arange(256)
    a256 = 2 * np.pi * np.outer(c, c) / 256.0
    cs = np.concatenate([np.cos(a256), np.sin(a256)], axis=1)
    C["cs256"] = cs.reshape(2, 128, 512).astype(f)
    scl = 1.0 / np.sqrt(256.0 * L)
    aL = 2 * np.pi * ((np.outer(np.arange(L), np.arange(512))) % L) / float(L)
    C["fn_ec"] = (np.cos(aL) * scl).astype(f)
    C["fn_nes"] = (-np.sin(aL) * scl).astype(f)
    C["fn_c256s"] = (np.cos(a256) / 256.0).astype(f)
    C["fn_ns256s"] = (-np.sin(a256) / 256.0).astype(f)
    nkt = L // 512
    ph = np.zeros((128, 4, 8), f)
    p_ = np.arange(128)[:, None]
    kt_ = np.arange(nkt)[None, :]
    aphi = 2 * np.pi * ((p_ * kt_) % nkt) / float(nkt)
    ph[:, 0, :nkt] = np.cos(aphi)
    ph[:, 1, :nkt] = np.sin(aphi)
    ph[:, 2, :nkt] = -np.sin(aphi)
    C["fn_phi"] = ph
    return C


def ssm_layouts(inp):
    f = np.float32
    out = {}
    lre = np.asarray(inp["ssm_lambda_re"], f)
    lim = np.asarray(inp["ssm_lambda_im"], f)
    ldt = np.repeat(np.asarray(inp["ssm_log_dt"], f)[..., None], 64, axis=-1)
    par = np.stack([lre, lim, ldt], axis=2)
    p8 = par.reshape(DEPTH, 2, 3, 8, 128)
    pc = p8.transpose(0, 4, 2, 1, 3)
    pc = np.repeat(pc[..., None], 16, axis=-1)
    out["ssm_colp"] = np.ascontiguousarray(pc).reshape(DEPTH, 128, 768)
    p_r = p8.reshape(DEPTH, 2, 3, 2, 4, 128)
    p_r = p_r.transpose(0, 4, 1, 2, 3, 5)
    p_r = np.repeat(p_r[:, :, None], 32, axis=2)
    out["ssm_rowp"] = np.ascontiguousarray(p_r).reshape(DEPTH, 128, 1536)
    bt = np.zeros((DEPTH, 4, 32, 2, 2, 2, 128), f)
    bb = np.stack([np.asarray(inp["ssm_b_re"], f), np.asarray(inp["ssm_b_im"], f)], axis=2)
    ct = np.zeros((DEPTH, 128, 2, 2, 8, 128), f)
    cc_ = np.stack([np.asarray(inp["ssm_c_re"], f), np.asarray(inp["ssm_c_im"], f)], axis=2)
    for sc in range(8):
        cc, pg = sc // 4, sc % 4
        for gg in range(2):
            g = 2 * sc + gg
            bt[:, pg, gg * 16:(gg + 1) * 16, :, :, cc, gg * 64:(gg + 1) * 64] = bb[:, :, :, g].transpose(0, 4, 1, 2, 3)
            ct[:, gg * 64:(gg + 1) * 64, :, :, sc, pg * 32 + gg * 16:pg * 32 + (gg + 1) * 16] = cc_[:, :, :, g].transpose(0, 4, 1, 2, 3)
    out["ssm_bt"] = bt.reshape(DEPTH, 128, 1024)
    out["ssm_ct"] = ct.reshape(DEPTH, 128, 4096)
    dd = np.zeros((DEPTH, 128, 2, 128), f)
    sd = np.asarray(inp["ssm_d"], f).reshape(DEPTH, 2, 128)
    for i in range(128):
        dd[:, i, :, i] = sd[:, :, i]
    out["ssm_dd"] = dd
    out["ssm_w_glu"] = np.ascontiguousarray(inp["ssm_w_glu"], f)
    return out


def ssm_state_in(st):
    s = np.asarray(st, np.float32).reshape(DEPTH, 2, 8, 128, 2)
    return np.ascontiguousarray(s.transpose(0, 3, 1, 2, 4)).reshape(DEPTH, 128, 32)


def ssm_state_out(o):
    s = np.asarray(o, np.float32).reshape(128, NSEQ, DEPTH, 2, 8, 2)
    s = s.transpose(1, 2, 3, 4, 0, 5)
    return np.ascontiguousarray(s).reshape(NSEQ, DEPTH, 2, 16, 64, 2)


def make_in_maps(inp, cfg):
    f = np.float32
    shared = {
        "c_ctx": np.ascontiguousarray(inp["c_ctx"], f).reshape(8, 128),
        "w_mod": np.ascontiguousarray(inp["w_mod"], f),
        "b_mod": np.ascontiguousarray(inp["b_mod"], f).reshape(DEPTH, 72, 128),
        "final_norm": np.ascontiguousarray(inp["final_norm"], f).reshape(8, 128),
        "ident": np.eye(128, dtype=f),
        "w_in": np.ascontiguousarray(inp["w_in"], f),
        "w_out": np.ascontiguousarray(inp["w_out"], f),
        "ssm_d": np.ascontiguousarray(inp["ssm_d"], f).reshape(DEPTH, 2, 128),
        "ssm_b_glu": np.ascontiguousarray(inp["ssm_b_glu"], f).reshape(DEPTH, 2, 128),
        "fnet_b": np.ascontiguousarray(inp["fnet_b"], f).reshape(DEPTH, 2, 128),
        "ax_q_norm": np.ascontiguousarray(inp["ax_q_norm"], f).reshape(DEPTH, 1, 64),
        "ax_k_norm": np.ascontiguousarray(inp["ax_k_norm"], f).reshape(DEPTH, 1, 64),
    }
    for n in ("norm_ffn1", "norm_mix", "norm_ffn2"):
        shared[n] = np.ascontiguousarray(inp[n], f).reshape(DEPTH, 8, 128)
    shared.update(host_constants(cfg))
    shared["swa_sink"] = np.ascontiguousarray(inp["swa_sink"], f)
    shared["fnet_w"] = np.ascontiguousarray(inp["fnet_w"], f)
    shared.update(ssm_layouts(inp))
    for n in ("ffn1_w_gate", "ffn1_w_up", "ffn2_w_gate", "ffn2_w_up", "ffn1_w_down", "ffn2_w_down"):
        shared[n] = np.ascontiguousarray(inp[n], f)
    maps = []
    xp = np.asarray(inp["x_prompt"], f)
    xs = np.asarray(inp["x_sample"], f)
    for c in range(NCORE):
        m = dict(shared)
        m["x_lat"] = np.ascontiguousarray(xs[c, :cfg.l_lat])
        m["x_ctx"] = np.ascontiguousarray(xp[c * NSEQ:(c + 1) * NSEQ]).reshape(NSEQ * L_CTX, D)
        m["c_b"] = np.ascontiguousarray(inp["c"][c], f).reshape(8, 128)
        m["cache_swa"] = np.ascontiguousarray(inp["cache_swa_kv"][c], f).reshape(DEPTH, 2, L_CTX, 128)
        m["cache_ax"] = np.ascontiguousarray(inp["cache_axial_kv"][c], f).reshape(DEPTH, 2, L_CTX, 128)
        m["ssm_st0"] = ssm_state_in(inp["state_ssm"][c])
        maps.append(m)
    return maps


def kernel(**inp):
    cfg_key = (4096, ("fnet", "attn", "ssm"), DEPTH)
    cfg = Cfg(*cfg_key)
    nc = _get_program(cfg_key)
    maps = make_in_maps(inp, cfg)
    res = run_bass_kernel_spmd(nc, maps, core_ids=list(range(NCORE)))
    R = res.results
    y_prompt = np.concatenate([R[c]["y_ctx"].reshape(NSEQ, L_CTX, D) for c in range(NCORE)], axis=0)
    y_sample = np.stack([R[c]["y_lat"] for c in range(NCORE)], axis=0)
    o_swa = np.concatenate([R[c]["o_swa"].reshape(NSEQ, DEPTH, 2, L_CTX, 2, 64) for c in range(NCORE)], axis=0)
    o_ax = np.concatenate([R[c]["o_ax"].reshape(NSEQ, DEPTH, 2, L_CTX, 2, 64) for c in range(NCORE)], axis=0)
    o_st = np.concatenate([ssm_state_out(R[c]["o_st"]) for c in range(NCORE)], axis=0)
    return (y_prompt.astype(np.float32), y_sample.astype(np.float32), o_swa.astype(np.float32),
            o_ax.astype(np.float32), o_st.astype(np.float32))
```

```python
import numpy as np
import ml_dtypes
import concourse.bass as bass
import concourse.mybir as mybir
from concourse.bass_utils import run_bass_kernel_spmd
from contextlib import ExitStack

F32 = mybir.dt.float32
BF16 = mybir.dt.bfloat16
AF = mybir.ActivationFunctionType
ALU = mybir.AluOpType

D = 1024
DFF = 2816
NF = 22
PIN = 1536
DEPTH = 2
NCORE = 8
L_CTX = 256
NSEQ = 4
TS = 512
EPS = 1e-6


class Sem:
    def __init__(self, h):
        self.h = h
        self.val = 0


class Buf:
    __slots__ = ("name", "last_w", "readers")

    def __init__(self, name=""):
        self.name = name
        self.last_w = None
        self.readers = []


class TB:
    def __init__(self, t, name=""):
        self.t = t
        self.b = Buf(name)


class Prog:
    SAME_ENGINE_SYNC = True
    NDMA = 6
    SEM_LIMIT = 24000

    def __init__(self, nc, stack):
        self.nc = nc
        self.stack = stack
        self.engs = {"pe": nc.tensor, "act": nc.scalar, "dve": nc.vector,
                     "pool": nc.gpsimd, "sp": nc.sync}
        self.nsem = 0
        self.esem = {k: self._newsem() for k in self.engs}
        self.waited = {k: {} for k in self.engs}
        self.lists = {k: [] for k in self.engs}
        self.dsems = {}
        self.drr = {}
        for q in ("sp", "pool", "act"):
            self.dsems[q] = [self._newsem() for i in range(self.NDMA)]
            self.drr[q] = 0
        self.ntile = 0

    def _newsem(self):
        self.nsem += 1
        return Sem(self.stack.enter_context(self.nc.semaphore("sem%d" % self.nsem)))

    def sb(self, shape, dtype, stack=None):
        self.ntile += 1
        return (stack or self.stack).enter_context(
            self.nc.sbuf_tensor("t%d" % self.ntile, list(shape), dtype))

    def ps(self, shape, dtype=F32, stack=None):
        self.ntile += 1
        return (stack or self.stack).enter_context(
            self.nc.psum_tensor("p%d" % self.ntile, list(shape), dtype))

    def _deps(self, eng, reads, writes, is_dma=False):
        deps = {}

        def add(sv):
            s, v = sv
            if deps.get(s, 0) < v:
                deps[s] = v
        for b in reads:
            if b.last_w is not None:
                add(b.last_w)
        for b in writes:
            if b.last_w is not None:
                add(b.last_w)
            for r in b.readers:
                add(r)
        own = self.esem.get(eng)
        waits = []
        w = self.waited[eng]
        for s, v in deps.items():
            if s is own and not is_dma and (eng == "pe" or not self.SAME_ENGINE_SYNC):
                continue
            if w.get(s, 0) >= v:
                continue
            w[s] = v
            waits.append((s.h, v))
        return waits

    def op(self, eng, fn, reads=(), writes=(), inc=True, rotate_ok=True):
        waits = self._deps(eng, reads, writes)
        own = self.esem[eng]
        if rotate_ok and own.val >= self.SEM_LIMIT:
            own = self._newsem()
            self.esem[eng] = own
        if inc:
            own.val += 1
            val = own.val
        else:
            val = own.val + 1
        for b in reads:
            b.readers.append((own, val))
            if len(b.readers) > 24:
                last = {}
                for s, v in b.readers:
                    if last.get(s, 0) < v:
                        last[s] = v
                b.readers = list(last.items())
        for b in writes:
            b.last_w = (own, val)
            b.readers = []
        oh = own.h

        def run(e):
            for s, v in waits:
                e.wait_ge(s, v)
            ins = fn(e)
            if inc:
                ins.then_inc(oh, 1)
        self.lists[eng].append(run)

    def dma(self, q, out, in_, reads=(), writes=(), **kw):
        sems = self.dsems[q]
        i = self.drr[q] % len(sems)
        s = sems[i]
        if s.val >= self.SEM_LIMIT:
            old = s
            s = self._newsem()
            sems[i] = s
            w = self.waited[q]
            extra = [(old.h, old.val)] if w.get(old, 0) < old.val else []
            w[old] = old.val
        else:
            extra = []
        self.drr[q] += 1
        waits = extra + self._deps(q, reads, writes, is_dma=True)
        w = self.waited[q]
        if s.val > 0 and w.get(s, 0) < s.val:
            w[s] = s.val
            waits.append((s.h, s.val))
        s.val += 16
        val = s.val
        for b in reads:
            b.readers.append((s, val))
        for b in writes:
            b.last_w = (s, val)
            b.readers = []
        sh = s.h

        def run(e):
            for ss, v in waits:
                e.wait_ge(ss, v)
            e.dma_start(out=out, in_=in_, **kw).then_inc(sh, 16)
        self.lists[q].append(run)

    def barrier(self):
        allv = []
        for k, s in self.esem.items():
            if s.val > 0:
                allv.append(s)
        for q in self.dsems:
            for s in self.dsems[q]:
                if s.val > 0:
                    allv.append(s)
        for eng in self.engs:
            waits = []
            w = self.waited[eng]
            own = self.esem[eng]
            for s in allv:
                if s is own:
                    continue
                if w.get(s, 0) >= s.val:
                    continue
                w[s] = s.val
                waits.append((s.h, s.val))
            if waits:
                def run(e, waits=waits):
                    for s, v in waits:
                        e.wait_ge(s, v)
                self.lists[eng].append(run)

    def emit(self):
        nc = self.nc
        lists = self.lists
        with nc.Block() as block:
            @block.tensor
            def _(e):
                for f in lists["pe"]:
                    f(e)

            @block.scalar
            def _(e):
                for f in lists["act"]:
                    f(e)

            @block.vector
            def _(e):
                for f in lists["dve"]:
                    f(e)

            @block.gpsimd
            def _(e):
                for f in lists["pool"]:
                    f(e)

            @block.sync
            def _(e):
                for f in lists["sp"]:
                    f(e)


class Cfg:
    def __init__(self, l_lat=4096, mixers=("fnet", "attn", "ssm"), depth=DEPTH, debug=False):
        self.debug = debug
        self.l_lat = l_lat
        self.nt_lat = l_lat // TS
        self.nt = self.nt_lat + (NSEQ * L_CTX) // TS
        self.ntok = self.nt * TS
        self.mixers = mixers
        self.depth = depth


def build_program(cfg):
    nc = bass.Bass("TRN2", target_bir_lowering=False)
    L = cfg.l_lat
    NT = cfg.nt
    NTOK = cfg.ntok
    DP = cfg.depth

    def din(name, shape, dt=F32):
        return nc.dram_tensor(name, list(shape), dt, kind="ExternalInput").ap()

    def dout(name, shape, dt=F32):
        return nc.dram_tensor(name, list(shape), dt, kind="ExternalOutput").ap()

    def dscr(name, shape, dt=F32):
        return nc.dram_tensor(name, list(shape), dt, kind="Internal").ap()

    I = {}
    I["x_lat"] = din("x_lat", [L, D])
    I["x_ctx"] = din("x_ctx", [NSEQ * L_CTX, D])
    I["c_b"] = din("c_b", [8, 128])
    I["c_ctx"] = din("c_ctx", [8, 128])
    I["w_mod"] = din("w_mod", [DEPTH, D, 9 * D])
    I["b_mod"] = din("b_mod", [DEPTH, 72, 128])
    for n in ("norm_ffn1", "norm_mix", "norm_ffn2"):
        I[n] = din(n, [DEPTH, 8, 128])
    for n in ("ffn1_w_gate", "ffn1_w_up", "ffn2_w_gate", "ffn2_w_up"):
        I[n] = din(n, [DEPTH, D, DFF])
    for n in ("ffn1_w_down", "ffn2_w_down"):
        I[n] = din(n, [DEPTH, DFF, D])
    I["w_in"] = din("w_in", [DEPTH, D, PIN])
    I["w_out"] = din("w_out", [DEPTH, D, D])
    I["ssm_d"] = din("ssm_d", [DEPTH, 2, 128])
    I["ssm_b_glu"] = din("ssm_b_glu", [DEPTH, 2, 128])
    I["fnet_b"] = din("fnet_b", [DEPTH, 2, 128])
    I["ax_q_norm"] = din("ax_q_norm", [DEPTH, 1, 64])
    I["ax_k_norm"] = din("ax_k_norm", [DEPTH, 1, 64])
    I["final_norm"] = din("final_norm", [8, 128])
    I["ident"] = din("ident", [128, 128])
    I["cache_swa"] = din("cache_swa", [DEPTH, 2, L_CTX, 128])
    I["cache_ax"] = din("cache_ax", [DEPTH, 2, L_CTX, 128])
    I["ropeC"] = din("ropeC", [128, L])
    I["ropeS"] = din("ropeS", [128, L])
    I["rotm"] = din("rotm", [128, 128])
    I["masks"] = din("masks", [128, 256])
    I["swa_sink"] = din("swa_sink", [DEPTH, 4])
    I["cs256"] = din("cs256", [2, 128, 512])
    I["fn_ec"] = din("fn_ec", [L, 512])
    I["fn_nes"] = din("fn_nes", [L, 512])
    I["fn_c256s"] = din("fn_c256s", [256, 256])
    I["fn_ns256s"] = din("fn_ns256s", [256, 256])
    I["fn_phi"] = din("fn_phi", [128, 4, 8])
    I["fnet_w"] = din("fnet_w", [DEPTH, 256, 256])
    I["ssm_colp"] = din("ssm_colp", [DEPTH, 128, 768])
    I["ssm_rowp"] = din("ssm_rowp", [DEPTH, 128, 1536])
    I["ssm_bt"] = din("ssm_bt", [DEPTH, 128, 1024])
    I["ssm_ct"] = din("ssm_ct", [DEPTH, 128, 4096])
    I["ssm_dd"] = din("ssm_dd", [DEPTH, 128, 2, 128])
    I["ssm_st0"] = din("ssm_st0", [DEPTH, 128, 32])
    I["ssm_w_glu"] = din("ssm_w_glu", [DEPTH, 256, 256])

    O = {}
    O["y_lat"] = dout("y_lat", [L, D])
    O["y_ctx"] = dout("y_ctx", [NSEQ * L_CTX, D])
    O["o_swa"] = dout("o_swa", [NSEQ, DEPTH, 2, L_CTX, 128])
    O["o_ax"] = dout("o_ax", [NSEQ, DEPTH, 2, L_CTX, 128])
    O["o_st"] = dout("o_st", [128, NSEQ * DEPTH * 32])

    if cfg.debug:
        O["dbg_E"] = dout("dbg_E", [128, 2, 16, 256])
        O["dbg_rho"] = dout("dbg_rho", [128, 16])
        O["dbg_rhofull"] = dout("dbg_rhofull", [128, 256])
        O["dbg_sc"] = dout("dbg_sc", [128, 512])
        O["dbg_colp"] = dout("dbg_colp", [128, 768])
        O["dbg_bt"] = dout("dbg_bt", [128, 2, 512])
    hbuf = dscr("hbuf", [128, 8, NTOK])
    projT = dscr("projT", [128, 12, NTOK])
    if cfg.debug:
        mergT = dout("mergT", [128, 8, NTOK], BF16)
    else:
        mergT = dscr("mergT", [128, 8, NTOK], BF16)
    b_hbuf = [Buf() for _ in range(NT)]
    b_proj = [Buf() for _ in range(NT)]
    b_merg = [Buf() for _ in range(NT)]

    def x_rows(t, s):
        tok = t * TS + s * 128
        if tok < L:
            return I["x_lat"][tok:tok + 128, :]
        tok -= L
        return I["x_ctx"][tok:tok + 128, :]

    def y_rows(t, s):
        tok = t * TS + s * 128
        if tok < L:
            return O["y_lat"][tok:tok + 128, :]
        tok -= L
        return O["y_ctx"][tok:tok + 128, :]

    with ExitStack() as st:
        P = Prog(nc, st)

        ident = TB(P.sb([128, 128], F32))
        ones_bf = TB(P.sb([128, 128], BF16))
        bd64 = TB(P.sb([128, 128], BF16))
        epst = TB(P.sb([128, 1], F32))
        vec = [TB(P.sb([128, 128], F32)), TB(P.sb([128, 128], F32))]
        mod = [TB(P.sb([128, 2, 72], F32)) for _ in range(DEPTH)]
        coef = [[TB(P.sb([128, 6, 8], F32)) for _ in range(2)] for _ in range(DEPTH)]
        gps = [TB(P.ps([128, TS])) for _ in range(2)]
        ups = [TB(P.ps([128, TS])) for _ in range(2)]
        pd = P.ps([128, 2 * TS])
        b_pd = [Buf(), Buf()]
        ops_ = [TB(pd[:, 0:TS]), TB(pd[:, TS:2 * TS])]
        ops_[0].b = b_pd[0]
        ops_[1].b = b_pd[1]
        stat = TB(P.ps([128, TS]))
        pmisc = TB(P.ps([128, TS]))
        stout = TB(P.sb([128, NSEQ, DEPTH, 32], F32))
        P.op("dve", lambda e: e.memset(stout.t[:], 0.0), writes=[stout.b])

        P.op("dve", lambda e: e.memset(ones_bf.t[:], 1.0 / 1024.0), writes=[ones_bf.b])
        P.op("dve", lambda e: e.memset(bd64.t[:], 0.0), writes=[bd64.b])
        P.op("dve", lambda e: e.memset(bd64.t[0:64, 0:64], 1.0 / 64.0), writes=[bd64.b])
        P.op("dve", lambda e: e.memset(bd64.t[64:128, 64:128], 1.0 / 64.0), writes=[bd64.b])
        P.op("dve", lambda e: e.memset(epst.t[:], EPS), writes=[epst.b])
        P.dma("sp", ident.t[:], I["ident"], writes=[ident.b])

        class NS:
            pass

        def alloc_W(stk):
            W = NS()
            W.g = P.sb([128, 8, DFF], BF16, stk)
            W.u = P.sb([128, 8, DFF], BF16, stk)
            W.d = P.sb([128, NF, D], BF16, stk)
            W.bg = [Buf(), Buf()]
            W.bu = [Buf(), Buf()]
            W.bd = [Buf(), Buf()]
            return W

        def load_ffn(W, l, which):
            g = I["ffn%d_w_gate" % which][l].rearrange("(k p) n -> p k n", p=128)
            u = I["ffn%d_w_up" % which][l].rearrange("(k p) n -> p k n", p=128)
            d = I["ffn%d_w_down" % which][l].rearrange("(f p) n -> p f n", p=128)
            HC = 11 * 128
            for half in range(2):
                P.dma("pool", W.g[:, :, half * HC:(half + 1) * HC], g[:, :, half * HC:(half + 1) * HC],
                      writes=[W.bg[half]])
                P.dma("pool", W.u[:, :, half * HC:(half + 1) * HC], u[:, :, half * HC:(half + 1) * HC],
                      writes=[W.bu[half]])
                P.dma("pool", W.d[:, half * 11:(half + 1) * 11, :], d[:, half * 11:(half + 1) * 11, :],
                      writes=[W.bd[half]])

        def phase0():
            with ExitStack() as ph:
                stage = [P.sb([128, 128], F32, ph) for _ in range(2)]
                for l in range(DEPTH):
                    sg_ = stage[l]
                    bz = Buf()
                    P.op("dve", lambda e, sg_=sg_: e.memset(sg_[:], 0.0), writes=[bz])
                    bl = []
                    r = 0

                    def ld(dst, src):
                        b = Buf()
                        b.last_w = bz.last_w
                        P.dma("sp", dst, src, writes=[b])
                        bl.append(b)
                    for nm, nr in (("b_mod", 72), ("norm_ffn1", 8), ("norm_mix", 8), ("norm_ffn2", 8),
                                   ("ssm_d", 2), ("ssm_b_glu", 2), ("fnet_b", 2)):
                        ld(sg_[r:r + nr, :], I[nm][l])
                        r += nr
                    for j, nm in enumerate(("ax_q_norm", "ax_k_norm")):
                        for hh in range(2):
                            ld(sg_[102 + j:103 + j, hh * 64:(hh + 1) * 64], I[nm][l])
                    if l == 0:
                        ld(sg_[104:112, :], I["c_b"])
                        ld(sg_[112:120, :], I["c_ctx"])
                        ld(sg_[120:128, :], I["final_norm"])
                    P.op("pe", lambda e, sg_=sg_: e.transpose(out=pmisc.t[:, 0:128], in_=sg_[:], identity=ident.t[:]),
                         reads=bl + [ident.b], writes=[pmisc.b])
                    P.op("dve", lambda e, l=l: e.tensor_copy(out=vec[l].t[:], in_=pmisc.t[:, 0:128]),
                         reads=[pmisc.b], writes=[vec[l].b])
                scond = TB(P.sb([128, 8, 2], BF16, ph))
                P.op("act", lambda e: e.activation(out=scond.t[:, :, 0], in_=vec[0].t[:, 104:112], func=AF.Silu),
                     reads=[vec[0].b], writes=[scond.b])
                P.op("act", lambda e: e.activation(out=scond.t[:, :, 1], in_=vec[0].t[:, 112:120], func=AF.Silu),
                     reads=[vec[0].b], writes=[scond.b])
                wm = [TB(P.sb([128, 8, 512], F32, ph)) for _ in range(2)]
                wmb = [TB(P.sb([128, 8, 512], BF16, ph)) for _ in range(2)]
                nblk = 0
                for l in range(DP):
                    wsrc = I["w_mod"][l].rearrange("(k p) n -> p k n", p=128)
                    for cb in range(18):
                        w_ = wm[nblk % 2]
                        wb_ = wmb[nblk % 2]
                        ceng = ("act", "pool", "dve")[nblk % 3]
                        nblk += 1
                        P.dma("sp", w_.t[:], wsrc[:, :, cb * 512:(cb + 1) * 512], writes=[w_.b])
                        if ceng == "act":
                            P.op("act", lambda e, w_=w_, wb_=wb_: e.copy(out=wb_.t[:], in_=w_.t[:]), reads=[w_.b], writes=[wb_.b])
                        else:
                            P.op(ceng, lambda e, w_=w_, wb_=wb_: e.tensor_copy(out=wb_.t[:], in_=w_.t[:]), reads=[w_.b], writes=[wb_.b])
                        for j in range(4):
                            ch = cb * 4 + j
                            for k in range(8):
                                P.op("pe", lambda e, wb_=wb_, j=j, k=k, ch=ch: e.matmul(
                                    pmisc.t[:, ch * 2:ch * 2 + 2], lhsT=wb_.t[:, k, j * 128:(j + 1) * 128],
                                    rhs=scond.t[:, k, :], start=(k == 0), stop=(k == 7)),
                                    reads=[wb_.b, scond.b], writes=[pmisc.b])
                    pm = pmisc.t[:, 0:144].rearrange("p (c t) -> p c t", t=2)
                    for cond in range(2):
                        P.op("dve", lambda e, l=l, cond=cond, pm=pm: e.tensor_tensor(
                            out=mod[l].t[:, cond, :], in0=pm[:, :, cond], in1=vec[l].t[:, 0:72], op=ALU.add),
                            reads=[pmisc.b, vec[l].b], writes=[mod[l].b])
                        cf = coef[l][cond]
                        for j in range(3):
                            P.op("dve", lambda e, l=l, cond=cond, j=j, cf=cf: e.scalar_tensor_tensor(
                                out=cf.t[:, j, :], in0=mod[l].t[:, cond, (3 * j + 1) * 8:(3 * j + 2) * 8], scalar=1.0,
                                in1=vec[l].t[:, 72 + 8 * j:80 + 8 * j], op0=ALU.add, op1=ALU.mult),
                                reads=[mod[l].b, vec[l].b], writes=[cf.b])
                            gs = 0.5 if j != 1 else 1.0
                            P.op("dve", lambda e, l=l, cond=cond, j=j, cf=cf, gs=gs: e.tensor_scalar(
                                out=cf.t[:, 3 + j, :], in0=mod[l].t[:, cond, (3 * j + 2) * 8:(3 * j + 3) * 8],
                                scalar1=gs, scalar2=None, op0=ALU.mult),
                                reads=[mod[l].b], writes=[cf.b])
                P.barrier()

        def cond_of(t):
            return 0 if t < cfg.nt_lat else 1

        def alloc_common(ph, with_hff=False, with_xin=False):
            C = NS()
            C.h = P.sb([128, 8, TS], F32, ph)
            C.b_h = [Buf() for _ in range(8)]
            C.xT = P.sb([128, 8, TS], BF16, ph)
            C.b_x = [Buf() for _ in range(8)]
            C.sqb = [TB(P.sb([128, TS], BF16, ph)) for _ in range(2)]
            C.tmpb = [TB(P.sb([128, TS], F32, ph)) for _ in range(2)]
            C.rt = TB(P.sb([128, TS], F32, ph))
            C.rstd = TB(P.sb([128, TS], F32, ph))
            if with_hff:
                C.hff = P.sb([128, 11, TS], BF16, ph)
                C.b_hff = [Buf() for _ in range(11)]
                C.sgb = [TB(P.sb([128, TS], F32, ph)) for _ in range(2)]
            if with_xin:
                C.xin = [TB(P.sb([128, D], F32, ph)) for _ in range(2)]
                C.xin_ctr = 0
            return C

        def norm(C, A, S, coefb, out_fn):
            h, b_h = C.h, C.b_h
            for m in range(8):
                sq = C.sqb[m % 2]
                P.op("act", lambda e, sq=sq, m=m: e.activation(out=sq.t[:], in_=h[:, m, :], func=AF.Square),
                     reads=[b_h[m]], writes=[sq.b])
                P.op("pe", lambda e, sq=sq, m=m: e.matmul(stat.t[:], lhsT=ones_bf.t[:], rhs=sq.t[:],
                                                          start=(m == 0), stop=(m == 7)),
                     reads=[sq.b, ones_bf.b], writes=[stat.b])
            rt, rstd = C.rt, C.rstd
            P.op("act", lambda e: e.activation(out=rt.t[:], in_=stat.t[:], func=AF.Ln, bias=epst.t[:, 0:1], scale=1.0),
                 reads=[stat.b, epst.b], writes=[rt.b])
            P.op("act", lambda e: e.activation(out=rstd.t[:], in_=rt.t[:], func=AF.Exp, scale=-0.5),
                 reads=[rt.b], writes=[rstd.b])
            for m in range(8):
                tm = C.tmpb[m % 2]
                P.op("dve", lambda e, tm=tm, m=m: e.tensor_tensor(out=tm.t[:], in0=h[:, m, :], in1=rstd.t[:], op=ALU.mult),
                     reads=[b_h[m], rstd.b], writes=[tm.b])
                oap, ob = out_fn(m)
                if S is not None:
                    P.op("act", lambda e, tm=tm, m=m, oap=oap: e.activation(
                        out=oap, in_=tm.t[:], func=AF.Identity, scale=A(m), bias=S(m)),
                        reads=[tm.b] + coefb, writes=[ob])
                else:
                    P.op("act", lambda e, tm=tm, m=m, oap=oap: e.activation(
                        out=oap, in_=tm.t[:], func=AF.Identity, scale=A(m)),
                        reads=[tm.b] + coefb, writes=[ob])

        def norm_to_x(C, l, cond, j):
            cf = coef[l][cond]
            norm(C, lambda m: cf.t[:, j, m:m + 1],
                 lambda m: mod[l].t[:, cond, 3 * j * 8 + m:3 * j * 8 + m + 1],
                 [cf.b, mod[l].b],
                 lambda m: (C.xT[:, m, :], C.b_x[m]))

        def ffn(C, W, l, cond, j, mid_hook=None):
            cf = coef[l][cond]
            h, b_h, xT, b_x, hff, b_hff = C.h, C.b_h, C.xT, C.b_x, C.hff, C.b_hff
            for half in range(2):
                for f in range(11):
                    fc = half * 11 + f
                    gp, up = gps[f % 2], ups[f % 2]
                    for k in range(8):
                        P.op("pe", lambda e, gp=gp, k=k, fc=fc: e.matmul(
                            gp.t[:], lhsT=W.g[:, k, fc * 128:(fc + 1) * 128], rhs=xT[:, k, :],
                            start=(k == 0), stop=(k == 7)),
                            reads=[W.bg[half], b_x[k]], writes=[gp.b], inc=(k == 7), rotate_ok=(k == 0))
                    for k in range(8):
                        P.op("pe", lambda e, up=up, k=k, fc=fc: e.matmul(
                            up.t[:], lhsT=W.u[:, k, fc * 128:(fc + 1) * 128], rhs=xT[:, k, :],
                            start=(k == 0), stop=(k == 7)),
                            reads=[W.bu[half], b_x[k]], writes=[up.b], inc=(k == 7), rotate_ok=(k == 0))
                    sg = C.sgb[f % 2]
                    P.op("act", lambda e, sg=sg, gp=gp: e.activation(out=sg.t[:], in_=gp.t[:], func=AF.Silu),
                         reads=[gp.b], writes=[sg.b])
                    P.op("dve", lambda e, sg=sg, up=up, f=f: e.tensor_tensor(
                        out=hff[:, f, :], in0=sg.t[:], in1=up.t[:], op=ALU.mult),
                        reads=[sg.b, up.b], writes=[b_hff[f]])
                for m in range(8):
                    if half == 1 and m == 4 and mid_hook is not None:
                        mid_hook()
                    o_ = ops_[m % 2]
                    for f in range(11):
                        fc = half * 11 + f
                        P.op("pe", lambda e, o_=o_, f=f, fc=fc, m=m: e.matmul(
                            o_.t, lhsT=W.d[:, fc, m * 128:(m + 1) * 128], rhs=hff[:, f, :],
                            start=(f == 0), stop=(f == 10)),
                            reads=[W.bd[half], b_hff[f]], writes=[o_.b], inc=(f == 10), rotate_ok=(f == 0))
                    P.op("dve", lambda e, o_=o_, m=m, cf=cf, j=j: e.scalar_tensor_tensor(
                        out=h[:, m, :], in0=o_.t, scalar=cf.t[:, 3 + j, m:m + 1], in1=h[:, m, :],
                        op0=ALU.mult, op1=ALU.add),
                        reads=[o_.b, b_h[m], cf.b], writes=[b_h[m]])

        def load_h_x(C, t):
            for s in range(4):
                xi = C.xin[C.xin_ctr % 2]
                C.xin_ctr += 1
                P.dma("sp", xi.t[:], x_rows(t, s), writes=[xi.b])
                for m in range(8):
                    P.op("pe", lambda e, xi=xi, m=m: e.transpose(
                        out=pd[:, m * 128:(m + 1) * 128], in_=xi.t[:, m * 128:(m + 1) * 128], identity=ident.t[:]),
                        reads=[xi.b, ident.b], writes=[b_pd[m // 4]])
                P.op("act", lambda e, s=s: e.copy(out=C.h[:, :, s * 128:(s + 1) * 128],
                                                  in_=pd[:, :].rearrange("p (m t) -> p m t", m=8)),
                     reads=b_pd, writes=C.b_h)

        def load_h(C, t):
            P.dma("sp", C.h[:], hbuf[:, :, t * TS:(t + 1) * TS], reads=[b_hbuf[t]], writes=C.b_h)

        def store_h(C, t):
            P.dma("sp", hbuf[:, :, t * TS:(t + 1) * TS], C.h[:], reads=C.b_h, writes=[b_hbuf[t]])

        def final_out(C, t):
            fw = vec[0]
            h, b_h = C.h, C.b_h
            norm(C, lambda m: fw.t[:, 120 + m:121 + m], None, [fw.b], lambda m: (h[:, m, :], b_h[m]))
            for s in range(4):
                for hf in range(2):
                    xi = C.xin[C.xin_ctr[0] % 2]
                    C.xin_ctr[0] += 1
                    for m4 in range(4):
                        m = hf * 4 + m4
                        P.op("pe", lambda e, m=m, m4=m4, s=s, hf=hf: e.transpose(
                            out=pd[:, hf * 512 + m4 * 128:hf * 512 + (m4 + 1) * 128], in_=h[:, m, s * 128:(s + 1) * 128],
                            identity=ident.t[:]),
                            reads=[b_h[m], ident.b], writes=[b_pd[hf]])
                    if hf == 0:
                        P.op("act", lambda e, xi=xi, hf=hf: e.copy(out=xi.t[:], in_=pd[:, hf * 512:(hf + 1) * 512]),
                             reads=[b_pd[hf]], writes=[xi.b])
                    else:
                        P.op("dve", lambda e, xi=xi, hf=hf: e.tensor_copy(out=xi.t[:], in_=pd[:, hf * 512:(hf + 1) * 512]),
                             reads=[b_pd[hf]], writes=[xi.b])
                    P.dma("sp", y_rows(t, s)[:, hf * 512:(hf + 1) * 512], xi.t[:], reads=[xi.b])

        def phase_X0():
            with ExitStack() as ph:
                xin = [TB(P.sb([128, D], F32, ph)) for _ in range(3)]
                hh = [P.sb([128, 8, TS], F32, ph) for _ in range(2)]
                b_hh = [[Buf() for _ in range(8)] for _ in range(2)]
                ctr = 0
                for t in range(NT):
                    h = hh[t % 2]
                    bh = b_hh[t % 2]
                    for s in range(4):
                        xi = xin[ctr % 3]
                        ctr += 1
                        P.dma("sp", xi.t[:], x_rows(t, s), writes=[xi.b])
                        for m in range(8):
                            P.op("pe", lambda e, xi=xi, m=m: e.transpose(
                                out=pd[:, m * 128:(m + 1) * 128], in_=xi.t[:, m * 128:(m + 1) * 128], identity=ident.t[:]),
                                reads=[xi.b, ident.b], writes=[b_pd[m // 4]])
                        if s % 2 == 0:
                            P.op("act", lambda e, s=s, h=h: e.copy(out=h[:, :, s * 128:(s + 1) * 128],
                                                              in_=pd[:, :].rearrange("p (m t) -> p m t", m=8)),
                                 reads=b_pd, writes=bh)
                        else:
                            P.op("dve", lambda e, s=s, h=h: e.tensor_copy(out=h[:, :, s * 128:(s + 1) * 128],
                                                                     in_=pd[:, :].rearrange("p (m t) -> p m t", m=8)),
                                 reads=b_pd, writes=bh)
                    P.dma("act", hbuf[:, :, t * TS:(t + 1) * TS], h[:], reads=bh, writes=[b_hbuf[t]])
                P.barrier()

        def phase_F(W, l, which, last=False):
            with ExitStack() as ph:
                C0 = alloc_common(ph, with_hff=True)
                C0.xin = [TB(P.sb([128, TS], F32, ph)) for _ in range(2)]
                C0.xin_ctr = [0]
                C1 = NS()
                C1.__dict__.update(C0.__dict__)
                C1.h = P.sb([128, 8, TS], F32, ph)
                C1.b_h = [Buf() for _ in range(8)]
                Cs = [C0, C1]
                j = 0 if which == 1 else 2
                load_h(Cs[0], 0)
                norm_to_x(Cs[0], l, cond_of(0), j)
                for t in range(NT):
                    C = Cs[t % 2]
                    cond = cond_of(t)
                    hook = None
                    if t + 1 < NT:
                        Cn = Cs[(t + 1) % 2]
                        load_h(Cn, t + 1)
                        hook = (lambda Cn=Cn, t=t: norm_to_x(Cn, l, cond_of(t + 1), j))
                    ffn(C, W, l, cond, j, mid_hook=hook)
                    if last:
                        final_out(C, t)
                    else:
                        P.dma("act", hbuf[:, :, t * TS:(t + 1) * TS], C.h[:], reads=C.b_h, writes=[b_hbuf[t]])
                P.barrier()

        def phase_PJ(l):
            with ExitStack() as ph:
                Cs = [alloc_common(ph), alloc_common(ph)]
                win = P.sb([128, 8, PIN], BF16, ph)
                b_win = Buf()
                pjs = [P.sb([128, 12, TS], F32, ph) for _ in range(2)]
                b_pjs = [[Buf() for _ in range(12)] for _ in range(2)]
                kvo = [TB(P.sb([128, 512], F32, ph)) for _ in range(2)]
                kctr = 0
                sq3 = [TB(P.sb([128, TS], BF16, ph)) for _ in range(3)]
                rt3 = [TB(P.sb([128, TS], F32, ph)) for _ in range(3)]
                rs3 = [TB(P.sb([128, TS], F32, ph)) for _ in range(3)]
                tm3 = [TB(P.sb([128, TS], F32, ph)) for _ in range(3)]
                P.dma("pool", win[:], I["w_in"][l].rearrange("(k p) n -> p k n", p=128), writes=[b_win])
                load_h(Cs[0], 0)
                for t in range(NT):
                    cond = cond_of(t)
                    C = Cs[t % 2]
                    pj = pjs[t % 2]
                    b_pj = b_pjs[t % 2]
                    if t + 1 < NT:
                        load_h(Cs[(t + 1) % 2], t + 1)
                    norm_to_x(C, l, cond, 1)
                    for c in range(12):
                        pp = gps[c % 2] if (c // 2) % 2 == 0 else ups[c % 2]
                        for k in range(8):
                            P.op("pe", lambda e, pp=pp, k=k, c=c, C=C: e.matmul(
                                pp.t[:], lhsT=win[:, k, c * 128:(c + 1) * 128], rhs=C.xT[:, k, :],
                                start=(k == 0), stop=(k == 7)),
                                reads=[b_win, C.b_x[k]], writes=[pp.b])
                        if c % 2 == 0:
                            P.op("act", lambda e, pp=pp, c=c, pj=pj: e.copy(out=pj[:, c, :], in_=pp.t[:]),
                                 reads=[pp.b], writes=[b_pj[c]])
                        else:
                            P.op("dve", lambda e, pp=pp, c=c, pj=pj: e.tensor_copy(out=pj[:, c, :], in_=pp.t[:]),
                                 reads=[pp.b], writes=[b_pj[c]])
                    fx = [(6, 102, stat), (7, 102, pmisc), (8, 103, ops_[1])]
                    for i3, (c, gcol, bank) in enumerate(fx):
                        sq = sq3[i3]
                        P.op("act", lambda e, sq=sq, c=c, pj=pj: e.activation(out=sq.t[:], in_=pj[:, c, :], func=AF.Square),
                             reads=[b_pj[c]], writes=[sq.b])
                    for i3, (c, gcol, bank) in enumerate(fx):
                        sq = sq3[i3]
                        P.op("pe", lambda e, sq=sq, bank=bank: e.matmul(bank.t[:, 0:TS] if bank is not ops_[1] else bank.t, lhsT=bd64.t[:], rhs=sq.t[:], start=True, stop=True),
                             reads=[sq.b, bd64.b], writes=[bank.b])
                    for i3, (c, gcol, bank) in enumerate(fx):
                        rt_ = rt3[i3]
                        P.op("act", lambda e, rt_=rt_, bank=bank: e.activation(out=rt_.t[:], in_=bank.t[:, 0:TS] if bank is not ops_[1] else bank.t, func=AF.Ln, bias=epst.t[:, 0:1], scale=1.0),
                             reads=[bank.b, epst.b], writes=[rt_.b])
                    for i3, (c, gcol, bank) in enumerate(fx):
                        rt_, rs_ = rt3[i3], rs3[i3]
                        P.op("act", lambda e, rt_=rt_, rs_=rs_: e.activation(out=rs_.t[:], in_=rt_.t[:], func=AF.Exp, scale=-0.5),
                             reads=[rt_.b], writes=[rs_.b])
                    for i3, (c, gcol, bank) in enumerate(fx):
                        rs_, tm = rs3[i3], tm3[i3]
                        P.op("dve", lambda e, tm=tm, c=c, pj=pj, rs_=rs_: e.tensor_tensor(out=tm.t[:], in0=pj[:, c, :], in1=rs_.t[:], op=ALU.mult),
                             reads=[b_pj[c], rs_.b], writes=[tm.b])
                    for i3, (c, gcol, bank) in enumerate(fx):
                        tm = tm3[i3]
                        P.op("act", lambda e, tm=tm, c=c, gcol=gcol, pj=pj: e.activation(
                            out=pj[:, c, :], in_=tm.t[:], func=AF.Identity, scale=vec[l].t[:, gcol:gcol + 1]),
                            reads=[tm.b, vec[l].b], writes=[b_pj[c]])
                    P.dma("act", projT[:, :, t * TS:(t + 1) * TS], pj[:], reads=b_pj, writes=[b_proj[t]])
                    if cond == 1:
                        for s in range(4):
                            seq = (t - cfg.nt_lat) * 2 + s // 2
                            pos0 = (s % 2) * 128
                            xi = kvo[kctr % 2]
                            kctr += 1
                            for jj, c in enumerate((4, 5, 8, 9)):
                                P.op("pe", lambda e, jj=jj, c=c, s=s, pj=pj: e.transpose(
                                    out=pd[:, jj * 128:(jj + 1) * 128], in_=pj[:, c, s * 128:(s + 1) * 128],
                                    identity=ident.t[:]),
                                    reads=[b_pj[c], ident.b], writes=[b_pd[0]])
                            P.op("act", lambda e, xi=xi: e.copy(out=xi.t[:], in_=pd[:, 0:512]),
                                 reads=[b_pd[0]], writes=[xi.b])
                            P.dma("act", O["o_swa"][seq, l, 0, pos0:pos0 + 128, :], xi.t[:, 0:128], reads=[xi.b])
                            P.dma("act", O["o_swa"][seq, l, 1, pos0:pos0 + 128, :], xi.t[:, 128:256], reads=[xi.b])
                            P.dma("act", O["o_ax"][seq, l, 0, pos0:pos0 + 128, :], xi.t[:, 256:384], reads=[xi.b])
                            P.dma("act", O["o_ax"][seq, l, 1, pos0:pos0 + 128, :], xi.t[:, 384:512], reads=[xi.b])
                P.barrier()

        def phase_WO(l):
            with ExitStack() as ph:
                hs = [P.sb([128, 8, TS], F32, ph) for _ in range(2)]
                b_hs = [[Buf() for _ in range(8)] for _ in range(2)]
                xs = [P.sb([128, 8, TS], BF16, ph) for _ in range(2)]
                b_xs = [[Buf() for _ in range(8)] for _ in range(2)]
                wout = P.sb([128, 8, D], BF16, ph)
                b_wout = Buf()
                P.dma("pool", wout[:], I["w_out"][l].rearrange("(k p) n -> p k n", p=128), writes=[b_wout])

                def ld(t):
                    i = t % 2
                    P.dma("sp", hs[i][:], hbuf[:, :, t * TS:(t + 1) * TS], reads=[b_hbuf[t]], writes=b_hs[i])
                    P.dma("sp", xs[i][:], mergT[:, :, t * TS:(t + 1) * TS], reads=[b_merg[t]], writes=b_xs[i])
                ld(0)
                for t in range(NT):
                    cf = coef[l][cond_of(t)]
                    i = t % 2
                    h_, bh_, x_, bx_ = hs[i], b_hs[i], xs[i], b_xs[i]
                    if t + 1 < NT:
                        ld(t + 1)
                    for m in range(8):
                        o_ = ops_[m % 2]
                        for k in range(8):
                            P.op("pe", lambda e, o_=o_, k=k, m=m, x_=x_: e.matmul(
                                o_.t, lhsT=wout[:, k, m * 128:(m + 1) * 128], rhs=x_[:, k, :],
                                start=(k == 0), stop=(k == 7)),
                                reads=[b_wout, bx_[k]], writes=[o_.b])
                        P.op("dve", lambda e, o_=o_, m=m, cf=cf, h_=h_: e.scalar_tensor_tensor(
                            out=h_[:, m, :], in0=o_.t, scalar=cf.t[:, 4, m:m + 1], in1=h_[:, m, :],
                            op0=ALU.mult, op1=ALU.add),
                            reads=[o_.b, bh_[m], cf.b], writes=[bh_[m]])
                    P.dma("act", hbuf[:, :, t * TS:(t + 1) * TS], h_[:], reads=bh_, writes=[b_hbuf[t]])
                P.barrier()

        have_mix = len(cfg.mixers) > 0
        MIX = NS()

        def zero_merg(chunks):
            with ExitStack() as ph:
                z = TB(P.sb([128, TS], BF16, ph))
                P.op("dve", lambda e: e.memset(z.t[:], 0.0), writes=[z.b])
                for t in range(NT):
                    for c in chunks:
                        P.dma("sp", mergT[:, c, t * TS:(t + 1) * TS], z.t[:], reads=[z.b], writes=[b_merg[t]])
                P.barrier()

        def mx_attn(l, grp, ph, SH):
            qc0, kc, vc, mch = (2, 4, 5, 2) if grp == "swa" else (6, 8, 9, 4)
            cache = I["cache_swa"] if grp == "swa" else I["cache_ax"]
            if True:
                NKB = L // 128
                K2 = [[P.sb([128, L], BF16, ph) for _ in range(2)] for _ in range(2)]
                b_K2 = [Buf(), Buf()]
                Kc2 = [[TB(P.sb([128, 256], BF16, ph)) for _ in range(2)] for _ in range(2)]
                for kv in range(2):
                    for hh in range(2):
                        P.op("pool", lambda e, kv=kv, hh=hh: e.memset(K2[kv][hh][:], 0.0), writes=[b_K2[kv]])
                        P.op("pool", lambda e, kv=kv, hh=hh: e.memset(Kc2[kv][hh].t[:], 0.0), writes=[Kc2[kv][hh].b])
                Vx = P.sb([128, NKB + 2, 2, 192], BF16, ph)
                b_Vx = Buf()
                P.op("pool", lambda e: e.memset(Vx[:], 1.0), writes=[b_Vx])
                if not hasattr(SH, "ropeC"):
                    SH.ropeC = TB(P.sb([128, L], F32, ph))
                    SH.ropeS = TB(P.sb([128, L], F32, ph))
                    SH.rotm = TB(P.sb([128, 128], F32, ph))
                    SH.identb = TB(P.sb([128, 128], BF16, ph))
                    SH.masks = TB(P.sb([128, 256], BF16, ph))
                    SH.raw = [TB(P.sb([128, TS], F32, ph)) for _ in range(4)]
                    SH.rctr = [0]
                    SH.t1b = [TB(P.sb([128, TS], F32, ph)) for _ in range(2)]
                    SH.t2b = [TB(P.sb([128, TS], F32, ph)) for _ in range(2)]
                    SH.qr = [TB(P.sb([128, TS], BF16, ph)) for _ in range(2)]
                    SH.qu = [TB(P.sb([128, TS], BF16, ph)) for _ in range(2)]
                    SH.pT = [TB(P.sb([128, TS], BF16, ph)) for _ in range(3)]
                    SH.mg = [TB(P.sb([128, TS], BF16, ph)) for _ in range(2)]
                    SH.rect = [TB(P.sb([128, TS], F32, ph)) for _ in range(2)]
                    SH.ckv = TB(P.sb([128, 2, 128], F32, ph))
                    SH.sk = TB(P.sb([1, 4], F32, ph))
                    SH.esrow = TB(P.sb([1, 4, TS], F32, ph))
                    SH.onesrow = TB(P.sb([1, TS], F32, ph))
                    SH.sel = TB(P.sb([1, 2, 128], F32, ph))
                    P.dma("sp", SH.ropeC.t[:], I["ropeC"], writes=[SH.ropeC.b])
                    P.dma("sp", SH.ropeS.t[:], I["ropeS"], writes=[SH.ropeS.b])
                    P.dma("sp", SH.rotm.t[:], I["rotm"], writes=[SH.rotm.b])
                    P.dma("pool", SH.masks.t[:], I["masks"], writes=[SH.masks.b])
                    P.op("dve", lambda e: e.tensor_copy(out=SH.identb.t[:], in_=ident.t[:]), reads=[ident.b], writes=[SH.identb.b])
                    P.op("dve", lambda e: e.memset(SH.onesrow.t[:], 1.0), writes=[SH.onesrow.b])
                    P.op("dve", lambda e: e.memset(SH.sel.t[:], 0.0), writes=[SH.sel.b])
                    P.op("dve", lambda e: e.memset(SH.sel.t[0:1, 0, 64:128], 1.0), writes=[SH.sel.b])
                    P.op("dve", lambda e: e.memset(SH.sel.t[0:1, 1, 0:64], 1.0), writes=[SH.sel.b])
                ropeC, ropeS, rotm, identb, masks = SH.ropeC, SH.ropeS, SH.rotm, SH.identb, SH.masks
                raw, rctr, t1b, t2b, qr, qu, pT, mg, rect = SH.raw, SH.rctr, SH.t1b, SH.t2b, SH.qr, SH.qu, SH.pT, SH.mg, SH.rect
                ckv, sk, esrow, onesrow, sel = SH.ckv, SH.sk, SH.esrow, SH.onesrow, SH.sel
                sbank = [gps[0], ups[0], gps[1], ups[1]]
                if grp == "swa":
                    P.dma("sp", sk.t[:], I["swa_sink"][l:l + 1, :], writes=[sk.b])
                    P.op("act", lambda e: e.activation(out=sk.t[:], in_=sk.t[:], func=AF.Exp), reads=[sk.b], writes=[sk.b])
                    for hd in range(4):
                        P.op("dve", lambda e, hd=hd: e.tensor_scalar(
                            out=esrow.t[0:1, hd, :], in0=onesrow.t[:], scalar1=sk.t[0:1, hd:hd + 1], scalar2=None,
                            op0=ALU.mult), reads=[sk.b, onesrow.b], writes=[esrow.b])

                def rope(src, dsts, dst_b, p0, w):
                    P.op("pe", lambda e: e.matmul(stat.t[:, 0:w], lhsT=rotm.t[:], rhs=src.t[:, 0:w], start=True, stop=True),
                         reads=[src.b, rotm.b], writes=[stat.b])
                    ta, tb_ = t1b[rctr[0] % 2], t2b[rctr[0] % 2]
                    P.op("dve", lambda e: e.tensor_tensor(out=ta.t[:, 0:w], in0=src.t[:, 0:w], in1=ropeC.t[:, p0:p0 + w], op=ALU.mult),
                         reads=[src.b, ropeC.b], writes=[ta.b])
                    P.op("dve", lambda e: e.tensor_tensor(out=tb_.t[:, 0:w], in0=stat.t[:, 0:w], in1=ropeS.t[:, p0:p0 + w], op=ALU.mult),
                         reads=[stat.b, ropeS.b], writes=[tb_.b])
                    for (dap, ps_) in dsts:
                        P.op("dve", lambda e, dap=dap, ps_=ps_: e.tensor_tensor(out=dap, in0=ta.t[ps_, 0:w], in1=tb_.t[ps_, 0:w], op=ALU.add),
                             reads=[ta.b, tb_.b], writes=[dst_b])

                def attn_seq(tok0, Ls, latent, kcol0, voff, bK, b_Vx):
                    TW = TS if (latent or Ls == TS) else 256
                    nqt = Ls // TW
                    nkb = Ls // 128
                    for kv in range(2):
                        for it in range(nqt):
                            c0 = tok0 + it * TW
                            r_ = raw[rctr[0] % 4]
                            rctr[0] += 1
                            for hh in range(2):
                                P.dma("sp", r_.t[hh * 64:(hh + 1) * 64, 0:TW], projT[kv * 64:(kv + 1) * 64, kc, c0:c0 + TW],
                                      reads=[b_proj[c0 // TS]], writes=[r_.b])
                            if latent:
                                rope(r_, [(K2[kv][0][0:64, kcol0 + it * TW:kcol0 + (it + 1) * TW], slice(0, 64)),
                                          (K2[kv][1][64:128, kcol0 + it * TW:kcol0 + (it + 1) * TW], slice(64, 128))], bK[kv], it * TW, TW)
                            else:
                                P.op("act", lambda e, r_=r_, kv=kv, it=it: e.copy(out=K2[kv][0][0:64, kcol0 + it * TW:kcol0 + (it + 1) * TW], in_=r_.t[0:64, 0:TW]),
                                     reads=[r_.b], writes=[bK[kv]])
                                P.op("act", lambda e, r_=r_, kv=kv, it=it: e.copy(out=K2[kv][1][64:128, kcol0 + it * TW:kcol0 + (it + 1) * TW], in_=r_.t[64:128, 0:TW]),
                                     reads=[r_.b], writes=[bK[kv]])
                    for it in range(nqt):
                        c0 = tok0 + it * TW
                        r_ = raw[rctr[0] % 4]
                        rctr[0] += 1
                        P.dma("sp", r_.t[:, 0:TW], projT[:, vc, c0:c0 + TW], reads=[b_proj[c0 // TS]], writes=[r_.b])
                        for s in range(TW // 128):
                            blk = voff + it * (TW // 128) + s
                            P.op("pe", lambda e, r_=r_, s=s: e.transpose(out=pmisc.t[:, 0:128], in_=r_.t[:, s * 128:(s + 1) * 128],
                                                                     identity=ident.t[:]),
                                 reads=[r_.b, ident.b], writes=[pmisc.b])
                            P.op("dve", lambda e, blk=blk: e.tensor_copy(
                                out=Vx[:, blk, :, 64:128], in_=pmisc.t[:, 0:128].rearrange("p (k d) -> p k d", k=2)),
                                reads=[pmisc.b], writes=[b_Vx])
                    if latent:
                        for cb in range(2):
                            for kv in range(2):
                                for hh in range(2):
                                    P.dma("sp", ckv.t[:, kv, hh * 64:(hh + 1) * 64],
                                          cache[l, 0, cb * 128:(cb + 1) * 128, kv * 64:(kv + 1) * 64], writes=[ckv.b])
                            for kv in range(2):
                                P.op("pe", lambda e, kv=kv: e.transpose(out=pmisc.t[:, 0:128], in_=ckv.t[:, kv, 0:128], identity=ident.t[:]),
                                     reads=[ckv.b, ident.b], writes=[pmisc.b])
                                P.op("dve", lambda e, kv=kv, cb=cb: e.tensor_copy(out=Kc2[kv][0].t[0:64, cb * 128:(cb + 1) * 128], in_=pmisc.t[0:64, 0:128]),
                                     reads=[pmisc.b], writes=[Kc2[kv][0].b])
                                P.op("dve", lambda e, kv=kv, cb=cb: e.tensor_copy(out=Kc2[kv][1].t[64:128, cb * 128:(cb + 1) * 128], in_=pmisc.t[64:128, 0:128]),
                                     reads=[pmisc.b], writes=[Kc2[kv][1].b])
                            P.dma("pool", Vx[:, cb, :, 64:128],
                                  cache[l, 1, cb * 128:(cb + 1) * 128, :].rearrange("p (k d) -> p k d", k=2), writes=[b_Vx])
                    def head_entries(it, qi, hh):
                        kv = qi
                        qu_, qr_ = qu[qi], (qr[qi] if latent else qu[qi])
                        ent = []
                        if latent:
                            for cb in range(2):
                                ent.append((Kc2[kv][hh].t[:, cb * 128:(cb + 1) * 128], Kc2[kv][hh].b, qu_, 0, TW, [], cb))
                        if latent and grp == "swa":
                            qb0 = it * 4
                            for j in range(max(0, qb0 - 1), min(nkb, qb0 + 5)):
                                lo = max(j - 1, qb0)
                                hi = min(j + 1, qb0 + 3)
                                ml = []
                                if j - 1 >= qb0 and j - 1 <= qb0 + 3:
                                    ml.append(((j - 1 - qb0) * 128, 1))
                                if j + 1 >= qb0 and j + 1 <= qb0 + 3:
                                    ml.append(((j + 1 - qb0) * 128, 0))
                                ent.append((K2[kv][hh][:, kcol0 + j * 128:kcol0 + (j + 1) * 128], bK[kv], qr_, (lo - qb0) * 128,
                                            (hi - qb0 + 1) * 128, ml, voff + j))
                        elif latent:
                            for j in range(nkb):
                                ent.append((K2[kv][hh][:, kcol0 + j * 128:kcol0 + (j + 1) * 128], bK[kv], qr_, 0, TW, [], voff + j))
                        else:
                            for j in range(nkb):
                                a_ = (j // 2) * L_CTX
                                ent.append((K2[kv][hh][:, kcol0 + j * 128:kcol0 + (j + 1) * 128], bK[kv], qr_, a_, a_ + L_CTX, [], voff + j))
                        return ent

                    def prep_q(it, qi):
                        c0 = tok0 + it * TW
                        r_ = raw[rctr[0] % 4]
                        rctr[0] += 1
                        P.dma("sp", r_.t[:, 0:TW], projT[:, qc0 + qi, c0:c0 + TW], reads=[b_proj[c0 // TS]], writes=[r_.b])
                        qu_ = qu[qi]
                        P.op("act", lambda e: e.copy(out=qu_.t[:, 0:TW], in_=r_.t[:, 0:TW]), reads=[r_.b], writes=[qu_.b])
                        if latent:
                            qr_ = qr[qi]
                            rope(r_, [(qr_.t[:, 0:TW], slice(0, 128))], qr_.b, it * TW, TW)

                    def run_queries():
                        items = [(it, qi) for it in range(nqt) for qi in range(2)]
                        flat = []
                        for k, (it, qi) in enumerate(items):
                            for hh in range(2):
                                ent = head_entries(it, qi, hh)
                                H = NS()
                                H.kv, H.hd, H.hh, H.qi, H.it = qi, 2 * qi + hh, hh, qi, it
                                H.pr = slice(hh * 64, hh * 64 + 64)
                                H.sr = slice(64 - hh * 64, 128 - hh * 64)
                                H.vs = slice(64, 192) if hh == 0 else slice(0, 128)
                                H.po = ops_[hh]
                                H.mg = mg[qi]
                                H.c0 = tok0 + it * TW
                                n = len(ent)
                                for i, en in enumerate(ent):
                                    flat.append((en, H, i, n, k, (hh == 0 and i == 0), (hh == 1 and i == n - 1)))
                        N = len(flat)

                        def emit_S(g, G):
                            (kap, kb_, q_, a, b, ml, vb), H, i, n, k, fi, li = flat[g]
                            sbk = sbank[G % 4]
                            nm = len(ml)
                            qap = q_.t[:, a:b]
                            P.op("pe", lambda e: e.matmul(sbk.t[:, a:b], lhsT=kap, rhs=qap, start=True, stop=(nm == 0)),
                                 reads=[kb_, q_.b], writes=[sbk.b])
                            for mi, (mc, mk) in enumerate(ml):
                                P.op("pe", lambda e, mc=mc, mk=mk, mi=mi: e.matmul(
                                    sbk.t[:, mc:mc + 128], lhsT=identb.t[:], rhs=masks.t[:, mk * 128:(mk + 1) * 128],
                                    start=False, stop=(mi == nm - 1)),
                                    reads=[identb.b, masks.b], writes=[sbk.b])

                        def emit_PV(g, G):
                            (kap, kb_, q_, a, b, ml, vb), H, i, n, k, fi, li = flat[g]
                            sbk = sbank[G % 4]
                            p_ = pT[G % 3]
                            po = H.po
                            P.op("act", lambda e: e.activation(out=p_.t[:, a:b], in_=sbk.t[:, a:b], func=AF.Exp, scale=0.125),
                                 reads=[sbk.b], writes=[p_.b])
                            last = (i == n - 1) and grp != "swa"
                            vap = Vx[:, vb, H.kv, H.vs]
                            P.op("pe", lambda e: e.matmul(po.t[:, a:b], lhsT=vap, rhs=p_.t[:, a:b], start=(i == 0), stop=last),
                                 reads=[b_Vx, p_.b], writes=[po.b])
                            if i == n - 1:
                                pr, sr, mg_ = H.pr, H.sr, H.mg
                                if grp == "swa":
                                    P.op("pe", lambda e: e.matmul(po.t[:, 0:TW], lhsT=sel.t[0:1, H.hh, :], rhs=esrow.t[0:1, H.hd, 0:TW],
                                                                  start=False, stop=True),
                                         reads=[sel.b, esrow.b], writes=[po.b])
                                rc = rect[H.hh]
                                P.op("dve", lambda e: e.reciprocal(out=rc.t[pr, 0:TW], in_=po.t[sr, 0:TW]),
                                     reads=[po.b], writes=[rc.b])
                                P.op("dve", lambda e: e.tensor_tensor(
                                    out=mg_.t[pr, 0:TW], in0=po.t[pr, 0:TW], in1=rc.t[pr, 0:TW], op=ALU.mult),
                                    reads=[po.b, rc.b], writes=[mg_.b])
                                if li:
                                    P.dma("sp", mergT[:, mch + H.qi, H.c0:H.c0 + TW], mg_.t[:, 0:TW], reads=[mg_.b],
                                          writes=[b_merg[H.c0 // TS]])

                        steps = []
                        for g in range(N):
                            fi, k = flat[g][5], flat[g][4]
                            before = (lambda: prep_q(*items[0])) if g == 0 else None
                            after = (lambda k=k: prep_q(*items[k + 1])) if (fi and k + 1 < len(items)) else None
                            steps.append((before, (lambda G, g=g: emit_S(g, G)), after, (lambda G, g=g: emit_PV(g, G))))
                        return steps
                    return run_queries

                G = NS()

                def lat():
                    return attn_seq(0, L, True, 0, 2, b_K2, b_Vx)

                def ctx():
                    fs = []
                    for s in range(NSEQ // 2):
                        bK = [Buf(), Buf()]
                        for kv in range(2):
                            bK[kv].last_w = b_K2[kv].last_w
                            bK[kv].readers = list(b_K2[kv].readers)
                        bV = Buf()
                        bV.last_w = b_Vx.last_w
                        bV.readers = list(b_Vx.readers)
                        fs.append(attn_seq(L + s * 2 * L_CTX, 2 * L_CTX, False, s * 2 * L_CTX, 2 + 4 * s, bK, bV))
                    return fs
                G.lat, G.ctx = lat, ctx
                return G

        def mx_attn_all(l):
            with ExitStack() as ph:
                SH = NS()
                A = mx_attn(l, "swa", ph, SH)
                B = mx_attn(l, "ax", ph, SH)
                def drive(step_lists):
                    allsteps = [s for sl in step_lists for s in sl]
                    n_ = len(allsteps)
                    for G in range(n_ + 2):
                        if G < n_:
                            before, S_, after, _ = allsteps[G]
                            if before is not None:
                                before()
                            S_(G)
                            if after is not None:
                                after()
                        if G >= 2:
                            allsteps[G - 2][3](G - 2)
                qa = A.lat()
                qb = B.lat()
                drive([qa(), qb()])
                fa = A.ctx()
                fb = B.ctx()
                drive([f() for f in fa + fb])
                P.barrier()

        def mx_fnet(l):
            with ExitStack() as ph:
                NTB = L // 128
                NKT = L // TS
                cs256 = TB(P.sb([128, 2, 512], BF16, ph))
                Ec = TB(P.sb([128, NTB, 512], BF16, ph))
                nEs = TB(P.sb([128, NTB, 512], BF16, ph))
                c256s = TB(P.sb([128, 2, 256], BF16, ph))
                ns256s = TB(P.sb([128, 2, 256], BF16, ph))
                phi = TB(P.sb([128, 4, 8], F32, ph))
                fw = TB(P.sb([128, 2, 256], BF16, ph))
                ufT = TB(P.sb([128, 2, L], BF16, ph))
                UCS = P.sb([128, NTB, 512], BF16, ph)
                b_UCS = [Buf() for _ in range(NTB)]
                Yf = TB(P.sb([128, 2, L], BF16, ph))
                tA = [TB(P.sb([128, 256], BF16, ph)) for _ in range(3)]
                tBm = [TB(P.sb([128, 256], BF16, ph)) for _ in range(3)]
                UA = [TB(P.sb([128, 256], BF16, ph)) for _ in range(3)]
                UB = [TB(P.sb([128, 256], BF16, ph)) for _ in range(3)]
                mgf = [TB(P.sb([128, TS], BF16, ph)) for _ in range(2)]
                sbank = [gps[0], ups[0], gps[1], ups[1]]
                P.dma("pool", cs256.t[:], I["cs256"].rearrange("c p k -> p c k"), writes=[cs256.b])
                P.dma("pool", Ec.t[:], I["fn_ec"].rearrange("(tb p) k -> p tb k", p=128), writes=[Ec.b])
                P.dma("pool", nEs.t[:], I["fn_nes"].rearrange("(tb p) k -> p tb k", p=128), writes=[nEs.b])
                P.dma("pool", c256s.t[:], I["fn_c256s"].rearrange("(tb p) k -> p tb k", p=128), writes=[c256s.b])
                P.dma("pool", ns256s.t[:], I["fn_ns256s"].rearrange("(tb p) k -> p tb k", p=128), writes=[ns256s.b])
                P.dma("sp", phi.t[:], I["fn_phi"], writes=[phi.b])
                P.dma("pool", fw.t[:], I["fnet_w"][l].rearrange("(j p) n -> p j n", p=128), writes=[fw.b])

                def fnet_seq(tok0, Ls, latent):
                    ntb = Ls // 128
                    for c in range(2):
                        for t0 in range(0, Ls, TS):
                            w = min(TS, Ls - t0)
                            P.dma("pool", ufT.t[:, c, t0:t0 + w], projT[:, 10 + c, tok0 + t0:tok0 + t0 + w],
                                  reads=[b_proj[(tok0 + t0) // TS]], writes=[ufT.b])
                    for tb in range(ntb):
                        sbk = sbank[tb % 4]
                        for c in range(2):
                            P.op("pe", lambda e, sbk=sbk, c=c, tb=tb: e.matmul(
                                sbk.t[:], lhsT=ufT.t[:, c, tb * 128:(tb + 1) * 128], rhs=cs256.t[:, c, :],
                                start=(c == 0), stop=(c == 1)), reads=[ufT.b, cs256.b], writes=[sbk.b])
                        if tb % 2 == 0:
                            P.op("act", lambda e, sbk=sbk, tb=tb: e.copy(out=UCS[:, tb, :], in_=sbk.t[:]),
                                 reads=[sbk.b], writes=[b_UCS[tb]])
                        else:
                            P.op("dve", lambda e, sbk=sbk, tb=tb: e.tensor_copy(out=UCS[:, tb, :], in_=sbk.t[:]),
                                 reads=[sbk.b], writes=[b_UCS[tb]])
                    if latent:
                        for kt in range(NKT):
                            yb = [ops_[0], ops_[1]]
                            for tb in range(ntb):
                                if kt == 0:
                                    ua_ap, ub_ap = UCS[:, tb, 0:256], UCS[:, tb, 256:512]
                                    ua_b, ub_b = b_UCS[tb], b_UCS[tb]
                                else:
                                    i3 = tb % 3
                                    ta_, tb2_, ua_, ub_ = tA[i3], tBm[i3], UA[i3], UB[i3]
                                    P.op("act", lambda e, ta_=ta_, tb=tb, kt=kt: e.activation(
                                        out=ta_.t[:], in_=UCS[:, tb, 0:256], func=AF.Identity, scale=phi.t[:, 0, kt:kt + 1]),
                                        reads=[b_UCS[tb], phi.b], writes=[ta_.b])
                                    P.op("dve", lambda e, ua_=ua_, ta_=ta_, tb=tb, kt=kt: e.scalar_tensor_tensor(
                                        out=ua_.t[:], in0=UCS[:, tb, 256:512], scalar=phi.t[:, 2, kt:kt + 1], in1=ta_.t[:],
                                        op0=ALU.mult, op1=ALU.add), reads=[b_UCS[tb], phi.b, ta_.b], writes=[ua_.b])
                                    P.op("act", lambda e, tb2_=tb2_, tb=tb, kt=kt: e.activation(
                                        out=tb2_.t[:], in_=UCS[:, tb, 0:256], func=AF.Identity, scale=phi.t[:, 1, kt:kt + 1]),
                                        reads=[b_UCS[tb], phi.b], writes=[tb2_.b])
                                    P.op("dve", lambda e, ub_=ub_, tb2_=tb2_, tb=tb, kt=kt: e.scalar_tensor_tensor(
                                        out=ub_.t[:], in0=UCS[:, tb, 256:512], scalar=phi.t[:, 0, kt:kt + 1], in1=tb2_.t[:],
                                        op0=ALU.mult, op1=ALU.add), reads=[b_UCS[tb], phi.b, tb2_.b], writes=[ub_.b])
                                    ua_ap, ub_ap = ua_.t[:], ub_.t[:]
                                    ua_b, ub_b = ua_.b, ub_.b
                                for j in range(2):
                                    P.op("pe", lambda e, j=j, tb=tb, ua_ap=ua_ap: e.matmul(
                                        yb[j].t, lhsT=ua_ap[:, j * 128:(j + 1) * 128], rhs=Ec.t[:, tb, :],
                                        start=(tb == 0), stop=False), reads=[ua_b, Ec.b], writes=[yb[j].b])
                                    P.op("pe", lambda e, j=j, tb=tb, ub_ap=ub_ap: e.matmul(
                                        yb[j].t, lhsT=ub_ap[:, j * 128:(j + 1) * 128], rhs=nEs.t[:, tb, :],
                                        start=False, stop=(tb == ntb - 1)), reads=[ub_b, nEs.b], writes=[yb[j].b])
                            P.op("act", lambda e, kt=kt: e.copy(out=Yf.t[:, 0, kt * TS:(kt + 1) * TS], in_=yb[0].t),
                                 reads=[yb[0].b], writes=[Yf.b])
                            P.op("dve", lambda e, kt=kt: e.tensor_copy(out=Yf.t[:, 1, kt * TS:(kt + 1) * TS], in_=yb[1].t),
                                 reads=[yb[1].b], writes=[Yf.b])
                    else:
                        for j in range(2):
                            yb = ops_[j]
                            for tb in range(2):
                                P.op("pe", lambda e, j=j, tb=tb, yb=yb: e.matmul(
                                    yb.t[:, 0:256], lhsT=UCS[:, tb, j * 128:(j + 1) * 128], rhs=c256s.t[:, tb, :],
                                    start=(tb == 0), stop=False), reads=[b_UCS[tb], c256s.b], writes=[yb.b])
                                P.op("pe", lambda e, j=j, tb=tb, yb=yb: e.matmul(
                                    yb.t[:, 0:256], lhsT=UCS[:, tb, 256 + j * 128:256 + (j + 1) * 128], rhs=ns256s.t[:, tb, :],
                                    start=False, stop=(tb == 1)), reads=[b_UCS[tb], ns256s.b], writes=[yb.b])
                            P.op("act", lambda e, j=j, yb=yb: e.copy(out=Yf.t[:, j, 0:256], in_=yb.t[:, 0:256]),
                                 reads=[yb.b], writes=[Yf.b])
                    TW = TS if latent else 256
                    for t0 in range(0, Ls, TW):
                        for jo in range(2):
                            sbk = sbank[jo]
                            for ji in range(2):
                                P.op("pe", lambda e, sbk=sbk, ji=ji, jo=jo, t0=t0: e.matmul(
                                    sbk.t[:, 0:TW], lhsT=fw.t[:, ji, jo * 128:(jo + 1) * 128], rhs=Yf.t[:, ji, t0:t0 + TW],
                                    start=(ji == 0), stop=(ji == 1)), reads=[fw.b, Yf.b], writes=[sbk.b])
                            m_ = mgf[jo]
                            P.op("act", lambda e, sbk=sbk, m_=m_, jo=jo: e.activation(
                                out=m_.t[:, 0:TW], in_=sbk.t[:, 0:TW], func=AF.Identity, bias=vec[l].t[:, 100 + jo:101 + jo], scale=1.0),
                                reads=[sbk.b, vec[l].b], writes=[m_.b])
                            P.dma("sp", mergT[:, 6 + jo, tok0 + t0:tok0 + t0 + TW], m_.t[:, 0:TW], reads=[m_.b],
                                  writes=[b_merg[(tok0 + t0) // TS]])

                fnet_seq(0, L, True)
                for s in range(NSEQ):
                    fnet_seq(L + s * L_CTX, L_CTX, False)
                P.barrier()

        def mx_ssm(l):
            T = 256
            TWO_PI = 6.283185307179586
            with ExitStack() as ph:
                Ere = TB(P.sb([128, 16, T], F32, ph))
                Eim = TB(P.sb([128, 16, T], F32, ph))
                rho_c = TB(P.sb([128, 16], F32, ph))
                BtR = TB(P.sb([128, 2, 2, 128], BF16, ph))
                BtI = TB(P.sb([128, 2, 2, 128], BF16, ph))
                Ct = TB(P.sb([128, 2, 3, 8, 128], BF16, ph))
                Dd = TB(P.sb([128, 2, 128], BF16, ph))
                wglu = TB(P.sb([128, 2, 256], BF16, ph))
                uT = TB(P.sb([128, 2, L], BF16, ph))
                yacc = TB(P.sb([128, 2, L], F32, ph))
                carry = TB(P.sb([128, 2, 8, 2], F32, ph))
                sbank = [gps[0], ups[0], gps[1], ups[1]]
                P.dma("pool", Ct.t[:, :, 0:2, :, :], I["ssm_ct"][l].rearrange("p (d r s c) -> p d r s c", d=2, r=2, s=8), writes=[Ct.b])
                P.op("act", lambda e: e.activation(out=Ct.t[:, :, 1, :, :], in_=Ct.t[:, :, 1, :, :], func=AF.Identity, scale=-1.0),
                     reads=[Ct.b], writes=[Ct.b])
                P.op("act", lambda e: e.activation(out=Ct.t[:, :, 2, :, :], in_=Ct.t[:, :, 0, :, :], func=AF.Identity, scale=-1.0),
                     reads=[Ct.b], writes=[Ct.b])
                P.dma("pool", Dd.t[:], I["ssm_dd"][l], writes=[Dd.b])
                P.dma("pool", wglu.t[:], I["ssm_w_glu"][l].rearrange("(j p) n -> p j n", p=128), writes=[wglu.b])

                def sincos(th, n, stk):
                    a2 = TB(P.sb([128, 2 * n], F32, stk))
                    ki = TB(P.sb([128, 2 * n], mybir.dt.int32, stk))
                    kf = TB(P.sb([128, 2 * n], F32, stk))
                    mk = TB(P.sb([128, 2 * n], F32, stk))
                    P.op("dve", lambda e: e.tensor_copy(out=a2.t[:, 0:n], in_=th.t[:]), reads=[th.b], writes=[a2.b])
                    P.op("dve", lambda e: e.tensor_scalar(out=a2.t[:, n:2 * n], in0=th.t[:], scalar1=TWO_PI / 4, scalar2=None, op0=ALU.add),
                         reads=[th.b], writes=[a2.b])
                    P.op("dve", lambda e: e.tensor_scalar(out=kf.t[:], in0=a2.t[:], scalar1=1.0 / TWO_PI, scalar2=0.5, op0=ALU.mult, op1=ALU.add),
                         reads=[a2.b], writes=[kf.b])
                    P.op("dve", lambda e: e.tensor_copy(out=ki.t[:], in_=kf.t[:]), reads=[kf.b], writes=[ki.b])
                    P.op("dve", lambda e: e.tensor_copy(out=kf.t[:], in_=ki.t[:]), reads=[ki.b], writes=[kf.b])
                    C1 = 6.28125
                    C2 = TWO_PI - C1
                    P.op("dve", lambda e: e.scalar_tensor_tensor(out=a2.t[:], in0=kf.t[:], scalar=-C1, in1=a2.t[:], op0=ALU.mult, op1=ALU.add),
                         reads=[kf.b, a2.b], writes=[a2.b])
                    P.op("dve", lambda e: e.scalar_tensor_tensor(out=a2.t[:], in0=kf.t[:], scalar=-C2, in1=a2.t[:], op0=ALU.mult, op1=ALU.add),
                         reads=[kf.b, a2.b], writes=[a2.b])
                    P.op("dve", lambda e: e.tensor_scalar(out=mk.t[:], in0=a2.t[:], scalar1=-TWO_PI / 2, scalar2=TWO_PI, op0=ALU.is_lt, op1=ALU.mult),
                         reads=[a2.b], writes=[mk.b])
                    P.op("dve", lambda e: e.tensor_tensor(out=a2.t[:], in0=a2.t[:], in1=mk.t[:], op=ALU.add), reads=[a2.b, mk.b], writes=[a2.b])
                    P.op("dve", lambda e: e.tensor_scalar(out=mk.t[:], in0=a2.t[:], scalar1=TWO_PI / 2, scalar2=-TWO_PI, op0=ALU.is_gt, op1=ALU.mult),
                         reads=[a2.b], writes=[mk.b])
                    P.op("dve", lambda e: e.tensor_tensor(out=a2.t[:], in0=a2.t[:], in1=mk.t[:], op=ALU.add), reads=[a2.b, mk.b], writes=[a2.b])
                    P.op("dve", lambda e: e.tensor_scalar(out=a2.t[:], in0=a2.t[:], scalar1=-3.1415925, scalar2=3.1415925, op0=ALU.max, op1=ALU.min),
                         reads=[a2.b], writes=[a2.b])
                    sc_ = TB(P.sb([128, 2 * n], F32, stk))
                    P.op("act", lambda e: e.activation(out=sc_.t[:], in_=a2.t[:], func=AF.Sin), reads=[a2.b], writes=[sc_.b])
                    return sc_

                def zoh(lre_ap, lim_ap, ldt_ap, n, srcb, stk):
                    dt_ = TB(P.sb([128, n], F32, stk))
                    er = TB(P.sb([128, n], F32, stk))
                    th = TB(P.sb([128, n], F32, stk))
                    rho = TB(P.sb([128, n], F32, stk))
                    P.op("act", lambda e: e.activation(out=dt_.t[:], in_=ldt_ap, func=AF.Exp), reads=[srcb], writes=[dt_.b])
                    P.op("dve", lambda e: e.tensor_tensor(out=er.t[:], in0=lre_ap, in1=dt_.t[:], op=ALU.mult), reads=[srcb, dt_.b], writes=[er.b])
                    P.op("dve", lambda e: e.tensor_tensor(out=th.t[:], in0=lim_ap, in1=dt_.t[:], op=ALU.mult), reads=[srcb, dt_.b], writes=[th.b])
                    P.op("act", lambda e: e.activation(out=rho.t[:], in_=er.t[:], func=AF.Exp), reads=[er.b], writes=[rho.b])
                    sc_ = sincos(th, n, stk)
                    return rho, sc_

                with ExitStack() as pp:
                    colP = TB(P.sb([128, 3, 256], F32, pp))
                    P.dma("sp", colP.t[:], I["ssm_colp"][l].rearrange("p (k s) -> p k s", k=3), writes=[colP.b])
                    rho, sc_ = zoh(colP.t[:, 0, :], colP.t[:, 1, :], colP.t[:, 2, :], 256, colP.b, pp)
                    P.op("act", lambda e, rho=rho: e.copy(out=rho_c.t[:], in_=rho.t[:].rearrange("p (j r) -> p j r", r=16)[:, :, 0]), reads=[rho.b], writes=[rho_c.b])
                    P.op("act", lambda e, sc_=sc_: e.copy(out=Eim.t[:, :, 0], in_=sc_.t[:, 0:256].rearrange("p (j r) -> p j r", r=16)[:, :, 0]), reads=[sc_.b], writes=[Eim.b])
                    P.op("act", lambda e, sc_=sc_: e.copy(out=Ere.t[:, :, 0], in_=sc_.t[:, 256:512].rearrange("p (j r) -> p j r", r=16)[:, :, 0]), reads=[sc_.b], writes=[Ere.b])
                    if cfg.debug:
                        P.dma("sp", O["dbg_rhofull"], rho.t[:], reads=[rho.b])
                        P.dma("sp", O["dbg_sc"], sc_.t[:], reads=[sc_.b])
                        P.dma("sp", O["dbg_colp"], colP.t[:].rearrange("p k s -> p (k s)"), reads=[colP.b])
                    tq = [TB(P.sb([128, 16, T // 2], F32, pp)) for _ in range(2)]
                    n = 1
                    while n < T:
                        cb_ = Ere.t[:, :, n - 1:n].to_broadcast([128, 16, n])
                        sb_ = Eim.t[:, :, n - 1:n].to_broadcast([128, 16, n])
                        a_, b_ = tq[0], tq[1]
                        P.op("dve", lambda e, n=n, cb_=cb_: e.tensor_tensor(out=a_.t[:, :, 0:n], in0=Ere.t[:, :, 0:n], in1=cb_, op=ALU.mult),
                             reads=[Ere.b], writes=[a_.b])
                        P.op("dve", lambda e, n=n, sb_=sb_: e.tensor_tensor(out=b_.t[:, :, 0:n], in0=Eim.t[:, :, 0:n], in1=sb_, op=ALU.mult),
                             reads=[Eim.b], writes=[b_.b])
                        P.op("dve", lambda e, n=n: e.tensor_tensor(out=Ere.t[:, :, n:2 * n], in0=a_.t[:, :, 0:n], in1=b_.t[:, :, 0:n], op=ALU.subtract),
                             reads=[a_.b, b_.b], writes=[Ere.b])
                        P.op("dve", lambda e, n=n, sb_=sb_: e.tensor_tensor(out=a_.t[:, :, 0:n], in0=Ere.t[:, :, 0:n], in1=sb_, op=ALU.mult),
                             reads=[Ere.b], writes=[a_.b])
                        P.op("dve", lambda e, n=n, cb_=cb_: e.tensor_tensor(out=b_.t[:, :, 0:n], in0=Eim.t[:, :, 0:n], in1=cb_, op=ALU.mult),
                             reads=[Eim.b], writes=[b_.b])
                        P.op("dve", lambda e, n=n: e.tensor_tensor(out=Eim.t[:, :, n:2 * n], in0=a_.t[:, :, 0:n], in1=b_.t[:, :, 0:n], op=ALU.add),
                             reads=[a_.b, b_.b], writes=[Eim.b])
                        n *= 2
                    rowP = TB(P.sb([128, 2, 3, 256], F32, pp))
                    P.dma("sp", rowP.t[:], I["ssm_rowp"][l].rearrange("p (d k q) -> p d k q", d=2, k=3), writes=[rowP.b])
                    braw = TB(P.sb([128, 2, 2, 256], F32, pp))
                    P.dma("sp", braw.t[:], I["ssm_bt"][l].rearrange("p (d r q) -> p d r q", d=2, r=2), writes=[braw.b])
                    for d in range(2):
                        lre, lim = rowP.t[:, d, 0, :], rowP.t[:, d, 1, :]
                        rho, sc_ = zoh(lre, lim, rowP.t[:, d, 2, :], 256, rowP.b, pp)
                        nr = TB(P.sb([128, 256], F32, pp))
                        ni = TB(P.sb([128, 256], F32, pp))
                        den = TB(P.sb([128, 256], F32, pp))
                        t_ = TB(P.sb([128, 256], F32, pp))
                        cr = TB(P.sb([128, 256], F32, pp))
                        ci = TB(P.sb([128, 256], F32, pp))
                        P.op("dve", lambda e, rho=rho, sc_=sc_, nr=nr: e.tensor_tensor(out=nr.t[:], in0=rho.t[:], in1=sc_.t[:, 256:512], op=ALU.mult),
                             reads=[rho.b, sc_.b], writes=[nr.b])
                        P.op("dve", lambda e, nr=nr: e.tensor_scalar(out=nr.t[:], in0=nr.t[:], scalar1=-1.0, scalar2=None, op0=ALU.add),
                             reads=[nr.b], writes=[nr.b])
                        P.op("dve", lambda e, rho=rho, sc_=sc_, ni=ni: e.tensor_tensor(out=ni.t[:], in0=rho.t[:], in1=sc_.t[:, 0:256], op=ALU.mult),
                             reads=[rho.b, sc_.b], writes=[ni.b])
                        P.op("dve", lambda e, den=den, lre=lre: e.tensor_tensor(out=den.t[:], in0=lre, in1=lre, op=ALU.mult), reads=[rowP.b], writes=[den.b])
                        P.op("dve", lambda e, t_=t_, lim=lim: e.tensor_tensor(out=t_.t[:], in0=lim, in1=lim, op=ALU.mult), reads=[rowP.b], writes=[t_.b])
                        P.op("dve", lambda e, den=den, t_=t_: e.tensor_tensor(out=den.t[:], in0=den.t[:], in1=t_.t[:], op=ALU.add), reads=[den.b, t_.b], writes=[den.b])
                        P.op("dve", lambda e, den=den: e.reciprocal(out=den.t[:], in_=den.t[:]), reads=[den.b], writes=[den.b])
                        P.op("dve", lambda e, cr=cr, nr=nr, lre=lre: e.tensor_tensor(out=cr.t[:], in0=nr.t[:], in1=lre, op=ALU.mult), reads=[nr.b, rowP.b], writes=[cr.b])
                        P.op("dve", lambda e, t_=t_, ni=ni, lim=lim: e.tensor_tensor(out=t_.t[:], in0=ni.t[:], in1=lim, op=ALU.mult), reads=[ni.b, rowP.b], writes=[t_.b])
                        P.op("dve", lambda e, cr=cr, t_=t_: e.tensor_tensor(out=cr.t[:], in0=cr.t[:], in1=t_.t[:], op=ALU.add), reads=[cr.b, t_.b], writes=[cr.b])
                        P.op("dve", lambda e, cr=cr, den=den: e.tensor_tensor(out=cr.t[:], in0=cr.t[:], in1=den.t[:], op=ALU.mult), reads=[cr.b, den.b], writes=[cr.b])
                        P.op("dve", lambda e, ci=ci, ni=ni, lre=lre: e.tensor_tensor(out=ci.t[:], in0=ni.t[:], in1=lre, op=ALU.mult), reads=[ni.b, rowP.b], writes=[ci.b])
                        P.op("dve", lambda e, t_=t_, nr=nr, lim=lim: e.tensor_tensor(out=t_.t[:], in0=nr.t[:], in1=lim, op=ALU.mult), reads=[nr.b, rowP.b], writes=[t_.b])
                        P.op("dve", lambda e, ci=ci, t_=t_: e.tensor_tensor(out=ci.t[:], in0=ci.t[:], in1=t_.t[:], op=ALU.subtract), reads=[ci.b, t_.b], writes=[ci.b])
                        P.op("dve", lambda e, ci=ci, den=den: e.tensor_tensor(out=ci.t[:], in0=ci.t[:], in1=den.t[:], op=ALU.mult), reads=[ci.b, den.b], writes=[ci.b])
                        bre, bim = braw.t[:, d, 0, :], braw.t[:, d, 1, :]
                        x1 = TB(P.sb([128, 256], F32, pp))
                        x2 = TB(P.sb([128, 256], F32, pp))
                        btr = BtR.t[:, d, :, :].rearrange("p c q -> p (c q)")
                        bti = BtI.t[:, d, :, :].rearrange("p c q -> p (c q)")
                        P.op("dve", lambda e, x1=x1, bre=bre, cr=cr: e.tensor_tensor(out=x1.t[:], in0=bre, in1=cr.t[:], op=ALU.mult), reads=[braw.b, cr.b], writes=[x1.b])
                        P.op("dve", lambda e, x2=x2, bim=bim, ci=ci: e.tensor_tensor(out=x2.t[:], in0=bim, in1=ci.t[:], op=ALU.mult), reads=[braw.b, ci.b], writes=[x2.b])
                        P.op("dve", lambda e, x1=x1, x2=x2, btr=btr: e.tensor_tensor(out=btr, in0=x1.t[:], in1=x2.t[:], op=ALU.subtract), reads=[x1.b, x2.b], writes=[BtR.b])
                        P.op("dve", lambda e, x1=x1, bre=bre, ci=ci: e.tensor_tensor(out=x1.t[:], in0=bre, in1=ci.t[:], op=ALU.mult), reads=[braw.b, ci.b], writes=[x1.b])
                        P.op("dve", lambda e, x2=x2, bim=bim, cr=cr: e.tensor_tensor(out=x2.t[:], in0=bim, in1=cr.t[:], op=ALU.mult), reads=[braw.b, cr.b], writes=[x2.b])
                        P.op("dve", lambda e, x1=x1, x2=x2, bti=bti: e.tensor_tensor(out=bti, in0=x1.t[:], in1=x2.t[:], op=ALU.add), reads=[x1.b, x2.b], writes=[BtI.b])
                    P.barrier()

                if cfg.debug:
                    P.dma("sp", O["dbg_E"][:, 0], Ere.t[:], reads=[Ere.b])
                    P.dma("sp", O["dbg_E"][:, 1], Eim.t[:], reads=[Eim.b])
                    P.dma("sp", O["dbg_rho"], rho_c.t[:], reads=[rho_c.b])
                    P.dma("pool", O["dbg_bt"][:, 0], BtR.t[:].rearrange("p d c q -> p (d c q)"), reads=[BtR.b])
                    P.dma("pool", O["dbg_bt"][:, 1], BtI.t[:].rearrange("p d c q -> p (d c q)"), reads=[BtI.b])
                NB = 4
                NBA = 8
                def mk(n, dt=F32):
                    return [TB(P.sb([128, T], dt, ph)) for _ in range(n)]
                bre_t, bim_t = mk(NBA), mk(NBA)
                t1, t2, t3, t4 = mk(NB), mk(NB), mk(NB), mk(NB)
                brp, bip = mk(NB), mk(NB)
                rr_t, ri_t = mk(NB), mk(NB)
                o1, o2, o3, o4 = mk(NB, BF16), mk(NB, BF16), mk(NB, BF16), mk(NB, BF16)
                ctmp = [TB(P.sb([128, 4], F32, ph)) for _ in range(NB)]
                y32 = [TB(P.sb([128, T], F32, ph)) for _ in range(2)]
                g1 = [TB(P.sb([128, T], F32, ph)) for _ in range(2)]
                g2 = [TB(P.sb([128, T], F32, ph)) for _ in range(2)]
                z32 = [TB(P.sb([128, T], F32, ph)) for _ in range(2)]
                zb = [TB(P.sb([128, T], BF16, ph)) for _ in range(2)]
                sg = [TB(P.sb([128, T], F32, ph)) for _ in range(2)]
                ob = [TB(P.sb([128, T], BF16, ph)) for _ in range(2)]
                st0 = TB(P.sb([128, 32], F32, ph))
                uctr = [0]

                def tt(eng, o, a, b, op, rb, wb):
                    P.op(eng, lambda e: e.tensor_tensor(out=o, in0=a, in1=b, op=op), reads=rb, writes=wb)

                def unit_pre(d, tc, sc):
                    cc, pg = sc // 4, sc % 4
                    rows = slice(pg * 32, pg * 32 + 32)
                    t0 = tc * T
                    i = uctr[0] % NB
                    ia = uctr[0] % NBA
                    sbk = sbank[uctr[0] % 4]
                    uctr[0] += 1
                    j = d * 8 + sc
                    u_ap = uT.t[rows, cc, t0:t0 + T]
                    if d == 1:
                        u_ap = u_ap[:, ::-1]
                    P.op("pe", lambda e: e.matmul(sbk.t[:, 0:T], lhsT=BtR.t[rows, d, cc, :], rhs=u_ap, start=True, stop=True,
                                                  tile_position=(pg * 32, 0)),
                         reads=[BtR.b, uT.b], writes=[sbk.b])
                    P.op("pe", lambda e: e.matmul(sbk.t[:, T:2 * T], lhsT=BtI.t[rows, d, cc, :], rhs=u_ap, start=True, stop=True,
                                                  tile_position=(pg * 32, 0)),
                         reads=[BtI.b, uT.b], writes=[sbk.b])
                    bre, bim = bre_t[ia], bim_t[ia]
                    P.op("act", lambda e: e.copy(out=bre.t[:], in_=sbk.t[:, 0:T]), reads=[sbk.b], writes=[bre.b])
                    P.op("act", lambda e: e.copy(out=bim.t[:], in_=sbk.t[:, T:2 * T]), reads=[sbk.b], writes=[bim.b])
                    return (d, tc, sc, i, j, ia)

                def unit_pre_b(stt):
                    d, tc, sc, i, j, ia = stt
                    bre, bim = bre_t[ia], bim_t[ia]
                    ec, es = Ere.t[:, j, :], Eim.t[:, j, :]
                    return [
                        lambda: tt("dve", t1[i].t[:], bre.t[:], ec, ALU.mult, [bre.b, Ere.b], [t1[i].b]),
                        lambda: tt("dve", t2[i].t[:], bim.t[:], es, ALU.mult, [bim.b, Eim.b], [t2[i].b]),
                        lambda: tt("dve", t3[i].t[:], bim.t[:], ec, ALU.mult, [bim.b, Ere.b], [t3[i].b]),
                        lambda: tt("dve", t4[i].t[:], bre.t[:], es, ALU.mult, [bre.b, Eim.b], [t4[i].b]),
                        lambda: tt("dve", brp[i].t[:], t1[i].t[:], t2[i].t[:], ALU.add, [t1[i].b, t2[i].b], [brp[i].b]),
                        lambda: tt("dve", bip[i].t[:], t3[i].t[:], t4[i].t[:], ALU.subtract, [t3[i].b, t4[i].b], [bip[i].b]),
                    ]

                def unit_post(stt, first, last_extra):
                    d, tc, sc, i, j, ia = stt
                    cc = sc // 4
                    ec, es = Ere.t[:, j, :], Eim.t[:, j, :]
                    rb_ = rho_c.t[:, j:j + 1].to_broadcast([128, T])
                    rr, ri = rr_t[i], ri_t[i]
                    dv = [
                        lambda: P.op("dve", lambda e: e.tensor_tensor_scan(out=rr.t[:], data0=rb_, data1=brp[i].t[:],
                                                                           initial=carry.t[:, d, sc, 0:1], op0=ALU.mult, op1=ALU.add),
                                     reads=[rho_c.b, brp[i].b, carry.b], writes=[rr.b]),
                        lambda: P.op("dve", lambda e: e.tensor_tensor_scan(out=ri.t[:], data0=rb_, data1=bip[i].t[:],
                                                                           initial=carry.t[:, d, sc, 1:2], op0=ALU.mult, op1=ALU.add),
                                     reads=[rho_c.b, bip[i].b, carry.b], writes=[ri.b]),
                        lambda: tt("dve", o1[i].t[:], rr.t[:], ec, ALU.mult, [rr.b, Ere.b], [o1[i].b]),
                        lambda: tt("dve", o3[i].t[:], rr.t[:], es, ALU.mult, [rr.b, Eim.b], [o3[i].b]),
                        lambda: tt("dve", o2[i].t[:], ri.t[:], es, ALU.mult, [ri.b, Eim.b], [o2[i].b]),
                        lambda: tt("dve", o4[i].t[:], ri.t[:], ec, ALU.mult, [ri.b, Ere.b], [o4[i].b]),
                    ]

                    def rest():
                        ct_ = ctmp[i]
                        ecl, esl = Ere.t[:, j, T - 1:T], Eim.t[:, j, T - 1:T]
                        rrl, ril = rr.t[:, T - 1:T], ri.t[:, T - 1:T]
                        P.op("act", lambda e: e.activation(out=ct_.t[:, 0:1], in_=rrl, func=AF.Identity, scale=ecl), reads=[rr.b, Ere.b], writes=[ct_.b])
                        P.op("act", lambda e: e.activation(out=ct_.t[:, 1:2], in_=rrl, func=AF.Identity, scale=esl), reads=[rr.b, Eim.b], writes=[ct_.b])
                        P.op("act", lambda e: e.activation(out=ct_.t[:, 2:3], in_=ril, func=AF.Identity, scale=esl), reads=[ri.b, Eim.b], writes=[ct_.b])
                        P.op("act", lambda e: e.activation(out=ct_.t[:, 3:4], in_=ril, func=AF.Identity, scale=ecl), reads=[ri.b, Ere.b], writes=[ct_.b])
                        P.op("act", lambda e: e.activation(out=carry.t[:, d, sc, 0:1], in_=ct_.t[:, 2:3], func=AF.Identity, scale=-1.0, bias=ct_.t[:, 0:1]),
                             reads=[ct_.b], writes=[carry.b])
                        P.op("act", lambda e: e.activation(out=carry.t[:, d, sc, 1:2], in_=ct_.t[:, 3:4], func=AF.Identity, scale=1.0, bias=ct_.t[:, 1:2]),
                             reads=[ct_.b], writes=[carry.b])
                        yp = ops_[cc]
                        aps = [o1[i].t[:], o2[i].t[:], o3[i].t[:], o4[i].t[:]]
                        if d == 1:
                            aps = [a_[:, ::-1] for a_ in aps]
                        var = [0, 2, 1, 1]
                        bufs = [o1[i].b, o2[i].b, o3[i].b, o4[i].b]
                        for q_ in range(4):
                            P.op("pe", lambda e, q_=q_: e.matmul(yp.t[:, 0:T], lhsT=Ct.t[:, d, var[q_], sc, :], rhs=aps[q_],
                                                                start=(first and q_ == 0), stop=(last_extra and q_ == 3)),
                                 reads=[Ct.b, bufs[q_]], writes=[yp.b])
                    return dv, rest

                def run_units(ulist, tail_fn):
                    LA, LB = 5, 2
                    n = len(ulist)
                    stts = [None] * n
                    for step in range(n + LA):
                        ia = step
                        ib = step - (LA - LB)
                        ip = step - LA
                        if ia < n:
                            d, tc, sc, first, last_extra = ulist[ia]
                            stts[ia] = unit_pre(d, tc, sc)
                        pre = unit_pre_b(stts[ib]) if 0 <= ib < n else []
                        if 0 <= ip < n:
                            d, tc, sc, first, last_extra = ulist[ip]
                            dv, rest = unit_post(stts[ip], first, last_extra)
                        else:
                            dv, rest = [], None
                        if pre and dv:
                            order = [dv[0], pre[0], dv[1], pre[1], dv[2], pre[2], dv[3], pre[3], dv[4], pre[4], dv[5], pre[5]]
                        else:
                            order = list(pre) + list(dv)
                        for th in order:
                            th()
                        if rest is not None:
                            rest()
                            d, tc, sc, first, last_extra = ulist[ip]
                            if sc == 7:
                                tail_fn(d, tc)

                def ssm_seq(tok0, Ls, latent, seq):
                    nT = Ls // T
                    for c in range(2):
                        for t0 in range(0, Ls, TS):
                            w = min(TS, Ls - t0)
                            P.dma("pool", uT.t[:, c, t0:t0 + w], projT[:, c, tok0 + t0:tok0 + t0 + w],
                                  reads=[b_proj[(tok0 + t0) // TS]], writes=[uT.b])
                    if latent:
                        P.dma("sp", st0.t[:], I["ssm_st0"][l], writes=[st0.b])
                        P.op("dve", lambda e: e.tensor_copy(out=carry.t[:].rearrange("p d s r -> p (d s r)"), in_=st0.t[:]),
                             reads=[st0.b], writes=[carry.b])
                    else:
                        P.op("dve", lambda e: e.memset(carry.t[:], 0.0), writes=[carry.b])
                    def tail(d, tc):
                        t0 = tc * T
                        if d == 0:
                            for cc in range(2):
                                P.op("act", lambda e, cc=cc, tc=tc: e.copy(out=yacc.t[:, cc, tc * T:(tc + 1) * T], in_=ops_[cc].t[:, 0:T]),
                                     reads=[ops_[cc].b], writes=[yacc.b])
                            return
                        for cc in range(2):
                            yp = ops_[cc]
                            P.op("pe", lambda e, cc=cc, yp=yp, t0=t0: e.matmul(yp.t[:, 0:T], lhsT=Dd.t[:, cc, :], rhs=uT.t[:, cc, t0:t0 + T],
                                                                        start=False, stop=True),
                                 reads=[Dd.b, uT.b], writes=[yp.b])
                            y_, a_, b_, z_, zb_ = y32[cc], g1[cc], g2[cc], z32[cc], zb[cc]
                            P.op("dve", lambda e, y_=y_, yp=yp, cc=cc, t0=t0: e.tensor_tensor(
                                out=y_.t[:], in0=yp.t[:, 0:T], in1=yacc.t[:, cc, t0:t0 + T], op=ALU.add),
                                reads=[yp.b, yacc.b], writes=[y_.b])
                            P.op("pool", lambda e, y_=y_, a_=a_: e.tensor_tensor(out=a_.t[:], in0=y_.t[:], in1=y_.t[:], op=ALU.mult),
                                 reads=[y_.b], writes=[a_.b])
                            P.op("pool", lambda e, a_=a_: e.tensor_scalar(out=a_.t[:], in0=a_.t[:], scalar1=0.044715, scalar2=1.0,
                                                                         op0=ALU.mult, op1=ALU.add), reads=[a_.b], writes=[a_.b])
                            P.op("pool", lambda e, a_=a_, b_=b_, y_=y_: e.tensor_tensor(out=b_.t[:], in0=a_.t[:], in1=y_.t[:], op=ALU.mult),
                                 reads=[a_.b, y_.b], writes=[b_.b])
                            P.op("act", lambda e, a_=a_, b_=b_: e.activation(out=a_.t[:], in_=b_.t[:], func=AF.Sigmoid, scale=1.5957691216057308),
                                 reads=[b_.b], writes=[a_.b])
                            P.op("pool", lambda e, a_=a_, z_=z_, y_=y_: e.tensor_tensor(out=z_.t[:], in0=a_.t[:], in1=y_.t[:], op=ALU.mult),
                                 reads=[a_.b, y_.b], writes=[z_.b])
                            P.op("act", lambda e, z_=z_, zb_=zb_: e.copy(out=zb_.t[:], in_=z_.t[:]), reads=[z_.b], writes=[zb_.b])
                        for jo in range(2):
                            gp_ = stat if jo == 0 else pmisc
                            for ji in range(2):
                                P.op("pe", lambda e, gp_=gp_, ji=ji, jo=jo: e.matmul(
                                    gp_.t[:, 0:T], lhsT=wglu.t[:, ji, jo * 128:(jo + 1) * 128], rhs=zb[ji].t[:],
                                    start=(ji == 0), stop=(ji == 1)), reads=[wglu.b, zb[ji].b], writes=[gp_.b])
                            P.op("act", lambda e, gp_=gp_, jo=jo: e.activation(
                                out=sg[jo].t[:], in_=gp_.t[:, 0:T], func=AF.Sigmoid, bias=vec[l].t[:, 98 + jo:99 + jo], scale=1.0),
                                reads=[gp_.b, vec[l].b], writes=[sg[jo].b])
                            P.op("pool", lambda e, jo=jo: e.tensor_tensor(out=ob[jo].t[:], in0=z32[jo].t[:], in1=sg[jo].t[:], op=ALU.mult),
                                 reads=[z32[jo].b, sg[jo].b], writes=[ob[jo].b])
                            P.dma("sp", mergT[:, jo, tok0 + t0:tok0 + t0 + T], ob[jo].t[:], reads=[ob[jo].b],
                                  writes=[b_merg[(tok0 + t0) // TS]])

                    ul = []
                    for tc in range(nT):
                        for sc in range(8):
                            ul.append((0, tc, sc, sc % 4 == 0, sc % 4 == 3))
                    for tc in reversed(range(nT)):
                        for sc in range(8):
                            ul.append((1, tc, sc, sc % 4 == 0, False))
                    run_units(ul, tail)
                    if not latent:
                        P.op("dve", lambda e, seq=seq: e.tensor_copy(
                            out=stout.t[:, seq, l, :], in_=carry.t[:].rearrange("p d s r -> p (d s r)")),
                            reads=[carry.b], writes=[stout.b])

                ssm_seq(0, L, True, -1)
                for s in range(NSEQ):
                    ssm_seq(L + s * L_CTX, L_CTX, False, s)
                P.barrier()

        def phase_MX(l):
            zc = []
            if "attn" in cfg.mixers:
                mx_attn_all(l)
            else:
                zc += [2, 3, 4, 5]
            if "fnet" in cfg.mixers:
                mx_fnet(l)
            else:
                zc += [6, 7]
            if "ssm" in cfg.mixers:
                mx_ssm(l)
            else:
                zc += [0, 1]
            if zc:
                zero_merg(zc)

        with ExitStack() as wst:
            W = alloc_W(wst)
            load_ffn(W, 0, 1)
            phase_X0()
            phase0()
            phase_F(W, 0, 1)
        for l in range(DP):
            phase_PJ(l)
            if have_mix:
                phase_MX(l)
            with ExitStack() as wst:
                W = alloc_W(wst)
                load_ffn(W, l, 2)
                if have_mix:
                    phase_WO(l)
                phase_F(W, l, 2, last=(l == DP - 1))
                if l + 1 < DP:
                    load_ffn(W, l + 1, 1)
                    phase_F(W, l + 1, 1)

        P.dma("sp", O["o_st"], stout.t[:].rearrange("p s l x -> p (s l x)"), reads=[stout.b])
        P.barrier()
        P.emit()
    return nc


_CACHE = {}


def _get_program(cfg_key):
    if cfg_key not in _CACHE:
        _CACHE[cfg_key] = build_program(Cfg(*cfg_key))
    return _CACHE[cfg_key]


def host_constants(cfg):
    f = np.float32
    L = cfg.l_lat
    C = {}
    t = np.arange(L)
    row = (t // 64).astype(f)
    col = (t % 64).astype(f)
    inv = (f(10000.0) ** (-np.arange(16, dtype=f) / f(16))).astype(f)
    ang = np.concatenate([row[:, None] * inv[None, :], col[:, None] * inv[None, :]], axis=1).astype(f)
    cosT = np.cos(ang).astype(f).T
    sinT = np.sin(ang).astype(f).T
    C["ropeC"] = np.ascontiguousarray(np.tile(cosT, (4, 1)))
    C["ropeS"] = np.ascontiguousarray(np.tile(sinT, (4, 1)))
    R = np.zeros((128, 128), f)
    for base in (0, 64):
        for i in range(32):
            R[base + i + 32, base + i] = -1.0
            R[base + i, base + i + 32] = 1.0
    C["rotm"] = R
    kp = np.arange(128)[:, None]
    qf = np.arange(128)[None, :]
    NEG = -30000.0
    m1 = np.where(qf <= kp, 0.0, NEG)
    m2 = np.where(kp <= qf, 0.0, NEG)
    C["masks"] = np.concatenate([m1, m2], axis=1).astype(f)
    c = np.arange(256)
    a256 = 2 * np.pi * np.outer(c, c) / 256.0
    cs = np.concatenate([np.cos(a256), np.sin(a256)], axis=1)
    C["cs256"] = cs.reshape(2, 128, 512).astype(f)
    scl = 1.0 / np.sqrt(256.0 * L)
    aL = 2 * np.pi * ((np.outer(np.arange(L), np.arange(512))) % L) / float(L)
    C["fn_ec"] = (np.cos(aL) * scl).astype(f)
    C["fn_nes"] = (-np.sin(aL) * scl).astype(f)
    C["fn_c256s"] = (np.cos(a256) / 256.0).astype(f)
    C["fn_ns256s"] = (-np.sin(a256) / 256.0).astype(f)
    nkt = L // 512
    ph = np.zeros((128, 4, 8), f)
    p_ = np.arange(128)[:, None]
    kt_ = np.arange(nkt)[None, :]
    aphi = 2 * np.pi * ((p_ * kt_) % nkt) / float(nkt)
    ph[:, 0, :nkt] = np.cos(aphi)
    ph[:, 1, :nkt] = np.sin(aphi)
    ph[:, 2, :nkt] = -np.sin(aphi)
    C["fn_phi"] = ph
    return C


def ssm_layouts(inp):
    f = np.float32
    out = {}
    lre = np.asarray(inp["ssm_lambda_re"], f)
    lim = np.asarray(inp["ssm_lambda_im"], f)
    ldt = np.repeat(np.asarray(inp["ssm_log_dt"], f)[..., None], 64, axis=-1)
    par = np.stack([lre, lim, ldt], axis=2)
    p8 = par.reshape(DEPTH, 2, 3, 8, 128)
    pc = p8.transpose(0, 4, 2, 1, 3)
    pc = np.repeat(pc[..., None], 16, axis=-1)
    out["ssm_colp"] = np.ascontiguousarray(pc).reshape(DEPTH, 128, 768)
    p_r = p8.reshape(DEPTH, 2, 3, 2, 4, 128)
    p_r = p_r.transpose(0, 4, 1, 2, 3, 5)
    p_r = np.repeat(p_r[:, :, None], 32, axis=2)
    out["ssm_rowp"] = np.ascontiguousarray(p_r).reshape(DEPTH, 128, 1536)
    bt = np.zeros((DEPTH, 4, 32, 2, 2, 2, 128), f)
    bb = np.stack([np.asarray(inp["ssm_b_re"], f), np.asarray(inp["ssm_b_im"], f)], axis=2)
    ct = np.zeros((DEPTH, 128, 2, 2, 8, 128), f)
    cc_ = np.stack([np.asarray(inp["ssm_c_re"], f), np.asarray(inp["ssm_c_im"], f)], axis=2)
    for sc in range(8):
        cc, pg = sc // 4, sc % 4
        for gg in range(2):
            g = 2 * sc + gg
            bt[:, pg, gg * 16:(gg + 1) * 16, :, :, cc, gg * 64:(gg + 1) * 64] = bb[:, :, :, g].transpose(0, 4, 1, 2, 3)
            ct[:, gg * 64:(gg + 1) * 64, :, :, sc, pg * 32 + gg * 16:pg * 32 + (gg + 1) * 16] = cc_[:, :, :, g].transpose(0, 4, 1, 2, 3)
    out["ssm_bt"] = bt.reshape(DEPTH, 128, 1024)
    out["ssm_ct"] = ct.reshape(DEPTH, 128, 4096)
    dd = np.zeros((DEPTH, 128, 2, 128), f)
    sd = np.asarray(inp["ssm_d"], f).reshape(DEPTH, 2, 128)
    for i in range(128):
        dd[:, i, :, i] = sd[:, :, i]
    out["ssm_dd"] = dd
    out["ssm_w_glu"] = np.ascontiguousarray(inp["ssm_w_glu"], f)
    return out


def ssm_state_in(st):
    s = np.asarray(st, np.float32).reshape(DEPTH, 2, 8, 128, 2)
    return np.ascontiguousarray(s.transpose(0, 3, 1, 2, 4)).reshape(DEPTH, 128, 32)


def ssm_state_out(o):
    s = np.asarray(o, np.float32).reshape(128, NSEQ, DEPTH, 2, 8, 2)
    s = s.transpose(1, 2, 3, 4, 0, 5)
    return np.ascontiguousarray(s).reshape(NSEQ, DEPTH, 2, 16, 64, 2)


def make_in_maps(inp, cfg):
    f = np.float32
    shared = {
        "c_ctx": np.ascontiguousarray(inp["c_ctx"], f).reshape(8, 128),
        "w_mod": np.ascontiguousarray(inp["w_mod"], f),
        "b_mod": np.ascontiguousarray(inp["b_mod"], f).reshape(DEPTH, 72, 128),
        "final_norm": np.ascontiguousarray(inp["final_norm"], f).reshape(8, 128),
        "ident": np.eye(128, dtype=f),
        "w_in": np.ascontiguousarray(inp["w_in"], f),
        "w_out": np.ascontiguousarray(inp["w_out"], f),
        "ssm_d": np.ascontiguousarray(inp["ssm_d"], f).reshape(DEPTH, 2, 128),
        "ssm_b_glu": np.ascontiguousarray(inp["ssm_b_glu"], f).reshape(DEPTH, 2, 128),
        "fnet_b": np.ascontiguousarray(inp["fnet_b"], f).reshape(DEPTH, 2, 128),
        "ax_q_norm": np.ascontiguousarray(inp["ax_q_norm"], f).reshape(DEPTH, 1, 64),
        "ax_k_norm": np.ascontiguousarray(inp["ax_k_norm"], f).reshape(DEPTH, 1, 64),
    }
    for n in ("norm_ffn1", "norm_mix", "norm_ffn2"):
        shared[n] = np.ascontiguousarray(inp[n], f).reshape(DEPTH, 8, 128)
    shared.update(host_constants(cfg))
    shared["swa_sink"] = np.ascontiguousarray(inp["swa_sink"], f)
    shared["fnet_w"] = np.ascontiguousarray(inp["fnet_w"], f)
    shared.update(ssm_layouts(inp))
    for n in ("ffn1_w_gate", "ffn1_w_up", "ffn2_w_gate", "ffn2_w_up", "ffn1_w_down", "ffn2_w_down"):
        shared[n] = np.ascontiguousarray(inp[n], f)
    maps = []
    xp = np.asarray(inp["x_prompt"], f)
    xs = np.asarray(inp["x_sample"], f)
    for c in range(NCORE):
        m = dict(shared)
        m["x_lat"] = np.ascontiguousarray(xs[c, :cfg.l_lat])
        m["x_ctx"] = np.ascontiguousarray(xp[c * NSEQ:(c + 1) * NSEQ]).reshape(NSEQ * L_CTX, D)
        m["c_b"] = np.ascontiguousarray(inp["c"][c], f).reshape(8, 128)
        m["cache_swa"] = np.ascontiguousarray(inp["cache_swa_kv"][c], f).reshape(DEPTH, 2, L_CTX, 128)
        m["cache_ax"] = np.ascontiguousarray(inp["cache_axial_kv"][c], f).reshape(DEPTH, 2, L_CTX, 128)
        m["ssm_st0"] = ssm_state_in(inp["state_ssm"][c])
        maps.append(m)
    return maps


def kernel(**inp):
    cfg_key = (4096, ("fnet", "attn", "ssm"), DEPTH)
    cfg = Cfg(*cfg_key)
    nc = _get_program(cfg_key)
    maps = make_in_maps(inp, cfg)
    res = run_bass_kernel_spmd(nc, maps, core_ids=list(range(NCORE)))
    R = res.results
    y_prompt = np.concatenate([R[c]["y_ctx"].reshape(NSEQ, L_CTX, D) for c in range(NCORE)], axis=0)
    y_sample = np.stack([R[c]["y_lat"] for c in range(NCORE)], axis=0)
    o_swa = np.concatenate([R[c]["o_swa"].reshape(NSEQ, DEPTH, 2, L_CTX, 2, 64) for c in range(NCORE)], axis=0)
    o_ax = np.concatenate([R[c]["o_ax"].reshape(NSEQ, DEPTH, 2, L_CTX, 2, 64) for c in range(NCORE)], axis=0)
    o_st = np.concatenate([ssm_state_out(R[c]["o_st"]) for c in range(NCORE)], axis=0)
    return (y_prompt.astype(np.float32), y_sample.astype(np.float32), o_swa.astype(np.float32),
            o_ax.astype(np.float32), o_st.astype(np.float32))
```

```python
import numpy as np
import ml_dtypes
import concourse.bass as bass
import concourse.mybir as mybir
from concourse.bass_utils import run_bass_kernel_spmd
from contextlib import ExitStack

F32 = mybir.dt.float32
BF16 = mybir.dt.bfloat16
AF = mybir.ActivationFunctionType
ALU = mybir.AluOpType

D = 1024
DFF = 2816
NF = 22
PIN = 1536
DEPTH = 2
NCORE = 8
L_CTX = 256
NSEQ = 4
TS = 512
EPS = 1e-6


class Sem:
    def __init__(self, h):
        self.h = h
        self.val = 0


class Buf:
    __slots__ = ("name", "last_w", "readers")

    def __init__(self, name=""):
        self.name = name
        self.last_w = None
        self.readers = []


class TB:
    def __init__(self, t, name=""):
        self.t = t
        self.b = Buf(name)


class Prog:
    SAME_ENGINE_SYNC = True
    NDMA = 12
    SEM_LIMIT = 24000

    def __init__(self, nc, stack):
        self.nc = nc
        self.stack = stack
        self.engs = {"pe": nc.tensor, "act": nc.scalar, "dve": nc.vector,
                     "pool": nc.gpsimd, "sp": nc.sync}
        self.nsem = 0
        self.esem = {k: self._newsem() for k in self.engs}
        self.waited = {k: {} for k in self.engs}
        self.lists = {k: [] for k in self.engs}
        self.dsems = {}
        self.drr = {}
        for q in ("sp", "pool", "act"):
            self.dsems[q] = [self._newsem() for i in range(self.NDMA)]
            self.drr[q] = 0
        self.ntile = 0

    def _newsem(self):
        self.nsem += 1
        return Sem(self.stack.enter_context(self.nc.semaphore("sem%d" % self.nsem)))

    def sb(self, shape, dtype, stack=None):
        self.ntile += 1
        return (stack or self.stack).enter_context(
            self.nc.sbuf_tensor("t%d" % self.ntile, list(shape), dtype))

    def ps(self, shape, dtype=F32, stack=None):
        self.ntile += 1
        return (stack or self.stack).enter_context(
            self.nc.psum_tensor("p%d" % self.ntile, list(shape), dtype))

    def _deps(self, eng, reads, writes, is_dma=False):
        deps = {}

        def add(sv):
            s, v = sv
            if deps.get(s, 0) < v:
                deps[s] = v
        for b in reads:
            if b.last_w is not None:
                add(b.last_w)
        for b in writes:
            if b.last_w is not None:
                add(b.last_w)
            for r in b.readers:
                add(r)
        own = self.esem.get(eng)
        waits = []
        w = self.waited[eng]
        for s, v in deps.items():
            if s is own and not is_dma and (eng == "pe" or not self.SAME_ENGINE_SYNC):
                continue
            if w.get(s, 0) >= v:
                continue
            w[s] = v
            waits.append((s.h, v))
        return waits

    def op(self, eng, fn, reads=(), writes=()):
        waits = self._deps(eng, reads, writes)
        own = self.esem[eng]
        if own.val >= self.SEM_LIMIT:
            own = self._newsem()
            self.esem[eng] = own
        own.val += 1
        val = own.val
        for b in reads:
            b.readers.append((own, val))
            if len(b.readers) > 24:
                last = {}
                for s, v in b.readers:
                    if last.get(s, 0) < v:
                        last[s] = v
                b.readers = list(last.items())
        for b in writes:
            b.last_w = (own, val)
            b.readers = []
        oh = own.h

        def run(e):
            for s, v in waits:
                e.wait_ge(s, v)
            fn(e).then_inc(oh, 1)
        self.lists[eng].append(run)

    def dma(self, q, out, in_, reads=(), writes=(), **kw):
        sems = self.dsems[q]
        i = self.drr[q] % len(sems)
        s = sems[i]
        if s.val >= self.SEM_LIMIT:
            old = s
            s = self._newsem()
            sems[i] = s
            w = self.waited[q]
            extra = [(old.h, old.val)] if w.get(old, 0) < old.val else []
            w[old] = old.val
        else:
            extra = []
        self.drr[q] += 1
        waits = extra + self._deps(q, reads, writes, is_dma=True)
        w = self.waited[q]
        if s.val > 0 and w.get(s, 0) < s.val:
            w[s] = s.val
            waits.append((s.h, s.val))
        s.val += 16
        val = s.val
        for b in reads:
            b.readers.append((s, val))
        for b in writes:
            b.last_w = (s, val)
            b.readers = []
        sh = s.h

        def run(e):
            for ss, v in waits:
                e.wait_ge(ss, v)
            e.dma_start(out=out, in_=in_, **kw).then_inc(sh, 16)
        self.lists[q].append(run)

    def barrier(self):
        allv = []
        for k, s in self.esem.items():
            if s.val > 0:
                allv.append(s)
        for q in self.dsems:
            for s in self.dsems[q]:
                if s.val > 0:
                    allv.append(s)
        for eng in self.engs:
            waits = []
            w = self.waited[eng]
            own = self.esem[eng]
            for s in allv:
                if s is own:
                    continue
                if w.get(s, 0) >= s.val:
                    continue
                w[s] = s.val
                waits.append((s.h, s.val))
            if waits:
                def run(e, waits=waits):
                    for s, v in waits:
                        e.wait_ge(s, v)
                self.lists[eng].append(run)

    def emit(self):
        nc = self.nc
        lists = self.lists
        with nc.Block() as block:
            @block.tensor
            def _(e):
                for f in lists["pe"]:
                    f(e)

            @block.scalar
            def _(e):
                for f in lists["act"]:
                    f(e)

            @block.vector
            def _(e):
                for f in lists["dve"]:
                    f(e)

            @block.gpsimd
            def _(e):
                for f in lists["pool"]:
                    f(e)

            @block.sync
            def _(e):
                for f in lists["sp"]:
                    f(e)


class Cfg:
    def __init__(self, l_lat=4096, mixers=("fnet", "attn", "ssm"), depth=DEPTH, debug=False):
        self.debug = debug
        self.l_lat = l_lat
        self.nt_lat = l_lat // TS
        self.nt = self.nt_lat + (NSEQ * L_CTX) // TS
        self.ntok = self.nt * TS
        self.mixers = mixers
        self.depth = depth


def build_program(cfg):
    nc = bass.Bass("TRN2", target_bir_lowering=False)
    L = cfg.l_lat
    NT = cfg.nt
    NTOK = cfg.ntok
    DP = cfg.depth

    def din(name, shape, dt=F32):
        return nc.dram_tensor(name, list(shape), dt, kind="ExternalInput").ap()

    def dout(name, shape, dt=F32):
        return nc.dram_tensor(name, list(shape), dt, kind="ExternalOutput").ap()

    def dscr(name, shape, dt=F32):
        return nc.dram_tensor(name, list(shape), dt, kind="Internal").ap()

    I = {}
    I["x_lat"] = din("x_lat", [L, D])
    I["x_ctx"] = din("x_ctx", [NSEQ * L_CTX, D])
    I["c_b"] = din("c_b", [8, 128])
    I["c_ctx"] = din("c_ctx", [8, 128])
    I["w_mod"] = din("w_mod", [DEPTH, D, 9 * D])
    I["b_mod"] = din("b_mod", [DEPTH, 72, 128])
    for n in ("norm_ffn1", "norm_mix", "norm_ffn2"):
        I[n] = din(n, [DEPTH, 8, 128])
    for n in ("ffn1_w_gate", "ffn1_w_up", "ffn2_w_gate", "ffn2_w_up"):
        I[n] = din(n, [DEPTH, D, DFF])
    for n in ("ffn1_w_down", "ffn2_w_down"):
        I[n] = din(n, [DEPTH, DFF, D])
    I["w_in"] = din("w_in", [DEPTH, D, PIN])
    I["w_out"] = din("w_out", [DEPTH, D, D])
    I["ssm_d"] = din("ssm_d", [DEPTH, 2, 128])
    I["ssm_b_glu"] = din("ssm_b_glu", [DEPTH, 2, 128])
    I["fnet_b"] = din("fnet_b", [DEPTH, 2, 128])
    I["ax_q_norm"] = din("ax_q_norm", [DEPTH, 1, 64])
    I["ax_k_norm"] = din("ax_k_norm", [DEPTH, 1, 64])
    I["final_norm"] = din("final_norm", [8, 128])
    I["ident"] = din("ident", [128, 128])
    I["cache_swa"] = din("cache_swa", [DEPTH, 2, L_CTX, 128])
    I["cache_ax"] = din("cache_ax", [DEPTH, 2, L_CTX, 128])
    I["ropeC"] = din("ropeC", [128, L])
    I["ropeS"] = din("ropeS", [128, L])
    I["rotm"] = din("rotm", [128, 128])
    I["masks"] = din("masks", [128, 256])
    I["swa_sink"] = din("swa_sink", [DEPTH, 4])
    I["cs256"] = din("cs256", [2, 128, 512])
    I["fn_ec"] = din("fn_ec", [L, 512])
    I["fn_nes"] = din("fn_nes", [L, 512])
    I["fn_c256s"] = din("fn_c256s", [256, 256])
    I["fn_ns256s"] = din("fn_ns256s", [256, 256])
    I["fn_phi"] = din("fn_phi", [128, 4, 8])
    I["fnet_w"] = din("fnet_w", [DEPTH, 256, 256])
    I["ssm_colp"] = din("ssm_colp", [DEPTH, 128, 768])
    I["ssm_rowp"] = din("ssm_rowp", [DEPTH, 128, 1536])
    I["ssm_bt"] = din("ssm_bt", [DEPTH, 128, 1024])
    I["ssm_ct"] = din("ssm_ct", [DEPTH, 128, 4096])
    I["ssm_dd"] = din("ssm_dd", [DEPTH, 128, 2, 128])
    I["ssm_st0"] = din("ssm_st0", [DEPTH, 128, 32])
    I["ssm_w_glu"] = din("ssm_w_glu", [DEPTH, 256, 256])

    O = {}
    O["y_lat"] = dout("y_lat", [L, D])
    O["y_ctx"] = dout("y_ctx", [NSEQ * L_CTX, D])
    O["o_swa"] = dout("o_swa", [NSEQ, DEPTH, 2, L_CTX, 128])
    O["o_ax"] = dout("o_ax", [NSEQ, DEPTH, 2, L_CTX, 128])
    O["o_st"] = dout("o_st", [128, NSEQ * DEPTH * 32])

    if cfg.debug:
        O["dbg_E"] = dout("dbg_E", [128, 2, 16, 256])
        O["dbg_rho"] = dout("dbg_rho", [128, 16])
        O["dbg_rhofull"] = dout("dbg_rhofull", [128, 256])
        O["dbg_sc"] = dout("dbg_sc", [128, 512])
        O["dbg_colp"] = dout("dbg_colp", [128, 768])
        O["dbg_bt"] = dout("dbg_bt", [128, 2, 512])
    hbuf = dscr("hbuf", [128, 8, NTOK])
    projT = dscr("projT", [128, 12, NTOK])
    if cfg.debug:
        mergT = dout("mergT", [128, 8, NTOK], BF16)
    else:
        mergT = dscr("mergT", [128, 8, NTOK], BF16)
    b_hbuf = [Buf() for _ in range(NT)]
    b_proj = [Buf() for _ in range(NT)]
    b_merg = [Buf() for _ in range(NT)]

    def x_rows(t, s):
        tok = t * TS + s * 128
        if tok < L:
            return I["x_lat"][tok:tok + 128, :]
        tok -= L
        return I["x_ctx"][tok:tok + 128, :]

    def y_rows(t, s):
        tok = t * TS + s * 128
        if tok < L:
            return O["y_lat"][tok:tok + 128, :]
        tok -= L
        return O["y_ctx"][tok:tok + 128, :]

    with ExitStack() as st:
        P = Prog(nc, st)

        ident = TB(P.sb([128, 128], F32))
        ones_bf = TB(P.sb([128, 128], BF16))
        bd64 = TB(P.sb([128, 128], BF16))
        epst = TB(P.sb([128, 1], F32))
        vec = [TB(P.sb([128, 128], F32)), TB(P.sb([128, 128], F32))]
        mod = [TB(P.sb([128, 2, 72], F32)) for _ in range(DEPTH)]
        coef = [[TB(P.sb([128, 6, 8], F32)) for _ in range(2)] for _ in range(DEPTH)]
        gps = [TB(P.ps([128, TS])) for _ in range(2)]
        ups = [TB(P.ps([128, TS])) for _ in range(2)]
        pd = P.ps([128, 2 * TS])
        b_pd = [Buf(), Buf()]
        ops_ = [TB(pd[:, 0:TS]), TB(pd[:, TS:2 * TS])]
        ops_[0].b = b_pd[0]
        ops_[1].b = b_pd[1]
        stat = TB(P.ps([128, TS]))
        pmisc = TB(P.ps([128, TS]))
        stout = TB(P.sb([128, NSEQ, DEPTH, 32], F32))
        P.op("dve", lambda e: e.memset(stout.t[:], 0.0), writes=[stout.b])

        P.op("dve", lambda e: e.memset(ones_bf.t[:], 1.0 / 1024.0), writes=[ones_bf.b])
        P.op("dve", lambda e: e.memset(bd64.t[:], 0.0), writes=[bd64.b])
        P.op("dve", lambda e: e.memset(bd64.t[0:64, 0:64], 1.0 / 64.0), writes=[bd64.b])
        P.op("dve", lambda e: e.memset(bd64.t[64:128, 64:128], 1.0 / 64.0), writes=[bd64.b])
        P.op("dve", lambda e: e.memset(epst.t[:], EPS), writes=[epst.b])
        P.dma("sp", ident.t[:], I["ident"], writes=[ident.b])

        class NS:
            pass

        def alloc_W(stk):
            W = NS()
            W.g = P.sb([128, 8, DFF], BF16, stk)
            W.u = P.sb([128, 8, DFF], BF16, stk)
            W.d = P.sb([128, NF, D], BF16, stk)
            W.bg = [Buf(), Buf()]
            W.bu = [Buf(), Buf()]
            W.bd = [Buf(), Buf()]
            return W

        def load_ffn(W, l, which):
            g = I["ffn%d_w_gate" % which][l].rearrange("(k p) n -> p k n", p=128)
            u = I["ffn%d_w_up" % which][l].rearrange("(k p) n -> p k n", p=128)
            d = I["ffn%d_w_down" % which][l].rearrange("(f p) n -> p f n", p=128)
            HC = 11 * 128
            for half in range(2):
                P.dma("pool", W.g[:, :, half * HC:(half + 1) * HC], g[:, :, half * HC:(half + 1) * HC],
                      writes=[W.bg[half]])
                P.dma("pool", W.u[:, :, half * HC:(half + 1) * HC], u[:, :, half * HC:(half + 1) * HC],
                      writes=[W.bu[half]])
                P.dma("pool", W.d[:, half * 11:(half + 1) * 11, :], d[:, half * 11:(half + 1) * 11, :],
                      writes=[W.bd[half]])

        def phase0():
            with ExitStack() as ph:
                stage = [P.sb([128, 128], F32, ph) for _ in range(2)]
                for l in range(DEPTH):
                    sg_ = stage[l]
                    bz = Buf()
                    P.op("dve", lambda e, sg_=sg_: e.memset(sg_[:], 0.0), writes=[bz])
                    bl = []
                    r = 0

                    def ld(dst, src):
                        b = Buf()
                        b.last_w = bz.last_w
                        P.dma("sp", dst, src, writes=[b])
                        bl.append(b)
                    for nm, nr in (("b_mod", 72), ("norm_ffn1", 8), ("norm_mix", 8), ("norm_ffn2", 8),
                                   ("ssm_d", 2), ("ssm_b_glu", 2), ("fnet_b", 2)):
                        ld(sg_[r:r + nr, :], I[nm][l])
                        r += nr
                    for j, nm in enumerate(("ax_q_norm", "ax_k_norm")):
                        for hh in range(2):
                            ld(sg_[102 + j:103 + j, hh * 64:(hh + 1) * 64], I[nm][l])
                    if l == 0:
                        ld(sg_[104:112, :], I["c_b"])
                        ld(sg_[112:120, :], I["c_ctx"])
                        ld(sg_[120:128, :], I["final_norm"])
                    P.op("pe", lambda e, sg_=sg_: e.transpose(out=pmisc.t[:, 0:128], in_=sg_[:], identity=ident.t[:]),
                         reads=bl + [ident.b], writes=[pmisc.b])
                    P.op("dve", lambda e, l=l: e.tensor_copy(out=vec[l].t[:], in_=pmisc.t[:, 0:128]),
                         reads=[pmisc.b], writes=[vec[l].b])
                scond = TB(P.sb([128, 8, 2], BF16, ph))
                P.op("act", lambda e: e.activation(out=scond.t[:, :, 0], in_=vec[0].t[:, 104:112], func=AF.Silu),
                     reads=[vec[0].b], writes=[scond.b])
                P.op("act", lambda e: e.activation(out=scond.t[:, :, 1], in_=vec[0].t[:, 112:120], func=AF.Silu),
                     reads=[vec[0].b], writes=[scond.b])
                wm = [TB(P.sb([128, 8, 512], F32, ph)) for _ in range(2)]
                wmb = [TB(P.sb([128, 8, 512], BF16, ph)) for _ in range(2)]
                nblk = 0
                for l in range(DP):
                    wsrc = I["w_mod"][l].rearrange("(k p) n -> p k n", p=128)
                    for cb in range(18):
                        w_ = wm[nblk % 2]
                        wb_ = wmb[nblk % 2]
                        ceng = ("act", "pool", "dve")[nblk % 3]
                        nblk += 1
                        P.dma("sp", w_.t[:], wsrc[:, :, cb * 512:(cb + 1) * 512], writes=[w_.b])
                        if ceng == "act":
                            P.op("act", lambda e, w_=w_, wb_=wb_: e.copy(out=wb_.t[:], in_=w_.t[:]), reads=[w_.b], writes=[wb_.b])
                        else:
                            P.op(ceng, lambda e, w_=w_, wb_=wb_: e.tensor_copy(out=wb_.t[:], in_=w_.t[:]), reads=[w_.b], writes=[wb_.b])
                        for j in range(4):
                            ch = cb * 4 + j
                            for k in range(8):
                                P.op("pe", lambda e, wb_=wb_, j=j, k=k, ch=ch: e.matmul(
                                    pmisc.t[:, ch * 2:ch * 2 + 2], lhsT=wb_.t[:, k, j * 128:(j + 1) * 128],
                                    rhs=scond.t[:, k, :], start=(k == 0), stop=(k == 7)),
                                    reads=[wb_.b, scond.b], writes=[pmisc.b])
                    pm = pmisc.t[:, 0:144].rearrange("p (c t) -> p c t", t=2)
                    for cond in range(2):
                        P.op("dve", lambda e, l=l, cond=cond, pm=pm: e.tensor_tensor(
                            out=mod[l].t[:, cond, :], in0=pm[:, :, cond], in1=vec[l].t[:, 0:72], op=ALU.add),
                            reads=[pmisc.b, vec[l].b], writes=[mod[l].b])
                        cf = coef[l][cond]
                        for j in range(3):
                            P.op("dve", lambda e, l=l, cond=cond, j=j, cf=cf: e.scalar_tensor_tensor(
                                out=cf.t[:, j, :], in0=mod[l].t[:, cond, (3 * j + 1) * 8:(3 * j + 2) * 8], scalar=1.0,
                                in1=vec[l].t[:, 72 + 8 * j:80 + 8 * j], op0=ALU.add, op1=ALU.mult),
                                reads=[mod[l].b, vec[l].b], writes=[cf.b])
                            gs = 0.5 if j != 1 else 1.0
                            P.op("dve", lambda e, l=l, cond=cond, j=j, cf=cf, gs=gs: e.tensor_scalar(
                                out=cf.t[:, 3 + j, :], in0=mod[l].t[:, cond, (3 * j + 2) * 8:(3 * j + 3) * 8],
                                scalar1=gs, scalar2=None, op0=ALU.mult),
                                reads=[mod[l].b], writes=[cf.b])
                P.barrier()

        def cond_of(t):
            return 0 if t < cfg.nt_lat else 1

        def alloc_common(ph, with_hff=False, with_xin=False):
            C = NS()
            C.h = P.sb([128, 8, TS], F32, ph)
            C.b_h = [Buf() for _ in range(8)]
            C.xT = P.sb([128, 8, TS], BF16, ph)
            C.b_x = [Buf() for _ in range(8)]
            C.sqb = [TB(P.sb([128, TS], BF16, ph)) for _ in range(2)]
            C.tmpb = [TB(P.sb([128, TS], F32, ph)) for _ in range(2)]
            C.rt = TB(P.sb([128, TS], F32, ph))
            C.rstd = TB(P.sb([128, TS], F32, ph))
            if with_hff:
                C.hff = P.sb([128, 11, TS], BF16, ph)
                C.b_hff = [Buf() for _ in range(11)]
                C.sgb = [TB(P.sb([128, TS], F32, ph)) for _ in range(2)]
            if with_xin:
                C.xin = [TB(P.sb([128, D], F32, ph)) for _ in range(2)]
                C.xin_ctr = 0
            return C

        def norm(C, A, S, coefb, out_fn):
            h, b_h = C.h, C.b_h
            for m in range(8):
                sq = C.sqb[m % 2]
                P.op("act", lambda e, sq=sq, m=m: e.activation(out=sq.t[:], in_=h[:, m, :], func=AF.Square),
                     reads=[b_h[m]], writes=[sq.b])
                P.op("pe", lambda e, sq=sq, m=m: e.matmul(stat.t[:], lhsT=ones_bf.t[:], rhs=sq.t[:],
                                                          start=(m == 0), stop=(m == 7)),
                     reads=[sq.b, ones_bf.b], writes=[stat.b])
            rt, rstd = C.rt, C.rstd
            P.op("act", lambda e: e.activation(out=rt.t[:], in_=stat.t[:], func=AF.Ln, bias=epst.t[:, 0:1], scale=1.0),
                 reads=[stat.b, epst.b], writes=[rt.b])
            P.op("act", lambda e: e.activation(out=rstd.t[:], in_=rt.t[:], func=AF.Exp, scale=-0.5),
                 reads=[rt.b], writes=[rstd.b])
            for m in range(8):
                tm = C.tmpb[m % 2]
                P.op("dve", lambda e, tm=tm, m=m: e.tensor_tensor(out=tm.t[:], in0=h[:, m, :], in1=rstd.t[:], op=ALU.mult),
                     reads=[b_h[m], rstd.b], writes=[tm.b])
                oap, ob = out_fn(m)
                if S is not None:
                    P.op("act", lambda e, tm=tm, m=m, oap=oap: e.activation(
                        out=oap, in_=tm.t[:], func=AF.Identity, scale=A(m), bias=S(m)),
                        reads=[tm.b] + coefb, writes=[ob])
                else:
                    P.op("act", lambda e, tm=tm, m=m, oap=oap: e.activation(
                        out=oap, in_=tm.t[:], func=AF.Identity, scale=A(m)),
                        reads=[tm.b] + coefb, writes=[ob])

        def norm_to_x(C, l, cond, j):
            cf = coef[l][cond]
            norm(C, lambda m: cf.t[:, j, m:m + 1],
                 lambda m: mod[l].t[:, cond, 3 * j * 8 + m:3 * j * 8 + m + 1],
                 [cf.b, mod[l].b],
                 lambda m: (C.xT[:, m, :], C.b_x[m]))

        def ffn(C, W, l, cond, j, mid_hook=None):
            cf = coef[l][cond]
            h, b_h, xT, b_x, hff, b_hff = C.h, C.b_h, C.xT, C.b_x, C.hff, C.b_hff
            for half in range(2):
                for f in range(11):
                    fc = half * 11 + f
                    gp, up = gps[f % 2], ups[f % 2]
                    for k in range(8):
                        P.op("pe", lambda e, gp=gp, k=k, fc=fc: e.matmul(
                            gp.t[:], lhsT=W.g[:, k, fc * 128:(fc + 1) * 128], rhs=xT[:, k, :],
                            start=(k == 0), stop=(k == 7)),
                            reads=[W.bg[half], b_x[k]], writes=[gp.b])
                    for k in range(8):
                        P.op("pe", lambda e, up=up, k=k, fc=fc: e.matmul(
                            up.t[:], lhsT=W.u[:, k, fc * 128:(fc + 1) * 128], rhs=xT[:, k, :],
                            start=(k == 0), stop=(k == 7)),
                            reads=[W.bu[half], b_x[k]], writes=[up.b])
                    sg = C.sgb[f % 2]
                    P.op("act", lambda e, sg=sg, gp=gp: e.activation(out=sg.t[:], in_=gp.t[:], func=AF.Silu),
                         reads=[gp.b], writes=[sg.b])
                    P.op("dve", lambda e, sg=sg, up=up, f=f: e.tensor_tensor(
                        out=hff[:, f, :], in0=sg.t[:], in1=up.t[:], op=ALU.mult),
                        reads=[sg.b, up.b], writes=[b_hff[f]])
                for m in range(8):
                    if half == 1 and m == 4 and mid_hook is not None:
                        mid_hook()
                    o_ = ops_[m % 2]
                    for f in range(11):
                        fc = half * 11 + f
                        P.op("pe", lambda e, o_=o_, f=f, fc=fc, m=m: e.matmul(
                            o_.t, lhsT=W.d[:, fc, m * 128:(m + 1) * 128], rhs=hff[:, f, :],
                            start=(f == 0), stop=(f == 10)),
                            reads=[W.bd[half], b_hff[f]], writes=[o_.b])
                    P.op("dve", lambda e, o_=o_, m=m, cf=cf, j=j: e.scalar_tensor_tensor(
                        out=h[:, m, :], in0=o_.t, scalar=cf.t[:, 3 + j, m:m + 1], in1=h[:, m, :],
                        op0=ALU.mult, op1=ALU.add),
                        reads=[o_.b, b_h[m], cf.b], writes=[b_h[m]])

        def load_h_x(C, t):
            for s in range(4):
                xi = C.xin[C.xin_ctr % 2]
                C.xin_ctr += 1
                P.dma("sp", xi.t[:], x_rows(t, s), writes=[xi.b])
                for m in range(8):
                    P.op("pe", lambda e, xi=xi, m=m: e.transpose(
                        out=pd[:, m * 128:(m + 1) * 128], in_=xi.t[:, m * 128:(m + 1) * 128], identity=ident.t[:]),
                        reads=[xi.b, ident.b], writes=[b_pd[m // 4]])
                P.op("act", lambda e, s=s: e.copy(out=C.h[:, :, s * 128:(s + 1) * 128],
                                                  in_=pd[:, :].rearrange("p (m t) -> p m t", m=8)),
                     reads=b_pd, writes=C.b_h)

        def load_h(C, t):
            P.dma("sp", C.h[:], hbuf[:, :, t * TS:(t + 1) * TS], reads=[b_hbuf[t]], writes=C.b_h)

        def store_h(C, t):
            P.dma("sp", hbuf[:, :, t * TS:(t + 1) * TS], C.h[:], reads=C.b_h, writes=[b_hbuf[t]])

        def final_out(C, t):
            fw = vec[0]
            h, b_h = C.h, C.b_h
            norm(C, lambda m: fw.t[:, 120 + m:121 + m], None, [fw.b], lambda m: (h[:, m, :], b_h[m]))
            for s in range(4):
                for hf in range(2):
                    xi = C.xin[C.xin_ctr[0] % 2]
                    C.xin_ctr[0] += 1
                    for m4 in range(4):
                        m = hf * 4 + m4
                        P.op("pe", lambda e, m=m, m4=m4, s=s, hf=hf: e.transpose(
                            out=pd[:, hf * 512 + m4 * 128:hf * 512 + (m4 + 1) * 128], in_=h[:, m, s * 128:(s + 1) * 128],
                            identity=ident.t[:]),
                            reads=[b_h[m], ident.b], writes=[b_pd[hf]])
                    if hf == 0:
                        P.op("act", lambda e, xi=xi, hf=hf: e.copy(out=xi.t[:], in_=pd[:, hf * 512:(hf + 1) * 512]),
                             reads=[b_pd[hf]], writes=[xi.b])
                    else:
                        P.op("dve", lambda e, xi=xi, hf=hf: e.tensor_copy(out=xi.t[:], in_=pd[:, hf * 512:(hf + 1) * 512]),
                             reads=[b_pd[hf]], writes=[xi.b])
                    P.dma("sp", y_rows(t, s)[:, hf * 512:(hf + 1) * 512], xi.t[:], reads=[xi.b])

        def phase_X0():
            with ExitStack() as ph:
                xin = [TB(P.sb([128, D], F32, ph)) for _ in range(3)]
                hh = [P.sb([128, 8, TS], F32, ph) for _ in range(2)]
                b_hh = [[Buf() for _ in range(8)] for _ in range(2)]
                ctr = 0
                for t in range(NT):
                    h = hh[t % 2]
                    bh = b_hh[t % 2]
                    for s in range(4):
                        xi = xin[ctr % 3]
                        ctr += 1
                        P.dma("sp", xi.t[:], x_rows(t, s), writes=[xi.b])
                        for m in range(8):
                            P.op("pe", lambda e, xi=xi, m=m: e.transpose(
                                out=pd[:, m * 128:(m + 1) * 128], in_=xi.t[:, m * 128:(m + 1) * 128], identity=ident.t[:]),
                                reads=[xi.b, ident.b], writes=[b_pd[m // 4]])
                        if s % 2 == 0:
                            P.op("act", lambda e, s=s, h=h: e.copy(out=h[:, :, s * 128:(s + 1) * 128],
                                                              in_=pd[:, :].rearrange("p (m t) -> p m t", m=8)),
                                 reads=b_pd, writes=bh)
                        else:
                            P.op("dve", lambda e, s=s, h=h: e.tensor_copy(out=h[:, :, s * 128:(s + 1) * 128],
                                                                     in_=pd[:, :].rearrange("p (m t) -> p m t", m=8)),
                                 reads=b_pd, writes=bh)
                    P.dma("act", hbuf[:, :, t * TS:(t + 1) * TS], h[:], reads=bh, writes=[b_hbuf[t]])
                P.barrier()

        def phase_F(W, l, which, last=False):
            with ExitStack() as ph:
                C0 = alloc_common(ph, with_hff=True)
                C0.xin = [TB(P.sb([128, TS], F32, ph)) for _ in range(2)]
                C0.xin_ctr = [0]
                C1 = NS()
                C1.__dict__.update(C0.__dict__)
                C1.h = P.sb([128, 8, TS], F32, ph)
                C1.b_h = [Buf() for _ in range(8)]
                Cs = [C0, C1]
                j = 0 if which == 1 else 2
                load_h(Cs[0], 0)
                norm_to_x(Cs[0], l, cond_of(0), j)
                for t in range(NT):
                    C = Cs[t % 2]
                    cond = cond_of(t)
                    hook = None
                    if t + 1 < NT:
                        Cn = Cs[(t + 1) % 2]
                        load_h(Cn, t + 1)
                        hook = (lambda Cn=Cn, t=t: norm_to_x(Cn, l, cond_of(t + 1), j))
                    ffn(C, W, l, cond, j, mid_hook=hook)
                    if last:
                        final_out(C, t)
                    else:
                        P.dma("act", hbuf[:, :, t * TS:(t + 1) * TS], C.h[:], reads=C.b_h, writes=[b_hbuf[t]])
                P.barrier()

        def phase_PJ(l):
            with ExitStack() as ph:
                Cs = [alloc_common(ph), alloc_common(ph)]
                win = P.sb([128, 8, PIN], BF16, ph)
                b_win = Buf()
                pjs = [P.sb([128, 12, TS], F32, ph) for _ in range(2)]
                b_pjs = [[Buf() for _ in range(12)] for _ in range(2)]
                kvo = [TB(P.sb([128, 512], F32, ph)) for _ in range(2)]
                kctr = 0
                sq3 = [TB(P.sb([128, TS], BF16, ph)) for _ in range(3)]
                rt3 = [TB(P.sb([128, TS], F32, ph)) for _ in range(3)]
                rs3 = [TB(P.sb([128, TS], F32, ph)) for _ in range(3)]
                tm3 = [TB(P.sb([128, TS], F32, ph)) for _ in range(3)]
                P.dma("pool", win[:], I["w_in"][l].rearrange("(k p) n -> p k n", p=128), writes=[b_win])
                load_h(Cs[0], 0)
                for t in range(NT):
                    cond = cond_of(t)
                    C = Cs[t % 2]
                    pj = pjs[t % 2]
                    b_pj = b_pjs[t % 2]
                    if t + 1 < NT:
                        load_h(Cs[(t + 1) % 2], t + 1)
                    norm_to_x(C, l, cond, 1)
                    for c in range(12):
                        pp = gps[c % 2] if (c // 2) % 2 == 0 else ups[c % 2]
                        for k in range(8):
                            P.op("pe", lambda e, pp=pp, k=k, c=c, C=C: e.matmul(
                                pp.t[:], lhsT=win[:, k, c * 128:(c + 1) * 128], rhs=C.xT[:, k, :],
                                start=(k == 0), stop=(k == 7)),
                                reads=[b_win, C.b_x[k]], writes=[pp.b])
                        if c % 2 == 0:
                            P.op("act", lambda e, pp=pp, c=c, pj=pj: e.copy(out=pj[:, c, :], in_=pp.t[:]),
                                 reads=[pp.b], writes=[b_pj[c]])
                        else:
                            P.op("dve", lambda e, pp=pp, c=c, pj=pj: e.tensor_copy(out=pj[:, c, :], in_=pp.t[:]),
                                 reads=[pp.b], writes=[b_pj[c]])
                    fx = [(6, 102, stat), (7, 102, pmisc), (8, 103, ops_[1])]
                    for i3, (c, gcol, bank) in enumerate(fx):
                        sq = sq3[i3]
                        P.op("act", lambda e, sq=sq, c=c, pj=pj: e.activation(out=sq.t[:], in_=pj[:, c, :], func=AF.Square),
                             reads=[b_pj[c]], writes=[sq.b])
                    for i3, (c, gcol, bank) in enumerate(fx):
                        sq = sq3[i3]
                        P.op("pe", lambda e, sq=sq, bank=bank: e.matmul(bank.t[:, 0:TS] if bank is not ops_[1] else bank.t, lhsT=bd64.t[:], rhs=sq.t[:], start=True, stop=True),
                             reads=[sq.b, bd64.b], writes=[bank.b])
                    for i3, (c, gcol, bank) in enumerate(fx):
                        rt_ = rt3[i3]
                        P.op("act", lambda e, rt_=rt_, bank=bank: e.activation(out=rt_.t[:], in_=bank.t[:, 0:TS] if bank is not ops_[1] else bank.t, func=AF.Ln, bias=epst.t[:, 0:1], scale=1.0),
                             reads=[bank.b, epst.b], writes=[rt_.b])
                    for i3, (c, gcol, bank) in enumerate(fx):
                        rt_, rs_ = rt3[i3], rs3[i3]
                        P.op("act", lambda e, rt_=rt_, rs_=rs_: e.activation(out=rs_.t[:], in_=rt_.t[:], func=AF.Exp, scale=-0.5),
                             reads=[rt_.b], writes=[rs_.b])
                    for i3, (c, gcol, bank) in enumerate(fx):
                        rs_, tm = rs3[i3], tm3[i3]
                        P.op("dve", lambda e, tm=tm, c=c, pj=pj, rs_=rs_: e.tensor_tensor(out=tm.t[:], in0=pj[:, c, :], in1=rs_.t[:], op=ALU.mult),
                             reads=[b_pj[c], rs_.b], writes=[tm.b])
                    for i3, (c, gcol, bank) in enumerate(fx):
                        tm = tm3[i3]
                        P.op("act", lambda e, tm=tm, c=c, gcol=gcol, pj=pj: e.activation(
                            out=pj[:, c, :], in_=tm.t[:], func=AF.Identity, scale=vec[l].t[:, gcol:gcol + 1]),
                            reads=[tm.b, vec[l].b], writes=[b_pj[c]])
                    P.dma("act", projT[:, :, t * TS:(t + 1) * TS], pj[:], reads=b_pj, writes=[b_proj[t]])
                    if cond == 1:
                        for s in range(4):
                            seq = (t - cfg.nt_lat) * 2 + s // 2
                            pos0 = (s % 2) * 128
                            xi = kvo[kctr % 2]
                            kctr += 1
                            for jj, c in enumerate((4, 5, 8, 9)):
                                P.op("pe", lambda e, jj=jj, c=c, s=s, pj=pj: e.transpose(
                                    out=pd[:, jj * 128:(jj + 1) * 128], in_=pj[:, c, s * 128:(s + 1) * 128],
                                    identity=ident.t[:]),
                                    reads=[b_pj[c], ident.b], writes=[b_pd[0]])
                            P.op("act", lambda e, xi=xi: e.copy(out=xi.t[:], in_=pd[:, 0:512]),
                                 reads=[b_pd[0]], writes=[xi.b])
                            P.dma("act", O["o_swa"][seq, l, 0, pos0:pos0 + 128, :], xi.t[:, 0:128], reads=[xi.b])
                            P.dma("act", O["o_swa"][seq, l, 1, pos0:pos0 + 128, :], xi.t[:, 128:256], reads=[xi.b])
                            P.dma("act", O["o_ax"][seq, l, 0, pos0:pos0 + 128, :], xi.t[:, 256:384], reads=[xi.b])
                            P.dma("act", O["o_ax"][seq, l, 1, pos0:pos0 + 128, :], xi.t[:, 384:512], reads=[xi.b])
                P.barrier()

        def phase_WO(l):
            with ExitStack() as ph:
                hs = [P.sb([128, 8, TS], F32, ph) for _ in range(2)]
                b_hs = [[Buf() for _ in range(8)] for _ in range(2)]
                xs = [P.sb([128, 8, TS], BF16, ph) for _ in range(2)]
                b_xs = [[Buf() for _ in range(8)] for _ in range(2)]
                wout = P.sb([128, 8, D], BF16, ph)
                b_wout = Buf()
                P.dma("pool", wout[:], I["w_out"][l].rearrange("(k p) n -> p k n", p=128), writes=[b_wout])

                def ld(t):
                    i = t % 2
                    P.dma("sp", hs[i][:], hbuf[:, :, t * TS:(t + 1) * TS], reads=[b_hbuf[t]], writes=b_hs[i])
                    P.dma("sp", xs[i][:], mergT[:, :, t * TS:(t + 1) * TS], reads=[b_merg[t]], writes=b_xs[i])
                ld(0)
                for t in range(NT):
                    cf = coef[l][cond_of(t)]
                    i = t % 2
                    h_, bh_, x_, bx_ = hs[i], b_hs[i], xs[i], b_xs[i]
                    if t + 1 < NT:
                        ld(t + 1)
                    for m in range(8):
                        o_ = ops_[m % 2]
                        for k in range(8):
                            P.op("pe", lambda e, o_=o_, k=k, m=m, x_=x_: e.matmul(
                                o_.t, lhsT=wout[:, k, m * 128:(m + 1) * 128], rhs=x_[:, k, :],
                                start=(k == 0), stop=(k == 7)),
                                reads=[b_wout, bx_[k]], writes=[o_.b])
                        P.op("dve", lambda e, o_=o_, m=m, cf=cf, h_=h_: e.scalar_tensor_tensor(
                            out=h_[:, m, :], in0=o_.t, scalar=cf.t[:, 4, m:m + 1], in1=h_[:, m, :],
                            op0=ALU.mult, op1=ALU.add),
                            reads=[o_.b, bh_[m], cf.b], writes=[bh_[m]])
                    P.dma("act", hbuf[:, :, t * TS:(t + 1) * TS], h_[:], reads=bh_, writes=[b_hbuf[t]])
                P.barrier()

        have_mix = len(cfg.mixers) > 0
        MIX = NS()

        def zero_merg(chunks):
            with ExitStack() as ph:
                z = TB(P.sb([128, TS], BF16, ph))
                P.op("dve", lambda e: e.memset(z.t[:], 0.0), writes=[z.b])
                for t in range(NT):
                    for c in chunks:
                        P.dma("sp", mergT[:, c, t * TS:(t + 1) * TS], z.t[:], reads=[z.b], writes=[b_merg[t]])
                P.barrier()

        def mx_attn(l, grp, ph, SH):
            qc0, kc, vc, mch = (2, 4, 5, 2) if grp == "swa" else (6, 8, 9, 4)
            cache = I["cache_swa"] if grp == "swa" else I["cache_ax"]
            if True:
                NKB = L // 128
                K2 = [[P.sb([128, L], BF16, ph) for _ in range(2)] for _ in range(2)]
                b_K2 = [Buf(), Buf()]
                Kc2 = [[TB(P.sb([128, 256], BF16, ph)) for _ in range(2)] for _ in range(2)]
                for kv in range(2):
                    for hh in range(2):
                        P.op("pool", lambda e, kv=kv, hh=hh: e.memset(K2[kv][hh][:], 0.0), writes=[b_K2[kv]])
                        P.op("pool", lambda e, kv=kv, hh=hh: e.memset(Kc2[kv][hh].t[:], 0.0), writes=[Kc2[kv][hh].b])
                Vx = P.sb([128, NKB + 2, 2, 192], BF16, ph)
                b_Vx = Buf()
                P.op("pool", lambda e: e.memset(Vx[:], 1.0), writes=[b_Vx])
                if not hasattr(SH, "ropeC"):
                    SH.ropeC = TB(P.sb([128, L], F32, ph))
                    SH.ropeS = TB(P.sb([128, L], F32, ph))
                    SH.rotm = TB(P.sb([128, 128], F32, ph))
                    SH.identb = TB(P.sb([128, 128], BF16, ph))
                    SH.masks = TB(P.sb([128, 256], BF16, ph))
                    SH.raw = [TB(P.sb([128, TS], F32, ph)) for _ in range(4)]
                    SH.rctr = [0]
                    SH.t1b = [TB(P.sb([128, TS], F32, ph)) for _ in range(2)]
                    SH.t2b = [TB(P.sb([128, TS], F32, ph)) for _ in range(2)]
                    SH.qr = [TB(P.sb([128, TS], BF16, ph)) for _ in range(2)]
                    SH.qu = [TB(P.sb([128, TS], BF16, ph)) for _ in range(2)]
                    SH.pT = [TB(P.sb([128, TS], BF16, ph)) for _ in range(3)]
                    SH.mg = [TB(P.sb([128, TS], BF16, ph)) for _ in range(2)]
                    SH.rect = [TB(P.sb([128, TS], F32, ph)) for _ in range(2)]
                    SH.ckv = TB(P.sb([128, 2, 128], F32, ph))
                    SH.sk = TB(P.sb([1, 4], F32, ph))
                    SH.esrow = TB(P.sb([1, 4, TS], F32, ph))
                    SH.onesrow = TB(P.sb([1, TS], F32, ph))
                    SH.sel = TB(P.sb([1, 2, 128], F32, ph))
                    P.dma("sp", SH.ropeC.t[:], I["ropeC"], writes=[SH.ropeC.b])
                    P.dma("sp", SH.ropeS.t[:], I["ropeS"], writes=[SH.ropeS.b])
                    P.dma("sp", SH.rotm.t[:], I["rotm"], writes=[SH.rotm.b])
                    P.dma("pool", SH.masks.t[:], I["masks"], writes=[SH.masks.b])
                    P.op("dve", lambda e: e.tensor_copy(out=SH.identb.t[:], in_=ident.t[:]), reads=[ident.b], writes=[SH.identb.b])
                    P.op("dve", lambda e: e.memset(SH.onesrow.t[:], 1.0), writes=[SH.onesrow.b])
                    P.op("dve", lambda e: e.memset(SH.sel.t[:], 0.0), writes=[SH.sel.b])
                    P.op("dve", lambda e: e.memset(SH.sel.t[0:1, 0, 64:128], 1.0), writes=[SH.sel.b])
                    P.op("dve", lambda e: e.memset(SH.sel.t[0:1, 1, 0:64], 1.0), writes=[SH.sel.b])
                ropeC, ropeS, rotm, identb, masks = SH.ropeC, SH.ropeS, SH.rotm, SH.identb, SH.masks
                raw, rctr, t1b, t2b, qr, qu, pT, mg, rect = SH.raw, SH.rctr, SH.t1b, SH.t2b, SH.qr, SH.qu, SH.pT, SH.mg, SH.rect
                ckv, sk, esrow, onesrow, sel = SH.ckv, SH.sk, SH.esrow, SH.onesrow, SH.sel
                sbank = [gps[0], ups[0], gps[1], ups[1]]
                if grp == "swa":
                    P.dma("sp", sk.t[:], I["swa_sink"][l:l + 1, :], writes=[sk.b])
                    P.op("act", lambda e: e.activation(out=sk.t[:], in_=sk.t[:], func=AF.Exp), reads=[sk.b], writes=[sk.b])
                    for hd in range(4):
                        P.op("dve", lambda e, hd=hd: e.tensor_scalar(
                            out=esrow.t[0:1, hd, :], in0=onesrow.t[:], scalar1=sk.t[0:1, hd:hd + 1], scalar2=None,
                            op0=ALU.mult), reads=[sk.b, onesrow.b], writes=[esrow.b])

                def rope(src, dsts, dst_b, p0, w):
                    P.op("pe", lambda e: e.matmul(stat.t[:, 0:w], lhsT=rotm.t[:], rhs=src.t[:, 0:w], start=True, stop=True),
                         reads=[src.b, rotm.b], writes=[stat.b])
                    ta, tb_ = t1b[rctr[0] % 2], t2b[rctr[0] % 2]
                    P.op("dve", lambda e: e.tensor_tensor(out=ta.t[:, 0:w], in0=src.t[:, 0:w], in1=ropeC.t[:, p0:p0 + w], op=ALU.mult),
                         reads=[src.b, ropeC.b], writes=[ta.b])
                    P.op("dve", lambda e: e.tensor_tensor(out=tb_.t[:, 0:w], in0=stat.t[:, 0:w], in1=ropeS.t[:, p0:p0 + w], op=ALU.mult),
                         reads=[stat.b, ropeS.b], writes=[tb_.b])
                    for (dap, ps_) in dsts:
                        P.op("dve", lambda e, dap=dap, ps_=ps_: e.tensor_tensor(out=dap, in0=ta.t[ps_, 0:w], in1=tb_.t[ps_, 0:w], op=ALU.add),
                             reads=[ta.b, tb_.b], writes=[dst_b])

                def attn_seq(tok0, Ls, latent, kcol0, voff, bK, b_Vx):
                    TW = TS if (latent or Ls == TS) else 256
                    nqt = Ls // TW
                    nkb = Ls // 128
                    for kv in range(2):
                        for it in range(nqt):
                            c0 = tok0 + it * TW
                            r_ = raw[rctr[0] % 4]
                            rctr[0] += 1
                            for hh in range(2):
                                P.dma("sp", r_.t[hh * 64:(hh + 1) * 64, 0:TW], projT[kv * 64:(kv + 1) * 64, kc, c0:c0 + TW],
                                      reads=[b_proj[c0 // TS]], writes=[r_.b])
                            if latent:
                                rope(r_, [(K2[kv][0][0:64, kcol0 + it * TW:kcol0 + (it + 1) * TW], slice(0, 64)),
                                          (K2[kv][1][64:128, kcol0 + it * TW:kcol0 + (it + 1) * TW], slice(64, 128))], bK[kv], it * TW, TW)
                            else:
                                P.op("act", lambda e, r_=r_, kv=kv, it=it: e.copy(out=K2[kv][0][0:64, kcol0 + it * TW:kcol0 + (it + 1) * TW], in_=r_.t[0:64, 0:TW]),
                                     reads=[r_.b], writes=[bK[kv]])
                                P.op("act", lambda e, r_=r_, kv=kv, it=it: e.copy(out=K2[kv][1][64:128, kcol0 + it * TW:kcol0 + (it + 1) * TW], in_=r_.t[64:128, 0:TW]),
                                     reads=[r_.b], writes=[bK[kv]])
                    for it in range(nqt):
                        c0 = tok0 + it * TW
                        r_ = raw[rctr[0] % 4]
                        rctr[0] += 1
                        P.dma("sp", r_.t[:, 0:TW], projT[:, vc, c0:c0 + TW], reads=[b_proj[c0 // TS]], writes=[r_.b])
                        for s in range(TW // 128):
                            blk = voff + it * (TW // 128) + s
                            P.op("pe", lambda e, r_=r_, s=s: e.transpose(out=pmisc.t[:, 0:128], in_=r_.t[:, s * 128:(s + 1) * 128],
                                                                     identity=ident.t[:]),
                                 reads=[r_.b, ident.b], writes=[pmisc.b])
                            P.op("dve", lambda e, blk=blk: e.tensor_copy(
                                out=Vx[:, blk, :, 64:128], in_=pmisc.t[:, 0:128].rearrange("p (k d) -> p k d", k=2)),
                                reads=[pmisc.b], writes=[b_Vx])
                    if latent:
                        for cb in range(2):
                            for kv in range(2):
                                for hh in range(2):
                                    P.dma("sp", ckv.t[:, kv, hh * 64:(hh + 1) * 64],
                                          cache[l, 0, cb * 128:(cb + 1) * 128, kv * 64:(kv + 1) * 64], writes=[ckv.b])
                            for kv in range(2):
                                P.op("pe", lambda e, kv=kv: e.transpose(out=pmisc.t[:, 0:128], in_=ckv.t[:, kv, 0:128], identity=ident.t[:]),
                                     reads=[ckv.b, ident.b], writes=[pmisc.b])
                                P.op("dve", lambda e, kv=kv, cb=cb: e.tensor_copy(out=Kc2[kv][0].t[0:64, cb * 128:(cb + 1) * 128], in_=pmisc.t[0:64, 0:128]),
                                     reads=[pmisc.b], writes=[Kc2[kv][0].b])
                                P.op("dve", lambda e, kv=kv, cb=cb: e.tensor_copy(out=Kc2[kv][1].t[64:128, cb * 128:(cb + 1) * 128], in_=pmisc.t[64:128, 0:128]),
                                     reads=[pmisc.b], writes=[Kc2[kv][1].b])
                            P.dma("pool", Vx[:, cb, :, 64:128],
                                  cache[l, 1, cb * 128:(cb + 1) * 128, :].rearrange("p (k d) -> p k d", k=2), writes=[b_Vx])
                    def head_entries(it, qi, hh):
                        kv = qi
                        qu_, qr_ = qu[qi], (qr[qi] if latent else qu[qi])
                        ent = []
                        if latent:
                            for cb in range(2):
                                ent.append((Kc2[kv][hh].t[:, cb * 128:(cb + 1) * 128], Kc2[kv][hh].b, qu_, 0, TW, [], cb))
                        if latent and grp == "swa":
                            qb0 = it * 4
                            for j in range(max(0, qb0 - 1), min(nkb, qb0 + 5)):
                                lo = max(j - 1, qb0)
                                hi = min(j + 1, qb0 + 3)
                                ml = []
                                if j - 1 >= qb0 and j - 1 <= qb0 + 3:
                                    ml.append(((j - 1 - qb0) * 128, 1))
                                if j + 1 >= qb0 and j + 1 <= qb0 + 3:
                                    ml.append(((j + 1 - qb0) * 128, 0))
                                ent.append((K2[kv][hh][:, kcol0 + j * 128:kcol0 + (j + 1) * 128], bK[kv], qr_, (lo - qb0) * 128,
                                            (hi - qb0 + 1) * 128, ml, voff + j))
                        elif latent:
                            for j in range(nkb):
                                ent.append((K2[kv][hh][:, kcol0 + j * 128:kcol0 + (j + 1) * 128], bK[kv], qr_, 0, TW, [], voff + j))
                        else:
                            for j in range(nkb):
                                a_ = (j // 2) * L_CTX
                                ent.append((K2[kv][hh][:, kcol0 + j * 128:kcol0 + (j + 1) * 128], bK[kv], qr_, a_, a_ + L_CTX, [], voff + j))
                        return ent

                    def prep_q(it, qi):
                        c0 = tok0 + it * TW
                        r_ = raw[rctr[0] % 4]
                        rctr[0] += 1
                        P.dma("sp", r_.t[:, 0:TW], projT[:, qc0 + qi, c0:c0 + TW], reads=[b_proj[c0 // TS]], writes=[r_.b])
                        qu_ = qu[qi]
                        P.op("act", lambda e: e.copy(out=qu_.t[:, 0:TW], in_=r_.t[:, 0:TW]), reads=[r_.b], writes=[qu_.b])
                        if latent:
                            qr_ = qr[qi]
                            rope(r_, [(qr_.t[:, 0:TW], slice(0, 128))], qr_.b, it * TW, TW)

                    def run_queries():
                        items = [(it, qi) for it in range(nqt) for qi in range(2)]
                        flat = []
                        for k, (it, qi) in enumerate(items):
                            for hh in range(2):
                                ent = head_entries(it, qi, hh)
                                H = NS()
                                H.kv, H.hd, H.hh, H.qi, H.it = qi, 2 * qi + hh, hh, qi, it
                                H.pr = slice(hh * 64, hh * 64 + 64)
                                H.sr = slice(64 - hh * 64, 128 - hh * 64)
                                H.vs = slice(64, 192) if hh == 0 else slice(0, 128)
                                H.po = ops_[hh]
                                H.mg = mg[qi]
                                H.c0 = tok0 + it * TW
                                n = len(ent)
                                for i, en in enumerate(ent):
                                    flat.append((en, H, i, n, k, (hh == 0 and i == 0), (hh == 1 and i == n - 1)))
                        N = len(flat)

                        def emit_S(g, G):
                            (kap, kb_, q_, a, b, ml, vb), H, i, n, k, fi, li = flat[g]
                            sbk = sbank[G % 4]
                            nm = len(ml)
                            qap = q_.t[:, a:b]
                            P.op("pe", lambda e: e.matmul(sbk.t[:, a:b], lhsT=kap, rhs=qap, start=True, stop=(nm == 0)),
                                 reads=[kb_, q_.b], writes=[sbk.b])
                            for mi, (mc, mk) in enumerate(ml):
                                P.op("pe", lambda e, mc=mc, mk=mk, mi=mi: e.matmul(
                                    sbk.t[:, mc:mc + 128], lhsT=identb.t[:], rhs=masks.t[:, mk * 128:(mk + 1) * 128],
                                    start=False, stop=(mi == nm - 1)),
                                    reads=[identb.b, masks.b], writes=[sbk.b])

                        def emit_PV(g, G):
                            (kap, kb_, q_, a, b, ml, vb), H, i, n, k, fi, li = flat[g]
                            sbk = sbank[G % 4]
                            p_ = pT[G % 3]
                            po = H.po
                            P.op("act", lambda e: e.activation(out=p_.t[:, a:b], in_=sbk.t[:, a:b], func=AF.Exp, scale=0.125),
                                 reads=[sbk.b], writes=[p_.b])
                            last = (i == n - 1) and grp != "swa"
                            vap = Vx[:, vb, H.kv, H.vs]
                            P.op("pe", lambda e: e.matmul(po.t[:, a:b], lhsT=vap, rhs=p_.t[:, a:b], start=(i == 0), stop=last),
                                 reads=[b_Vx, p_.b], writes=[po.b])
                            if i == n - 1:
                                pr, sr, mg_ = H.pr, H.sr, H.mg
                                if grp == "swa":
                                    P.op("pe", lambda e: e.matmul(po.t[:, 0:TW], lhsT=sel.t[0:1, H.hh, :], rhs=esrow.t[0:1, H.hd, 0:TW],
                                                                  start=False, stop=True),
                                         reads=[sel.b, esrow.b], writes=[po.b])
                                rc = rect[H.hh]
                                P.op("dve", lambda e: e.reciprocal(out=rc.t[pr, 0:TW], in_=po.t[sr, 0:TW]),
                                     reads=[po.b], writes=[rc.b])
                                P.op("dve", lambda e: e.tensor_tensor(
                                    out=mg_.t[pr, 0:TW], in0=po.t[pr, 0:TW], in1=rc.t[pr, 0:TW], op=ALU.mult),
                                    reads=[po.b, rc.b], writes=[mg_.b])
                                if li:
                                    P.dma("sp", mergT[:, mch + H.qi, H.c0:H.c0 + TW], mg_.t[:, 0:TW], reads=[mg_.b],
                                          writes=[b_merg[H.c0 // TS]])

                        steps = []
                        for g in range(N):
                            fi, k = flat[g][5], flat[g][4]
                            before = (lambda: prep_q(*items[0])) if g == 0 else None
                            after = (lambda k=k: prep_q(*items[k + 1])) if (fi and k + 1 < len(items)) else None
                            steps.append((before, (lambda G, g=g: emit_S(g, G)), after, (lambda G, g=g: emit_PV(g, G))))
                        return steps
                    return run_queries

                G = NS()

                def lat():
                    return attn_seq(0, L, True, 0, 2, b_K2, b_Vx)

                def ctx():
                    fs = []
                    for s in range(NSEQ // 2):
                        bK = [Buf(), Buf()]
                        for kv in range(2):
                            bK[kv].last_w = b_K2[kv].last_w
                            bK[kv].readers = list(b_K2[kv].readers)
                        bV = Buf()
                        bV.last_w = b_Vx.last_w
                        bV.readers = list(b_Vx.readers)
                        fs.append(attn_seq(L + s * 2 * L_CTX, 2 * L_CTX, False, s * 2 * L_CTX, 2 + 4 * s, bK, bV))
                    return fs
                G.lat, G.ctx = lat, ctx
                return G

        def mx_attn_all(l):
            with ExitStack() as ph:
                SH = NS()
                A = mx_attn(l, "swa", ph, SH)
                B = mx_attn(l, "ax", ph, SH)
                def drive(step_lists):
                    allsteps = [s for sl in step_lists for s in sl]
                    n_ = len(allsteps)
                    for G in range(n_ + 2):
                        if G < n_:
                            before, S_, after, _ = allsteps[G]
                            if before is not None:
                                before()
                            S_(G)
                            if after is not None:
                                after()
                        if G >= 2:
                            allsteps[G - 2][3](G - 2)
                qa = A.lat()
                qb = B.lat()
                drive([qa(), qb()])
                fa = A.ctx()
                fb = B.ctx()
                drive([f() for f in fa + fb])
                P.barrier()

        def mx_fnet(l):
            with ExitStack() as ph:
                NTB = L // 128
                NKT = L // TS
                cs256 = TB(P.sb([128, 2, 512], BF16, ph))
                Ec = TB(P.sb([128, NTB, 512], BF16, ph))
                nEs = TB(P.sb([128, NTB, 512], BF16, ph))
                c256s = TB(P.sb([128, 2, 256], BF16, ph))
                ns256s = TB(P.sb([128, 2, 256], BF16, ph))
                phi = TB(P.sb([128, 4, 8], F32, ph))
                fw = TB(P.sb([128, 2, 256], BF16, ph))
                ufT = TB(P.sb([128, 2, L], BF16, ph))
                UCS = P.sb([128, NTB, 512], BF16, ph)
                b_UCS = [Buf() for _ in range(NTB)]
                Yf = TB(P.sb([128, 2, L], BF16, ph))
                tA = [TB(P.sb([128, 256], BF16, ph)) for _ in range(3)]
                tBm = [TB(P.sb([128, 256], BF16, ph)) for _ in range(3)]
                UA = [TB(P.sb([128, 256], BF16, ph)) for _ in range(3)]
                UB = [TB(P.sb([128, 256], BF16, ph)) for _ in range(3)]
                mgf = [TB(P.sb([128, TS], BF16, ph)) for _ in range(2)]
                sbank = [gps[0], ups[0], gps[1], ups[1]]
                P.dma("pool", cs256.t[:], I["cs256"].rearrange("c p k -> p c k"), writes=[cs256.b])
                P.dma("pool", Ec.t[:], I["fn_ec"].rearrange("(tb p) k -> p tb k", p=128), writes=[Ec.b])
                P.dma("pool", nEs.t[:], I["fn_nes"].rearrange("(tb p) k -> p tb k", p=128), writes=[nEs.b])
                P.dma("pool", c256s.t[:], I["fn_c256s"].rearrange("(tb p) k -> p tb k", p=128), writes=[c256s.b])
                P.dma("pool", ns256s.t[:], I["fn_ns256s"].rearrange("(tb p) k -> p tb k", p=128), writes=[ns256s.b])
                P.dma("sp", phi.t[:], I["fn_phi"], writes=[phi.b])
                P.dma("pool", fw.t[:], I["fnet_w"][l].rearrange("(j p) n -> p j n", p=128), writes=[fw.b])

                def fnet_seq(tok0, Ls, latent):
                    ntb = Ls // 128
                    for c in range(2):
                        for t0 in range(0, Ls, TS):
                            w = min(TS, Ls - t0)
                            P.dma("pool", ufT.t[:, c, t0:t0 + w], projT[:, 10 + c, tok0 + t0:tok0 + t0 + w],
                                  reads=[b_proj[(tok0 + t0) // TS]], writes=[ufT.b])
                    for tb in range(ntb):
                        sbk = sbank[tb % 4]
                        for c in range(2):
                            P.op("pe", lambda e, sbk=sbk, c=c, tb=tb: e.matmul(
                                sbk.t[:], lhsT=ufT.t[:, c, tb * 128:(tb + 1) * 128], rhs=cs256.t[:, c, :],
                                start=(c == 0), stop=(c == 1)), reads=[ufT.b, cs256.b], writes=[sbk.b])
                        if tb % 2 == 0:
                            P.op("act", lambda e, sbk=sbk, tb=tb: e.copy(out=UCS[:, tb, :], in_=sbk.t[:]),
                                 reads=[sbk.b], writes=[b_UCS[tb]])
                        else:
                            P.op("dve", lambda e, sbk=sbk, tb=tb: e.tensor_copy(out=UCS[:, tb, :], in_=sbk.t[:]),
                                 reads=[sbk.b], writes=[b_UCS[tb]])
                    if latent:
                        for kt in range(NKT):
                            yb = [ops_[0], ops_[1]]
                            for tb in range(ntb):
                                if kt == 0:
                                    ua_ap, ub_ap = UCS[:, tb, 0:256], UCS[:, tb, 256:512]
                                    ua_b, ub_b = b_UCS[tb], b_UCS[tb]
                                else:
                                    i3 = tb % 3
                                    ta_, tb2_, ua_, ub_ = tA[i3], tBm[i3], UA[i3], UB[i3]
                                    P.op("act", lambda e, ta_=ta_, tb=tb, kt=kt: e.activation(
                                        out=ta_.t[:], in_=UCS[:, tb, 0:256], func=AF.Identity, scale=phi.t[:, 0, kt:kt + 1]),
                                        reads=[b_UCS[tb], phi.b], writes=[ta_.b])
                                    P.op("dve", lambda e, ua_=ua_, ta_=ta_, tb=tb, kt=kt: e.scalar_tensor_tensor(
                                        out=ua_.t[:], in0=UCS[:, tb, 256:512], scalar=phi.t[:, 2, kt:kt + 1], in1=ta_.t[:],
                                        op0=ALU.mult, op1=ALU.add), reads=[b_UCS[tb], phi.b, ta_.b], writes=[ua_.b])
                                    P.op("act", lambda e, tb2_=tb2_, tb=tb, kt=kt: e.activation(
                                        out=tb2_.t[:], in_=UCS[:, tb, 0:256], func=AF.Identity, scale=phi.t[:, 1, kt:kt + 1]),
                                        reads=[b_UCS[tb], phi.b], writes=[tb2_.b])
                                    P.op("dve", lambda e, ub_=ub_, tb2_=tb2_, tb=tb, kt=kt: e.scalar_tensor_tensor(
                                        out=ub_.t[:], in0=UCS[:, tb, 256:512], scalar=phi.t[:, 0, kt:kt + 1], in1=tb2_.t[:],
                                        op0=ALU.mult, op1=ALU.add), reads=[b_UCS[tb], phi.b, tb2_.b], writes=[ub_.b])
                                    ua_ap, ub_ap = ua_.t[:], ub_.t[:]
                                    ua_b, ub_b = ua_.b, ub_.b
                                for j in range(2):
                                    P.op("pe", lambda e, j=j, tb=tb, ua_ap=ua_ap: e.matmul(
                                        yb[j].t, lhsT=ua_ap[:, j * 128:(j + 1) * 128], rhs=Ec.t[:, tb, :],
                                        start=(tb == 0), stop=False), reads=[ua_b, Ec.b], writes=[yb[j].b])
                                    P.op("pe", lambda e, j=j, tb=tb, ub_ap=ub_ap: e.matmul(
                                        yb[j].t, lhsT=ub_ap[:, j * 128:(j + 1) * 128], rhs=nEs.t[:, tb, :],
                                        start=False, stop=(tb == ntb - 1)), reads=[ub_b, nEs.b], writes=[yb[j].b])
                            P.op("act", lambda e, kt=kt: e.copy(out=Yf.t[:, 0, kt * TS:(kt + 1) * TS], in_=yb[0].t),
                                 reads=[yb[0].b], writes=[Yf.b])
                            P.op("dve", lambda e, kt=kt: e.tensor_copy(out=Yf.t[:, 1, kt * TS:(kt + 1) * TS], in_=yb[1].t),
                                 reads=[yb[1].b], writes=[Yf.b])
                    else:
                        for j in range(2):
                            yb = ops_[j]
                            for tb in range(2):
                                P.op("pe", lambda e, j=j, tb=tb, yb=yb: e.matmul(
                                    yb.t[:, 0:256], lhsT=UCS[:, tb, j * 128:(j + 1) * 128], rhs=c256s.t[:, tb, :],
                                    start=(tb == 0), stop=False), reads=[b_UCS[tb], c256s.b], writes=[yb.b])
                                P.op("pe", lambda e, j=j, tb=tb, yb=yb: e.matmul(
                                    yb.t[:, 0:256], lhsT=UCS[:, tb, 256 + j * 128:256 + (j + 1) * 128], rhs=ns256s.t[:, tb, :],
                                    start=False, stop=(tb == 1)), reads=[b_UCS[tb], ns256s.b], writes=[yb.b])
                            P.op("act", lambda e, j=j, yb=yb: e.copy(out=Yf.t[:, j, 0:256], in_=yb.t[:, 0:256]),
                                 reads=[yb.b], writes=[Yf.b])
                    TW = TS if latent else 256
                    for t0 in range(0, Ls, TW):
                        for jo in range(2):
                            sbk = sbank[jo]
                            for ji in range(2):
                                P.op("pe", lambda e, sbk=sbk, ji=ji, jo=jo, t0=t0: e.matmul(
                                    sbk.t[:, 0:TW], lhsT=fw.t[:, ji, jo * 128:(jo + 1) * 128], rhs=Yf.t[:, ji, t0:t0 + TW],
                                    start=(ji == 0), stop=(ji == 1)), reads=[fw.b, Yf.b], writes=[sbk.b])
                            m_ = mgf[jo]
                            P.op("act", lambda e, sbk=sbk, m_=m_, jo=jo: e.activation(
                                out=m_.t[:, 0:TW], in_=sbk.t[:, 0:TW], func=AF.Identity, bias=vec[l].t[:, 100 + jo:101 + jo], scale=1.0),
                                reads=[sbk.b, vec[l].b], writes=[m_.b])
                            P.dma("sp", mergT[:, 6 + jo, tok0 + t0:tok0 + t0 + TW], m_.t[:, 0:TW], reads=[m_.b],
                                  writes=[b_merg[(tok0 + t0) // TS]])

                fnet_seq(0, L, True)
                for s in range(NSEQ):
                    fnet_seq(L + s * L_CTX, L_CTX, False)
                P.barrier()

        def mx_ssm(l):
            T = 256
            TWO_PI = 6.283185307179586
            with ExitStack() as ph:
                Ere = TB(P.sb([128, 16, T], F32, ph))
                Eim = TB(P.sb([128, 16, T], F32, ph))
                rho_c = TB(P.sb([128, 16], F32, ph))
                BtR = TB(P.sb([128, 2, 2, 128], BF16, ph))
                BtI = TB(P.sb([128, 2, 2, 128], BF16, ph))
                Ct = TB(P.sb([128, 2, 3, 8, 128], BF16, ph))
                Dd = TB(P.sb([128, 2, 128], BF16, ph))
                wglu = TB(P.sb([128, 2, 256], BF16, ph))
                uT = TB(P.sb([128, 2, L], BF16, ph))
                yacc = TB(P.sb([128, 2, L], F32, ph))
                carry = TB(P.sb([128, 2, 8, 2], F32, ph))
                sbank = [gps[0], ups[0], gps[1], ups[1]]
                P.dma("pool", Ct.t[:, :, 0:2, :, :], I["ssm_ct"][l].rearrange("p (d r s c) -> p d r s c", d=2, r=2, s=8), writes=[Ct.b])
                P.op("act", lambda e: e.activation(out=Ct.t[:, :, 1, :, :], in_=Ct.t[:, :, 1, :, :], func=AF.Identity, scale=-1.0),
                     reads=[Ct.b], writes=[Ct.b])
                P.op("act", lambda e: e.activation(out=Ct.t[:, :, 2, :, :], in_=Ct.t[:, :, 0, :, :], func=AF.Identity, scale=-1.0),
                     reads=[Ct.b], writes=[Ct.b])
                P.dma("pool", Dd.t[:], I["ssm_dd"][l], writes=[Dd.b])
                P.dma("pool", wglu.t[:], I["ssm_w_glu"][l].rearrange("(j p) n -> p j n", p=128), writes=[wglu.b])

                def sincos(th, n, stk):
                    a2 = TB(P.sb([128, 2 * n], F32, stk))
                    ki = TB(P.sb([128, 2 * n], mybir.dt.int32, stk))
                    kf = TB(P.sb([128, 2 * n], F32, stk))
                    mk = TB(P.sb([128, 2 * n], F32, stk))
                    P.op("dve", lambda e: e.tensor_copy(out=a2.t[:, 0:n], in_=th.t[:]), reads=[th.b], writes=[a2.b])
                    P.op("dve", lambda e: e.tensor_scalar(out=a2.t[:, n:2 * n], in0=th.t[:], scalar1=TWO_PI / 4, scalar2=None, op0=ALU.add),
                         reads=[th.b], writes=[a2.b])
                    P.op("dve", lambda e: e.tensor_scalar(out=kf.t[:], in0=a2.t[:], scalar1=1.0 / TWO_PI, scalar2=0.5, op0=ALU.mult, op1=ALU.add),
                         reads=[a2.b], writes=[kf.b])
                    P.op("dve", lambda e: e.tensor_copy(out=ki.t[:], in_=kf.t[:]), reads=[kf.b], writes=[ki.b])
                    P.op("dve", lambda e: e.tensor_copy(out=kf.t[:], in_=ki.t[:]), reads=[ki.b], writes=[kf.b])
                    C1 = 6.28125
                    C2 = TWO_PI - C1
                    P.op("dve", lambda e: e.scalar_tensor_tensor(out=a2.t[:], in0=kf.t[:], scalar=-C1, in1=a2.t[:], op0=ALU.mult, op1=ALU.add),
                         reads=[kf.b, a2.b], writes=[a2.b])
                    P.op("dve", lambda e: e.scalar_tensor_tensor(out=a2.t[:], in0=kf.t[:], scalar=-C2, in1=a2.t[:], op0=ALU.mult, op1=ALU.add),
                         reads=[kf.b, a2.b], writes=[a2.b])
                    P.op("dve", lambda e: e.tensor_scalar(out=mk.t[:], in0=a2.t[:], scalar1=-TWO_PI / 2, scalar2=TWO_PI, op0=ALU.is_lt, op1=ALU.mult),
                         reads=[a2.b], writes=[mk.b])
                    P.op("dve", lambda e: e.tensor_tensor(out=a2.t[:], in0=a2.t[:], in1=mk.t[:], op=ALU.add), reads=[a2.b, mk.b], writes=[a2.b])
                    P.op("dve", lambda e: e.tensor_scalar(out=mk.t[:], in0=a2.t[:], scalar1=TWO_PI / 2, scalar2=-TWO_PI, op0=ALU.is_gt, op1=ALU.mult),
                         reads=[a2.b], writes=[mk.b])
                    P.op("dve", lambda e: e.tensor_tensor(out=a2.t[:], in0=a2.t[:], in1=mk.t[:], op=ALU.add), reads=[a2.b, mk.b], writes=[a2.b])
                    P.op("dve", lambda e: e.tensor_scalar(out=a2.t[:], in0=a2.t[:], scalar1=-3.1415925, scalar2=3.1415925, op0=ALU.max, op1=ALU.min),
                         reads=[a2.b], writes=[a2.b])
                    sc_ = TB(P.sb([128, 2 * n], F32, stk))
                    P.op("act", lambda e: e.activation(out=sc_.t[:], in_=a2.t[:], func=AF.Sin), reads=[a2.b], writes=[sc_.b])
                    return sc_

                def zoh(lre_ap, lim_ap, ldt_ap, n, srcb, stk):
                    dt_ = TB(P.sb([128, n], F32, stk))
                    er = TB(P.sb([128, n], F32, stk))
                    th = TB(P.sb([128, n], F32, stk))
                    rho = TB(P.sb([128, n], F32, stk))
                    P.op("act", lambda e: e.activation(out=dt_.t[:], in_=ldt_ap, func=AF.Exp), reads=[srcb], writes=[dt_.b])
                    P.op("dve", lambda e: e.tensor_tensor(out=er.t[:], in0=lre_ap, in1=dt_.t[:], op=ALU.mult), reads=[srcb, dt_.b], writes=[er.b])
                    P.op("dve", lambda e: e.tensor_tensor(out=th.t[:], in0=lim_ap, in1=dt_.t[:], op=ALU.mult), reads=[srcb, dt_.b], writes=[th.b])
                    P.op("act", lambda e: e.activation(out=rho.t[:], in_=er.t[:], func=AF.Exp), reads=[er.b], writes=[rho.b])
                    sc_ = sincos(th, n, stk)
                    return rho, sc_

                with ExitStack() as pp:
                    colP = TB(P.sb([128, 3, 256], F32, pp))
                    P.dma("sp", colP.t[:], I["ssm_colp"][l].rearrange("p (k s) -> p k s", k=3), writes=[colP.b])
                    rho, sc_ = zoh(colP.t[:, 0, :], colP.t[:, 1, :], colP.t[:, 2, :], 256, colP.b, pp)
                    P.op("act", lambda e, rho=rho: e.copy(out=rho_c.t[:], in_=rho.t[:].rearrange("p (j r) -> p j r", r=16)[:, :, 0]), reads=[rho.b], writes=[rho_c.b])
                    P.op("act", lambda e, sc_=sc_: e.copy(out=Eim.t[:, :, 0], in_=sc_.t[:, 0:256].rearrange("p (j r) -> p j r", r=16)[:, :, 0]), reads=[sc_.b], writes=[Eim.b])
                    P.op("act", lambda e, sc_=sc_: e.copy(out=Ere.t[:, :, 0], in_=sc_.t[:, 256:512].rearrange("p (j r) -> p j r", r=16)[:, :, 0]), reads=[sc_.b], writes=[Ere.b])
                    if cfg.debug:
                        P.dma("sp", O["dbg_rhofull"], rho.t[:], reads=[rho.b])
                        P.dma("sp", O["dbg_sc"], sc_.t[:], reads=[sc_.b])
                        P.dma("sp", O["dbg_colp"], colP.t[:].rearrange("p k s -> p (k s)"), reads=[colP.b])
                    tq = [TB(P.sb([128, 16, T // 2], F32, pp)) for _ in range(2)]
                    n = 1
                    while n < T:
                        cb_ = Ere.t[:, :, n - 1:n].to_broadcast([128, 16, n])
                        sb_ = Eim.t[:, :, n - 1:n].to_broadcast([128, 16, n])
                        a_, b_ = tq[0], tq[1]
                        P.op("dve", lambda e, n=n, cb_=cb_: e.tensor_tensor(out=a_.t[:, :, 0:n], in0=Ere.t[:, :, 0:n], in1=cb_, op=ALU.mult),
                             reads=[Ere.b], writes=[a_.b])
                        P.op("dve", lambda e, n=n, sb_=sb_: e.tensor_tensor(out=b_.t[:, :, 0:n], in0=Eim.t[:, :, 0:n], in1=sb_, op=ALU.mult),
                             reads=[Eim.b], writes=[b_.b])
                        P.op("dve", lambda e, n=n: e.tensor_tensor(out=Ere.t[:, :, n:2 * n], in0=a_.t[:, :, 0:n], in1=b_.t[:, :, 0:n], op=ALU.subtract),
                             reads=[a_.b, b_.b], writes=[Ere.b])
                        P.op("dve", lambda e, n=n, sb_=sb_: e.tensor_tensor(out=a_.t[:, :, 0:n], in0=Ere.t[:, :, 0:n], in1=sb_, op=ALU.mult),
                             reads=[Ere.b], writes=[a_.b])
                        P.op("dve", lambda e, n=n, cb_=cb_: e.tensor_tensor(out=b_.t[:, :, 0:n], in0=Eim.t[:, :, 0:n], in1=cb_, op=ALU.mult),
                             reads=[Eim.b], writes=[b_.b])
                        P.op("dve", lambda e, n=n: e.tensor_tensor(out=Eim.t[:, :, n:2 * n], in0=a_.t[:, :, 0:n], in1=b_.t[:, :, 0:n], op=ALU.add),
                             reads=[a_.b, b_.b], writes=[Eim.b])
                        n *= 2
                    rowP = TB(P.sb([128, 2, 3, 256], F32, pp))
                    P.dma("sp", rowP.t[:], I["ssm_rowp"][l].rearrange("p (d k q) -> p d k q", d=2, k=3), writes=[rowP.b])
                    braw = TB(P.sb([128, 2, 2, 256], F32, pp))
                    P.dma("sp", braw.t[:], I["ssm_bt"][l].rearrange("p (d r q) -> p d r q", d=2, r=2), writes=[braw.b])
                    for d in range(2):
                        lre, lim = rowP.t[:, d, 0, :], rowP.t[:, d, 1, :]
                        rho, sc_ = zoh(lre, lim, rowP.t[:, d, 2, :], 256, rowP.b, pp)
                        nr = TB(P.sb([128, 256], F32, pp))
                        ni = TB(P.sb([128, 256], F32, pp))
                        den = TB(P.sb([128, 256], F32, pp))
                        t_ = TB(P.sb([128, 256], F32, pp))
                        cr = TB(P.sb([128, 256], F32, pp))
                        ci = TB(P.sb([128, 256], F32, pp))
                        P.op("dve", lambda e, rho=rho, sc_=sc_, nr=nr: e.tensor_tensor(out=nr.t[:], in0=rho.t[:], in1=sc_.t[:, 256:512], op=ALU.mult),
                             reads=[rho.b, sc_.b], writes=[nr.b])
                        P.op("dve", lambda e, nr=nr: e.tensor_scalar(out=nr.t[:], in0=nr.t[:], scalar1=-1.0, scalar2=None, op0=ALU.add),
                             reads=[nr.b], writes=[nr.b])
                        P.op("dve", lambda e, rho=rho, sc_=sc_, ni=ni: e.tensor_tensor(out=ni.t[:], in0=rho.t[:], in1=sc_.t[:, 0:256], op=ALU.mult),
                             reads=[rho.b, sc_.b], writes=[ni.b])
                        P.op("dve", lambda e, den=den, lre=lre: e.tensor_tensor(out=den.t[:], in0=lre, in1=lre, op=ALU.mult), reads=[rowP.b], writes=[den.b])
                        P.op("dve", lambda e, t_=t_, lim=lim: e.tensor_tensor(out=t_.t[:], in0=lim, in1=lim, op=ALU.mult), reads=[rowP.b], writes=[t_.b])
                        P.op("dve", lambda e, den=den, t_=t_: e.tensor_tensor(out=den.t[:], in0=den.t[:], in1=t_.t[:], op=ALU.add), reads=[den.b, t_.b], writes=[den.b])
                        P.op("dve", lambda e, den=den: e.reciprocal(out=den.t[:], in_=den.t[:]), reads=[den.b], writes=[den.b])
                        P.op("dve", lambda e, cr=cr, nr=nr, lre=lre: e.tensor_tensor(out=cr.t[:], in0=nr.t[:], in1=lre, op=ALU.mult), reads=[nr.b, rowP.b], writes=[cr.b])
                        P.op("dve", lambda e, t_=t_, ni=ni, lim=lim: e.tensor_tensor(out=t_.t[:], in0=ni.t[:], in1=lim, op=ALU.mult), reads=[ni.b, rowP.b], writes=[t_.b])
                        P.op("dve", lambda e, cr=cr, t_=t_: e.tensor_tensor(out=cr.t[:], in0=cr.t[:], in1=t_.t[:], op=ALU.add), reads=[cr.b, t_.b], writes=[cr.b])
                        P.op("dve", lambda e, cr=cr, den=den: e.tensor_tensor(out=cr.t[:], in0=cr.t[:], in1=den.t[:], op=ALU.mult), reads=[cr.b, den.b], writes=[cr.b])
                        P.op("dve", lambda e, ci=ci, ni=ni, lre=lre: e.tensor_tensor(out=ci.t[:], in0=ni.t[:], in1=lre, op=ALU.mult), reads=[ni.b, rowP.b], writes=[ci.b])
                        P.op("dve", lambda e, t_=t_, nr=nr, lim=lim: e.tensor_tensor(out=t_.t[:], in0=nr.t[:], in1=lim, op=ALU.mult), reads=[nr.b, rowP.b], writes=[t_.b])
                        P.op("dve", lambda e, ci=ci, t_=t_: e.tensor_tensor(out=ci.t[:], in0=ci.t[:], in1=t_.t[:], op=ALU.subtract), reads=[ci.b, t_.b], writes=[ci.b])
                        P.op("dve", lambda e, ci=ci, den=den: e.tensor_tensor(out=ci.t[:], in0=ci.t[:], in1=den.t[:], op=ALU.mult), reads=[ci.b, den.b], writes=[ci.b])
                        bre, bim = braw.t[:, d, 0, :], braw.t[:, d, 1, :]
                        x1 = TB(P.sb([128, 256], F32, pp))
                        x2 = TB(P.sb([128, 256], F32, pp))
                        btr = BtR.t[:, d, :, :].rearrange("p c q -> p (c q)")
                        bti = BtI.t[:, d, :, :].rearrange("p c q -> p (c q)")
                        P.op("dve", lambda e, x1=x1, bre=bre, cr=cr: e.tensor_tensor(out=x1.t[:], in0=bre, in1=cr.t[:], op=ALU.mult), reads=[braw.b, cr.b], writes=[x1.b])
                        P.op("dve", lambda e, x2=x2, bim=bim, ci=ci: e.tensor_tensor(out=x2.t[:], in0=bim, in1=ci.t[:], op=ALU.mult), reads=[braw.b, ci.b], writes=[x2.b])
                        P.op("dve", lambda e, x1=x1, x2=x2, btr=btr: e.tensor_tensor(out=btr, in0=x1.t[:], in1=x2.t[:], op=ALU.subtract), reads=[x1.b, x2.b], writes=[BtR.b])
                        P.op("dve", lambda e, x1=x1, bre=bre, ci=ci: e.tensor_tensor(out=x1.t[:], in0=bre, in1=ci.t[:], op=ALU.mult), reads=[braw.b, ci.b], writes=[x1.b])
                        P.op("dve", lambda e, x2=x2, bim=bim, cr=cr: e.tensor_tensor(out=x2.t[:], in0=bim, in1=cr.t[:], op=ALU.mult), reads=[braw.b, cr.b], writes=[x2.b])
                        P.op("dve", lambda e, x1=x1, x2=x2, bti=bti: e.tensor_tensor(out=bti, in0=x1.t[:], in1=x2.t[:], op=ALU.add), reads=[x1.b, x2.b], writes=[BtI.b])
                    P.barrier()

                if cfg.debug:
                    P.dma("sp", O["dbg_E"][:, 0], Ere.t[:], reads=[Ere.b])
                    P.dma("sp", O["dbg_E"][:, 1], Eim.t[:], reads=[Eim.b])
                    P.dma("sp", O["dbg_rho"], rho_c.t[:], reads=[rho_c.b])
                    P.dma("pool", O["dbg_bt"][:, 0], BtR.t[:].rearrange("p d c q -> p (d c q)"), reads=[BtR.b])
                    P.dma("pool", O["dbg_bt"][:, 1], BtI.t[:].rearrange("p d c q -> p (d c q)"), reads=[BtI.b])
                NB = 4
                NBA = 8
                def mk(n, dt=F32):
                    return [TB(P.sb([128, T], dt, ph)) for _ in range(n)]
                bre_t, bim_t = mk(NBA), mk(NBA)
                t1, t2, t3, t4 = mk(NB), mk(NB), mk(NB), mk(NB)
                brp, bip = mk(NB), mk(NB)
                rr_t, ri_t = mk(NB), mk(NB)
                o1, o2, o3, o4 = mk(NB, BF16), mk(NB, BF16), mk(NB, BF16), mk(NB, BF16)
                ctmp = [TB(P.sb([128, 4], F32, ph)) for _ in range(NB)]
                y32 = [TB(P.sb([128, T], F32, ph)) for _ in range(2)]
                g1 = [TB(P.sb([128, T], F32, ph)) for _ in range(2)]
                g2 = [TB(P.sb([128, T], F32, ph)) for _ in range(2)]
                z32 = [TB(P.sb([128, T], F32, ph)) for _ in range(2)]
                zb = [TB(P.sb([128, T], BF16, ph)) for _ in range(2)]
                sg = [TB(P.sb([128, T], F32, ph)) for _ in range(2)]
                ob = [TB(P.sb([128, T], BF16, ph)) for _ in range(2)]
                st0 = TB(P.sb([128, 32], F32, ph))
                uctr = [0]

                def tt(eng, o, a, b, op, rb, wb):
                    P.op(eng, lambda e: e.tensor_tensor(out=o, in0=a, in1=b, op=op), reads=rb, writes=wb)

                def unit_pre(d, tc, sc):
                    cc, pg = sc // 4, sc % 4
                    rows = slice(pg * 32, pg * 32 + 32)
                    t0 = tc * T
                    i = uctr[0] % NB
                    ia = uctr[0] % NBA
                    sbk = sbank[uctr[0] % 4]
                    uctr[0] += 1
                    j = d * 8 + sc
                    u_ap = uT.t[rows, cc, t0:t0 + T]
                    if d == 1:
                        u_ap = u_ap[:, ::-1]
                    P.op("pe", lambda e: e.matmul(sbk.t[:, 0:T], lhsT=BtR.t[rows, d, cc, :], rhs=u_ap, start=True, stop=True,
                                                  tile_position=(pg * 32, 0)),
                         reads=[BtR.b, uT.b], writes=[sbk.b])
                    P.op("pe", lambda e: e.matmul(sbk.t[:, T:2 * T], lhsT=BtI.t[rows, d, cc, :], rhs=u_ap, start=True, stop=True,
                                                  tile_position=(pg * 32, 0)),
                         reads=[BtI.b, uT.b], writes=[sbk.b])
                    bre, bim = bre_t[ia], bim_t[ia]
                    P.op("act", lambda e: e.copy(out=bre.t[:], in_=sbk.t[:, 0:T]), reads=[sbk.b], writes=[bre.b])
                    P.op("act", lambda e: e.copy(out=bim.t[:], in_=sbk.t[:, T:2 * T]), reads=[sbk.b], writes=[bim.b])
                    return (d, tc, sc, i, j, ia)

                def unit_pre_b(stt):
                    d, tc, sc, i, j, ia = stt
                    bre, bim = bre_t[ia], bim_t[ia]
                    ec, es = Ere.t[:, j, :], Eim.t[:, j, :]
                    return [
                        lambda: tt("dve", t1[i].t[:], bre.t[:], ec, ALU.mult, [bre.b, Ere.b], [t1[i].b]),
                        lambda: tt("dve", t2[i].t[:], bim.t[:], es, ALU.mult, [bim.b, Eim.b], [t2[i].b]),
                        lambda: tt("dve", t3[i].t[:], bim.t[:], ec, ALU.mult, [bim.b, Ere.b], [t3[i].b]),
                        lambda: tt("dve", t4[i].t[:], bre.t[:], es, ALU.mult, [bre.b, Eim.b], [t4[i].b]),
                        lambda: tt("dve", brp[i].t[:], t1[i].t[:], t2[i].t[:], ALU.add, [t1[i].b, t2[i].b], [brp[i].b]),
                        lambda: tt("dve", bip[i].t[:], t3[i].t[:], t4[i].t[:], ALU.subtract, [t3[i].b, t4[i].b], [bip[i].b]),
                    ]

                def unit_post(stt, first, last_extra):
                    d, tc, sc, i, j, ia = stt
                    cc = sc // 4
                    ec, es = Ere.t[:, j, :], Eim.t[:, j, :]
                    rb_ = rho_c.t[:, j:j + 1].to_broadcast([128, T])
                    rr, ri = rr_t[i], ri_t[i]
                    dv = [
                        lambda: P.op("dve", lambda e: e.tensor_tensor_scan(out=rr.t[:], data0=rb_, data1=brp[i].t[:],
                                                                           initial=carry.t[:, d, sc, 0:1], op0=ALU.mult, op1=ALU.add),
                                     reads=[rho_c.b, brp[i].b, carry.b], writes=[rr.b]),
                        lambda: P.op("dve", lambda e: e.tensor_tensor_scan(out=ri.t[:], data0=rb_, data1=bip[i].t[:],
                                                                           initial=carry.t[:, d, sc, 1:2], op0=ALU.mult, op1=ALU.add),
                                     reads=[rho_c.b, bip[i].b, carry.b], writes=[ri.b]),
                        lambda: tt("dve", o1[i].t[:], rr.t[:], ec, ALU.mult, [rr.b, Ere.b], [o1[i].b]),
                        lambda: tt("dve", o3[i].t[:], rr.t[:], es, ALU.mult, [rr.b, Eim.b], [o3[i].b]),
                        lambda: tt("dve", o2[i].t[:], ri.t[:], es, ALU.mult, [ri.b, Eim.b], [o2[i].b]),
                        lambda: tt("dve", o4[i].t[:], ri.t[:], ec, ALU.mult, [ri.b, Ere.b], [o4[i].b]),
                    ]

                    def rest():
                        ct_ = ctmp[i]
                        ecl, esl = Ere.t[:, j, T - 1:T], Eim.t[:, j, T - 1:T]
                        rrl, ril = rr.t[:, T - 1:T], ri.t[:, T - 1:T]
                        P.op("act", lambda e: e.activation(out=ct_.t[:, 0:1], in_=rrl, func=AF.Identity, scale=ecl), reads=[rr.b, Ere.b], writes=[ct_.b])
                        P.op("act", lambda e: e.activation(out=ct_.t[:, 1:2], in_=rrl, func=AF.Identity, scale=esl), reads=[rr.b, Eim.b], writes=[ct_.b])
                        P.op("act", lambda e: e.activation(out=ct_.t[:, 2:3], in_=ril, func=AF.Identity, scale=esl), reads=[ri.b, Eim.b], writes=[ct_.b])
                        P.op("act", lambda e: e.activation(out=ct_.t[:, 3:4], in_=ril, func=AF.Identity, scale=ecl), reads=[ri.b, Ere.b], writes=[ct_.b])
                        P.op("act", lambda e: e.activation(out=carry.t[:, d, sc, 0:1], in_=ct_.t[:, 2:3], func=AF.Identity, scale=-1.0, bias=ct_.t[:, 0:1]),
                             reads=[ct_.b], writes=[carry.b])
                        P.op("act", lambda e: e.activation(out=carry.t[:, d, sc, 1:2], in_=ct_.t[:, 3:4], func=AF.Identity, scale=1.0, bias=ct_.t[:, 1:2]),
                             reads=[ct_.b], writes=[carry.b])
                        yp = ops_[cc]
                        aps = [o1[i].t[:], o2[i].t[:], o3[i].t[:], o4[i].t[:]]
                        if d == 1:
                            aps = [a_[:, ::-1] for a_ in aps]
                        var = [0, 2, 1, 1]
                        bufs = [o1[i].b, o2[i].b, o3[i].b, o4[i].b]
                        for q_ in range(4):
                            P.op("pe", lambda e, q_=q_: e.matmul(yp.t[:, 0:T], lhsT=Ct.t[:, d, var[q_], sc, :], rhs=aps[q_],
                                                                start=(first and q_ == 0), stop=(last_extra and q_ == 3)),
                                 reads=[Ct.b, bufs[q_]], writes=[yp.b])
                    return dv, rest

                def run_units(ulist, tail_fn):
                    LA, LB = 5, 2
                    n = len(ulist)
                    stts = [None] * n
                    for step in range(n + LA):
                        ia = step
                        ib = step - (LA - LB)
                        ip = step - LA
                        if ia < n:
                            d, tc, sc, first, last_extra = ulist[ia]
                            stts[ia] = unit_pre(d, tc, sc)
                        pre = unit_pre_b(stts[ib]) if 0 <= ib < n else []
                        if 0 <= ip < n:
                            d, tc, sc, first, last_extra = ulist[ip]
                            dv, rest = unit_post(stts[ip], first, last_extra)
                        else:
                            dv, rest = [], None
                        if pre and dv:
                            order = [dv[0], pre[0], dv[1], pre[1], dv[2], pre[2], dv[3], pre[3], dv[4], pre[4], dv[5], pre[5]]
                        else:
                            order = list(pre) + list(dv)
                        for th in order:
                            th()
                        if rest is not None:
                            rest()
                            d, tc, sc, first, last_extra = ulist[ip]
                            if sc == 7:
                                tail_fn(d, tc)

                def ssm_seq(tok0, Ls, latent, seq):
                    nT = Ls // T
                    for c in range(2):
                        for t0 in range(0, Ls, TS):
                            w = min(TS, Ls - t0)
                            P.dma("pool", uT.t[:, c, t0:t0 + w], projT[:, c, tok0 + t0:tok0 + t0 + w],
                                  reads=[b_proj[(tok0 + t0) // TS]], writes=[uT.b])
                    if latent:
                        P.dma("sp", st0.t[:], I["ssm_st0"][l], writes=[st0.b])
                        P.op("dve", lambda e: e.tensor_copy(out=carry.t[:].rearrange("p d s r -> p (d s r)"), in_=st0.t[:]),
                             reads=[st0.b], writes=[carry.b])
                    else:
                        P.op("dve", lambda e: e.memset(carry.t[:], 0.0), writes=[carry.b])
                    def tail(d, tc):
                        t0 = tc * T
                        if d == 0:
                            for cc in range(2):
                                P.op("act", lambda e, cc=cc, tc=tc: e.copy(out=yacc.t[:, cc, tc * T:(tc + 1) * T], in_=ops_[cc].t[:, 0:T]),
                                     reads=[ops_[cc].b], writes=[yacc.b])
                            return
                        for cc in range(2):
                            yp = ops_[cc]
                            P.op("pe", lambda e, cc=cc, yp=yp, t0=t0: e.matmul(yp.t[:, 0:T], lhsT=Dd.t[:, cc, :], rhs=uT.t[:, cc, t0:t0 + T],
                                                                        start=False, stop=True),
                                 reads=[Dd.b, uT.b], writes=[yp.b])
                            y_, a_, b_, z_, zb_ = y32[cc], g1[cc], g2[cc], z32[cc], zb[cc]
                            P.op("dve", lambda e, y_=y_, yp=yp, cc=cc, t0=t0: e.tensor_tensor(
                                out=y_.t[:], in0=yp.t[:, 0:T], in1=yacc.t[:, cc, t0:t0 + T], op=ALU.add),
                                reads=[yp.b, yacc.b], writes=[y_.b])
                            P.op("pool", lambda e, y_=y_, a_=a_: e.tensor_tensor(out=a_.t[:], in0=y_.t[:], in1=y_.t[:], op=ALU.mult),
                                 reads=[y_.b], writes=[a_.b])
                            P.op("pool", lambda e, a_=a_: e.tensor_scalar(out=a_.t[:], in0=a_.t[:], scalar1=0.044715, scalar2=1.0,
                                                                         op0=ALU.mult, op1=ALU.add), reads=[a_.b], writes=[a_.b])
                            P.op("pool", lambda e, a_=a_, b_=b_, y_=y_: e.tensor_tensor(out=b_.t[:], in0=a_.t[:], in1=y_.t[:], op=ALU.mult),
                                 reads=[a_.b, y_.b], writes=[b_.b])
                            P.op("act", lambda e, a_=a_, b_=b_: e.activation(out=a_.t[:], in_=b_.t[:], func=AF.Sigmoid, scale=1.5957691216057308),
                                 reads=[b_.b], writes=[a_.b])
                            P.op("pool", lambda e, a_=a_, z_=z_, y_=y_: e.tensor_tensor(out=z_.t[:], in0=a_.t[:], in1=y_.t[:], op=ALU.mult),
                                 reads=[a_.b, y_.b], writes=[z_.b])
                            P.op("act", lambda e, z_=z_, zb_=zb_: e.copy(out=zb_.t[:], in_=z_.t[:]), reads=[z_.b], writes=[zb_.b])
                        for jo in range(2):
                            gp_ = stat if jo == 0 else pmisc
                            for ji in range(2):
                                P.op("pe", lambda e, gp_=gp_, ji=ji, jo=jo: e.matmul(
                                    gp_.t[:, 0:T], lhsT=wglu.t[:, ji, jo * 128:(jo + 1) * 128], rhs=zb[ji].t[:],
                                    start=(ji == 0), stop=(ji == 1)), reads=[wglu.b, zb[ji].b], writes=[gp_.b])
                            P.op("act", lambda e, gp_=gp_, jo=jo: e.activation(
                                out=sg[jo].t[:], in_=gp_.t[:, 0:T], func=AF.Sigmoid, bias=vec[l].t[:, 98 + jo:99 + jo], scale=1.0),
                                reads=[gp_.b, vec[l].b], writes=[sg[jo].b])
                            P.op("pool", lambda e, jo=jo: e.tensor_tensor(out=ob[jo].t[:], in0=z32[jo].t[:], in1=sg[jo].t[:], op=ALU.mult),
                                 reads=[z32[jo].b, sg[jo].b], writes=[ob[jo].b])
                            P.dma("sp", mergT[:, jo, tok0 + t0:tok0 + t0 + T], ob[jo].t[:], reads=[ob[jo].b],
                                  writes=[b_merg[(tok0 + t0) // TS]])

                    ul = []
                    for tc in range(nT):
                        for sc in range(8):
                            ul.append((0, tc, sc, sc % 4 == 0, sc % 4 == 3))
                    for tc in reversed(range(nT)):
                        for sc in range(8):
                            ul.append((1, tc, sc, sc % 4 == 0, False))
                    run_units(ul, tail)
                    if not latent:
                        P.op("dve", lambda e, seq=seq: e.tensor_copy(
                            out=stout.t[:, seq, l, :], in_=carry.t[:].rearrange("p d s r -> p (d s r)")),
                            reads=[carry.b], writes=[stout.b])

                ssm_seq(0, L, True, -1)
                for s in range(NSEQ):
                    ssm_seq(L + s * L_CTX, L_CTX, False, s)
                P.barrier()

        def phase_MX(l):
            zc = []
            if "attn" in cfg.mixers:
                mx_attn_all(l)
            else:
                zc += [2, 3, 4, 5]
            if "fnet" in cfg.mixers:
                mx_fnet(l)
            else:
                zc += [6, 7]
            if "ssm" in cfg.mixers:
                mx_ssm(l)
            else:
                zc += [0, 1]
            if zc:
                zero_merg(zc)

        with ExitStack() as wst:
            W = alloc_W(wst)
            load_ffn(W, 0, 1)
            phase_X0()
            phase0()
            phase_F(W, 0, 1)
        for l in range(DP):
            phase_PJ(l)
            if have_mix:
                phase_MX(l)
            with ExitStack() as wst:
                W = alloc_W(wst)
                load_ffn(W, l, 2)
                if have_mix:
                    phase_WO(l)
                phase_F(W, l, 2, last=(l == DP - 1))
                if l + 1 < DP:
                    load_ffn(W, l + 1, 1)
                    phase_F(W, l + 1, 1)

        P.dma("sp", O["o_st"], stout.t[:].rearrange("p s l x -> p (s l x)"), reads=[stout.b])
        P.barrier()
        P.emit()
    return nc


_CACHE = {}


def _get_program(cfg_key):
    if cfg_key not in _CACHE:
        _CACHE[cfg_key] = build_program(Cfg(*cfg_key))
    return _CACHE[cfg_key]


def host_constants(cfg):
    f = np.float32
    L = cfg.l_lat
    C = {}
    t = np.arange(L)
    row = (t // 64).astype(f)
    col = (t % 64).astype(f)
    inv = (f(10000.0) ** (-np.arange(16, dtype=f) / f(16))).astype(f)
    ang = np.concatenate([row[:, None] * inv[None, :], col[:, None] * inv[None, :]], axis=1).astype(f)
    cosT = np.cos(ang).astype(f).T
    sinT = np.sin(ang).astype(f).T
    C["ropeC"] = np.ascontiguousarray(np.tile(cosT, (4, 1)))
    C["ropeS"] = np.ascontiguousarray(np.tile(sinT, (4, 1)))
    R = np.zeros((128, 128), f)
    for base in (0, 64):
        for i in range(32):
            R[base + i + 32, base + i] = -1.0
            R[base + i, base + i + 32] = 1.0
    C["rotm"] = R
    kp = np.arange(128)[:, None]
    qf = np.arange(128)[None, :]
    NEG = -30000.0
    m1 = np.where(qf <= kp, 0.0, NEG)
    m2 = np.where(kp <= qf, 0.0, NEG)
    C["masks"] = np.concatenate([m1, m2], axis=1).astype(f)
    c = np.arange(256)
    a256 = 2 * np.pi * np.outer(c, c) / 256.0
    cs = np.concatenate([np.cos(a256), np.sin(a256)], axis=1)
    C["cs256"] = cs.reshape(2, 128, 512).astype(f)
    scl = 1.0 / np.sqrt(256.0 * L)
    aL = 2 * np.pi * ((np.outer(np.arange(L), np.arange(512))) % L) / float(L)
    C["fn_ec"] = (np.cos(aL) * scl).astype(f)
    C["fn_nes"] = (-np.sin(aL) * scl).astype(f)
    C["fn_c256s"] = (np.cos(a256) / 256.0).astype(f)
    C["fn_ns256s"] = (-np.sin(a256) / 256.0).astype(f)
    nkt = L // 512
    ph = np.zeros((128, 4, 8), f)
    p_ = np.arange(128)[:, None]
    kt_ = np.arange(nkt)[None, :]
    aphi = 2 * np.pi * ((p_ * kt_) % nkt) / float(nkt)
    ph[:, 0, :nkt] = np.cos(aphi)
    ph[:, 1, :nkt] = np.sin(aphi)
    ph[:, 2, :nkt] = -np.sin(aphi)
    C["fn_phi"] = ph
    return C


def ssm_layouts(inp):
    f = np.float32
    out = {}
    lre = np.asarray(inp["ssm_lambda_re"], f)
    lim = np.asarray(inp["ssm_lambda_im"], f)
    ldt = np.repeat(np.asarray(inp["ssm_log_dt"], f)[..., None], 64, axis=-1)
    par = np.stack([lre, lim, ldt], axis=2)
    p8 = par.reshape(DEPTH, 2, 3, 8, 128)
    pc = p8.transpose(0, 4, 2, 1, 3)
    pc = np.repeat(pc[..., None], 16, axis=-1)
    out["ssm_colp"] = np.ascontiguousarray(pc).reshape(DEPTH, 128, 768)
    p_r = p8.reshape(DEPTH, 2, 3, 2, 4, 128)
    p_r = p_r.transpose(0, 4, 1, 2, 3, 5)
    p_r = np.repeat(p_r[:, :, None], 32, axis=2)
    out["ssm_rowp"] = np.ascontiguousarray(p_r).reshape(DEPTH, 128, 1536)
    bt = np.zeros((DEPTH, 4, 32, 2, 2, 2, 128), f)
    bb = np.stack([np.asarray(inp["ssm_b_re"], f), np.asarray(inp["ssm_b_im"], f)], axis=2)
    ct = np.zeros((DEPTH, 128, 2, 2, 8, 128), f)
    cc_ = np.stack([np.asarray(inp["ssm_c_re"], f), np.asarray(inp["ssm_c_im"], f)], axis=2)
    for sc in range(8):
        cc, pg = sc // 4, sc % 4
        for gg in range(2):
            g = 2 * sc + gg
            bt[:, pg, gg * 16:(gg + 1) * 16, :, :, cc, gg * 64:(gg + 1) * 64] = bb[:, :, :, g].transpose(0, 4, 1, 2, 3)
            ct[:, gg * 64:(gg + 1) * 64, :, :, sc, pg * 32 + gg * 16:pg * 32 + (gg + 1) * 16] = cc_[:, :, :, g].transpose(0, 4, 1, 2, 3)
    out["ssm_bt"] = bt.reshape(DEPTH, 128, 1024)
    out["ssm_ct"] = ct.reshape(DEPTH, 128, 4096)
    dd = np.zeros((DEPTH, 128, 2, 128), f)
    sd = np.asarray(inp["ssm_d"], f).reshape(DEPTH, 2, 128)
    for i in range(128):
        dd[:, i, :, i] = sd[:, :, i]
    out["ssm_dd"] = dd
    out["ssm_w_glu"] = np.ascontiguousarray(inp["ssm_w_glu"], f)
    return out


def ssm_state_in(st):
    s = np.asarray(st, np.float32).reshape(DEPTH, 2, 8, 128, 2)
    return np.ascontiguousarray(s.transpose(0, 3, 1, 2, 4)).reshape(DEPTH, 128, 32)


def ssm_state_out(o):
    s = np.asarray(o, np.float32).reshape(128, NSEQ, DEPTH, 2, 8, 2)
    s = s.transpose(1, 2, 3, 4, 0, 5)
    return np.ascontiguousarray(s).reshape(NSEQ, DEPTH, 2, 16, 64, 2)


def make_in_maps(inp, cfg):
    f = np.float32
    shared = {
        "c_ctx": np.ascontiguousarray(inp["c_ctx"], f).reshape(8, 128),
        "w_mod": np.ascontiguousarray(inp["w_mod"], f),
        "b_mod": np.ascontiguousarray(inp["b_mod"], f).reshape(DEPTH, 72, 128),
        "final_norm": np.ascontiguousarray(inp["final_norm"], f).reshape(8, 128),
        "ident": np.eye(128, dtype=f),
        "w_in": np.ascontiguousarray(inp["w_in"], f),
        "w_out": np.ascontiguousarray(inp["w_out"], f),
        "ssm_d": np.ascontiguousarray(inp["ssm_d"], f).reshape(DEPTH, 2, 128),
        "ssm_b_glu": np.ascontiguousarray(inp["ssm_b_glu"], f).reshape(DEPTH, 2, 128),
        "fnet_b": np.ascontiguousarray(inp["fnet_b"], f).reshape(DEPTH, 2, 128),
        "ax_q_norm": np.ascontiguousarray(inp["ax_q_norm"], f).reshape(DEPTH, 1, 64),
        "ax_k_norm": np.ascontiguousarray(inp["ax_k_norm"], f).reshape(DEPTH, 1, 64),
    }
    for n in ("norm_ffn1", "norm_mix", "norm_ffn2"):
        shared[n] = np.ascontiguousarray(inp[n], f).reshape(DEPTH, 8, 128)
    shared.update(host_constants(cfg))
    shared["swa_sink"] = np.ascontiguousarray(inp["swa_sink"], f)
    shared["fnet_w"] = np.ascontiguousarray(inp["fnet_w"], f)
    shared.update(ssm_layouts(inp))
    for n in ("ffn1_w_gate", "ffn1_w_up", "ffn2_w_gate", "ffn2_w_up", "ffn1_w_down", "ffn2_w_down"):
        shared[n] = np.ascontiguousarray(inp[n], f)
    maps = []
    xp = np.asarray(inp["x_prompt"], f)
    xs = np.asarray(inp["x_sample"], f)
    for c in range(NCORE):
        m = dict(shared)
        m["x_lat"] = np.ascontiguousarray(xs[c, :cfg.l_lat])
        m["x_ctx"] = np.ascontiguousarray(xp[c * NSEQ:(c + 1) * NSEQ]).reshape(NSEQ * L_CTX, D)
        m["c_b"] = np.ascontiguousarray(inp["c"][c], f).reshape(8, 128)
        m["cache_swa"] = np.ascontiguousarray(inp["cache_swa_kv"][c], f).reshape(DEPTH, 2, L_CTX, 128)
        m["cache_ax"] = np.ascontiguousarray(inp["cache_axial_kv"][c], f).reshape(DEPTH, 2, L_CTX, 128)
        m["ssm_st0"] = ssm_state_in(inp["state_ssm"][c])
        maps.append(m)
    return maps


def kernel(**inp):
    cfg_key = (4096, ("fnet", "attn", "ssm"), DEPTH)
    cfg = Cfg(*cfg_key)
    nc = _get_program(cfg_key)
    maps = make_in_maps(inp, cfg)
    res = run_bass_kernel_spmd(nc, maps, core_ids=list(range(NCORE)))
    R = res.results
    y_prompt = np.concatenate([R[c]["y_ctx"].reshape(NSEQ, L_CTX, D) for c in range(NCORE)], axis=0)
    y_sample = np.stack([R[c]["y_lat"] for c in range(NCORE)], axis=0)
    o_swa = np.concatenate([R[c]["o_swa"].reshape(NSEQ, DEPTH, 2, L_CTX, 2, 64) for c in range(NCORE)], axis=0)
    o_ax = np.concatenate([R[c]["o_ax"].reshape(NSEQ, DEPTH, 2, L_CTX, 2, 64) for c in range(NCORE)], axis=0)
    o_st = np.concatenate([ssm_state_out(R[c]["o_st"]) for c in range(NCORE)], axis=0)
    return (y_prompt.astype(np.float32), y_sample.astype(np.float32), o_swa.astype(np.float32),
            o_ax.astype(np.float32), o_st.astype(np.float32))
```

```python
import numpy as np
import ml_dtypes
import concourse.bass as bass
import concourse.mybir as mybir
from concourse.bass_utils import run_bass_kernel_spmd
from contextlib import ExitStack

F32 = mybir.dt.float32
BF16 = mybir.dt.bfloat16
AF = mybir.ActivationFunctionType
ALU = mybir.AluOpType

D = 1024
DFF = 2816
NF = 22
PIN = 1536
DEPTH = 2
NCORE = 8
L_CTX = 256
NSEQ = 4
TS = 512
EPS = 1e-6


class Sem:
    def __init__(self, h):
        self.h = h
        self.val = 0


class Buf:
    __slots__ = ("name", "last_w", "readers")

    def __init__(self, name=""):
        self.name = name
        self.last_w = None
        self.readers = []


class TB:
    def __init__(self, t, name=""):
        self.t = t
        self.b = Buf(name)


class Prog:
    SAME_ENGINE_SYNC = True
    NDMA = 6
    SEM_LIMIT = 24000

    def __init__(self, nc, stack):
        self.nc = nc
        self.stack = stack
        self.engs = {"pe": nc.tensor, "act": nc.scalar, "dve": nc.vector,
                     "pool": nc.gpsimd, "sp": nc.sync}
        self.nsem = 0
        self.esem = {k: self._newsem() for k in self.engs}
        self.waited = {k: {} for k in self.engs}
        self.lists = {k: [] for k in self.engs}
        self.dsems = {}
        self.drr = {}
        for q in ("sp", "pool", "act"):
            self.dsems[q] = [self._newsem() for i in range(self.NDMA)]
            self.drr[q] = 0
        self.ntile = 0

    def _newsem(self):
        self.nsem += 1
        return Sem(self.stack.enter_context(self.nc.semaphore("sem%d" % self.nsem)))

    def sb(self, shape, dtype, stack=None):
        self.ntile += 1
        return (stack or self.stack).enter_context(
            self.nc.sbuf_tensor("t%d" % self.ntile, list(shape), dtype))

    def ps(self, shape, dtype=F32, stack=None):
        self.ntile += 1
        return (stack or self.stack).enter_context(
            self.nc.psum_tensor("p%d" % self.ntile, list(shape), dtype))

    def _deps(self, eng, reads, writes, is_dma=False):
        deps = {}

        def add(sv):
            s, v = sv
            if deps.get(s, 0) < v:
                deps[s] = v
        for b in reads:
            if b.last_w is not None:
                add(b.last_w)
        for b in writes:
            if b.last_w is not None:
                add(b.last_w)
            for r in b.readers:
                add(r)
        own = self.esem.get(eng)
        waits = []
        w = self.waited[eng]
        for s, v in deps.items():
            if s is own and not is_dma and (eng == "pe" or not self.SAME_ENGINE_SYNC):
                continue
            if w.get(s, 0) >= v:
                continue
            w[s] = v
            waits.append((s.h, v))
        return waits

    def op(self, eng, fn, reads=(), writes=()):
        waits = self._deps(eng, reads, writes)
        own = self.esem[eng]
        if own.val >= self.SEM_LIMIT:
            own = self._newsem()
            self.esem[eng] = own
        own.val += 1
        val = own.val
        for b in reads:
            b.readers.append((own, val))
            if len(b.readers) > 24:
                last = {}
                for s, v in b.readers:
                    if last.get(s, 0) < v:
                        last[s] = v
                b.readers = list(last.items())
        for b in writes:
            b.last_w = (own, val)
            b.readers = []
        oh = own.h

        def run(e):
            for s, v in waits:
                e.wait_ge(s, v)
            fn(e).then_inc(oh, 1)
        self.lists[eng].append(run)

    def dma(self, q, out, in_, reads=(), writes=(), **kw):
        sems = self.dsems[q]
        i = self.drr[q] % len(sems)
        s = sems[i]
        if s.val >= self.SEM_LIMIT:
            old = s
            s = self._newsem()
            sems[i] = s
            w = self.waited[q]
            extra = [(old.h, old.val)] if w.get(old, 0) < old.val else []
            w[old] = old.val
        else:
            extra = []
        self.drr[q] += 1
        waits = extra + self._deps(q, reads, writes, is_dma=True)
        w = self.waited[q]
        if s.val > 0 and w.get(s, 0) < s.val:
            w[s] = s.val
            waits.append((s.h, s.val))
        s.val += 16
        val = s.val
        for b in reads:
            b.readers.append((s, val))
        for b in writes:
            b.last_w = (s, val)
            b.readers = []
        sh = s.h

        def run(e):
            for ss, v in waits:
                e.wait_ge(ss, v)
            e.dma_start(out=out, in_=in_, **kw).then_inc(sh, 16)
        self.lists[q].append(run)

    def barrier(self):
        allv = []
        for k, s in self.esem.items():
            if s.val > 0:
                allv.append(s)
        for q in self.dsems:
            for s in self.dsems[q]:
                if s.val > 0:
                    allv.append(s)
        for eng in self.engs:
            waits = []
            w = self.waited[eng]
            own = self.esem[eng]
            for s in allv:
                if s is own:
                    continue
                if w.get(s, 0) >= s.val:
                    continue
                w[s] = s.val
                waits.append((s.h, s.val))
            if waits:
                def run(e, waits=waits):
                    for s, v in waits:
                        e.wait_ge(s, v)
                self.lists[eng].append(run)

    def emit(self):
        nc = self.nc
        lists = self.lists
        with nc.Block() as block:
            @block.tensor
            def _(e):
                for f in lists["pe"]:
                    f(e)

            @block.scalar
            def _(e):
                for f in lists["act"]:
                    f(e)

            @block.vector
            def _(e):
                for f in lists["dve"]:
                    f(e)

            @block.gpsimd
            def _(e):
                for f in lists["pool"]:
                    f(e)

            @block.sync
            def _(e):
                for f in lists["sp"]:
                    f(e)


class Cfg:
    def __init__(self, l_lat=4096, mixers=("fnet", "attn", "ssm"), depth=DEPTH, debug=False):
        self.debug = debug
        self.l_lat = l_lat
        self.nt_lat = l_lat // TS
        self.nt = self.nt_lat + (NSEQ * L_CTX) // TS
        self.ntok = self.nt * TS
        self.mixers = mixers
        self.depth = depth


def build_program(cfg):
    nc = bass.Bass("TRN2", target_bir_lowering=False)
    L = cfg.l_lat
    NT = cfg.nt
    NTOK = cfg.ntok
    DP = cfg.depth

    def din(name, shape, dt=F32):
        return nc.dram_tensor(name, list(shape), dt, kind="ExternalInput").ap()

    def dout(name, shape, dt=F32):
        return nc.dram_tensor(name, list(shape), dt, kind="ExternalOutput").ap()

    def dscr(name, shape, dt=F32):
        return nc.dram_tensor(name, list(shape), dt, kind="Internal").ap()

    I = {}
    I["x_lat"] = din("x_lat", [L, D])
    I["x_ctx"] = din("x_ctx", [NSEQ * L_CTX, D])
    I["c_b"] = din("c_b", [8, 128])
    I["c_ctx"] = din("c_ctx", [8, 128])
    I["w_mod"] = din("w_mod", [DEPTH, D, 9 * D])
    I["b_mod"] = din("b_mod", [DEPTH, 72, 128])
    for n in ("norm_ffn1", "norm_mix", "norm_ffn2"):
        I[n] = din(n, [DEPTH, 8, 128])
    for n in ("ffn1_w_gate", "ffn1_w_up", "ffn2_w_gate", "ffn2_w_up"):
        I[n] = din(n, [DEPTH, D, DFF])
    for n in ("ffn1_w_down", "ffn2_w_down"):
        I[n] = din(n, [DEPTH, DFF, D])
    I["w_in"] = din("w_in", [DEPTH, D, PIN])
    I["w_out"] = din("w_out", [DEPTH, D, D])
    I["ssm_d"] = din("ssm_d", [DEPTH, 2, 128])
    I["ssm_b_glu"] = din("ssm_b_glu", [DEPTH, 2, 128])
    I["fnet_b"] = din("fnet_b", [DEPTH, 2, 128])
    I["ax_q_norm"] = din("ax_q_norm", [DEPTH, 1, 64])
    I["ax_k_norm"] = din("ax_k_norm", [DEPTH, 1, 64])
    I["final_norm"] = din("final_norm", [8, 128])
    I["ident"] = din("ident", [128, 128])
    I["cache_swa"] = din("cache_swa", [DEPTH, 2, L_CTX, 128])
    I["cache_ax"] = din("cache_ax", [DEPTH, 2, L_CTX, 128])
    I["ropeC"] = din("ropeC", [128, L])
    I["ropeS"] = din("ropeS", [128, L])
    I["rotm"] = din("rotm", [128, 128])
    I["masks"] = din("masks", [128, 256])
    I["swa_sink"] = din("swa_sink", [DEPTH, 4])
    I["cs256"] = din("cs256", [2, 128, 512])
    I["fn_ec"] = din("fn_ec", [L, 512])
    I["fn_nes"] = din("fn_nes", [L, 512])
    I["fn_c256s"] = din("fn_c256s", [256, 256])
    I["fn_ns256s"] = din("fn_ns256s", [256, 256])
    I["fn_phi"] = din("fn_phi", [128, 4, 8])
    I["fnet_w"] = din("fnet_w", [DEPTH, 256, 256])
    I["ssm_colp"] = din("ssm_colp", [DEPTH, 128, 768])
    I["ssm_rowp"] = din("ssm_rowp", [DEPTH, 128, 1536])
    I["ssm_bt"] = din("ssm_bt", [DEPTH, 128, 1024])
    I["ssm_ct"] = din("ssm_ct", [DEPTH, 128, 4096])
    I["ssm_dd"] = din("ssm_dd", [DEPTH, 128, 2, 128])
    I["ssm_st0"] = din("ssm_st0", [DEPTH, 128, 32])
    I["ssm_w_glu"] = din("ssm_w_glu", [DEPTH, 256, 256])

    O = {}
    O["y_lat"] = dout("y_lat", [L, D])
    O["y_ctx"] = dout("y_ctx", [NSEQ * L_CTX, D])
    O["o_swa"] = dout("o_swa", [NSEQ, DEPTH, 2, L_CTX, 128])
    O["o_ax"] = dout("o_ax", [NSEQ, DEPTH, 2, L_CTX, 128])
    O["o_st"] = dout("o_st", [128, NSEQ * DEPTH * 32])

    if cfg.debug:
        O["dbg_E"] = dout("dbg_E", [128, 2, 16, 256])
        O["dbg_rho"] = dout("dbg_rho", [128, 16])
        O["dbg_rhofull"] = dout("dbg_rhofull", [128, 256])
        O["dbg_sc"] = dout("dbg_sc", [128, 512])
        O["dbg_colp"] = dout("dbg_colp", [128, 768])
        O["dbg_bt"] = dout("dbg_bt", [128, 2, 512])
    hbuf = dscr("hbuf", [128, 8, NTOK])
    projT = dscr("projT", [128, 12, NTOK])
    if cfg.debug:
        mergT = dout("mergT", [128, 8, NTOK], BF16)
    else:
        mergT = dscr("mergT", [128, 8, NTOK], BF16)
    b_hbuf = [Buf() for _ in range(NT)]
    b_proj = [Buf() for _ in range(NT)]
    b_merg = [Buf() for _ in range(NT)]

    def x_rows(t, s):
        tok = t * TS + s * 128
        if tok < L:
            return I["x_lat"][tok:tok + 128, :]
        tok -= L
        return I["x_ctx"][tok:tok + 128, :]

    def y_rows(t, s):
        tok = t * TS + s * 128
        if tok < L:
            return O["y_lat"][tok:tok + 128, :]
        tok -= L
        return O["y_ctx"][tok:tok + 128, :]

    with ExitStack() as st:
        P = Prog(nc, st)

        ident = TB(P.sb([128, 128], F32))
        ones_bf = TB(P.sb([128, 128], BF16))
        bd64 = TB(P.sb([128, 128], BF16))
        epst = TB(P.sb([128, 1], F32))
        vec = [TB(P.sb([128, 128], F32)), TB(P.sb([128, 128], F32))]
        mod = [TB(P.sb([128, 2, 72], F32)) for _ in range(DEPTH)]
        coef = [[TB(P.sb([128, 6, 8], F32)) for _ in range(2)] for _ in range(DEPTH)]
        gps = [TB(P.ps([128, TS])) for _ in range(2)]
        ups = [TB(P.ps([128, TS])) for _ in range(2)]
        pd = P.ps([128, 2 * TS])
        b_pd = [Buf(), Buf()]
        ops_ = [TB(pd[:, 0:TS]), TB(pd[:, TS:2 * TS])]
        ops_[0].b = b_pd[0]
        ops_[1].b = b_pd[1]
        stat = TB(P.ps([128, TS]))
        pmisc = TB(P.ps([128, TS]))
        stout = TB(P.sb([128, NSEQ, DEPTH, 32], F32))
        P.op("dve", lambda e: e.memset(stout.t[:], 0.0), writes=[stout.b])

        P.op("dve", lambda e: e.memset(ones_bf.t[:], 1.0 / 1024.0), writes=[ones_bf.b])
        P.op("dve", lambda e: e.memset(bd64.t[:], 0.0), writes=[bd64.b])
        P.op("dve", lambda e: e.memset(bd64.t[0:64, 0:64], 1.0 / 64.0), writes=[bd64.b])
        P.op("dve", lambda e: e.memset(bd64.t[64:128, 64:128], 1.0 / 64.0), writes=[bd64.b])
        P.op("dve", lambda e: e.memset(epst.t[:], EPS), writes=[epst.b])
        P.dma("sp", ident.t[:], I["ident"], writes=[ident.b])

        class NS:
            pass

        def alloc_W(stk):
            W = NS()
            W.g = P.sb([128, 8, DFF], BF16, stk)
            W.u = P.sb([128, 8, DFF], BF16, stk)
            W.d = P.sb([128, NF, D], BF16, stk)
            W.bg = [Buf(), Buf()]
            W.bu = [Buf(), Buf()]
            W.bd = [Buf(), Buf()]
            return W

        def load_ffn(W, l, which):
            g = I["ffn%d_w_gate" % which][l].rearrange("(k p) n -> p k n", p=128)
            u = I["ffn%d_w_up" % which][l].rearrange("(k p) n -> p k n", p=128)
            d = I["ffn%d_w_down" % which][l].rearrange("(f p) n -> p f n", p=128)
            HC = 11 * 128
            for half in range(2):
                P.dma("pool", W.g[:, :, half * HC:(half + 1) * HC], g[:, :, half * HC:(half + 1) * HC],
                      writes=[W.bg[half]])
                P.dma("pool", W.u[:, :, half * HC:(half + 1) * HC], u[:, :, half * HC:(half + 1) * HC],
                      writes=[W.bu[half]])
                P.dma("pool", W.d[:, half * 11:(half + 1) * 11, :], d[:, half * 11:(half + 1) * 11, :],
                      writes=[W.bd[half]])

        def phase0():
            with ExitStack() as ph:
                stage = [P.sb([128, 128], F32, ph) for _ in range(2)]
                for l in range(DEPTH):
                    sg_ = stage[l]
                    bz = Buf()
                    P.op("dve", lambda e, sg_=sg_: e.memset(sg_[:], 0.0), writes=[bz])
                    bl = []
                    r = 0

                    def ld(dst, src):
                        b = Buf()
                        b.last_w = bz.last_w
                        P.dma("sp", dst, src, writes=[b])
                        bl.append(b)
                    for nm, nr in (("b_mod", 72), ("norm_ffn1", 8), ("norm_mix", 8), ("norm_ffn2", 8),
                                   ("ssm_d", 2), ("ssm_b_glu", 2), ("fnet_b", 2)):
                        ld(sg_[r:r + nr, :], I[nm][l])
                        r += nr
                    for j, nm in enumerate(("ax_q_norm", "ax_k_norm")):
                        for hh in range(2):
                            ld(sg_[102 + j:103 + j, hh * 64:(hh + 1) * 64], I[nm][l])
                    if l == 0:
                        ld(sg_[104:112, :], I["c_b"])
                        ld(sg_[112:120, :], I["c_ctx"])
                        ld(sg_[120:128, :], I["final_norm"])
                    P.op("pe", lambda e, sg_=sg_: e.transpose(out=pmisc.t[:, 0:128], in_=sg_[:], identity=ident.t[:]),
                         reads=bl + [ident.b], writes=[pmisc.b])
                    P.op("dve", lambda e, l=l: e.tensor_copy(out=vec[l].t[:], in_=pmisc.t[:, 0:128]),
                         reads=[pmisc.b], writes=[vec[l].b])
                scond = TB(P.sb([128, 8, 2], BF16, ph))
                P.op("act", lambda e: e.activation(out=scond.t[:, :, 0], in_=vec[0].t[:, 104:112], func=AF.Silu),
                     reads=[vec[0].b], writes=[scond.b])
                P.op("act", lambda e: e.activation(out=scond.t[:, :, 1], in_=vec[0].t[:, 112:120], func=AF.Silu),
                     reads=[vec[0].b], writes=[scond.b])
                wm = [TB(P.sb([128, 8, 512], F32, ph)) for _ in range(2)]
                wmb = [TB(P.sb([128, 8, 512], BF16, ph)) for _ in range(2)]
                nblk = 0
                for l in range(DP):
                    wsrc = I["w_mod"][l].rearrange("(k p) n -> p k n", p=128)
                    for cb in range(18):
                        w_ = wm[nblk % 2]
                        wb_ = wmb[nblk % 2]
                        ceng = ("act", "pool", "dve")[nblk % 3]
                        nblk += 1
                        P.dma("sp", w_.t[:], wsrc[:, :, cb * 512:(cb + 1) * 512], writes=[w_.b])
                        if ceng == "act":
                            P.op("act", lambda e, w_=w_, wb_=wb_: e.copy(out=wb_.t[:], in_=w_.t[:]), reads=[w_.b], writes=[wb_.b])
                        else:
                            P.op(ceng, lambda e, w_=w_, wb_=wb_: e.tensor_copy(out=wb_.t[:], in_=w_.t[:]), reads=[w_.b], writes=[wb_.b])
                        for j in range(4):
                            ch = cb * 4 + j
                            for k in range(8):
                                P.op("pe", lambda e, wb_=wb_, j=j, k=k, ch=ch: e.matmul(
                                    pmisc.t[:, ch * 2:ch * 2 + 2], lhsT=wb_.t[:, k, j * 128:(j + 1) * 128],
                                    rhs=scond.t[:, k, :], start=(k == 0), stop=(k == 7)),
                                    reads=[wb_.b, scond.b], writes=[pmisc.b])
                    pm = pmisc.t[:, 0:144].rearrange("p (c t) -> p c t", t=2)
                    for cond in range(2):
                        P.op("dve", lambda e, l=l, cond=cond, pm=pm: e.tensor_tensor(
                            out=mod[l].t[:, cond, :], in0=pm[:, :, cond], in1=vec[l].t[:, 0:72], op=ALU.add),
                            reads=[pmisc.b, vec[l].b], writes=[mod[l].b])
                        cf = coef[l][cond]
                        for j in range(3):
                            P.op("dve", lambda e, l=l, cond=cond, j=j, cf=cf: e.scalar_tensor_tensor(
                                out=cf.t[:, j, :], in0=mod[l].t[:, cond, (3 * j + 1) * 8:(3 * j + 2) * 8], scalar=1.0,
                                in1=vec[l].t[:, 72 + 8 * j:80 + 8 * j], op0=ALU.add, op1=ALU.mult),
                                reads=[mod[l].b, vec[l].b], writes=[cf.b])
                            gs = 0.5 if j != 1 else 1.0
                            P.op("dve", lambda e, l=l, cond=cond, j=j, cf=cf, gs=gs: e.tensor_scalar(
                                out=cf.t[:, 3 + j, :], in0=mod[l].t[:, cond, (3 * j + 2) * 8:(3 * j + 3) * 8],
                                scalar1=gs, scalar2=None, op0=ALU.mult),
                                reads=[mod[l].b], writes=[cf.b])
                P.barrier()

        def cond_of(t):
            return 0 if t < cfg.nt_lat else 1

        def alloc_common(ph, with_hff=False, with_xin=False):
            C = NS()
            C.h = P.sb([128, 8, TS], F32, ph)
            C.b_h = [Buf() for _ in range(8)]
            C.xT = P.sb([128, 8, TS], BF16, ph)
            C.b_x = [Buf() for _ in range(8)]
            C.sqb = [TB(P.sb([128, TS], BF16, ph)) for _ in range(2)]
            C.tmpb = [TB(P.sb([128, TS], F32, ph)) for _ in range(2)]
            C.rt = TB(P.sb([128, TS], F32, ph))
            C.rstd = TB(P.sb([128, TS], F32, ph))
            if with_hff:
                C.hff = P.sb([128, 11, TS], BF16, ph)
                C.b_hff = [Buf() for _ in range(11)]
                C.sgb = [TB(P.sb([128, TS], F32, ph)) for _ in range(2)]
            if with_xin:
                C.xin = [TB(P.sb([128, D], F32, ph)) for _ in range(2)]
                C.xin_ctr = 0
            return C

        def norm(C, A, S, coefb, out_fn):
            h, b_h = C.h, C.b_h
            for m in range(8):
                sq = C.sqb[m % 2]
                P.op("act", lambda e, sq=sq, m=m: e.activation(out=sq.t[:], in_=h[:, m, :], func=AF.Square),
                     reads=[b_h[m]], writes=[sq.b])
                P.op("pe", lambda e, sq=sq, m=m: e.matmul(stat.t[:], lhsT=ones_bf.t[:], rhs=sq.t[:],
                                                          start=(m == 0), stop=(m == 7)),
                     reads=[sq.b, ones_bf.b], writes=[stat.b])
            rt, rstd = C.rt, C.rstd
            P.op("act", lambda e: e.activation(out=rt.t[:], in_=stat.t[:], func=AF.Ln, bias=epst.t[:, 0:1], scale=1.0),
                 reads=[stat.b, epst.b], writes=[rt.b])
            P.op("act", lambda e: e.activation(out=rstd.t[:], in_=rt.t[:], func=AF.Exp, scale=-0.5),
                 reads=[rt.b], writes=[rstd.b])
            for m in range(8):
                tm = C.tmpb[m % 2]
                P.op("dve", lambda e, tm=tm, m=m: e.tensor_tensor(out=tm.t[:], in0=h[:, m, :], in1=rstd.t[:], op=ALU.mult),
                     reads=[b_h[m], rstd.b], writes=[tm.b])
                oap, ob = out_fn(m)
                if S is not None:
                    P.op("act", lambda e, tm=tm, m=m, oap=oap: e.activation(
                        out=oap, in_=tm.t[:], func=AF.Identity, scale=A(m), bias=S(m)),
                        reads=[tm.b] + coefb, writes=[ob])
                else:
                    P.op("act", lambda e, tm=tm, m=m, oap=oap: e.activation(
                        out=oap, in_=tm.t[:], func=AF.Identity, scale=A(m)),
                        reads=[tm.b] + coefb, writes=[ob])

        def norm_to_x(C, l, cond, j):
            cf = coef[l][cond]
            norm(C, lambda m: cf.t[:, j, m:m + 1],
                 lambda m: mod[l].t[:, cond, 3 * j * 8 + m:3 * j * 8 + m + 1],
                 [cf.b, mod[l].b],
                 lambda m: (C.xT[:, m, :], C.b_x[m]))

        def ffn(C, W, l, cond, j, mid_hook=None):
            cf = coef[l][cond]
            h, b_h, xT, b_x, hff, b_hff = C.h, C.b_h, C.xT, C.b_x, C.hff, C.b_hff
            for half in range(2):
                for f in range(11):
                    fc = half * 11 + f
                    gp, up = gps[f % 2], ups[f % 2]
                    for k in range(8):
                        P.op("pe", lambda e, gp=gp, k=k, fc=fc: e.matmul(
                            gp.t[:], lhsT=W.g[:, k, fc * 128:(fc + 1) * 128], rhs=xT[:, k, :],
                            start=(k == 0), stop=(k == 7)),
                            reads=[W.bg[half], b_x[k]], writes=[gp.b])
                    for k in range(8):
                        P.op("pe", lambda e, up=up, k=k, fc=fc: e.matmul(
                            up.t[:], lhsT=W.u[:, k, fc * 128:(fc + 1) * 128], rhs=xT[:, k, :],
                            start=(k == 0), stop=(k == 7)),
                            reads=[W.bu[half], b_x[k]], writes=[up.b])
                    sg = C.sgb[f % 2]
                    P.op("act", lambda e, sg=sg, gp=gp: e.activation(out=sg.t[:], in_=gp.t[:], func=AF.Silu),
                         reads=[gp.b], writes=[sg.b])
                    P.op("dve", lambda e, sg=sg, up=up, f=f: e.tensor_tensor(
                        out=hff[:, f, :], in0=sg.t[:], in1=up.t[:], op=ALU.mult),
                        reads=[sg.b, up.b], writes=[b_hff[f]])
                for m in range(8):
                    if half == 1 and m == 1 and mid_hook is not None:
                        mid_hook()
                    o_ = ops_[m % 2]
                    for f in range(11):
                        fc = half * 11 + f
                        P.op("pe", lambda e, o_=o_, f=f, fc=fc, m=m: e.matmul(
                            o_.t, lhsT=W.d[:, fc, m * 128:(m + 1) * 128], rhs=hff[:, f, :],
                            start=(f == 0), stop=(f == 10)),
                            reads=[W.bd[half], b_hff[f]], writes=[o_.b])
                    P.op("dve", lambda e, o_=o_, m=m, cf=cf, j=j: e.scalar_tensor_tensor(
                        out=h[:, m, :], in0=o_.t, scalar=cf.t[:, 3 + j, m:m + 1], in1=h[:, m, :],
                        op0=ALU.mult, op1=ALU.add),
                        reads=[o_.b, b_h[m], cf.b], writes=[b_h[m]])

        def load_h_x(C, t):
            for s in range(4):
                xi = C.xin[C.xin_ctr % 2]
                C.xin_ctr += 1
                P.dma("sp", xi.t[:], x_rows(t, s), writes=[xi.b])
                for m in range(8):
                    P.op("pe", lambda e, xi=xi, m=m: e.transpose(
                        out=pd[:, m * 128:(m + 1) * 128], in_=xi.t[:, m * 128:(m + 1) * 128], identity=ident.t[:]),
                        reads=[xi.b, ident.b], writes=[b_pd[m // 4]])
                P.op("act", lambda e, s=s: e.copy(out=C.h[:, :, s * 128:(s + 1) * 128],
                                                  in_=pd[:, :].rearrange("p (m t) -> p m t", m=8)),
                     reads=b_pd, writes=C.b_h)

        def load_h(C, t):
            P.dma("sp", C.h[:], hbuf[:, :, t * TS:(t + 1) * TS], reads=[b_hbuf[t]], writes=C.b_h)

        def store_h(C, t):
            P.dma("sp", hbuf[:, :, t * TS:(t + 1) * TS], C.h[:], reads=C.b_h, writes=[b_hbuf[t]])

        def final_out(C, t):
            fw = vec[0]
            h, b_h = C.h, C.b_h
            norm(C, lambda m: fw.t[:, 120 + m:121 + m], None, [fw.b], lambda m: (h[:, m, :], b_h[m]))
            for s in range(4):
                for hf in range(2):
                    xi = C.xin[C.xin_ctr[0] % 2]
                    C.xin_ctr[0] += 1
                    for m4 in range(4):
                        m = hf * 4 + m4
                        P.op("pe", lambda e, m=m, m4=m4, s=s, hf=hf: e.transpose(
                            out=pd[:, hf * 512 + m4 * 128:hf * 512 + (m4 + 1) * 128], in_=h[:, m, s * 128:(s + 1) * 128],
                            identity=ident.t[:]),
                            reads=[b_h[m], ident.b], writes=[b_pd[hf]])
                    if hf == 0:
                        P.op("act", lambda e, xi=xi, hf=hf: e.copy(out=xi.t[:], in_=pd[:, hf * 512:(hf + 1) * 512]),
                             reads=[b_pd[hf]], writes=[xi.b])
                    else:
                        P.op("dve", lambda e, xi=xi, hf=hf: e.tensor_copy(out=xi.t[:], in_=pd[:, hf * 512:(hf + 1) * 512]),
                             reads=[b_pd[hf]], writes=[xi.b])
                    P.dma("sp", y_rows(t, s)[:, hf * 512:(hf + 1) * 512], xi.t[:], reads=[xi.b])

        def phase_X0():
            with ExitStack() as ph:
                xin = [TB(P.sb([128, D], F32, ph)) for _ in range(3)]
                hh = [P.sb([128, 8, TS], F32, ph) for _ in range(2)]
                b_hh = [[Buf() for _ in range(8)] for _ in range(2)]
                ctr = 0
                for t in range(NT):
                    h = hh[t % 2]
                    bh = b_hh[t % 2]
                    for s in range(4):
                        xi = xin[ctr % 3]
                        ctr += 1
                        P.dma("sp", xi.t[:], x_rows(t, s), writes=[xi.b])
                        for m in range(8):
                            P.op("pe", lambda e, xi=xi, m=m: e.transpose(
                                out=pd[:, m * 128:(m + 1) * 128], in_=xi.t[:, m * 128:(m + 1) * 128], identity=ident.t[:]),
                                reads=[xi.b, ident.b], writes=[b_pd[m // 4]])
                        if s % 2 == 0:
                            P.op("act", lambda e, s=s, h=h: e.copy(out=h[:, :, s * 128:(s + 1) * 128],
                                                              in_=pd[:, :].rearrange("p (m t) -> p m t", m=8)),
                                 reads=b_pd, writes=bh)
                        else:
                            P.op("dve", lambda e, s=s, h=h: e.tensor_copy(out=h[:, :, s * 128:(s + 1) * 128],
                                                                     in_=pd[:, :].rearrange("p (m t) -> p m t", m=8)),
                                 reads=b_pd, writes=bh)
                    P.dma("act", hbuf[:, :, t * TS:(t + 1) * TS], h[:], reads=bh, writes=[b_hbuf[t]])
                P.barrier()

        def phase_F(W, l, which, last=False):
            with ExitStack() as ph:
                C0 = alloc_common(ph, with_hff=True)
                C0.xin = [TB(P.sb([128, TS], F32, ph)) for _ in range(2)]
                C0.xin_ctr = [0]
                C1 = NS()
                C1.__dict__.update(C0.__dict__)
                C1.h = P.sb([128, 8, TS], F32, ph)
                C1.b_h = [Buf() for _ in range(8)]
                Cs = [C0, C1]
                j = 0 if which == 1 else 2
                load_h(Cs[0], 0)
                norm_to_x(Cs[0], l, cond_of(0), j)
                for t in range(NT):
                    C = Cs[t % 2]
                    cond = cond_of(t)
                    hook = None
                    if t + 1 < NT:
                        Cn = Cs[(t + 1) % 2]
                        load_h(Cn, t + 1)
                        hook = (lambda Cn=Cn, t=t: norm_to_x(Cn, l, cond_of(t + 1), j))
                    ffn(C, W, l, cond, j, mid_hook=hook)
                    if last:
                        final_out(C, t)
                    else:
                        P.dma("act", hbuf[:, :, t * TS:(t + 1) * TS], C.h[:], reads=C.b_h, writes=[b_hbuf[t]])
                P.barrier()

        def phase_PJ(l):
            with ExitStack() as ph:
                Cs = [alloc_common(ph), alloc_common(ph)]
                win = P.sb([128, 8, PIN], BF16, ph)
                b_win = Buf()
                pjs = [P.sb([128, 12, TS], F32, ph) for _ in range(2)]
                b_pjs = [[Buf() for _ in range(12)] for _ in range(2)]
                kvo = [TB(P.sb([128, 512], F32, ph)) for _ in range(2)]
                kctr = 0
                sq3 = [TB(P.sb([128, TS], BF16, ph)) for _ in range(3)]
                rt3 = [TB(P.sb([128, TS], F32, ph)) for _ in range(3)]
                rs3 = [TB(P.sb([128, TS], F32, ph)) for _ in range(3)]
                tm3 = [TB(P.sb([128, TS], F32, ph)) for _ in range(3)]
                P.dma("pool", win[:], I["w_in"][l].rearrange("(k p) n -> p k n", p=128), writes=[b_win])
                load_h(Cs[0], 0)
                for t in range(NT):
                    cond = cond_of(t)
                    C = Cs[t % 2]
                    pj = pjs[t % 2]
                    b_pj = b_pjs[t % 2]
                    if t + 1 < NT:
                        load_h(Cs[(t + 1) % 2], t + 1)
                    norm_to_x(C, l, cond, 1)
                    for c in range(12):
                        pp = gps[c % 2] if (c // 2) % 2 == 0 else ups[c % 2]
                        for k in range(8):
                            P.op("pe", lambda e, pp=pp, k=k, c=c, C=C: e.matmul(
                                pp.t[:], lhsT=win[:, k, c * 128:(c + 1) * 128], rhs=C.xT[:, k, :],
                                start=(k == 0), stop=(k == 7)),
                                reads=[b_win, C.b_x[k]], writes=[pp.b])
                        if c % 2 == 0:
                            P.op("act", lambda e, pp=pp, c=c, pj=pj: e.copy(out=pj[:, c, :], in_=pp.t[:]),
                                 reads=[pp.b], writes=[b_pj[c]])
                        else:
                            P.op("dve", lambda e, pp=pp, c=c, pj=pj: e.tensor_copy(out=pj[:, c, :], in_=pp.t[:]),
                                 reads=[pp.b], writes=[b_pj[c]])
                    fx = [(6, 102, stat), (7, 102, pmisc), (8, 103, ops_[1])]
                    for i3, (c, gcol, bank) in enumerate(fx):
                        sq = sq3[i3]
                        P.op("act", lambda e, sq=sq, c=c, pj=pj: e.activation(out=sq.t[:], in_=pj[:, c, :], func=AF.Square),
                             reads=[b_pj[c]], writes=[sq.b])
                    for i3, (c, gcol, bank) in enumerate(fx):
                        sq = sq3[i3]
                        P.op("pe", lambda e, sq=sq, bank=bank: e.matmul(bank.t[:, 0:TS] if bank is not ops_[1] else bank.t, lhsT=bd64.t[:], rhs=sq.t[:], start=True, stop=True),
                             reads=[sq.b, bd64.b], writes=[bank.b])
                    for i3, (c, gcol, bank) in enumerate(fx):
                        rt_ = rt3[i3]
                        P.op("act", lambda e, rt_=rt_, bank=bank: e.activation(out=rt_.t[:], in_=bank.t[:, 0:TS] if bank is not ops_[1] else bank.t, func=AF.Ln, bias=epst.t[:, 0:1], scale=1.0),
                             reads=[bank.b, epst.b], writes=[rt_.b])
                    for i3, (c, gcol, bank) in enumerate(fx):
                        rt_, rs_ = rt3[i3], rs3[i3]
                        P.op("act", lambda e, rt_=rt_, rs_=rs_: e.activation(out=rs_.t[:], in_=rt_.t[:], func=AF.Exp, scale=-0.5),
                             reads=[rt_.b], writes=[rs_.b])
                    for i3, (c, gcol, bank) in enumerate(fx):
                        rs_, tm = rs3[i3], tm3[i3]
                        P.op("dve", lambda e, tm=tm, c=c, pj=pj, rs_=rs_: e.tensor_tensor(out=tm.t[:], in0=pj[:, c, :], in1=rs_.t[:], op=ALU.mult),
                             reads=[b_pj[c], rs_.b], writes=[tm.b])
                    for i3, (c, gcol, bank) in enumerate(fx):
                        tm = tm3[i3]
                        P.op("act", lambda e, tm=tm, c=c, gcol=gcol, pj=pj: e.activation(
                            out=pj[:, c, :], in_=tm.t[:], func=AF.Identity, scale=vec[l].t[:, gcol:gcol + 1]),
                            reads=[tm.b, vec[l].b], writes=[b_pj[c]])
                    P.dma("act", projT[:, :, t * TS:(t + 1) * TS], pj[:], reads=b_pj, writes=[b_proj[t]])
                    if cond == 1:
                        for s in range(4):
                            seq = (t - cfg.nt_lat) * 2 + s // 2
                            pos0 = (s % 2) * 128
                            xi = kvo[kctr % 2]
                            kctr += 1
                            for jj, c in enumerate((4, 5, 8, 9)):
                                P.op("pe", lambda e, jj=jj, c=c, s=s, pj=pj: e.transpose(
                                    out=pd[:, jj * 128:(jj + 1) * 128], in_=pj[:, c, s * 128:(s + 1) * 128],
                                    identity=ident.t[:]),
                                    reads=[b_pj[c], ident.b], writes=[b_pd[0]])
                            P.op("act", lambda e, xi=xi: e.copy(out=xi.t[:], in_=pd[:, 0:512]),
                                 reads=[b_pd[0]], writes=[xi.b])
                            P.dma("act", O["o_swa"][seq, l, 0, pos0:pos0 + 128, :], xi.t[:, 0:128], reads=[xi.b])
                            P.dma("act", O["o_swa"][seq, l, 1, pos0:pos0 + 128, :], xi.t[:, 128:256], reads=[xi.b])
                            P.dma("act", O["o_ax"][seq, l, 0, pos0:pos0 + 128, :], xi.t[:, 256:384], reads=[xi.b])
                            P.dma("act", O["o_ax"][seq, l, 1, pos0:pos0 + 128, :], xi.t[:, 384:512], reads=[xi.b])
                P.barrier()

        def phase_WO(l):
            with ExitStack() as ph:
                hs = [P.sb([128, 8, TS], F32, ph) for _ in range(2)]
                b_hs = [[Buf() for _ in range(8)] for _ in range(2)]
                xs = [P.sb([128, 8, TS], BF16, ph) for _ in range(2)]
                b_xs = [[Buf() for _ in range(8)] for _ in range(2)]
                wout = P.sb([128, 8, D], BF16, ph)
                b_wout = Buf()
                P.dma("pool", wout[:], I["w_out"][l].rearrange("(k p) n -> p k n", p=128), writes=[b_wout])

                def ld(t):
                    i = t % 2
                    P.dma("sp", hs[i][:], hbuf[:, :, t * TS:(t + 1) * TS], reads=[b_hbuf[t]], writes=b_hs[i])
                    P.dma("sp", xs[i][:], mergT[:, :, t * TS:(t + 1) * TS], reads=[b_merg[t]], writes=b_xs[i])
                ld(0)
                for t in range(NT):
                    cf = coef[l][cond_of(t)]
                    i = t % 2
                    h_, bh_, x_, bx_ = hs[i], b_hs[i], xs[i], b_xs[i]
                    if t + 1 < NT:
                        ld(t + 1)
                    for m in range(8):
                        o_ = ops_[m % 2]
                        for k in range(8):
                            P.op("pe", lambda e, o_=o_, k=k, m=m, x_=x_: e.matmul(
                                o_.t, lhsT=wout[:, k, m * 128:(m + 1) * 128], rhs=x_[:, k, :],
                                start=(k == 0), stop=(k == 7)),
                                reads=[b_wout, bx_[k]], writes=[o_.b])
                        P.op("dve", lambda e, o_=o_, m=m, cf=cf, h_=h_: e.scalar_tensor_tensor(
                            out=h_[:, m, :], in0=o_.t, scalar=cf.t[:, 4, m:m + 1], in1=h_[:, m, :],
                            op0=ALU.mult, op1=ALU.add),
                            reads=[o_.b, bh_[m], cf.b], writes=[bh_[m]])
                    P.dma("act", hbuf[:, :, t * TS:(t + 1) * TS], h_[:], reads=bh_, writes=[b_hbuf[t]])
                P.barrier()

        have_mix = len(cfg.mixers) > 0
        MIX = NS()

        def zero_merg(chunks):
            with ExitStack() as ph:
                z = TB(P.sb([128, TS], BF16, ph))
                P.op("dve", lambda e: e.memset(z.t[:], 0.0), writes=[z.b])
                for t in range(NT):
                    for c in chunks:
                        P.dma("sp", mergT[:, c, t * TS:(t + 1) * TS], z.t[:], reads=[z.b], writes=[b_merg[t]])
                P.barrier()

        def mx_attn(l, grp, ph, SH):
            qc0, kc, vc, mch = (2, 4, 5, 2) if grp == "swa" else (6, 8, 9, 4)
            cache = I["cache_swa"] if grp == "swa" else I["cache_ax"]
            if True:
                NKB = L // 128
                K2 = [[P.sb([128, L], BF16, ph) for _ in range(2)] for _ in range(2)]
                b_K2 = [Buf(), Buf()]
                Kc2 = [[TB(P.sb([128, 256], BF16, ph)) for _ in range(2)] for _ in range(2)]
                for kv in range(2):
                    for hh in range(2):
                        P.op("pool", lambda e, kv=kv, hh=hh: e.memset(K2[kv][hh][:], 0.0), writes=[b_K2[kv]])
                        P.op("pool", lambda e, kv=kv, hh=hh: e.memset(Kc2[kv][hh].t[:], 0.0), writes=[Kc2[kv][hh].b])
                Vx = P.sb([128, NKB + 2, 2, 192], BF16, ph)
                b_Vx = Buf()
                P.op("pool", lambda e: e.memset(Vx[:], 1.0), writes=[b_Vx])
                if not hasattr(SH, "ropeC"):
                    SH.ropeC = TB(P.sb([128, L], F32, ph))
                    SH.ropeS = TB(P.sb([128, L], F32, ph))
                    SH.rotm = TB(P.sb([128, 128], F32, ph))
                    SH.identb = TB(P.sb([128, 128], BF16, ph))
                    SH.masks = TB(P.sb([128, 256], BF16, ph))
                    SH.raw = [TB(P.sb([128, TS], F32, ph)) for _ in range(4)]
                    SH.rctr = [0]
                    SH.t1b = [TB(P.sb([128, TS], F32, ph)) for _ in range(2)]
                    SH.t2b = [TB(P.sb([128, TS], F32, ph)) for _ in range(2)]
                    SH.qr = [TB(P.sb([128, TS], BF16, ph)) for _ in range(2)]
                    SH.qu = [TB(P.sb([128, TS], BF16, ph)) for _ in range(2)]
                    SH.pT = [TB(P.sb([128, TS], BF16, ph)) for _ in range(3)]
                    SH.mg = [TB(P.sb([128, TS], BF16, ph)) for _ in range(2)]
                    SH.rect = [TB(P.sb([128, TS], F32, ph)) for _ in range(2)]
                    SH.ckv = TB(P.sb([128, 2, 128], F32, ph))
                    SH.sk = TB(P.sb([1, 4], F32, ph))
                    SH.esrow = TB(P.sb([1, 4, TS], F32, ph))
                    SH.onesrow = TB(P.sb([1, TS], F32, ph))
                    SH.sel = TB(P.sb([1, 2, 128], F32, ph))
                    P.dma("sp", SH.ropeC.t[:], I["ropeC"], writes=[SH.ropeC.b])
                    P.dma("sp", SH.ropeS.t[:], I["ropeS"], writes=[SH.ropeS.b])
                    P.dma("sp", SH.rotm.t[:], I["rotm"], writes=[SH.rotm.b])
                    P.dma("pool", SH.masks.t[:], I["masks"], writes=[SH.masks.b])
                    P.op("dve", lambda e: e.tensor_copy(out=SH.identb.t[:], in_=ident.t[:]), reads=[ident.b], writes=[SH.identb.b])
                    P.op("dve", lambda e: e.memset(SH.onesrow.t[:], 1.0), writes=[SH.onesrow.b])
                    P.op("dve", lambda e: e.memset(SH.sel.t[:], 0.0), writes=[SH.sel.b])
                    P.op("dve", lambda e: e.memset(SH.sel.t[0:1, 0, 64:128], 1.0), writes=[SH.sel.b])
                    P.op("dve", lambda e: e.memset(SH.sel.t[0:1, 1, 0:64], 1.0), writes=[SH.sel.b])
                ropeC, ropeS, rotm, identb, masks = SH.ropeC, SH.ropeS, SH.rotm, SH.identb, SH.masks
                raw, rctr, t1b, t2b, qr, qu, pT, mg, rect = SH.raw, SH.rctr, SH.t1b, SH.t2b, SH.qr, SH.qu, SH.pT, SH.mg, SH.rect
                ckv, sk, esrow, onesrow, sel = SH.ckv, SH.sk, SH.esrow, SH.onesrow, SH.sel
                sbank = [gps[0], ups[0], gps[1], ups[1]]
                if grp == "swa":
                    P.dma("sp", sk.t[:], I["swa_sink"][l:l + 1, :], writes=[sk.b])
                    P.op("act", lambda e: e.activation(out=sk.t[:], in_=sk.t[:], func=AF.Exp), reads=[sk.b], writes=[sk.b])
                    for hd in range(4):
                        P.op("dve", lambda e, hd=hd: e.tensor_scalar(
                            out=esrow.t[0:1, hd, :], in0=onesrow.t[:], scalar1=sk.t[0:1, hd:hd + 1], scalar2=None,
                            op0=ALU.mult), reads=[sk.b, onesrow.b], writes=[esrow.b])

                def rope(src, dsts, dst_b, p0, w):
                    P.op("pe", lambda e: e.matmul(stat.t[:, 0:w], lhsT=rotm.t[:], rhs=src.t[:, 0:w], start=True, stop=True),
                         reads=[src.b, rotm.b], writes=[stat.b])
                    ta, tb_ = t1b[rctr[0] % 2], t2b[rctr[0] % 2]
                    P.op("dve", lambda e: e.tensor_tensor(out=ta.t[:, 0:w], in0=src.t[:, 0:w], in1=ropeC.t[:, p0:p0 + w], op=ALU.mult),
                         reads=[src.b, ropeC.b], writes=[ta.b])
                    P.op("dve", lambda e: e.tensor_tensor(out=tb_.t[:, 0:w], in0=stat.t[:, 0:w], in1=ropeS.t[:, p0:p0 + w], op=ALU.mult),
                         reads=[stat.b, ropeS.b], writes=[tb_.b])
                    for (dap, ps_) in dsts:
                        P.op("dve", lambda e, dap=dap, ps_=ps_: e.tensor_tensor(out=dap, in0=ta.t[ps_, 0:w], in1=tb_.t[ps_, 0:w], op=ALU.add),
                             reads=[ta.b, tb_.b], writes=[dst_b])

                def attn_seq(tok0, Ls, latent, kcol0, voff, bK, b_Vx):
                    TW = TS if (latent or Ls == TS) else 256
                    nqt = Ls // TW
                    nkb = Ls // 128
                    for kv in range(2):
                        for it in range(nqt):
                            c0 = tok0 + it * TW
                            r_ = raw[rctr[0] % 4]
                            rctr[0] += 1
                            for hh in range(2):
                                P.dma("sp", r_.t[hh * 64:(hh + 1) * 64, 0:TW], projT[kv * 64:(kv + 1) * 64, kc, c0:c0 + TW],
                                      reads=[b_proj[c0 // TS]], writes=[r_.b])
                            if latent:
                                rope(r_, [(K2[kv][0][0:64, kcol0 + it * TW:kcol0 + (it + 1) * TW], slice(0, 64)),
                                          (K2[kv][1][64:128, kcol0 + it * TW:kcol0 + (it + 1) * TW], slice(64, 128))], bK[kv], it * TW, TW)
                            else:
                                P.op("act", lambda e, r_=r_, kv=kv, it=it: e.copy(out=K2[kv][0][0:64, kcol0 + it * TW:kcol0 + (it + 1) * TW], in_=r_.t[0:64, 0:TW]),
                                     reads=[r_.b], writes=[bK[kv]])
                                P.op("act", lambda e, r_=r_, kv=kv, it=it: e.copy(out=K2[kv][1][64:128, kcol0 + it * TW:kcol0 + (it + 1) * TW], in_=r_.t[64:128, 0:TW]),
                                     reads=[r_.b], writes=[bK[kv]])
                    for it in range(nqt):
                        c0 = tok0 + it * TW
                        r_ = raw[rctr[0] % 4]
                        rctr[0] += 1
                        P.dma("sp", r_.t[:, 0:TW], projT[:, vc, c0:c0 + TW], reads=[b_proj[c0 // TS]], writes=[r_.b])
                        for s in range(TW // 128):
                            blk = voff + it * (TW // 128) + s
                            P.op("pe", lambda e, r_=r_, s=s: e.transpose(out=pmisc.t[:, 0:128], in_=r_.t[:, s * 128:(s + 1) * 128],
                                                                     identity=ident.t[:]),
                                 reads=[r_.b, ident.b], writes=[pmisc.b])
                            P.op("dve", lambda e, blk=blk: e.tensor_copy(
                                out=Vx[:, blk, :, 64:128], in_=pmisc.t[:, 0:128].rearrange("p (k d) -> p k d", k=2)),
                                reads=[pmisc.b], writes=[b_Vx])
                    if latent:
                        for cb in range(2):
                            for kv in range(2):
                                for hh in range(2):
                                    P.dma("sp", ckv.t[:, kv, hh * 64:(hh + 1) * 64],
                                          cache[l, 0, cb * 128:(cb + 1) * 128, kv * 64:(kv + 1) * 64], writes=[ckv.b])
                            for kv in range(2):
                                P.op("pe", lambda e, kv=kv: e.transpose(out=pmisc.t[:, 0:128], in_=ckv.t[:, kv, 0:128], identity=ident.t[:]),
                                     reads=[ckv.b, ident.b], writes=[pmisc.b])
                                P.op("dve", lambda e, kv=kv, cb=cb: e.tensor_copy(out=Kc2[kv][0].t[0:64, cb * 128:(cb + 1) * 128], in_=pmisc.t[0:64, 0:128]),
                                     reads=[pmisc.b], writes=[Kc2[kv][0].b])
                                P.op("dve", lambda e, kv=kv, cb=cb: e.tensor_copy(out=Kc2[kv][1].t[64:128, cb * 128:(cb + 1) * 128], in_=pmisc.t[64:128, 0:128]),
                                     reads=[pmisc.b], writes=[Kc2[kv][1].b])
                            P.dma("pool", Vx[:, cb, :, 64:128],
                                  cache[l, 1, cb * 128:(cb + 1) * 128, :].rearrange("p (k d) -> p k d", k=2), writes=[b_Vx])
                    def head_entries(it, qi, hh):
                        kv = qi
                        qu_, qr_ = qu[qi], (qr[qi] if latent else qu[qi])
                        ent = []
                        if latent:
                            for cb in range(2):
                                ent.append((Kc2[kv][hh].t[:, cb * 128:(cb + 1) * 128], Kc2[kv][hh].b, qu_, 0, TW, [], cb))
                        if latent and grp == "swa":
                            qb0 = it * 4
                            for j in range(max(0, qb0 - 1), min(nkb, qb0 + 5)):
                                lo = max(j - 1, qb0)
                                hi = min(j + 1, qb0 + 3)
                                ml = []
                                if j - 1 >= qb0 and j - 1 <= qb0 + 3:
                                    ml.append(((j - 1 - qb0) * 128, 1))
                                if j + 1 >= qb0 and j + 1 <= qb0 + 3:
                                    ml.append(((j + 1 - qb0) * 128, 0))
                                ent.append((K2[kv][hh][:, kcol0 + j * 128:kcol0 + (j + 1) * 128], bK[kv], qr_, (lo - qb0) * 128,
                                            (hi - qb0 + 1) * 128, ml, voff + j))
                        elif latent:
                            for j in range(nkb):
                                ent.append((K2[kv][hh][:, kcol0 + j * 128:kcol0 + (j + 1) * 128], bK[kv], qr_, 0, TW, [], voff + j))
                        else:
                            for j in range(nkb):
                                a_ = (j // 2) * L_CTX
                                ent.append((K2[kv][hh][:, kcol0 + j * 128:kcol0 + (j + 1) * 128], bK[kv], qr_, a_, a_ + L_CTX, [], voff + j))
                        return ent

                    def prep_q(it, qi):
                        c0 = tok0 + it * TW
                        r_ = raw[rctr[0] % 4]
                        rctr[0] += 1
                        P.dma("sp", r_.t[:, 0:TW], projT[:, qc0 + qi, c0:c0 + TW], reads=[b_proj[c0 // TS]], writes=[r_.b])
                        qu_ = qu[qi]
                        P.op("act", lambda e: e.copy(out=qu_.t[:, 0:TW], in_=r_.t[:, 0:TW]), reads=[r_.b], writes=[qu_.b])
                        if latent:
                            qr_ = qr[qi]
                            rope(r_, [(qr_.t[:, 0:TW], slice(0, 128))], qr_.b, it * TW, TW)

                    def run_queries():
                        items = [(it, qi) for it in range(nqt) for qi in range(2)]
                        flat = []
                        for k, (it, qi) in enumerate(items):
                            for hh in range(2):
                                ent = head_entries(it, qi, hh)
                                H = NS()
                                H.kv, H.hd, H.hh, H.qi, H.it = qi, 2 * qi + hh, hh, qi, it
                                H.pr = slice(hh * 64, hh * 64 + 64)
                                H.sr = slice(64 - hh * 64, 128 - hh * 64)
                                H.vs = slice(64, 192) if hh == 0 else slice(0, 128)
                                H.po = ops_[hh]
                                H.mg = mg[qi]
                                H.c0 = tok0 + it * TW
                                n = len(ent)
                                for i, en in enumerate(ent):
                                    flat.append((en, H, i, n, k, (hh == 0 and i == 0), (hh == 1 and i == n - 1)))
                        N = len(flat)

                        def emit_S(g, G):
                            (kap, kb_, q_, a, b, ml, vb), H, i, n, k, fi, li = flat[g]
                            sbk = sbank[G % 4]
                            nm = len(ml)
                            qap = q_.t[:, a:b]
                            P.op("pe", lambda e: e.matmul(sbk.t[:, a:b], lhsT=kap, rhs=qap, start=True, stop=(nm == 0)),
                                 reads=[kb_, q_.b], writes=[sbk.b])
                            for mi, (mc, mk) in enumerate(ml):
                                P.op("pe", lambda e, mc=mc, mk=mk, mi=mi: e.matmul(
                                    sbk.t[:, mc:mc + 128], lhsT=identb.t[:], rhs=masks.t[:, mk * 128:(mk + 1) * 128],
                                    start=False, stop=(mi == nm - 1)),
                                    reads=[identb.b, masks.b], writes=[sbk.b])

                        def emit_PV(g, G):
                            (kap, kb_, q_, a, b, ml, vb), H, i, n, k, fi, li = flat[g]
                            sbk = sbank[G % 4]
                            p_ = pT[G % 3]
                            po = H.po
                            P.op("act", lambda e: e.activation(out=p_.t[:, a:b], in_=sbk.t[:, a:b], func=AF.Exp, scale=0.125),
                                 reads=[sbk.b], writes=[p_.b])
                            last = (i == n - 1) and grp != "swa"
                            vap = Vx[:, vb, H.kv, H.vs]
                            P.op("pe", lambda e: e.matmul(po.t[:, a:b], lhsT=vap, rhs=p_.t[:, a:b], start=(i == 0), stop=last),
                                 reads=[b_Vx, p_.b], writes=[po.b])
                            if i == n - 1:
                                pr, sr, mg_ = H.pr, H.sr, H.mg
                                if grp == "swa":
                                    P.op("pe", lambda e: e.matmul(po.t[:, 0:TW], lhsT=sel.t[0:1, H.hh, :], rhs=esrow.t[0:1, H.hd, 0:TW],
                                                                  start=False, stop=True),
                                         reads=[sel.b, esrow.b], writes=[po.b])
                                rc = rect[H.hh]
                                P.op("dve", lambda e: e.reciprocal(out=rc.t[pr, 0:TW], in_=po.t[sr, 0:TW]),
                                     reads=[po.b], writes=[rc.b])
                                P.op("dve", lambda e: e.tensor_tensor(
                                    out=mg_.t[pr, 0:TW], in0=po.t[pr, 0:TW], in1=rc.t[pr, 0:TW], op=ALU.mult),
                                    reads=[po.b, rc.b], writes=[mg_.b])
                                if li:
                                    P.dma("sp", mergT[:, mch + H.qi, H.c0:H.c0 + TW], mg_.t[:, 0:TW], reads=[mg_.b],
                                          writes=[b_merg[H.c0 // TS]])

                        steps = []
                        for g in range(N):
                            fi, k = flat[g][5], flat[g][4]
                            before = (lambda: prep_q(*items[0])) if g == 0 else None
                            after = (lambda k=k: prep_q(*items[k + 1])) if (fi and k + 1 < len(items)) else None
                            steps.append((before, (lambda G, g=g: emit_S(g, G)), after, (lambda G, g=g: emit_PV(g, G))))
                        return steps
                    return run_queries

                G = NS()

                def lat():
                    return attn_seq(0, L, True, 0, 2, b_K2, b_Vx)

                def ctx():
                    fs = []
                    for s in range(NSEQ // 2):
                        bK = [Buf(), Buf()]
                        for kv in range(2):
                            bK[kv].last_w = b_K2[kv].last_w
                            bK[kv].readers = list(b_K2[kv].readers)
                        bV = Buf()
                        bV.last_w = b_Vx.last_w
                        bV.readers = list(b_Vx.readers)
                        fs.append(attn_seq(L + s * 2 * L_CTX, 2 * L_CTX, False, s * 2 * L_CTX, 2 + 4 * s, bK, bV))
                    return fs
                G.lat, G.ctx = lat, ctx
                return G

        def mx_attn_all(l):
            with ExitStack() as ph:
                SH = NS()
                A = mx_attn(l, "swa", ph, SH)
                B = mx_attn(l, "ax", ph, SH)
                def drive(step_lists):
                    allsteps = [s for sl in step_lists for s in sl]
                    n_ = len(allsteps)
                    for G in range(n_ + 2):
                        if G < n_:
                            before, S_, after, _ = allsteps[G]
                            if before is not None:
                                before()
                            S_(G)
                            if after is not None:
                                after()
                        if G >= 2:
                            allsteps[G - 2][3](G - 2)
                qa = A.lat()
                qb = B.lat()
                drive([qa(), qb()])
                fa = A.ctx()
                fb = B.ctx()
                drive([f() for f in fa + fb])
                P.barrier()

        def mx_fnet(l):
            with ExitStack() as ph:
                NTB = L // 128
                NKT = L // TS
                cs256 = TB(P.sb([128, 2, 512], BF16, ph))
                Ec = TB(P.sb([128, NTB, 512], BF16, ph))
                nEs = TB(P.sb([128, NTB, 512], BF16, ph))
                c256s = TB(P.sb([128, 2, 256], BF16, ph))
                ns256s = TB(P.sb([128, 2, 256], BF16, ph))
                phi = TB(P.sb([128, 4, 8], F32, ph))
                fw = TB(P.sb([128, 2, 256], BF16, ph))
                ufT = TB(P.sb([128, 2, L], BF16, ph))
                UCS = P.sb([128, NTB, 512], BF16, ph)
                b_UCS = [Buf() for _ in range(NTB)]
                Yf = TB(P.sb([128, 2, L], BF16, ph))
                tA = [TB(P.sb([128, 256], BF16, ph)) for _ in range(3)]
                tBm = [TB(P.sb([128, 256], BF16, ph)) for _ in range(3)]
                UA = [TB(P.sb([128, 256], BF16, ph)) for _ in range(3)]
                UB = [TB(P.sb([128, 256], BF16, ph)) for _ in range(3)]
                mgf = [TB(P.sb([128, TS], BF16, ph)) for _ in range(2)]
                sbank = [gps[0], ups[0], gps[1], ups[1]]
                P.dma("pool", cs256.t[:], I["cs256"].rearrange("c p k -> p c k"), writes=[cs256.b])
                P.dma("pool", Ec.t[:], I["fn_ec"].rearrange("(tb p) k -> p tb k", p=128), writes=[Ec.b])
                P.dma("pool", nEs.t[:], I["fn_nes"].rearrange("(tb p) k -> p tb k", p=128), writes=[nEs.b])
                P.dma("pool", c256s.t[:], I["fn_c256s"].rearrange("(tb p) k -> p tb k", p=128), writes=[c256s.b])
                P.dma("pool", ns256s.t[:], I["fn_ns256s"].rearrange("(tb p) k -> p tb k", p=128), writes=[ns256s.b])
                P.dma("sp", phi.t[:], I["fn_phi"], writes=[phi.b])
                P.dma("pool", fw.t[:], I["fnet_w"][l].rearrange("(j p) n -> p j n", p=128), writes=[fw.b])

                def fnet_seq(tok0, Ls, latent):
                    ntb = Ls // 128
                    for c in range(2):
                        for t0 in range(0, Ls, TS):
                            w = min(TS, Ls - t0)
                            P.dma("pool", ufT.t[:, c, t0:t0 + w], projT[:, 10 + c, tok0 + t0:tok0 + t0 + w],
                                  reads=[b_proj[(tok0 + t0) // TS]], writes=[ufT.b])
                    for tb in range(ntb):
                        sbk = sbank[tb % 4]
                        for c in range(2):
                            P.op("pe", lambda e, sbk=sbk, c=c, tb=tb: e.matmul(
                                sbk.t[:], lhsT=ufT.t[:, c, tb * 128:(tb + 1) * 128], rhs=cs256.t[:, c, :],
                                start=(c == 0), stop=(c == 1)), reads=[ufT.b, cs256.b], writes=[sbk.b])
                        if tb % 2 == 0:
                            P.op("act", lambda e, sbk=sbk, tb=tb: e.copy(out=UCS[:, tb, :], in_=sbk.t[:]),
                                 reads=[sbk.b], writes=[b_UCS[tb]])
                        else:
                            P.op("dve", lambda e, sbk=sbk, tb=tb: e.tensor_copy(out=UCS[:, tb, :], in_=sbk.t[:]),
                                 reads=[sbk.b], writes=[b_UCS[tb]])
                    if latent:
                        for kt in range(NKT):
                            yb = [ops_[0], ops_[1]]
                            for tb in range(ntb):
                                if kt == 0:
                                    ua_ap, ub_ap = UCS[:, tb, 0:256], UCS[:, tb, 256:512]
                                    ua_b, ub_b = b_UCS[tb], b_UCS[tb]
                                else:
                                    i3 = tb % 3
                                    ta_, tb2_, ua_, ub_ = tA[i3], tBm[i3], UA[i3], UB[i3]
                                    P.op("act", lambda e, ta_=ta_, tb=tb, kt=kt: e.activation(
                                        out=ta_.t[:], in_=UCS[:, tb, 0:256], func=AF.Identity, scale=phi.t[:, 0, kt:kt + 1]),
                                        reads=[b_UCS[tb], phi.b], writes=[ta_.b])
                                    P.op("dve", lambda e, ua_=ua_, ta_=ta_, tb=tb, kt=kt: e.scalar_tensor_tensor(
                                        out=ua_.t[:], in0=UCS[:, tb, 256:512], scalar=phi.t[:, 2, kt:kt + 1], in1=ta_.t[:],
                                        op0=ALU.mult, op1=ALU.add), reads=[b_UCS[tb], phi.b, ta_.b], writes=[ua_.b])
                                    P.op("act", lambda e, tb2_=tb2_, tb=tb, kt=kt: e.activation(
                                        out=tb2_.t[:], in_=UCS[:, tb, 0:256], func=AF.Identity, scale=phi.t[:, 1, kt:kt + 1]),
                                        reads=[b_UCS[tb], phi.b], writes=[tb2_.b])
                                    P.op("dve", lambda e, ub_=ub_, tb2_=tb2_, tb=tb, kt=kt: e.scalar_tensor_tensor(
                                        out=ub_.t[:], in0=UCS[:, tb, 256:512], scalar=phi.t[:, 0, kt:kt + 1], in1=tb2_.t[:],
                                        op0=ALU.mult, op1=ALU.add), reads=[b_UCS[tb], phi.b, tb2_.b], writes=[ub_.b])
                                    ua_ap, ub_ap = ua_.t[:], ub_.t[:]
                                    ua_b, ub_b = ua_.b, ub_.b
                                for j in range(2):
                                    P.op("pe", lambda e, j=j, tb=tb, ua_ap=ua_ap: e.matmul(
                                        yb[j].t, lhsT=ua_ap[:, j * 128:(j + 1) * 128], rhs=Ec.t[:, tb, :],
                                        start=(tb == 0), stop=False), reads=[ua_b, Ec.b], writes=[yb[j].b])
                                    P.op("pe", lambda e, j=j, tb=tb, ub_ap=ub_ap: e.matmul(
                                        yb[j].t, lhsT=ub_ap[:, j * 128:(j + 1) * 128], rhs=nEs.t[:, tb, :],
                                        start=False, stop=(tb == ntb - 1)), reads=[ub_b, nEs.b], writes=[yb[j].b])
                            P.op("act", lambda e, kt=kt: e.copy(out=Yf.t[:, 0, kt * TS:(kt + 1) * TS], in_=yb[0].t),
                                 reads=[yb[0].b], writes=[Yf.b])
                            P.op("dve", lambda e, kt=kt: e.tensor_copy(out=Yf.t[:, 1, kt * TS:(kt + 1) * TS], in_=yb[1].t),
                                 reads=[yb[1].b], writes=[Yf.b])
                    else:
                        for j in range(2):
                            yb = ops_[j]
                            for tb in range(2):
                                P.op("pe", lambda e, j=j, tb=tb, yb=yb: e.matmul(
                                    yb.t[:, 0:256], lhsT=UCS[:, tb, j * 128:(j + 1) * 128], rhs=c256s.t[:, tb, :],
                                    start=(tb == 0), stop=False), reads=[b_UCS[tb], c256s.b], writes=[yb.b])
                                P.op("pe", lambda e, j=j, tb=tb, yb=yb: e.matmul(
                                    yb.t[:, 0:256], lhsT=UCS[:, tb, 256 + j * 128:256 + (j + 1) * 128], rhs=ns256s.t[:, tb, :],
                                    start=False, stop=(tb == 1)), reads=[b_UCS[tb], ns256s.b], writes=[yb.b])
                            P.op("act", lambda e, j=j, yb=yb: e.copy(out=Yf.t[:, j, 0:256], in_=yb.t[:, 0:256]),
                                 reads=[yb.b], writes=[Yf.b])
                    TW = TS if latent else 256
                    for t0 in range(0, Ls, TW):
                        for jo in range(2):
                            sbk = sbank[jo]
                            for ji in range(2):
                                P.op("pe", lambda e, sbk=sbk, ji=ji, jo=jo, t0=t0: e.matmul(
                                    sbk.t[:, 0:TW], lhsT=fw.t[:, ji, jo * 128:(jo + 1) * 128], rhs=Yf.t[:, ji, t0:t0 + TW],
                                    start=(ji == 0), stop=(ji == 1)), reads=[fw.b, Yf.b], writes=[sbk.b])
                            m_ = mgf[jo]
                            P.op("act", lambda e, sbk=sbk, m_=m_, jo=jo: e.activation(
                                out=m_.t[:, 0:TW], in_=sbk.t[:, 0:TW], func=AF.Identity, bias=vec[l].t[:, 100 + jo:101 + jo], scale=1.0),
                                reads=[sbk.b, vec[l].b], writes=[m_.b])
                            P.dma("sp", mergT[:, 6 + jo, tok0 + t0:tok0 + t0 + TW], m_.t[:, 0:TW], reads=[m_.b],
                                  writes=[b_merg[(tok0 + t0) // TS]])

                fnet_seq(0, L, True)
                for s in range(NSEQ):
                    fnet_seq(L + s * L_CTX, L_CTX, False)
                P.barrier()

        def mx_ssm(l):
            T = 256
            TWO_PI = 6.283185307179586
            with ExitStack() as ph:
                Ere = TB(P.sb([128, 16, T], F32, ph))
                Eim = TB(P.sb([128, 16, T], F32, ph))
                rho_c = TB(P.sb([128, 16], F32, ph))
                BtR = TB(P.sb([128, 2, 2, 128], BF16, ph))
                BtI = TB(P.sb([128, 2, 2, 128], BF16, ph))
                Ct = TB(P.sb([128, 2, 3, 8, 128], BF16, ph))
                Dd = TB(P.sb([128, 2, 128], BF16, ph))
                wglu = TB(P.sb([128, 2, 256], BF16, ph))
                uT = TB(P.sb([128, 2, L], BF16, ph))
                yacc = TB(P.sb([128, 2, L], F32, ph))
                carry = TB(P.sb([128, 2, 8, 2], F32, ph))
                sbank = [gps[0], ups[0], gps[1], ups[1]]
                P.dma("pool", Ct.t[:, :, 0:2, :, :], I["ssm_ct"][l].rearrange("p (d r s c) -> p d r s c", d=2, r=2, s=8), writes=[Ct.b])
                P.op("act", lambda e: e.activation(out=Ct.t[:, :, 1, :, :], in_=Ct.t[:, :, 1, :, :], func=AF.Identity, scale=-1.0),
                     reads=[Ct.b], writes=[Ct.b])
                P.op("act", lambda e: e.activation(out=Ct.t[:, :, 2, :, :], in_=Ct.t[:, :, 0, :, :], func=AF.Identity, scale=-1.0),
                     reads=[Ct.b], writes=[Ct.b])
                P.dma("pool", Dd.t[:], I["ssm_dd"][l], writes=[Dd.b])
                P.dma("pool", wglu.t[:], I["ssm_w_glu"][l].rearrange("(j p) n -> p j n", p=128), writes=[wglu.b])

                def sincos(th, n, stk):
                    a2 = TB(P.sb([128, 2 * n], F32, stk))
                    ki = TB(P.sb([128, 2 * n], mybir.dt.int32, stk))
                    kf = TB(P.sb([128, 2 * n], F32, stk))
                    mk = TB(P.sb([128, 2 * n], F32, stk))
                    P.op("dve", lambda e: e.tensor_copy(out=a2.t[:, 0:n], in_=th.t[:]), reads=[th.b], writes=[a2.b])
                    P.op("dve", lambda e: e.tensor_scalar(out=a2.t[:, n:2 * n], in0=th.t[:], scalar1=TWO_PI / 4, scalar2=None, op0=ALU.add),
                         reads=[th.b], writes=[a2.b])
                    P.op("dve", lambda e: e.tensor_scalar(out=kf.t[:], in0=a2.t[:], scalar1=1.0 / TWO_PI, scalar2=0.5, op0=ALU.mult, op1=ALU.add),
                         reads=[a2.b], writes=[kf.b])
                    P.op("dve", lambda e: e.tensor_copy(out=ki.t[:], in_=kf.t[:]), reads=[kf.b], writes=[ki.b])
                    P.op("dve", lambda e: e.tensor_copy(out=kf.t[:], in_=ki.t[:]), reads=[ki.b], writes=[kf.b])
                    C1 = 6.28125
                    C2 = TWO_PI - C1
                    P.op("dve", lambda e: e.scalar_tensor_tensor(out=a2.t[:], in0=kf.t[:], scalar=-C1, in1=a2.t[:], op0=ALU.mult, op1=ALU.add),
                         reads=[kf.b, a2.b], writes=[a2.b])
                    P.op("dve", lambda e: e.scalar_tensor_tensor(out=a2.t[:], in0=kf.t[:], scalar=-C2, in1=a2.t[:], op0=ALU.mult, op1=ALU.add),
                         reads=[kf.b, a2.b], writes=[a2.b])
                    P.op("dve", lambda e: e.tensor_scalar(out=mk.t[:], in0=a2.t[:], scalar1=-TWO_PI / 2, scalar2=TWO_PI, op0=ALU.is_lt, op1=ALU.mult),
                         reads=[a2.b], writes=[mk.b])
                    P.op("dve", lambda e: e.tensor_tensor(out=a2.t[:], in0=a2.t[:], in1=mk.t[:], op=ALU.add), reads=[a2.b, mk.b], writes=[a2.b])
                    P.op("dve", lambda e: e.tensor_scalar(out=mk.t[:], in0=a2.t[:], scalar1=TWO_PI / 2, scalar2=-TWO_PI, op0=ALU.is_gt, op1=ALU.mult),
                         reads=[a2.b], writes=[mk.b])
                    P.op("dve", lambda e: e.tensor_tensor(out=a2.t[:], in0=a2.t[:], in1=mk.t[:], op=ALU.add), reads=[a2.b, mk.b], writes=[a2.b])
                    P.op("dve", lambda e: e.tensor_scalar(out=a2.t[:], in0=a2.t[:], scalar1=-3.1415925, scalar2=3.1415925, op0=ALU.max, op1=ALU.min),
                         reads=[a2.b], writes=[a2.b])
                    sc_ = TB(P.sb([128, 2 * n], F32, stk))
                    P.op("act", lambda e: e.activation(out=sc_.t[:], in_=a2.t[:], func=AF.Sin), reads=[a2.b], writes=[sc_.b])
                    return sc_

                def zoh(lre_ap, lim_ap, ldt_ap, n, srcb, stk):
                    dt_ = TB(P.sb([128, n], F32, stk))
                    er = TB(P.sb([128, n], F32, stk))
                    th = TB(P.sb([128, n], F32, stk))
                    rho = TB(P.sb([128, n], F32, stk))
                    P.op("act", lambda e: e.activation(out=dt_.t[:], in_=ldt_ap, func=AF.Exp), reads=[srcb], writes=[dt_.b])
                    P.op("dve", lambda e: e.tensor_tensor(out=er.t[:], in0=lre_ap, in1=dt_.t[:], op=ALU.mult), reads=[srcb, dt_.b], writes=[er.b])
                    P.op("dve", lambda e: e.tensor_tensor(out=th.t[:], in0=lim_ap, in1=dt_.t[:], op=ALU.mult), reads=[srcb, dt_.b], writes=[th.b])
                    P.op("act", lambda e: e.activation(out=rho.t[:], in_=er.t[:], func=AF.Exp), reads=[er.b], writes=[rho.b])
                    sc_ = sincos(th, n, stk)
                    return rho, sc_

                with ExitStack() as pp:
                    colP = TB(P.sb([128, 3, 256], F32, pp))
                    P.dma("sp", colP.t[:], I["ssm_colp"][l].rearrange("p (k s) -> p k s", k=3), writes=[colP.b])
                    rho, sc_ = zoh(colP.t[:, 0, :], colP.t[:, 1, :], colP.t[:, 2, :], 256, colP.b, pp)
                    P.op("act", lambda e, rho=rho: e.copy(out=rho_c.t[:], in_=rho.t[:].rearrange("p (j r) -> p j r", r=16)[:, :, 0]), reads=[rho.b], writes=[rho_c.b])
                    P.op("act", lambda e, sc_=sc_: e.copy(out=Eim.t[:, :, 0], in_=sc_.t[:, 0:256].rearrange("p (j r) -> p j r", r=16)[:, :, 0]), reads=[sc_.b], writes=[Eim.b])
                    P.op("act", lambda e, sc_=sc_: e.copy(out=Ere.t[:, :, 0], in_=sc_.t[:, 256:512].rearrange("p (j r) -> p j r", r=16)[:, :, 0]), reads=[sc_.b], writes=[Ere.b])
                    if cfg.debug:
                        P.dma("sp", O["dbg_rhofull"], rho.t[:], reads=[rho.b])
                        P.dma("sp", O["dbg_sc"], sc_.t[:], reads=[sc_.b])
                        P.dma("sp", O["dbg_colp"], colP.t[:].rearrange("p k s -> p (k s)"), reads=[colP.b])
                    tq = [TB(P.sb([128, 16, T // 2], F32, pp)) for _ in range(2)]
                    n = 1
                    while n < T:
                        cb_ = Ere.t[:, :, n - 1:n].to_broadcast([128, 16, n])
                        sb_ = Eim.t[:, :, n - 1:n].to_broadcast([128, 16, n])
                        a_, b_ = tq[0], tq[1]
                        P.op("dve", lambda e, n=n, cb_=cb_: e.tensor_tensor(out=a_.t[:, :, 0:n], in0=Ere.t[:, :, 0:n], in1=cb_, op=ALU.mult),
                             reads=[Ere.b], writes=[a_.b])
                        P.op("dve", lambda e, n=n, sb_=sb_: e.tensor_tensor(out=b_.t[:, :, 0:n], in0=Eim.t[:, :, 0:n], in1=sb_, op=ALU.mult),
                             reads=[Eim.b], writes=[b_.b])
                        P.op("dve", lambda e, n=n: e.tensor_tensor(out=Ere.t[:, :, n:2 * n], in0=a_.t[:, :, 0:n], in1=b_.t[:, :, 0:n], op=ALU.subtract),
                             reads=[a_.b, b_.b], writes=[Ere.b])
                        P.op("dve", lambda e, n=n, sb_=sb_: e.tensor_tensor(out=a_.t[:, :, 0:n], in0=Ere.t[:, :, 0:n], in1=sb_, op=ALU.mult),
                             reads=[Ere.b], writes=[a_.b])
                        P.op("dve", lambda e, n=n, cb_=cb_: e.tensor_tensor(out=b_.t[:, :, 0:n], in0=Eim.t[:, :, 0:n], in1=cb_, op=ALU.mult),
                             reads=[Eim.b], writes=[b_.b])
                        P.op("dve", lambda e, n=n: e.tensor_tensor(out=Eim.t[:, :, n:2 * n], in0=a_.t[:, :, 0:n], in1=b_.t[:, :, 0:n], op=ALU.add),
                             reads=[a_.b, b_.b], writes=[Eim.b])
                        n *= 2
                    rowP = TB(P.sb([128, 2, 3, 256], F32, pp))
                    P.dma("sp", rowP.t[:], I["ssm_rowp"][l].rearrange("p (d k q) -> p d k q", d=2, k=3), writes=[rowP.b])
                    braw = TB(P.sb([128, 2, 2, 256], F32, pp))
                    P.dma("sp", braw.t[:], I["ssm_bt"][l].rearrange("p (d r q) -> p d r q", d=2, r=2), writes=[braw.b])
                    for d in range(2):
                        lre, lim = rowP.t[:, d, 0, :], rowP.t[:, d, 1, :]
                        rho, sc_ = zoh(lre, lim, rowP.t[:, d, 2, :], 256, rowP.b, pp)
                        nr = TB(P.sb([128, 256], F32, pp))
                        ni = TB(P.sb([128, 256], F32, pp))
                        den = TB(P.sb([128, 256], F32, pp))
                        t_ = TB(P.sb([128, 256], F32, pp))
                        cr = TB(P.sb([128, 256], F32, pp))
                        ci = TB(P.sb([128, 256], F32, pp))
                        P.op("dve", lambda e, rho=rho, sc_=sc_, nr=nr: e.tensor_tensor(out=nr.t[:], in0=rho.t[:], in1=sc_.t[:, 256:512], op=ALU.mult),
                             reads=[rho.b, sc_.b], writes=[nr.b])
                        P.op("dve", lambda e, nr=nr: e.tensor_scalar(out=nr.t[:], in0=nr.t[:], scalar1=-1.0, scalar2=None, op0=ALU.add),
                             reads=[nr.b], writes=[nr.b])
                        P.op("dve", lambda e, rho=rho, sc_=sc_, ni=ni: e.tensor_tensor(out=ni.t[:], in0=rho.t[:], in1=sc_.t[:, 0:256], op=ALU.mult),
                             reads=[rho.b, sc_.b], writes=[ni.b])
                        P.op("dve", lambda e, den=den, lre=lre: e.tensor_tensor(out=den.t[:], in0=lre, in1=lre, op=ALU.mult), reads=[rowP.b], writes=[den.b])
                        P.op("dve", lambda e, t_=t_, lim=lim: e.tensor_tensor(out=t_.t[:], in0=lim, in1=lim, op=ALU.mult), reads=[rowP.b], writes=[t_.b])
                        P.op("dve", lambda e, den=den, t_=t_: e.tensor_tensor(out=den.t[:], in0=den.t[:], in1=t_.t[:], op=ALU.add), reads=[den.b, t_.b], writes=[den.b])
                        P.op("dve", lambda e, den=den: e.reciprocal(out=den.t[:], in_=den.t[:]), reads=[den.b], writes=[den.b])
                        P.op("dve", lambda e, cr=cr, nr=nr, lre=lre: e.tensor_tensor(out=cr.t[:], in0=nr.t[:], in1=lre, op=ALU.mult), reads=[nr.b, rowP.b], writes=[cr.b])
                        P.op("dve", lambda e, t_=t_, ni=ni, lim=lim: e.tensor_tensor(out=t_.t[:], in0=ni.t[:], in1=lim, op=ALU.mult), reads=[ni.b, rowP.b], writes=[t_.b])
                        P.op("dve", lambda e, cr=cr, t_=t_: e.tensor_tensor(out=cr.t[:], in0=cr.t[:], in1=t_.t[:], op=ALU.add), reads=[cr.b, t_.b], writes=[cr.b])
                        P.op("dve", lambda e, cr=cr, den=den: e.tensor_tensor(out=cr.t[:], in0=cr.t[:], in1=den.t[:], op=ALU.mult), reads=[cr.b, den.b], writes=[cr.b])
                        P.op("dve", lambda e, ci=ci, ni=ni, lre=lre: e.tensor_tensor(out=ci.t[:], in0=ni.t[:], in1=lre, op=ALU.mult), reads=[ni.b, rowP.b], writes=[ci.b])
                        P.op("dve", lambda e, t_=t_, nr=nr, lim=lim: e.tensor_tensor(out=t_.t[:], in0=nr.t[:], in1=lim, op=ALU.mult), reads=[nr.b, rowP.b], writes=[t_.b])
                        P.op("dve", lambda e, ci=ci, t_=t_: e.tensor_tensor(out=ci.t[:], in0=ci.t[:], in1=t_.t[:], op=ALU.subtract), reads=[ci.b, t_.b], writes=[ci.b])
                        P.op("dve", lambda e, ci=ci, den=den: e.tensor_tensor(out=ci.t[:], in0=ci.t[:], in1=den.t[:], op=ALU.mult), reads=[ci.b, den.b], writes=[ci.b])
                        bre, bim = braw.t[:, d, 0, :], braw.t[:, d, 1, :]
                        x1 = TB(P.sb([128, 256], F32, pp))
                        x2 = TB(P.sb([128, 256], F32, pp))
                        btr = BtR.t[:, d, :, :].rearrange("p c q -> p (c q)")
                        bti = BtI.t[:, d, :, :].rearrange("p c q -> p (c q)")
                        P.op("dve", lambda e, x1=x1, bre=bre, cr=cr: e.tensor_tensor(out=x1.t[:], in0=bre, in1=cr.t[:], op=ALU.mult), reads=[braw.b, cr.b], writes=[x1.b])
                        P.op("dve", lambda e, x2=x2, bim=bim, ci=ci: e.tensor_tensor(out=x2.t[:], in0=bim, in1=ci.t[:], op=ALU.mult), reads=[braw.b, ci.b], writes=[x2.b])
                        P.op("dve", lambda e, x1=x1, x2=x2, btr=btr: e.tensor_tensor(out=btr, in0=x1.t[:], in1=x2.t[:], op=ALU.subtract), reads=[x1.b, x2.b], writes=[BtR.b])
                        P.op("dve", lambda e, x1=x1, bre=bre, ci=ci: e.tensor_tensor(out=x1.t[:], in0=bre, in1=ci.t[:], op=ALU.mult), reads=[braw.b, ci.b], writes=[x1.b])
                        P.op("dve", lambda e, x2=x2, bim=bim, cr=cr: e.tensor_tensor(out=x2.t[:], in0=bim, in1=cr.t[:], op=ALU.mult), reads=[braw.b, cr.b], writes=[x2.b])
                        P.op("dve", lambda e, x1=x1, x2=x2, bti=bti: e.tensor_tensor(out=bti, in0=x1.t[:], in1=x2.t[:], op=ALU.add), reads=[x1.b, x2.b], writes=[BtI.b])
                    P.barrier()

                if cfg.debug:
                    P.dma("sp", O["dbg_E"][:, 0], Ere.t[:], reads=[Ere.b])
                    P.dma("sp", O["dbg_E"][:, 1], Eim.t[:], reads=[Eim.b])
                    P.dma("sp", O["dbg_rho"], rho_c.t[:], reads=[rho_c.b])
                    P.dma("pool", O["dbg_bt"][:, 0], BtR.t[:].rearrange("p d c q -> p (d c q)"), reads=[BtR.b])
                    P.dma("pool", O["dbg_bt"][:, 1], BtI.t[:].rearrange("p d c q -> p (d c q)"), reads=[BtI.b])
                NB = 4
                NBA = 8
                def mk(n, dt=F32):
                    return [TB(P.sb([128, T], dt, ph)) for _ in range(n)]
                bre_t, bim_t = mk(NBA), mk(NBA)
                t1, t2, t3, t4 = mk(NB), mk(NB), mk(NB), mk(NB)
                brp, bip = mk(NB), mk(NB)
                rr_t, ri_t = mk(NB), mk(NB)
                o1, o2, o3, o4 = mk(NB, BF16), mk(NB, BF16), mk(NB, BF16), mk(NB, BF16)
                ctmp = [TB(P.sb([128, 4], F32, ph)) for _ in range(NB)]
                y32 = [TB(P.sb([128, T], F32, ph)) for _ in range(2)]
                g1 = [TB(P.sb([128, T], F32, ph)) for _ in range(2)]
                g2 = [TB(P.sb([128, T], F32, ph)) for _ in range(2)]
                z32 = [TB(P.sb([128, T], F32, ph)) for _ in range(2)]
                zb = [TB(P.sb([128, T], BF16, ph)) for _ in range(2)]
                sg = [TB(P.sb([128, T], F32, ph)) for _ in range(2)]
                ob = [TB(P.sb([128, T], BF16, ph)) for _ in range(2)]
                st0 = TB(P.sb([128, 32], F32, ph))
                uctr = [0]

                def tt(eng, o, a, b, op, rb, wb):
                    P.op(eng, lambda e: e.tensor_tensor(out=o, in0=a, in1=b, op=op), reads=rb, writes=wb)

                def unit_pre(d, tc, sc):
                    cc, pg = sc // 4, sc % 4
                    rows = slice(pg * 32, pg * 32 + 32)
                    t0 = tc * T
                    i = uctr[0] % NB
                    ia = uctr[0] % NBA
                    sbk = sbank[uctr[0] % 4]
                    uctr[0] += 1
                    j = d * 8 + sc
                    u_ap = uT.t[rows, cc, t0:t0 + T]
                    if d == 1:
                        u_ap = u_ap[:, ::-1]
                    P.op("pe", lambda e: e.matmul(sbk.t[:, 0:T], lhsT=BtR.t[rows, d, cc, :], rhs=u_ap, start=True, stop=True,
                                                  tile_position=(pg * 32, 0)),
                         reads=[BtR.b, uT.b], writes=[sbk.b])
                    P.op("pe", lambda e: e.matmul(sbk.t[:, T:2 * T], lhsT=BtI.t[rows, d, cc, :], rhs=u_ap, start=True, stop=True,
                                                  tile_position=(pg * 32, 0)),
                         reads=[BtI.b, uT.b], writes=[sbk.b])
                    bre, bim = bre_t[ia], bim_t[ia]
                    P.op("act", lambda e: e.copy(out=bre.t[:], in_=sbk.t[:, 0:T]), reads=[sbk.b], writes=[bre.b])
                    P.op("act", lambda e: e.copy(out=bim.t[:], in_=sbk.t[:, T:2 * T]), reads=[sbk.b], writes=[bim.b])
                    return (d, tc, sc, i, j, ia)

                def unit_pre_b(stt):
                    d, tc, sc, i, j, ia = stt
                    bre, bim = bre_t[ia], bim_t[ia]
                    ec, es = Ere.t[:, j, :], Eim.t[:, j, :]
                    return [
                        lambda: tt("dve", t1[i].t[:], bre.t[:], ec, ALU.mult, [bre.b, Ere.b], [t1[i].b]),
                        lambda: tt("dve", t2[i].t[:], bim.t[:], es, ALU.mult, [bim.b, Eim.b], [t2[i].b]),
                        lambda: tt("dve", t3[i].t[:], bim.t[:], ec, ALU.mult, [bim.b, Ere.b], [t3[i].b]),
                        lambda: tt("dve", t4[i].t[:], bre.t[:], es, ALU.mult, [bre.b, Eim.b], [t4[i].b]),
                        lambda: tt("dve", brp[i].t[:], t1[i].t[:], t2[i].t[:], ALU.add, [t1[i].b, t2[i].b], [brp[i].b]),
                        lambda: tt("dve", bip[i].t[:], t3[i].t[:], t4[i].t[:], ALU.subtract, [t3[i].b, t4[i].b], [bip[i].b]),
                    ]

                def unit_post(stt, first, last_extra):
                    d, tc, sc, i, j, ia = stt
                    cc = sc // 4
                    ec, es = Ere.t[:, j, :], Eim.t[:, j, :]
                    rb_ = rho_c.t[:, j:j + 1].to_broadcast([128, T])
                    rr, ri = rr_t[i], ri_t[i]
                    dv = [
                        lambda: P.op("dve", lambda e: e.tensor_tensor_scan(out=rr.t[:], data0=rb_, data1=brp[i].t[:],
                                                                           initial=carry.t[:, d, sc, 0:1], op0=ALU.mult, op1=ALU.add),
                                     reads=[rho_c.b, brp[i].b, carry.b], writes=[rr.b]),
                        lambda: P.op("dve", lambda e: e.tensor_tensor_scan(out=ri.t[:], data0=rb_, data1=bip[i].t[:],
                                                                           initial=carry.t[:, d, sc, 1:2], op0=ALU.mult, op1=ALU.add),
                                     reads=[rho_c.b, bip[i].b, carry.b], writes=[ri.b]),
                        lambda: tt("dve", o1[i].t[:], rr.t[:], ec, ALU.mult, [rr.b, Ere.b], [o1[i].b]),
                        lambda: tt("dve", o3[i].t[:], rr.t[:], es, ALU.mult, [rr.b, Eim.b], [o3[i].b]),
                        lambda: tt("dve", o2[i].t[:], ri.t[:], es, ALU.mult, [ri.b, Eim.b], [o2[i].b]),
                        lambda: tt("dve", o4[i].t[:], ri.t[:], ec, ALU.mult, [ri.b, Ere.b], [o4[i].b]),
                    ]

                    def rest():
                        ct_ = ctmp[i]
                        ecl, esl = Ere.t[:, j, T - 1:T], Eim.t[:, j, T - 1:T]
                        rrl, ril = rr.t[:, T - 1:T], ri.t[:, T - 1:T]
                        P.op("act", lambda e: e.activation(out=ct_.t[:, 0:1], in_=rrl, func=AF.Identity, scale=ecl), reads=[rr.b, Ere.b], writes=[ct_.b])
                        P.op("act", lambda e: e.activation(out=ct_.t[:, 1:2], in_=rrl, func=AF.Identity, scale=esl), reads=[rr.b, Eim.b], writes=[ct_.b])
                        P.op("act", lambda e: e.activation(out=ct_.t[:, 2:3], in_=ril, func=AF.Identity, scale=esl), reads=[ri.b, Eim.b], writes=[ct_.b])
                        P.op("act", lambda e: e.activation(out=ct_.t[:, 3:4], in_=ril, func=AF.Identity, scale=ecl), reads=[ri.b, Ere.b], writes=[ct_.b])
                        P.op("act", lambda e: e.activation(out=carry.t[:, d, sc, 0:1], in_=ct_.t[:, 2:3], func=AF.Identity, scale=-1.0, bias=ct_.t[:, 0:1]),
                             reads=[ct_.b], writes=[carry.b])
                        P.op("act", lambda e: e.activation(out=carry.t[:, d, sc, 1:2], in_=ct_.t[:, 3:4], func=AF.Identity, scale=1.0, bias=ct_.t[:, 1:2]),
                             reads=[ct_.b], writes=[carry.b])
                        yp = ops_[cc]
                        aps = [o1[i].t[:], o2[i].t[:], o3[i].t[:], o4[i].t[:]]
                        if d == 1:
                            aps = [a_[:, ::-1] for a_ in aps]
                        var = [0, 2, 1, 1]
                        bufs = [o1[i].b, o2[i].b, o3[i].b, o4[i].b]
                        for q_ in range(4):
                            P.op("pe", lambda e, q_=q_: e.matmul(yp.t[:, 0:T], lhsT=Ct.t[:, d, var[q_], sc, :], rhs=aps[q_],
                                                                start=(first and q_ == 0), stop=(last_extra and q_ == 3)),
                                 reads=[Ct.b, bufs[q_]], writes=[yp.b])
                    return dv, rest

                def run_units(ulist, tail_fn):
                    LA, LB = 5, 2
                    n = len(ulist)
                    stts = [None] * n
                    for step in range(n + LA):
                        ia = step
                        ib = step - (LA - LB)
                        ip = step - LA
                        if ia < n:
                            d, tc, sc, first, last_extra = ulist[ia]
                            stts[ia] = unit_pre(d, tc, sc)
                        pre = unit_pre_b(stts[ib]) if 0 <= ib < n else []
                        if 0 <= ip < n:
                            d, tc, sc, first, last_extra = ulist[ip]
                            dv, rest = unit_post(stts[ip], first, last_extra)
                        else:
                            dv, rest = [], None
                        if pre and dv:
                            order = [dv[0], pre[0], dv[1], pre[1], dv[2], pre[2], dv[3], pre[3], dv[4], pre[4], dv[5], pre[5]]
                        else:
                            order = list(pre) + list(dv)
                        for th in order:
                            th()
                        if rest is not None:
                            rest()
                            d, tc, sc, first, last_extra = ulist[ip]
                            if sc == 7:
                                tail_fn(d, tc)

                def ssm_seq(tok0, Ls, latent, seq):
                    nT = Ls // T
                    for c in range(2):
                        for t0 in range(0, Ls, TS):
                            w = min(TS, Ls - t0)
                            P.dma("pool", uT.t[:, c, t0:t0 + w], projT[:, c, tok0 + t0:tok0 + t0 + w],
                                  reads=[b_proj[(tok0 + t0) // TS]], writes=[uT.b])
                    if latent:
                        P.dma("sp", st0.t[:], I["ssm_st0"][l], writes=[st0.b])
                        P.op("dve", lambda e: e.tensor_copy(out=carry.t[:].rearrange("p d s r -> p (d s r)"), in_=st0.t[:]),
                             reads=[st0.b], writes=[carry.b])
                    else:
                        P.op("dve", lambda e: e.memset(carry.t[:], 0.0), writes=[carry.b])
                    def tail(d, tc):
                        t0 = tc * T
                        if d == 0:
                            for cc in range(2):
                                P.op("act", lambda e, cc=cc, tc=tc: e.copy(out=yacc.t[:, cc, tc * T:(tc + 1) * T], in_=ops_[cc].t[:, 0:T]),
                                     reads=[ops_[cc].b], writes=[yacc.b])
                            return
                        for cc in range(2):
                            yp = ops_[cc]
                            P.op("pe", lambda e, cc=cc, yp=yp, t0=t0: e.matmul(yp.t[:, 0:T], lhsT=Dd.t[:, cc, :], rhs=uT.t[:, cc, t0:t0 + T],
                                                                        start=False, stop=True),
                                 reads=[Dd.b, uT.b], writes=[yp.b])
                            y_, a_, b_, z_, zb_ = y32[cc], g1[cc], g2[cc], z32[cc], zb[cc]
                            P.op("dve", lambda e, y_=y_, yp=yp, cc=cc, t0=t0: e.tensor_tensor(
                                out=y_.t[:], in0=yp.t[:, 0:T], in1=yacc.t[:, cc, t0:t0 + T], op=ALU.add),
                                reads=[yp.b, yacc.b], writes=[y_.b])
                            P.op("pool", lambda e, y_=y_, a_=a_: e.tensor_tensor(out=a_.t[:], in0=y_.t[:], in1=y_.t[:], op=ALU.mult),
                                 reads=[y_.b], writes=[a_.b])
                            P.op("pool", lambda e, a_=a_: e.tensor_scalar(out=a_.t[:], in0=a_.t[:], scalar1=0.044715, scalar2=1.0,
                                                                         op0=ALU.mult, op1=ALU.add), reads=[a_.b], writes=[a_.b])
                            P.op("pool", lambda e, a_=a_, b_=b_, y_=y_: e.tensor_tensor(out=b_.t[:], in0=a_.t[:], in1=y_.t[:], op=ALU.mult),
                                 reads=[a_.b, y_.b], writes=[b_.b])
                            P.op("act", lambda e, a_=a_, b_=b_: e.activation(out=a_.t[:], in_=b_.t[:], func=AF.Sigmoid, scale=1.5957691216057308),
                                 reads=[b_.b], writes=[a_.b])
                            P.op("pool", lambda e, a_=a_, z_=z_, y_=y_: e.tensor_tensor(out=z_.t[:], in0=a_.t[:], in1=y_.t[:], op=ALU.mult),
                                 reads=[a_.b, y_.b], writes=[z_.b])
                            P.op("act", lambda e, z_=z_, zb_=zb_: e.copy(out=zb_.t[:], in_=z_.t[:]), reads=[z_.b], writes=[zb_.b])
                        for jo in range(2):
                            gp_ = stat if jo == 0 else pmisc
                            for ji in range(2):
                                P.op("pe", lambda e, gp_=gp_, ji=ji, jo=jo: e.matmul(
                                    gp_.t[:, 0:T], lhsT=wglu.t[:, ji, jo * 128:(jo + 1) * 128], rhs=zb[ji].t[:],
                                    start=(ji == 0), stop=(ji == 1)), reads=[wglu.b, zb[ji].b], writes=[gp_.b])
                            P.op("act", lambda e, gp_=gp_, jo=jo: e.activation(
                                out=sg[jo].t[:], in_=gp_.t[:, 0:T], func=AF.Sigmoid, bias=vec[l].t[:, 98 + jo:99 + jo], scale=1.0),
                                reads=[gp_.b, vec[l].b], writes=[sg[jo].b])
                            P.op("pool", lambda e, jo=jo: e.tensor_tensor(out=ob[jo].t[:], in0=z32[jo].t[:], in1=sg[jo].t[:], op=ALU.mult),
                                 reads=[z32[jo].b, sg[jo].b], writes=[ob[jo].b])
                            P.dma("sp", mergT[:, jo, tok0 + t0:tok0 + t0 + T], ob[jo].t[:], reads=[ob[jo].b],
                                  writes=[b_merg[(tok0 + t0) // TS]])

                    ul = []
                    for tc in range(nT):
                        for sc in range(8):
                            ul.append((0, tc, sc, sc % 4 == 0, sc % 4 == 3))
                    for tc in reversed(range(nT)):
                        for sc in range(8):
                            ul.append((1, tc, sc, sc % 4 == 0, False))
                    run_units(ul, tail)
                    if not latent:
                        P.op("dve", lambda e, seq=seq: e.tensor_copy(
                            out=stout.t[:, seq, l, :], in_=carry.t[:].rearrange("p d s r -> p (d s r)")),
                            reads=[carry.b], writes=[stout.b])

                ssm_seq(0, L, True, -1)
                for s in range(NSEQ):
                    ssm_seq(L + s * L_CTX, L_CTX, False, s)
                P.barrier()

        def phase_MX(l):
            zc = []
            if "attn" in cfg.mixers:
                mx_attn_all(l)
            else:
                zc += [2, 3, 4, 5]
            if "fnet" in cfg.mixers:
                mx_fnet(l)
            else:
                zc += [6, 7]
            if "ssm" in cfg.mixers:
                mx_ssm(l)
            else:
                zc += [0, 1]
            if zc:
                zero_merg(zc)

        with ExitStack() as wst:
            W = alloc_W(wst)
            load_ffn(W, 0, 1)
            phase_X0()
            phase0()
            phase_F(W, 0, 1)
        for l in range(DP):
            phase_PJ(l)
            if have_mix:
                phase_MX(l)
            with ExitStack() as wst:
                W = alloc_W(wst)
                load_ffn(W, l, 2)
                if have_mix:
                    phase_WO(l)
                phase_F(W, l, 2, last=(l == DP - 1))
                if l + 1 < DP:
                    load_ffn(W, l + 1, 1)
                    phase_F(W, l + 1, 1)

        P.dma("sp", O["o_st"], stout.t[:].rearrange("p s l x -> p (s l x)"), reads=[stout.b])
        P.barrier()
        P.emit()
    return nc


_CACHE = {}


def _get_program(cfg_key):
    if cfg_key not in _CACHE:
        _CACHE[cfg_key] = build_program(Cfg(*cfg_key))
    return _CACHE[cfg_key]


def host_constants(cfg):
    f = np.float32
    L = cfg.l_lat
    C = {}
    t = np.arange(L)
    row = (t // 64).astype(f)
    col = (t % 64).astype(f)
    inv = (f(10000.0) ** (-np.arange(16, dtype=f) / f(16))).astype(f)
    ang = np.concatenate([row[:, None] * inv[None, :], col[:, None] * inv[None, :]], axis=1).astype(f)
    cosT = np.cos(ang).astype(f).T
    sinT = np.sin(ang).astype(f).T
    C["ropeC"] = np.ascontiguousarray(np.tile(cosT, (4, 1)))
    C["ropeS"] = np.ascontiguousarray(np.tile(sinT, (4, 1)))
    R = np.zeros((128, 128), f)
    for base in (0, 64):
        for i in range(32):
            R[base + i + 32, base + i] = -1.0
            R[base + i, base + i + 32] = 1.0
    C["rotm"] = R
    kp = np.arange(128)[:, None]
    qf = np.arange(128)[None, :]
    NEG = -30000.0
    m1 = np.where(qf <= kp, 0.0, NEG)
    m2 = np.where(kp <= qf, 0.0, NEG)
    C["masks"] = np.concatenate([m1, m2], axis=1).astype(f)
    c = np.arange(256)
    a256 = 2 * np.pi * np.outer(c, c) / 256.0
    cs = np.concatenate([np.cos(a256), np.sin(a256)], axis=1)
    C["cs256"] = cs.reshape(2, 128, 512).astype(f)
    scl = 1.0 / np.sqrt(256.0 * L)
    aL = 2 * np.pi * ((np.outer(np.arange(L), np.arange(512))) % L) / float(L)
    C["fn_ec"] = (np.cos(aL) * scl).astype(f)
    C["fn_nes"] = (-np.sin(aL) * scl).astype(f)
    C["fn_c256s"] = (np.cos(a256) / 256.0).astype(f)
    C["fn_ns256s"] = (-np.sin(a256) / 256.0).astype(f)
    nkt = L // 512
    ph = np.zeros((128, 4, 8), f)
    p_ = np.arange(128)[:, None]
    kt_ = np.arange(nkt)[None, :]
    aphi = 2 * np.pi * ((p_ * kt_) % nkt) / float(nkt)
    ph[:, 0, :nkt] = np.cos(aphi)
    ph[:, 1, :nkt] = np.sin(aphi)
    ph[:, 2, :nkt] = -np.sin(aphi)
    C["fn_phi"] = ph
    return C


def ssm_layouts(inp):
    f = np.float32
    out = {}
    lre = np.asarray(inp["ssm_lambda_re"], f)
    lim = np.asarray(inp["ssm_lambda_im"], f)
    ldt = np.repeat(np.asarray(inp["ssm_log_dt"], f)[..., None], 64, axis=-1)
    par = np.stack([lre, lim, ldt], axis=2)
    p8 = par.reshape(DEPTH, 2, 3, 8, 128)
    pc = p8.transpose(0, 4, 2, 1, 3)
    pc = np.repeat(pc[..., None], 16, axis=-1)
    out["ssm_colp"] = np.ascontiguousarray(pc).reshape(DEPTH, 128, 768)
    p_r = p8.reshape(DEPTH, 2, 3, 2, 4, 128)
    p_r = p_r.transpose(0, 4, 1, 2, 3, 5)
    p_r = np.repeat(p_r[:, :, None], 32, axis=2)
    out["ssm_rowp"] = np.ascontiguousarray(p_r).reshape(DEPTH, 128, 1536)
    bt = np.zeros((DEPTH, 4, 32, 2, 2, 2, 128), f)
    bb = np.stack([np.asarray(inp["ssm_b_re"], f), np.asarray(inp["ssm_b_im"], f)], axis=2)
    ct = np.zeros((DEPTH, 128, 2, 2, 8, 128), f)
    cc_ = np.stack([np.asarray(inp["ssm_c_re"], f), np.asarray(inp["ssm_c_im"], f)], axis=2)
    for sc in range(8):
        cc, pg = sc // 4, sc % 4
        for gg in range(2):
            g = 2 * sc + gg
            bt[:, pg, gg * 16:(gg + 1) * 16, :, :, cc, gg * 64:(gg + 1) * 64] = bb[:, :, :, g].transpose(0, 4, 1, 2, 3)
            ct[:, gg * 64:(gg + 1) * 64, :, :, sc, pg * 32 + gg * 16:pg * 32 + (gg + 1) * 16] = cc_[:, :, :, g].transpose(0, 4, 1, 2, 3)
    out["ssm_bt"] = bt.reshape(DEPTH, 128, 1024)
    out["ssm_ct"] = ct.reshape(DEPTH, 128, 4096)
    dd = np.zeros((DEPTH, 128, 2, 128), f)
    sd = np.asarray(inp["ssm_d"], f).reshape(DEPTH, 2, 128)
    for i in range(128):
        dd[:, i, :, i] = sd[:, :, i]
    out["ssm_dd"] = dd
    out["ssm_w_glu"] = np.ascontiguousarray(inp["ssm_w_glu"], f)
    return out


def ssm_state_in(st):
    s = np.asarray(st, np.float32).reshape(DEPTH, 2, 8, 128, 2)
    return np.ascontiguousarray(s.transpose(0, 3, 1, 2, 4)).reshape(DEPTH, 128, 32)


def ssm_state_out(o):
    s = np.asarray(o, np.float32).reshape(128, NSEQ, DEPTH, 2, 8, 2)
    s = s.transpose(1, 2, 3, 4, 0, 5)
    return np.ascontiguousarray(s).reshape(NSEQ, DEPTH, 2, 16, 64, 2)


def make_in_maps(inp, cfg):
    f = np.float32
    shared = {
        "c_ctx": np.ascontiguousarray(inp["c_ctx"], f).reshape(8, 128),
        "w_mod": np.ascontiguousarray(inp["w_mod"], f),
        "b_mod": np.ascontiguousarray(inp["b_mod"], f).reshape(DEPTH, 72, 128),
        "final_norm": np.ascontiguousarray(inp["final_norm"], f).reshape(8, 128),
        "ident": np.eye(128, dtype=f),
        "w_in": np.ascontiguousarray(inp["w_in"], f),
        "w_out": np.ascontiguousarray(inp["w_out"], f),
        "ssm_d": np.ascontiguousarray(inp["ssm_d"], f).reshape(DEPTH, 2, 128),
        "ssm_b_glu": np.ascontiguousarray(inp["ssm_b_glu"], f).reshape(DEPTH, 2, 128),
        "fnet_b": np.ascontiguousarray(inp["fnet_b"], f).reshape(DEPTH, 2, 128),
        "ax_q_norm": np.ascontiguousarray(inp["ax_q_norm"], f).reshape(DEPTH, 1, 64),
        "ax_k_norm": np.ascontiguousarray(inp["ax_k_norm"], f).reshape(DEPTH, 1, 64),
    }
    for n in ("norm_ffn1", "norm_mix", "norm_ffn2"):
        shared[n] = np.ascontiguousarray(inp[n], f).reshape(DEPTH, 8, 128)
    shared.update(host_constants(cfg))
    shared["swa_sink"] = np.ascontiguousarray(inp["swa_sink"], f)
    shared["fnet_w"] = np.ascontiguousarray(inp["fnet_w"], f)
    shared.update(ssm_layouts(inp))
    for n in ("ffn1_w_gate", "ffn1_w_up", "ffn2_w_gate", "ffn2_w_up", "ffn1_w_down", "ffn2_w_down"):
        shared[n] = np.ascontiguousarray(inp[n], f)
    maps = []
    xp = np.asarray(inp["x_prompt"], f)
    xs = np.asarray(inp["x_sample"], f)
    for c in range(NCORE):
        m = dict(shared)
        m["x_lat"] = np.ascontiguousarray(xs[c, :cfg.l_lat])
        m["x_ctx"] = np.ascontiguousarray(xp[c * NSEQ:(c + 1) * NSEQ]).reshape(NSEQ * L_CTX, D)
        m["c_b"] = np.ascontiguousarray(inp["c"][c], f).reshape(8, 128)
        m["cache_swa"] = np.ascontiguousarray(inp["cache_swa_kv"][c], f).reshape(DEPTH, 2, L_CTX, 128)
        m["cache_ax"] = np.ascontiguousarray(inp["cache_axial_kv"][c], f).reshape(DEPTH, 2, L_CTX, 128)
        m["ssm_st0"] = ssm_state_in(inp["state_ssm"][c])
        maps.append(m)
    return maps


def kernel(**inp):
    cfg_key = (4096, ("fnet", "attn", "ssm"), DEPTH)
    cfg = Cfg(*cfg_key)
    nc = _get_program(cfg_key)
    maps = make_in_maps(inp, cfg)
    res = run_bass_kernel_spmd(nc, maps, core_ids=list(range(NCORE)))
    R = res.results
    y_prompt = np.concatenate([R[c]["y_ctx"].reshape(NSEQ, L_CTX, D) for c in range(NCORE)], axis=0)
    y_sample = np.stack([R[c]["y_lat"] for c in range(NCORE)], axis=0)
    o_swa = np.concatenate([R[c]["o_swa"].reshape(NSEQ, DEPTH, 2, L_CTX, 2, 64) for c in range(NCORE)], axis=0)
    o_ax = np.concatenate([R[c]["o_ax"].reshape(NSEQ, DEPTH, 2, L_CTX, 2, 64) for c in range(NCORE)], axis=0)
    o_st = np.concatenate([ssm_state_out(R[c]["o_st"]) for c in range(NCORE)], axis=0)
    return (y_prompt.astype(np.float32), y_sample.astype(np.float32), o_swa.astype(np.float32),
            o_ax.astype(np.float32), o_st.astype(np.float32))
```
